# Optimizing a Trainium2 kernel written in Bass

```python
import math
import jax
import jax.numpy as jnp
from jax import lax
import numpy as np

D_MODEL = 1024
BATCH = 8
SEQ = 2048
DEPTH = 4

N_MIXERS = 4
N_META = 16
CHUNK = 64
PAD = CHUNK - N_META
D_FF = 4 * D_MODEL
NORM_EPS = 1e-5
N_RWKV_LAYERS = (DEPTH + 3) // N_MIXERS
N_SSD_LAYERS = (DEPTH + 2) // N_MIXERS
N_GLA_LAYERS = (DEPTH + 1) // N_MIXERS
N_RET_LAYERS = DEPTH // N_MIXERS

RWKV_HEAD = 64
RWKV_HEADS = D_MODEL // RWKV_HEAD
RWKV_DECAY_LORA = 64
RWKV_AAA_LORA = 64
RWKV_GATE_LORA = 160
RWKV_GN_EPS = 64e-5

M2_D_INNER = 2 * D_MODEL
M2_HEAD = 64
M2_HEADS = M2_D_INNER // M2_HEAD
M2_GROUPS = 8
M2_HPG = M2_HEADS // M2_GROUPS
M2_STATE = 128
M2_CONV = 4
M2_CONV_DIM = M2_D_INNER + 2 * M2_GROUPS * M2_STATE
M2_PROJ = M2_D_INNER + M2_CONV_DIM + M2_HEADS

GLA_HEADS = 4
GLA_DK = (D_MODEL // 2) // GLA_HEADS
GLA_DV = D_MODEL // GLA_HEADS
GLA_GATE_LORA = 16
GLA_TAU = 16.0
GLA_PROJ = 2 * GLA_HEADS * GLA_DK + 2 * GLA_HEADS * GLA_DV + GLA_GATE_LORA

RET_HEADS = 4
RET_DK = D_MODEL // RET_HEADS
RET_DV = 2 * D_MODEL // RET_HEADS
RET_PROJ = 2 * RET_HEADS * RET_DK + 2 * RET_HEADS * RET_DV
ROPE_BASE = 10000.0

kernel_name = 'hybrid_rwkv7_ssd_gla_retnet_trunk'


def _rms(x, eps=NORM_EPS):
    xf = x.astype(jnp.float32)
    return xf * lax.rsqrt(jnp.mean(xf * xf, axis=-1, keepdims=True) + eps)


def rmsnorm(x, w):
    return (_rms(x) * w.astype(jnp.float32)).astype(x.dtype)


def _layernorm(x, eps):
    xf = x.astype(jnp.float32)
    mu = jnp.mean(xf, axis=-1, keepdims=True)
    var = jnp.mean(jnp.square(xf - mu), axis=-1, keepdims=True)
    return (xf - mu) * lax.rsqrt(var + eps)


def rotary(x, pos):
    half = x.shape[-1] // 2
    inv_freq = 1.0 / (ROPE_BASE ** jnp.linspace(0.0, 1.0, half, dtype=jnp.float32))
    ang = pos[:, None] * inv_freq[None, :]
    cos = jnp.cos(ang)[:, None, :].astype(x.dtype)
    sin = jnp.sin(ang)[:, None, :].astype(x.dtype)
    x1, x2 = x[..., :half], x[..., half:]
    return jnp.concatenate([x1 * cos - x2 * sin, x1 * sin + x2 * cos], axis=-1)


def _to_chunks(a, t_axis):
    pad = [(0, 0)] * a.ndim
    pad[t_axis] = (PAD, 0)
    a = jnp.pad(a, pad)
    n = a.shape[t_axis] // CHUNK
    a = a.reshape(a.shape[:t_axis] + (n, CHUNK) + a.shape[t_axis + 1:])
    return jnp.moveaxis(a, t_axis, 0)


def _from_chunks(y, t_axis):
    y = jnp.moveaxis(y, 0, t_axis)
    y = y.reshape(y.shape[:t_axis] + (-1,) + y.shape[t_axis + 2:])
    return lax.slice_in_dim(y, PAD, y.shape[t_axis], axis=t_axis)


def chunked_scalar_decay(q, k, v, log_a):
    B, G, R, _, dv = v.shape
    dk = q.shape[-1]
    causal = jnp.tril(jnp.ones((CHUNK, CHUNK), dtype=bool))

    def step(S, inp):
        qc, kc, vc, ac = inp
        cum = jnp.cumsum(ac, axis=-1)
        diff = cum[..., :, None] - cum[..., None, :]
        decay = jnp.exp(jnp.where(causal, diff, -jnp.inf))
        scores = jnp.einsum('bgik,bgjk->bgij', qc, kc)[:, :, None] * decay
        y = jnp.einsum('bgrij,bgrjv->bgriv', scores, vc)
        y = y + jnp.exp(cum)[..., None] * jnp.einsum('bgik,bgrkv->bgriv', qc, S)
        last = cum[..., -1:]
        S = jnp.exp(last)[..., None] * S + jnp.einsum('bgjk,bgrj,bgrjv->bgrkv', kc, jnp.exp(last - cum), vc)
        return S, y

    S0 = jnp.zeros((B, G, R, dk, dv), v.dtype)
    _, ys = lax.scan(step, S0, (_to_chunks(q, 2), _to_chunks(k, 2), _to_chunks(v, 3), _to_chunks(log_a, 3)))
    return _from_chunks(ys, 3)


def chunked_vector_decay(q, k, v, log_g):
    B, H, _, dk = q.shape
    dv = v.shape[-1]
    causal = jnp.tril(jnp.ones((CHUNK, CHUNK), dtype=bool))[:, :, None]

    def step(S, inp):
        qc, kc, vc, gc = inp
        cum = jnp.cumsum(gc, axis=2)
        diff = cum[:, :, :, None, :] - cum[:, :, None, :, :]
        decay = jnp.exp(jnp.where(causal, diff, -jnp.inf))
        scores = jnp.einsum('bhik,bhijk,bhjk->bhij', qc, decay, kc)
        y = jnp.einsum('bhij,bhjv->bhiv', scores, vc)
        y = y + jnp.einsum('bhik,bhkv->bhiv', qc * jnp.exp(cum), S)
        last = cum[:, :, -1:, :]
        S = jnp.exp(last[:, :, 0])[..., None] * S + jnp.einsum('bhjk,bhjv->bhkv', kc * jnp.exp(last - cum), vc)
        return S, y

    S0 = jnp.zeros((B, H, dk, dv), v.dtype)
    _, ys = lax.scan(step, S0, (_to_chunks(q, 2), _to_chunks(k, 2), _to_chunks(v, 2), _to_chunks(log_g, 2)))
    return _from_chunks(ys, 2)


def _rwkv7_scan(r, w, k, v, kk, a):
    B, L, H, N = r.shape

    def step(S, inp):
        rt, wt, kt, vt, kkt, at = inp
        sa = jnp.einsum('bhvk,bhk->bhv', S, -kkt)
        S = S * wt[:, :, None, :] + sa[..., None] * (kkt * at)[:, :, None, :] + vt[..., None] * kt[:, :, None, :]
        return S, jnp.einsum('bhvk,bhk->bhv', S, rt)

    S0 = jnp.zeros((B, H, N, N), jnp.float32)
    seq = tuple(jnp.moveaxis(t.astype(jnp.float32), 1, 0) for t in (r, w, k, v, kk, a))
    _, ys = lax.scan(step, S0, seq)
    return jnp.moveaxis(ys, 0, 1)


def rwkv7_mix(u, mu, w_r, w_k, w_v, w0, w_lora_a, w_lora_b, a0, a_lora_a, a_lora_b,
              g_lora_a, g_lora_b, k_k, k_a, r_k, ln_w, ln_b, w_o):
    B, L, D = u.shape
    xx = jnp.pad(u, ((0, 0), (1, 0), (0, 0)))[:, :-1] - u
    xr, xw, xk, xv, xa, xg = [u + xx * mu[i] for i in range(6)]
    r = xr @ w_r
    k = xk @ w_k
    v = xv @ w_v
    w = -jax.nn.softplus(-(w0 + jnp.tanh(xw @ w_lora_a) @ w_lora_b)) - 0.5
    a = jax.nn.sigmoid(a0 + (xa @ a_lora_a) @ a_lora_b)
    g = jax.nn.sigmoid(xg @ g_lora_a) @ g_lora_b
    heads = lambda t: t.reshape(B, L, RWKV_HEADS, RWKV_HEAD)
    kkf = heads(k * k_k).astype(jnp.float32)
    kk = (kkf / jnp.maximum(jnp.sqrt(jnp.sum(kkf * kkf, axis=-1, keepdims=True)), 1e-12)).astype(u.dtype)
    k = k * (1.0 + (a - 1.0) * k_a)
    decay = jnp.exp(-jnp.exp(w))
    y = _rwkv7_scan(heads(r), heads(decay), heads(k), heads(v), kk, heads(a))
    y = _layernorm(y, RWKV_GN_EPS).reshape(B, L, D) * ln_w + ln_b
    bonus = (jnp.sum(heads(r) * heads(k) * r_k, axis=-1, keepdims=True) * heads(v)).reshape(B, L, D)
    return ((y.astype(u.dtype) + bonus) * g) @ w_o


def mamba2_mix(u, in_proj, conv_w, conv_b, dt_bias, a_log, d_skip, norm_w, out_proj):
    B, L, _ = u.shape
    z, xbc, dt = jnp.split(u @ in_proj, [M2_D_INNER, M2_D_INNER + M2_CONV_DIM], axis=-1)
    xbc = lax.conv_general_dilated(xbc, conv_w[:, None, :], window_strides=(1,), padding=[(M2_CONV - 1, 0)],
                                   dimension_numbers=('NWC', 'WIO', 'NWC'),
                                   feature_group_count=M2_CONV_DIM) + conv_b
    xbc = jax.nn.silu(xbc)
    xs, b_in, c_in = jnp.split(xbc, [M2_D_INNER, M2_D_INNER + M2_GROUPS * M2_STATE], axis=-1)
    dt = jax.nn.softplus(dt + dt_bias).reshape(B, L, M2_GROUPS, M2_HPG)
    a = -jnp.exp(a_log).reshape(M2_GROUPS, M2_HPG)
    xs = xs.reshape(B, L, M2_GROUPS, M2_HPG, M2_HEAD)
    q = c_in.reshape(B, L, M2_GROUPS, M2_STATE).transpose(0, 2, 1, 3)
    k = b_in.reshape(B, L, M2_GROUPS, M2_STATE).transpose(0, 2, 1, 3)
    v = (xs * dt[..., None]).transpose(0, 2, 3, 1, 4)
    log_a = (dt * a).transpose(0, 2, 3, 1)
    y = chunked_scalar_decay(q, k, v, log_a).transpose(0, 3, 1, 2, 4)
    y = y + d_skip.reshape(M2_GROUPS, M2_HPG)[..., None] * xs
    y = y.reshape(B, L, M2_D_INNER) * jax.nn.silu(z)
    y = _rms(y.reshape(B, L, M2_GROUPS, M2_D_INNER // M2_GROUPS)).reshape(B, L, M2_D_INNER)
    return (y * norm_w).astype(u.dtype) @ out_proj


def gla_mix(u, in_proj, gate_up, gate_bias, norm_w, out_proj):
    B, L, _ = u.shape
    qk = GLA_HEADS * GLA_DK
    vd = GLA_HEADS * GLA_DV
    q, k, v, r, glr = jnp.split(u @ in_proj, [qk, 2 * qk, 2 * qk + vd, 2 * qk + 2 * vd], axis=-1)
    gk = (jax.nn.log_sigmoid((glr @ gate_up + gate_bias).astype(jnp.float32)) / GLA_TAU).astype(u.dtype)
    heads = lambda t, d: t.reshape(B, L, GLA_HEADS, d).transpose(0, 2, 1, 3)
    o = chunked_vector_decay(heads(q, GLA_DK) * GLA_DK ** -0.5, heads(k, GLA_DK), heads(v, GLA_DV), heads(gk, GLA_DK))
    o = (_rms(o) * norm_w).astype(u.dtype).transpose(0, 2, 1, 3).reshape(B, L, vd)
    return (o * jax.nn.silu(r)) @ out_proj


def retnet_mix(u, pos, in_proj, out_proj):
    B, L, _ = u.shape
    qd = RET_HEADS * RET_DK
    vd = RET_HEADS * RET_DV
    q, k, v, g = jnp.split(u @ in_proj, [qd, 2 * qd, 2 * qd + vd], axis=-1)
    q = rotary(q.reshape(B, L, RET_HEADS, RET_DK), pos).transpose(0, 2, 1, 3)
    k = (rotary(k.reshape(B, L, RET_HEADS, RET_DK), pos) * RET_DK ** -0.5).transpose(0, 2, 1, 3)
    v = v.reshape(B, L, RET_HEADS, RET_DV).transpose(0, 2, 1, 3)[:, :, None]
    log_gamma = jnp.log1p(-jnp.exp2(-5.0 - jnp.arange(RET_HEADS, dtype=jnp.float32))).astype(u.dtype)
    log_a = jnp.broadcast_to(log_gamma[None, :, None, None], (B, RET_HEADS, 1, L))
    y = chunked_scalar_decay(q, k, v, log_a)[:, :, 0]
    y = _layernorm(y, NORM_EPS).astype(u.dtype).transpose(0, 2, 1, 3).reshape(B, L, vd)
    return (jax.nn.silu(g) * y) @ out_proj


def sqrelu_mlp(u, w_in, w_out):
    return jnp.square(jax.nn.relu(u @ w_in)) @ w_out


def setup_inputs(seed: int = 0) -> dict:
    key = jax.random.key(seed)
    keys = iter(jax.random.split(key, 64))
    f32 = jnp.float32
    D = D_MODEL

    def nrm(shape, fan_in):
        return jax.random.normal(next(keys), shape, f32) * fan_in ** -0.5

    def noise(shape, s):
        return s * jax.random.normal(next(keys), shape, f32)

    def gain(shape):
        return 1.0 + noise(shape, 0.02)

    NA, NB, NC, ND = N_RWKV_LAYERS, N_SSD_LAYERS, N_GLA_LAYERS, N_RET_LAYERS
    dt = jnp.exp(jax.random.uniform(next(keys), (NB, M2_HEADS), f32) * (math.log(0.1) - math.log(1e-3)) + math.log(1e-3))
    inp = {
        'x': jax.random.normal(next(keys), (BATCH, SEQ, D), f32),
        'meta_tokens': noise((N_META, D), 1.0),
        'norm_mix': gain((DEPTH, D)),
        'norm_mlp': gain((DEPTH, D)),
        'norm_final': gain((D,)),
        'mlp_w_in': nrm((DEPTH, D, D_FF), D),
        'mlp_w_out': nrm((DEPTH, D_FF, D), D_FF),
        'rwkv_mu': jax.random.uniform(next(keys), (NA, 6, D), f32),
        'rwkv_w_r': nrm((NA, D, D), D),
        'rwkv_w_k': nrm((NA, D, D), D),
        'rwkv_w_v': nrm((NA, D, D), D),
        'rwkv_w0': jnp.linspace(-6.0, -1.0, D, dtype=f32)[None, :] + noise((NA, D), 0.1),
        'rwkv_w_lora_a': nrm((NA, D, RWKV_DECAY_LORA), D),
        'rwkv_w_lora_b': nrm((NA, RWKV_DECAY_LORA, D), RWKV_DECAY_LORA),
        'rwkv_a0': noise((NA, D), 0.1),
        'rwkv_a_lora_a': nrm((NA, D, RWKV_AAA_LORA), D),
        'rwkv_a_lora_b': nrm((NA, RWKV_AAA_LORA, D), RWKV_AAA_LORA),
        'rwkv_g_lora_a': nrm((NA, D, RWKV_GATE_LORA), D),
        'rwkv_g_lora_b': nrm((NA, RWKV_GATE_LORA, D), RWKV_GATE_LORA),
        'rwkv_k_k': 0.85 + noise((NA, D), 0.05),
        'rwkv_k_a': 1.0 + noise((NA, D), 0.05),
        'rwkv_r_k': noise((NA, RWKV_HEADS, RWKV_HEAD), 0.1),
        'rwkv_ln_w': gain((NA, D)),
        'rwkv_ln_b': noise((NA, D), 0.01),
        'rwkv_w_o': nrm((NA, D, D), D),
        'm2_in_proj': nrm((NB, D, M2_PROJ), D),
        'm2_conv_w': nrm((NB, M2_CONV, M2_CONV_DIM), M2_CONV),
        'm2_conv_b': noise((NB, M2_CONV_DIM), 0.01),
        'm2_dt_bias': dt + jnp.log(-jnp.expm1(-dt)),
        'm2_a_log': jnp.log(jax.random.uniform(next(keys), (NB, M2_HEADS), f32, minval=1.0, maxval=16.0)),
        'm2_d': gain((NB, M2_HEADS)),
        'm2_norm_w': gain((NB, M2_D_INNER)),
        'm2_out_proj': nrm((NB, M2_D_INNER, D), M2_D_INNER),
        'gla_in_proj': nrm((NC, D, GLA_PROJ), D),
        'gla_gate_up': nrm((NC, GLA_GATE_LORA, GLA_HEADS * GLA_DK), GLA_GATE_LORA),
        'gla_gate_bias': noise((NC, GLA_HEADS * GLA_DK), 0.5),
        'gla_norm_w': gain((NC, GLA_DV)),
        'gla_out_proj': nrm((NC, GLA_HEADS * GLA_DV, D), GLA_HEADS * GLA_DV),
        'ret_in_proj': nrm((ND, D, RET_PROJ), D),
        'ret_out_proj': nrm((ND, RET_HEADS * RET_DV, D), RET_HEADS * RET_DV),
    }
    return inp


def reference(x, meta_tokens, norm_mix, norm_mlp, norm_final, mlp_w_in, mlp_w_out,
              rwkv_mu, rwkv_w_r, rwkv_w_k, rwkv_w_v, rwkv_w0, rwkv_w_lora_a, rwkv_w_lora_b,
              rwkv_a0, rwkv_a_lora_a, rwkv_a_lora_b, rwkv_g_lora_a, rwkv_g_lora_b,
              rwkv_k_k, rwkv_k_a, rwkv_r_k, rwkv_ln_w, rwkv_ln_b, rwkv_w_o,
              m2_in_proj, m2_conv_w, m2_conv_b, m2_dt_bias, m2_a_log, m2_d, m2_norm_w, m2_out_proj,
              gla_in_proj, gla_gate_up, gla_gate_bias, gla_norm_w, gla_out_proj,
              ret_in_proj, ret_out_proj):
    B, S, D = x.shape
    meta = jnp.broadcast_to(meta_tokens[None].astype(x.dtype), (B, N_META, D))
    h = jnp.concatenate([meta, x], axis=1)
    pos = jnp.arange(N_META + S, dtype=jnp.float32)
    for i in range(DEPTH):
        m, j = i % N_MIXERS, i // N_MIXERS
        u = rmsnorm(h, norm_mix[i])
        if m == 0:
            y = rwkv7_mix(u, rwkv_mu[j], rwkv_w_r[j], rwkv_w_k[j], rwkv_w_v[j], rwkv_w0[j],
                          rwkv_w_lora_a[j], rwkv_w_lora_b[j], rwkv_a0[j], rwkv_a_lora_a[j], rwkv_a_lora_b[j],
                          rwkv_g_lora_a[j], rwkv_g_lora_b[j], rwkv_k_k[j], rwkv_k_a[j], rwkv_r_k[j],
                          rwkv_ln_w[j], rwkv_ln_b[j], rwkv_w_o[j])
        elif m == 1:
            y = mamba2_mix(u, m2_in_proj[j], m2_conv_w[j], m2_conv_b[j], m2_dt_bias[j], m2_a_log[j],
                           m2_d[j], m2_norm_w[j], m2_out_proj[j])
        elif m == 2:
            y = gla_mix(u, gla_in_proj[j], gla_gate_up[j], gla_gate_bias[j], gla_norm_w[j], gla_out_proj[j])
        else:
            y = retnet_mix(u, pos, ret_in_proj[j], ret_out_proj[j])
        h = h + y
        h = h + sqrelu_mlp(rmsnorm(h, norm_mlp[i]), mlp_w_in[i], mlp_w_out[i])
    h = rmsnorm(h, norm_final)
    return h[:, N_META:]
```

```python
import math
from contextlib import ExitStack
import numpy as np
import concourse.bass as bass
import concourse.mybir as mybir
from concourse.alu_op_type import AluOpType as ALU
from concourse.bass_utils import run_bass_kernel_spmd

F32 = mybir.dt.float32
BF16 = mybir.dt.bfloat16
AF = mybir.ActivationFunctionType

D = 1024
SEQ = 2048
NMETA = 16
L = SEQ + NMETA
DEPTH = 4
DFF = 4096
EPS = 1e-5
NT = 17
TILES = [(0, 16)] + [(16 + 128 * i, 128) for i in range(16)]
GROUPS = [(0, 16, [0])] + [(16 + 512 * g, 512, [1 + 4 * g + j for j in range(4)]) for g in range(4)]


class Dep:
    __slots__ = ("w", "r", "excl")

    def __init__(self, excl=False):
        self.w = None
        self.r = {}
        self.excl = excl


class Sched:
    N_DMA_SEMS = 12

    def __init__(self, nc):
        self.nc = nc
        self.eng = {"pe": nc.tensor, "dve": nc.vector, "act": nc.scalar, "pool": nc.gpsimd, "sp": nc.sync}
        self.sems = {}
        self.count = {}
        self.known = {e: {} for e in self.eng}
        for e in self.eng:
            self.sems[e] = nc.alloc_semaphore("s_" + e)
            self.count[e] = 0
        self.dma_sems = {}
        self.dma_i = {}
        for q in ("sp", "pool"):
            self.dma_sems[q] = [nc.alloc_semaphore(f"d_{q}{i}") for i in range(self.N_DMA_SEMS)]
            self.dma_i[q] = 0
        self.n_inst = 0

    def _sem(self, key):
        if isinstance(key, str):
            return self.sems[key]
        q, i = key
        return self.dma_sems[q][i]

    def _wait(self, e, key, val):
        if self.known[e].get(key, 0) >= val:
            return
        self.eng[e].wait_ge(self._sem(key), val)
        self.known[e][key] = val

    def _collect(self, e, reads, writes):
        need = {}

        def add(k, v):
            if need.get(k, 0) < v:
                need[k] = v
        for d in reads:
            if d.w is not None:
                add(*d.w)
            if d.excl:
                for k, v in d.r.items():
                    if k != e:
                        add(k, v)
        for d in writes:
            if d.w is not None and d.w[0] != e:
                add(*d.w)
            for k, v in d.r.items():
                if k != e:
                    add(k, v)
        for k, v in need.items():
            self._wait(e, k, v)

    def _mark(self, tok, reads, writes):
        k, v = tok
        for d in reads:
            if d.r.get(k, 0) < v:
                d.r[k] = v
        for d in writes:
            d.w = tok
            d.r = {}

    def op(self, e, fn, reads=(), writes=()):
        self._collect(e, reads, writes)
        inst = fn(self.eng[e])
        self.count[e] += 1
        inst.then_inc(self.sems[e], 1)
        self._mark((e, self.count[e]), reads, writes)
        self.n_inst += 1
        return inst

    def dma(self, q, out, in_, reads=(), writes=(), **kw):
        i = self.dma_i[q]
        self.dma_i[q] += 1
        slot = i % self.N_DMA_SEMS
        rnd = i // self.N_DMA_SEMS
        key = (q, slot)
        if rnd > 0:
            self._wait(q, key, 16 * rnd)
        self._collect(q, reads, writes)
        inst = self.eng[q].dma_start(out=out, in_=in_, **kw)
        inst.then_inc(self.dma_sems[q][slot], 16)
        self._mark((key, 16 * (rnd + 1)), reads, writes)
        self.n_inst += 1
        return inst

    def barrier(self):
        for e in self.eng:
            for e2 in self.eng:
                if e2 != e and self.count[e2] > 0:
                    self._wait(e, e2, self.count[e2])
            for q in self.dma_sems:
                n = self.dma_i[q]
                for slot in range(min(n, self.N_DMA_SEMS)):
                    last_rnd = (n - 1 - slot) // self.N_DMA_SEMS
                    self._wait(e, (q, slot), 16 * (last_rnd + 1))


class T:
    def __init__(self, ap, excl=False):
        self.ap = ap
        self.deps = {}
        self.excl = excl

    def d(self, key=None):
        if key not in self.deps:
            self.deps[key] = Dep(self.excl)
        return self.deps[key]


def pack_rows(w):
    K, Fd = w.shape
    kc = K // 128
    return np.ascontiguousarray(w.reshape(kc, 128, Fd).transpose(1, 0, 2).reshape(128, kc * Fd))


def vec_cols(v):
    return np.ascontiguousarray(v.reshape(-1, 128).T)


class Builder:
    def __init__(self, cfg):
        self.cfg = cfg
        self.nc = bass.Bass("TRN2", target_bir_lowering=False)
        self.S = Sched(self.nc)
        self.dram = {}

    def din(self, name, shape):
        self.dram[name] = self.nc.dram_tensor(name, list(shape), F32, kind="ExternalInput").ap()
        return self.dram[name]

    def sb(self, es, name, shape, dt):
        self.uid = getattr(self, "uid", 0) + 1
        return T(es.enter_context(self.nc.sbuf_tensor(f"sb{self.uid}_{name}", list(shape), dt))[:])

    def mm(self, out, lhsT, rhs, start, stop, reads, writes):
        return self.S.op("pe", lambda e: e.matmul(out, lhsT=lhsT, rhs=rhs, start=start, stop=stop),
                         reads=reads, writes=writes)

    def tr(self, out, in_, n, reads, writes):
        idn = self.ident
        return self.S.op("pe", lambda e: e.transpose(out, in_, idn.ap[:n, :n]),
                         reads=list(reads) + [idn.d()], writes=writes)

    def build(self):
        nc, S, cfg = self.nc, self.S, self.cfg
        x_d = self.din("x", [SEQ, D])
        meta_d = self.din("meta", [NMETA, D])
        pv_d = self.din("pvec", [128, cfg["npv"]])
        rv_d = self.din("rvec", [1, cfg["nrv"]])
        cst_d = self.din("cst", [128, cfg["ncst"]])
        mlp_in_d = mlp_out_d = None
        if cfg["mlps"]:
            mlp_in_d = self.din("mlp_in", [DEPTH * 8, 128, 8 * 512])
            mlp_out_d = self.din("mlp_out", [DEPTH * 8, 128, 4 * 1024])
        if 0 in cfg["mixers"]:
            self.din("rw_cst", [128, 1408])
            self.din("rw_lora", [128, 8 * 288])
            self.din("rw_pair", [8, 128, 8 * 384])
            self.din("rw_wo", [8, 128, 1024])
            self.din("rw_bwa", [8, 128, 256])
            self.din("rw_bg", [8, 128, 256])
        if 1 in cfg["mixers"]:
            self.din("ssd_in", [8, 128, 8 * 772])
            self.din("ssd_out", [8, 128, 2 * 1024])
            self.din("ssd_cst", [128, 512])
        if 2 in cfg["mixers"]:
            self.din("gla_in", [4, 128, 8 * 784])
            self.din("gla_out", [4, 128, 2 * 1024])
            self.din("gla_gup", [1, 16, 512])
        if 3 in cfg["mixers"]:
            self.din("ret_in", [4, 128, 8 * 1536])
            self.din("ret_out", [4, 128, 4 * 1024])
            self.din("ret_cst", [128, cfg["nretc"]])
        out_d = nc.dram_tensor("out", [SEQ, D], F32, kind="ExternalOutput").ap()
        dbg_d = None
        if cfg.get("debug"):
            dbg_d = nc.dram_tensor("dbg", [DEPTH * 2, L, D], F32, kind="ExternalOutput").ap()

        with ExitStack() as es:
            self.h = self.sb(es, "h", [128, NT, D], F32)
            self.uT = self.sb(es, "uT", [128, 8, L], BF16)
            self.pv = self.sb(es, "pv", [128, cfg["npv"]], F32)
            self.cst = self.sb(es, "cst", [128, cfg["ncst"]], F32)
            self.ident = T(self.cst.ap[:, 0:128])
            self.ident.deps = self.cst.deps
            self.junk = self.sb(es, "junk", [128, D], F32)
            self.un = self.sb(es, "un", [128, D], F32)
            self.stat = self.sb(es, "stat", [128, 8], F32)
            self.ps = [T(es.enter_context(nc.psum_tensor(f"ps{i}", [128, 512], F32))[:], excl=True) for i in range(8)]
            h, uT = self.h, self.uT

            S.dma("sp", self.cst.ap, cst_d, writes=[self.cst.d()])
            S.dma("sp", self.pv.ap, pv_d, writes=[self.pv.d()])
            S.dma("sp", h.ap[0:16, 0, :], meta_d, writes=[h.d(0)])
            for g in range(4):
                S.dma("sp", h.ap[:, 1 + 4 * g:5 + 4 * g, :],
                      x_d[512 * g:512 * (g + 1), :].rearrange("(t p) d -> p t d", p=128),
                      writes=[h.d(1 + 4 * g + j) for j in range(4)])

            for layer in range(DEPTH):
                if layer in cfg["mixers"]:
                    self.emit_norm(self.pv.ap[:, cfg["pv_norm_mix"] + 8 * layer: cfg["pv_norm_mix"] + 8 * layer + 8])
                    [self.emit_rwkv, self.emit_ssd, self.emit_gla, self.emit_ret][layer % 4](layer // 4)
                    S.barrier()
                if dbg_d is not None:
                    self.emit_dump(dbg_d[2 * layer])
                if layer in cfg["mlps"]:
                    self.emit_norm(self.pv.ap[:, cfg["pv_norm_mlp"] + 8 * layer: cfg["pv_norm_mlp"] + 8 * layer + 8])
                    self.emit_mlp(layer, mlp_in_d, mlp_out_d)
                    S.barrier()
                if dbg_d is not None:
                    self.emit_dump(dbg_d[2 * layer + 1])

            self.emit_final(out_d)
        return nc

    def emit_dump(self, dst):
        S, h = self.S, self.h
        S.dma("sp", dst[0:16, :], h.ap[0:16, 0, :], reads=[h.d(0)])
        for t in range(1, NT):
            c0 = TILES[t][0]
            S.dma("sp", dst[c0:c0 + 128, :], h.ap[:, t, :], reads=[h.d(t)])

    def rstd_of(self, t, n):
        S, h, stat, junk = self.S, self.h, self.stat, self.junk
        S.op("dve", lambda e: e.scalar_tensor_tensor(out=junk.ap[:n, :], in0=h.ap[:n, t, :], scalar=1.0 / D,
                                                      in1=h.ap[:n, t, :], op0=ALU.mult, op1=ALU.mult,
                                                      accum_out=stat.ap[:n, 1:2]),
             reads=[h.d(t)], writes=[junk.d(), stat.d()])
        S.op("act", lambda e: e.activation(out=stat.ap[:n, 2:3], in_=stat.ap[:n, 1:2], func=AF.Sqrt, bias=EPS),
             reads=[stat.d()], writes=[stat.d()])
        S.op("dve", lambda e: e.reciprocal(out=stat.ap[:n, 0:1], in_=stat.ap[:n, 2:3]),
             reads=[stat.d()], writes=[stat.d()])

    def emit_norm(self, wcols):
        S, h, uT, un, stat = self.S, self.h, self.uT, self.un, self.stat
        for t, (c0, n) in enumerate(TILES):
            gi = 0 if t == 0 else 1 + (t - 1) // 4
            self.rstd_of(t, n)
            S.op("act", lambda e: e.activation(out=un.ap[:n, :], in_=h.ap[:n, t, :], func=AF.Copy,
                                               scale=stat.ap[:n, 0:1]),
                 reads=[h.d(t), stat.d()], writes=[un.d()])
            for half in range(2):
                pb = self.ps[half]
                pv3 = pb.ap.rearrange("p (j n) -> p j n", j=4)
                for j in range(4):
                    dc = half * 4 + j
                    self.tr(pv3[:, j, :n], un.ap[:n, dc * 128:(dc + 1) * 128], n, reads=[un.d()], writes=[pb.d()])
                S.op("dve", lambda e: e.tensor_tensor(out=uT.ap[:, half * 4:half * 4 + 4, c0:c0 + n],
                                                      in0=pv3[:, :, :n],
                                                      in1=wcols[:, half * 4:half * 4 + 4].unsqueeze(2).to_broadcast([128, 4, n]),
                                                      op=ALU.mult),
                     reads=[pb.d(), self.pv.d()], writes=[uT.d(gi)])

    def load_rv(self, es, off, n):
        t = self.sb(es, "rv", [128, n], F32)
        self.S.dma("sp", t.ap, self.dram["rvec"][:, off:off + n].partition_broadcast(128), writes=[t.d()])
        return t

    def emit_final(self, out_d):
        S, h, stat, cfg = self.S, self.h, self.stat, self.cfg
        es = ExitStack()
        self.rv = self.load_rv(es, cfg["rv_norm_final"], D)
        wb = self.rv.ap[:, 0:D]
        for t in range(1, NT):
            c0, n = TILES[t]
            self.rstd_of(t, n)
            S.op("dve", lambda e: e.scalar_tensor_tensor(out=h.ap[:, t, :], in0=h.ap[:, t, :], scalar=stat.ap[:, 0:1],
                                                          in1=wb, op0=ALU.mult, op1=ALU.mult),
                 reads=[h.d(t), stat.d(), self.rv.d()], writes=[h.d(t)])
            S.dma("sp", out_d[c0 - 16:c0 - 16 + 128, :], h.ap[:, t, :], reads=[h.d(t)])
        S.barrier()
        es.close()

    def emit_mlp(self, layer, mlp_in_d, mlp_out_d):
        nc, S, h, uT = self.nc, self.S, self.h, self.uT
        with ExitStack() as es:
            win = [self.sb(es, f"win{i}", [128, 8, 512], BF16) for i in range(2)]
            wout = [self.sb(es, f"wout{i}", [128, 4, 1024], BF16) for i in range(2)]
            hT = [self.sb(es, f"hT{i}", [128, 4, 512], BF16) for i in range(2)]
            rl = [self.sb(es, f"rl{i}", [128, 512], F32) for i in range(2)]
            cnt = 0
            for fb in range(8):
                wi, wo = win[fb % 2], wout[fb % 2]
                S.dma("pool", wi.ap.rearrange("p c f -> p (c f)"), mlp_in_d[layer * 8 + fb], writes=[wi.d()])
                S.dma("pool", wo.ap.rearrange("p c f -> p (c f)"), mlp_out_d[layer * 8 + fb], writes=[wo.d()])
                for gi, (g0, gn, gtiles) in enumerate(GROUPS):
                    hb = hT[cnt % 2]
                    for fc in range(4):
                        ph = self.ps[fc % 2]
                        for dc in range(8):
                            self.mm(ph.ap[:, :gn], wi.ap[:, dc, fc * 128:(fc + 1) * 128], uT.ap[:, dc, g0:g0 + gn],
                                    dc == 0, dc == 7, reads=[wi.d(), uT.d(gi)], writes=[ph.d()])
                        r = rl[fc % 2]
                        S.op("act", lambda e: e.activation(out=r.ap[:, :gn], in_=ph.ap[:, :gn], func=AF.Relu),
                             reads=[ph.d()], writes=[r.d()])
                        S.op("dve", lambda e: e.tensor_tensor(out=hb.ap[:, fc, :gn], in0=r.ap[:, :gn], in1=r.ap[:, :gn],
                                                              op=ALU.mult),
                             reads=[r.d()], writes=[hb.d()])
                    for ti, t in enumerate(gtiles):
                        n = TILES[t][1]
                        for half in range(2):
                            po = self.ps[2 + (ti * 2 + half) % 4]
                            for fc in range(4):
                                self.mm(po.ap[:n, :], hb.ap[:, fc, ti * 128:ti * 128 + n],
                                        wo.ap[:, fc, half * 512:(half + 1) * 512], fc == 0, fc == 3,
                                        reads=[hb.d(), wo.d()], writes=[po.d()])
                            S.op("dve", lambda e: e.tensor_tensor(out=h.ap[:n, t, half * 512:(half + 1) * 512],
                                                                  in0=h.ap[:n, t, half * 512:(half + 1) * 512],
                                                                  in1=po.ap[:n, :], op=ALU.add),
                                 reads=[po.d(), h.d(t)], writes=[h.d(t)])
                    cnt += 1


    def emit_rwkv(self, j):
        nc, S, h, uT, cfg = self.nc, self.S, self.h, self.uT, self.cfg
        dr = self.dram
        C0 = 0.6065306597126334
        GN_EPS = 64e-5
        pvb = cfg["pv_rwkv"]
        pvc = lambda idx, c: self.pv.ap[:, pvb + idx * 8 + c:pvb + idx * 8 + c + 1]
        mu = lambda i: self.pv.ap[:, pvb + i * 8:pvb + i * 8 + 8]
        I_W0, I_A0, I_KK, I_KA, I_RK, I_LNW, I_LNB = 6, 7, 8, 9, 10, 11, 12
        ident = self.cst.ap[:, 0:128]
        ps = self.ps
        RG = [(0, 16, [0])] + [(16 + 256 * g, 256, [1 + 2 * g, 2 + 2 * g]) for g in range(8)]
        with ExitStack() as es:
            xxT = self.sb(es, "xxT", [128, 8, L], BF16)
            hid = self.sb(es, "hid", [128, 3, L], BF16)
            rc = self.sb(es, "rwc", [128, 1408], F32)
            S.dma("sp", rc.ap, dr["rw_cst"], writes=[rc.d()])
            mskA = rc.ap[:, 0:512].rearrange("p (a i) -> p a i", a=4)
            mskB = rc.ap[:, 512:1024].rearrange("p (a i) -> p a i", a=4)
            mskC = rc.ap[:, 1024:1280].rearrange("p (a i) -> p a i", a=2)
            blk = rc.ap[:, 1280:1408]
            S.op("pool", lambda e: e.memset(hid.ap[:, 2, :], 0.0), writes=[hid.d()])
            S.op("dve", lambda e: e.tensor_tensor(out=xxT.ap[:, :, 1:L], in0=uT.ap[:, :, 0:L - 1], in1=uT.ap[:, :, 1:L], op=ALU.subtract),
                 reads=[uT.d(g) for g in range(5)], writes=[xxT.d()])
            S.op("dve", lambda e: e.tensor_scalar(out=xxT.ap[:, :, 0:1], in0=uT.ap[:, :, 0:1], scalar1=-1.0, scalar2=None, op0=ALU.mult),
                 reads=[uT.d(0)], writes=[xxT.d()])
            with ExitStack() as es2:
                lw_p = self.sb(es2, "lwp", [128, 8, 288], BF16)
                lw_m = self.sb(es2, "lwm", [128, 8, 288], BF16)
                stg = self.sb(es2, "lstg", [128, 8, 288], F32)
                S.dma("pool", lw_p.ap.rearrange("p c f -> p (c f)"), dr["rw_lora"], writes=[lw_p.d()])
                S.dma("sp", stg.ap.rearrange("p c f -> p (c f)"), dr["rw_lora"], writes=[stg.d()])
                for (c0_, c1_, mi) in ((0, 64, 1), (64, 128, 4), (128, 288, 5)):
                    S.op("dve", lambda e: e.tensor_tensor(out=lw_m.ap[:, :, c0_:c1_], in0=stg.ap[:, :, c0_:c1_],
                                                          in1=mu(mi).unsqueeze(2).to_broadcast([128, 8, c1_ - c0_]), op=ALU.mult),
                         reads=[stg.d(), self.pv.d()], writes=[lw_m.d()])
                for gi, (g0, gn, gtiles) in enumerate(GROUPS):
                    for a, (o0, o1) in enumerate(((0, 128), (128, 256), (256, 288))):
                        m = o1 - o0
                        for dc in range(8):
                            self.mm(ps[a].ap[:m, :gn], lw_p.ap[:, dc, o0:o1], uT.ap[:, dc, g0:g0 + gn], dc == 0, False,
                                    reads=[lw_p.d(), uT.d(gi)], writes=[ps[a].d()])
                        for dc in range(8):
                            self.mm(ps[a].ap[:m, :gn], lw_m.ap[:, dc, o0:o1], xxT.ap[:, dc, g0:g0 + gn], False, dc == 7,
                                    reads=[lw_m.d(), xxT.d()], writes=[ps[a].d()])
                    S.op("act", lambda e: e.activation(out=hid.ap[0:64, 0, g0:g0 + gn], in_=ps[0].ap[0:64, :gn], func=AF.Tanh),
                         reads=[ps[0].d()], writes=[hid.d()])
                    S.op("act", lambda e: e.activation(out=hid.ap[64:128, 0, g0:g0 + gn], in_=ps[0].ap[64:128, :gn], func=AF.Copy),
                         reads=[ps[0].d()], writes=[hid.d()])
                    S.op("act", lambda e: e.activation(out=hid.ap[:, 1, g0:g0 + gn], in_=ps[1].ap[:, :gn], func=AF.Sigmoid),
                         reads=[ps[1].d()], writes=[hid.d()])
                    S.op("act", lambda e: e.activation(out=hid.ap[0:32, 2, g0:g0 + gn], in_=ps[2].ap[0:32, :gn], func=AF.Sigmoid),
                         reads=[ps[2].d()], writes=[hid.d()])
                S.barrier()
            wp_p = self.sb(es, "wpp", [128, 8, 384], BF16)
            wp_m = self.sb(es, "wpm", [128, 8, 384], BF16)
            stg = self.sb(es, "wstg", [128, 8, 128], F32)
            wo = self.sb(es, "rwo", [128, 1024], BF16)
            Bwa = self.sb(es, "Bwa", [128, 256], BF16)
            Bg = self.sb(es, "Bg", [128, 256], BF16)
            M = self.sb(es, "wM", [128, 64], F32)
            Mb = self.sb(es, "wMb", [128, 64], BF16)
            f32t = {nm: self.sb(es, "w" + nm, [128, 256], F32) for nm in
                    ("r", "k", "v", "sg", "a", "kk", "tmp", "kmod", "bon", "cum", "ec", "en", "ecx", "g", "bv")}
            f32t["eh"], f32t["bh"], f32t["kh"] = f32t["ec"], f32t["en"], f32t["ecx"]
            bft = {nm: self.sb(es, "w" + nm, [128, 256], BF16) for nm in ("at0", "at1", "bt", "kt", "rt0", "rt1")}
            tok3c = [self.sb(es, f"tok3c{c}", [128, 3, 128], BF16) for c in range(2)]
            Psbc = [self.sb(es, f"Psbc{c}", [128, 128], BF16) for c in range(2)]
            Usbc = [self.sb(es, f"Usbc{c}", [128, 128], BF16) for c in range(2)]
            for tz in [bft["at0"], bft["at1"], bft["rt0"], bft["rt1"]] + tok3c + Psbc + Usbc:
                S.op("pool", lambda e: e.memset(tz.ap, 0.0), writes=[tz.d()])
            A1 = self.sb(es, "A1", [128, 4, 128], BF16)
            A2 = self.sb(es, "A2", [128, 4, 128], BF16)
            A3 = self.sb(es, "A3", [128, 2, 128], BF16)
            X2 = self.sb(es, "X2", [128, 4, 128], BF16)
            TT = self.sb(es, "TT", [128, 4, 128], BF16)
            Ysb = self.sb(es, "Ysb", [128, 128], F32)
            yln = self.sb(es, "yln", [128, 128], F32)
            og = self.sb(es, "og", [128, 128], F32)
            ogb = self.sb(es, "ogb", [128, 128], BF16)
            st = self.sb(es, "wst", [128, 24], F32)
            st2 = self.sb(es, "wst2", [128, 24], F32)
            F = f32t
            for hp in range(8):
                S.dma("pool", wp_p.ap.rearrange("p c f -> p (c f)"), dr["rw_pair"][hp], writes=[wp_p.d()])
                for i, mi in enumerate((0, 2, 3)):
                    S.dma("sp", stg.ap, dr["rw_pair"][hp].rearrange("p (c f) -> p c f", c=8)[:, :, i * 128:(i + 1) * 128], writes=[stg.d()])
                    S.op("dve", lambda e: e.tensor_tensor(out=wp_m.ap[:, :, i * 128:(i + 1) * 128], in0=stg.ap,
                                                          in1=mu(mi).unsqueeze(2).to_broadcast([128, 8, 128]), op=ALU.mult),
                         reads=[stg.d(), self.pv.d()], writes=[wp_m.d()])
                S.dma("pool", wo.ap, dr["rw_wo"][hp], writes=[wo.d()])
                S.dma("pool", Bwa.ap, dr["rw_bwa"][hp], writes=[Bwa.d()])
                S.dma("pool", Bg.ap, dr["rw_bg"][hp], writes=[Bg.d()])
                S.op("pool", lambda e: e.memset(M.ap, 0.0), writes=[M.d()])
                S.op("pool", lambda e: e.memset(Mb.ap, 0.0), writes=[Mb.d()])
                for gi, (g0, gn, gtiles) in enumerate(RG):
                    ugi = 0 if gi == 0 else 1 + (gi - 1) // 2
                    gs = slice(g0, g0 + gn)
                    for i, (pb, off) in enumerate(((ps[0], 0), (ps[0], 256), (ps[1], 0))):
                        for dc in range(8):
                            self.mm(pb.ap[:, off:off + gn], wp_p.ap[:, dc, i * 128:(i + 1) * 128], uT.ap[:, dc, gs], dc == 0, False,
                                    reads=[wp_p.d(), uT.d(ugi)], writes=[pb.d()])
                        for dc in range(8):
                            self.mm(pb.ap[:, off:off + gn], wp_m.ap[:, dc, i * 128:(i + 1) * 128], xxT.ap[:, dc, gs], False, dc == 7,
                                    reads=[wp_m.d(), xxT.d()], writes=[pb.d()])
                    self.mm(ps[1].ap[:, 256:256 + gn], Bg.ap[:, 0:128], hid.ap[:, 1, gs], True, False, reads=[Bg.d(), hid.d()], writes=[ps[1].d()])
                    self.mm(ps[1].ap[:, 256:256 + gn], Bg.ap[:, 128:256], hid.ap[:, 2, gs], False, True, reads=[Bg.d(), hid.d()], writes=[ps[1].d()])
                    self.mm(ps[2].ap[:, 0:gn], Bwa.ap[:, 0:128], hid.ap[:, 0, gs], True, True, reads=[Bwa.d(), hid.d()], writes=[ps[2].d()])
                    self.mm(ps[2].ap[:, 256:256 + gn], Bwa.ap[:, 128:256], hid.ap[:, 0, gs], True, True, reads=[Bwa.d(), hid.d()], writes=[ps[2].d()])
                    A = lambda nm: F[nm].ap[:, :gn]
                    D_ = lambda *nms: [F[n_].d() for n_ in nms]
                    S.op("act", lambda e: e.activation(out=A("r"), in_=ps[0].ap[:, 0:gn], func=AF.Copy), reads=[ps[0].d()], writes=D_("r"))
                    S.op("act", lambda e: e.activation(out=A("k"), in_=ps[0].ap[:, 256:256 + gn], func=AF.Copy), reads=[ps[0].d()], writes=D_("k"))
                    S.op("act", lambda e: e.activation(out=A("v"), in_=ps[1].ap[:, 0:gn], func=AF.Copy), reads=[ps[1].d()], writes=D_("v"))
                    S.op("act", lambda e: e.activation(out=A("g"), in_=ps[1].ap[:, 256:256 + gn], func=AF.Copy), reads=[ps[1].d()], writes=D_("g"))
                    S.op("act", lambda e: e.activation(out=A("sg"), in_=ps[2].ap[:, 0:gn], func=AF.Sigmoid, bias=pvc(I_W0, hp)),
                         reads=[ps[2].d(), self.pv.d()], writes=D_("sg"))
                    S.op("act", lambda e: e.activation(out=A("a"), in_=ps[2].ap[:, 256:256 + gn], func=AF.Sigmoid, bias=pvc(I_A0, hp)),
                         reads=[ps[2].d(), self.pv.d()], writes=D_("a"))
                    S.op("dve", lambda e: e.tensor_scalar(out=A("kk"), in0=A("k"), scalar1=pvc(I_KK, hp), scalar2=None, op0=ALU.mult),
                         reads=D_("k") + [self.pv.d()], writes=D_("kk"))
                    S.op("pool", lambda e: e.tensor_tensor(out=A("tmp"), in0=A("kk"), in1=A("kk"), op=ALU.mult), reads=D_("kk"), writes=D_("tmp"))
                    self.mm(ps[3].ap[:, 0:gn], blk, A("tmp"), True, True, reads=[rc.d()] + D_("tmp"), writes=[ps[3].d()])
                    S.op("act", lambda e: e.activation(out=A("tmp"), in_=ps[3].ap[:, 0:gn], func=AF.Sqrt), reads=[ps[3].d()], writes=D_("tmp"))
                    S.op("dve", lambda e: e.tensor_scalar(out=A("tmp"), in0=A("tmp"), scalar1=1e-12, scalar2=None, op0=ALU.max),
                         reads=D_("tmp"), writes=D_("tmp"))
                    S.op("dve", lambda e: e.reciprocal(out=A("tmp"), in_=A("tmp")), reads=D_("tmp"), writes=D_("tmp"))
                    S.op("dve", lambda e: e.tensor_tensor(out=A("kk"), in0=A("kk"), in1=A("tmp"), op=ALU.mult), reads=D_("kk", "tmp"), writes=D_("kk"))
                    S.op("dve", lambda e: e.tensor_scalar(out=A("tmp"), in0=A("a"), scalar1=-1.0, scalar2=pvc(I_KA, hp), op0=ALU.add, op1=ALU.mult),
                         reads=D_("a") + [self.pv.d()], writes=D_("tmp"))
                    S.op("dve", lambda e: e.scalar_tensor_tensor(out=A("kmod"), in0=A("tmp"), scalar=1.0, in1=A("k"), op0=ALU.add, op1=ALU.mult),
                         reads=D_("tmp", "k"), writes=D_("kmod"))
                    S.op("dve", lambda e: e.scalar_tensor_tensor(out=A("tmp"), in0=A("r"), scalar=pvc(I_RK, hp), in1=A("kmod"), op0=ALU.mult, op1=ALU.mult),
                         reads=D_("r", "kmod") + [self.pv.d()], writes=D_("tmp"))
                    self.mm(ps[3].ap[:, 256:256 + gn], blk, A("tmp"), True, True, reads=[rc.d()] + D_("tmp"), writes=[ps[3].d()])
                    S.op("dve", lambda e: e.tensor_tensor(out=A("bon"), in0=ps[3].ap[:, 256:256 + gn], in1=A("v"), op=ALU.mult),
                         reads=[ps[3].d()] + D_("v"), writes=D_("bon"))
                    S.op("pool", lambda e: e.tensor_tensor(out=A("bv"), in0=A("kk"), in1=A("a"), op=ALU.mult), reads=D_("kk", "a"), writes=D_("bv"))
                    chunks = []
                    for ti, t in enumerate(gtiles):
                        n = TILES[t][1]
                        for (o, m) in (((0, 16),) if n == 16 else ((0, 64), (64, 64))):
                            chunks.append((ti * 128 + o, m))
                    ones = self.cst.ap[:, 384:512]
                    for (o, m) in chunks:
                        S.op("dve", lambda e: e.tensor_tensor_scan(out=F["cum"].ap[:, o:o + m], data0=ones[:, :m], data1=F["sg"].ap[:, o:o + m],
                                                                   initial=0.0, op0=ALU.mult, op1=ALU.add),
                             reads=D_("sg") + [self.cst.d()], writes=D_("cum"))
                    S.op("act", lambda e: e.activation(out=A("ec"), in_=A("cum"), func=AF.Exp, scale=-C0), reads=D_("cum"), writes=D_("ec"))
                    S.op("act", lambda e: e.activation(out=A("en"), in_=A("cum"), func=AF.Exp, scale=C0), reads=D_("cum"), writes=D_("en"))
                    S.op("dve", lambda e: e.tensor_tensor(out=A("ecx"), in0=A("cum"), in1=A("sg"), op=ALU.subtract), reads=D_("cum", "sg"), writes=D_("ecx"))
                    S.op("act", lambda e: e.activation(out=A("ecx"), in_=A("ecx"), func=AF.Exp, scale=-C0), reads=D_("ecx"), writes=D_("ecx"))
                    for hd in range(2):
                        hs = slice(hd * 64, hd * 64 + 64)
                        S.op("dve", lambda e: e.scalar_tensor_tensor(out=bft[f"at{hd}"].ap[hs, :gn], in0=F["kk"].ap[hs, :gn], scalar=-1.0,
                                                                      in1=F["ecx"].ap[hs, :gn], op0=ALU.mult, op1=ALU.mult),
                             reads=D_("kk", "ecx"), writes=[bft[f"at{hd}"].d()])
                        S.op("dve", lambda e: e.tensor_tensor(out=bft[f"rt{hd}"].ap[hs, :gn], in0=F["r"].ap[hs, :gn], in1=F["ec"].ap[hs, :gn], op=ALU.mult),
                             reads=D_("r", "ec"), writes=[bft[f"rt{hd}"].d()])
                    S.op("pool", lambda e: e.tensor_tensor(out=bft["bt"].ap[:, :gn], in0=A("bv"), in1=A("en"), op=ALU.mult), reads=D_("bv", "en"), writes=[bft["bt"].d()])
                    S.op("pool", lambda e: e.tensor_tensor(out=bft["kt"].ap[:, :gn], in0=A("kmod"), in1=A("en"), op=ALU.mult), reads=D_("kmod", "en"), writes=[bft["kt"].d()])
                    for ci, (o, m) in enumerate(chunks):
                        last = F["cum"].ap[:, o + m - 1:o + m]
                        S.op("dve", lambda e: e.tensor_scalar(out=st.ap[:, 8 + ci:9 + ci], in0=last, scalar1=-C0, scalar2=None, op0=ALU.mult),
                             reads=D_("cum"), writes=[st.d()])
                        S.op("act", lambda e: e.activation(out=F["eh"].ap[:, o:o + m], in_=F["cum"].ap[:, o:o + m], func=AF.Exp, scale=C0,
                                                           bias=st.ap[:, 8 + ci:9 + ci]),
                             reads=D_("cum") + [st.d()], writes=D_("eh"))
                        S.op("act", lambda e: e.activation(out=st.ap[:, 12 + ci:13 + ci], in_=last, func=AF.Exp, scale=-C0), reads=D_("cum"), writes=[st.d()])
                    S.op("pool", lambda e: e.tensor_tensor(out=A("bh"), in0=A("bv"), in1=A("eh"), op=ALU.mult), reads=D_("bv", "eh"), writes=D_("bh"))
                    S.op("pool", lambda e: e.tensor_tensor(out=A("kh"), in0=A("kmod"), in1=A("eh"), op=ALU.mult), reads=D_("kmod", "eh"), writes=D_("kh"))
                    cidx = 0
                    for ti, t in enumerate(gtiles):
                        c0, n = TILES[t]
                        lo = ti * 128
                        tl = slice(lo, lo + n)
                        for q, nm in enumerate(("v", "bh", "kh")):
                            self.tr(ps[4].ap[:n, q * 128:(q + 1) * 128], F[nm].ap[:, tl], 128, reads=D_(nm), writes=[ps[4].d()])
                        tchunks = ((0, 16),) if n == 16 else ((0, 64), (64, 64))
                        for c, (o, m) in enumerate(tchunks):
                            S.op("act", lambda e: e.activation(out=tok3c[c].ap[o:o + m, :, :],
                                                               in_=ps[4].ap[o:o + m, 0:384].rearrange("p (q c) -> p q c", q=3), func=AF.Copy),
                                 reads=[ps[4].d()], writes=[tok3c[c].d()])
                        bA = ps[5].ap.rearrange("p (a i) -> p a i", a=4)
                        bB = ps[6].ap.rearrange("p (a i) -> p a i", a=4)
                        bC = ps[7].ap.rearrange("p (a i) -> p a i", a=4)
                        for hd in range(2):
                            at, bt, kt, rt = (bft[x].ap[:, tl] for x in (f"at{hd}", "bt", "kt", f"rt{hd}"))
                            rd = [bft[x].d() for x in (f"at{hd}", "bt", "kt", f"rt{hd}")]
                            self.mm(bA[:n, 2 * hd, :n], bt, at, True, True, reads=rd, writes=[ps[5].d()])
                            self.mm(bA[:n, 2 * hd + 1, :n], at, bt, True, True, reads=rd, writes=[ps[5].d()])
                            self.mm(bB[:n, hd, :n], kt, at, True, True, reads=rd, writes=[ps[6].d()])
                            self.mm(bB[:n, 2 + hd, :n], bt, rt, True, True, reads=rd, writes=[ps[6].d()])
                            self.mm(bC[:n, hd, :n], kt, rt, True, True, reads=rd, writes=[ps[7].d()])
                        S.op("dve", lambda e: e.tensor_tensor(out=A1.ap[:n, :, :n], in0=bA[:n, :, :n], in1=mskA[:n, :, :n], op=ALU.mult),
                             reads=[ps[5].d(), rc.d()], writes=[A1.d()])
                        S.op("dve", lambda e: e.tensor_tensor(out=A2.ap[:n, :, :n], in0=bB[:n, :, :n], in1=mskB[:n, :, :n], op=ALU.mult),
                             reads=[ps[6].d(), rc.d()], writes=[A2.d()])
                        S.op("dve", lambda e: e.tensor_tensor(out=A3.ap[:n, :, :n], in0=bC[:n, 0:2, :n], in1=mskC[:n, :, :n], op=ALU.mult),
                             reads=[ps[7].d(), rc.d()], writes=[A3.d()])
                        S.op("dve", lambda e: e.tensor_tensor(out=TT.ap[:n, :, :n], in0=A1.ap[:n, :, :n],
                                                              in1=ident[:n, :n].unsqueeze(1).to_broadcast([n, 4, n]), op=ALU.add),
                             reads=[A1.d(), self.cst.d()], writes=[TT.d()])
                        Xc = A1
                        nlev = 5 if n == 128 else 3
                        for lev in range(nlev):
                            pq = ps[5 + (lev % 2)]
                            pq3 = pq.ap.rearrange("p (a i) -> p a i", a=4)
                            for hd in range(2):
                                Xm, Ym = Xc.ap[:n, 2 * hd, :n], Xc.ap[:n, 2 * hd + 1, :n]
                                self.mm(pq3[:n, 2 * hd, :n], Ym, Xm, True, True, reads=[Xc.d()], writes=[pq.d()])
                                self.mm(pq3[:n, 2 * hd + 1, :n], Xm, Ym, True, True, reads=[Xc.d()], writes=[pq.d()])
                            S.op("act", lambda e: e.activation(out=X2.ap[:n, :, :n], in_=pq3[:n, :, :n], func=AF.Copy), reads=[pq.d()], writes=[X2.d()])
                            Xc = X2
                            pr = ps[7]
                            pr3 = pr.ap.rearrange("p (a i) -> p a i", a=4)
                            for hd in range(2):
                                Tt_, T_ = TT.ap[:n, 2 * hd, :n], TT.ap[:n, 2 * hd + 1, :n]
                                self.mm(pr3[:n, 2 * hd, :n], T_, X2.ap[:n, 2 * hd, :n], True, True, reads=[TT.d(), X2.d()], writes=[pr.d()])
                                self.mm(pr3[:n, 2 * hd + 1, :n], X2.ap[:n, 2 * hd, :n], T_, True, True, reads=[TT.d(), X2.d()], writes=[pr.d()])
                            S.op("dve", lambda e: e.tensor_tensor(out=TT.ap[:n, :, :n], in0=TT.ap[:n, :, :n], in1=pr3[:n, :, :n], op=ALU.add),
                                 reads=[TT.d(), pr.d()], writes=[TT.d()])
                        for c, (o, m) in enumerate(tchunks):
                            cs = slice(o, o + m)
                            ci = cidx
                            cidx += 1
                            pP, pU, pY, pM = ps[0], ps[1], ps[2], ps[3]
                            tk, Pc, Uc = tok3c[c], Psbc[c], Usbc[c]
                            for hd in range(2):
                                hs = slice(hd * 64, hd * 64 + 64)
                                self.mm(pP.ap[:n, hs], bft[f"at{hd}"].ap[:, tl], Mb.ap[:, :], True, False, reads=[bft[f"at{hd}"].d(), Mb.d()], writes=[pP.d()])
                                self.mm(pP.ap[:n, hs], A2.ap[:n, hd, :n], tk.ap[:n, 0, hs], False, True, reads=[A2.d(), tk.d()], writes=[pP.d()])
                            S.op("act", lambda e: e.activation(out=Pc.ap[cs, :], in_=pP.ap[cs, 0:128], func=AF.Copy), reads=[pP.d()], writes=[Pc.d()])
                            for hd in range(2):
                                hs = slice(hd * 64, hd * 64 + 64)
                                self.mm(pU.ap[:n, hs], TT.ap[:n, 2 * hd, :n], Pc.ap[:n, hs], True, True, reads=[TT.d(), Pc.d()], writes=[pU.d()])
                            S.op("dve", lambda e: e.tensor_copy(out=Uc.ap[cs, :], in_=pU.ap[cs, 0:128]), reads=[pU.d()], writes=[Uc.d()])
                            for hd in range(2):
                                hs = slice(hd * 64, hd * 64 + 64)
                                self.mm(pY.ap[:n, hs], bft[f"rt{hd}"].ap[:, tl], Mb.ap[:, :], True, False, reads=[bft[f"rt{hd}"].d(), Mb.d()], writes=[pY.d()])
                                self.mm(pY.ap[:n, hs], A2.ap[:n, 2 + hd, :n], Uc.ap[:n, hs], False, False, reads=[A2.d(), Uc.d()], writes=[pY.d()])
                                self.mm(pY.ap[:n, hs], A3.ap[:n, hd, :n], tk.ap[:n, 0, hs], False, True, reads=[A3.d(), tk.d()], writes=[pY.d()])
                            S.op("act", lambda e: e.activation(out=Ysb.ap[cs, :], in_=pY.ap[cs, 0:128], func=AF.Copy), reads=[pY.d()], writes=[Ysb.d()])
                            self.mm(pM.ap[:, 0:128], tk.ap[:n, 1, :], Uc.ap[:n, :], True, False, reads=[tk.d(), Uc.d()], writes=[pM.d()])
                            self.mm(pM.ap[:, 0:128], tk.ap[:n, 2, :], tk.ap[:n, 0, :], False, True, reads=[tk.d()], writes=[pM.d()])
                            for hd in range(2):
                                hs = slice(hd * 64, hd * 64 + 64)
                                S.op("dve", lambda e: e.scalar_tensor_tensor(out=M.ap[hs, :], in0=M.ap[hs, :], scalar=st.ap[hs, 12 + ci:13 + ci],
                                                                              in1=pM.ap[hs, hs], op0=ALU.mult, op1=ALU.add),
                                     reads=[M.d(), st.d(), pM.d()], writes=[M.d()])
                            S.op("act", lambda e: e.activation(out=Mb.ap, in_=M.ap, func=AF.Copy), reads=[M.d()], writes=[Mb.d()])
                        for hd in range(2):
                            hs = slice(hd * 64, hd * 64 + 64)
                            b0 = hd * 12
                            S.op("dve", lambda e: e.bn_stats(out=st2.ap[:n, b0:b0 + 6], in_=Ysb.ap[:n, hs]), reads=[Ysb.d()], writes=[st2.d()])
                            S.op("dve", lambda e: e.bn_aggr(out=st2.ap[:n, b0 + 6:b0 + 8], in_=st2.ap[:n, b0:b0 + 6]), reads=[st2.d()], writes=[st2.d()])
                            S.op("act", lambda e: e.activation(out=st2.ap[:n, b0 + 7:b0 + 8], in_=st2.ap[:n, b0 + 7:b0 + 8], func=AF.Sqrt, bias=GN_EPS),
                                 reads=[st2.d()], writes=[st2.d()])
                            S.op("dve", lambda e: e.reciprocal(out=st2.ap[:n, b0 + 7:b0 + 8], in_=st2.ap[:n, b0 + 7:b0 + 8]), reads=[st2.d()], writes=[st2.d()])
                            S.op("dve", lambda e: e.tensor_scalar(out=yln.ap[:n, hs], in0=Ysb.ap[:n, hs], scalar1=st2.ap[:n, b0 + 6:b0 + 7],
                                                                  scalar2=st2.ap[:n, b0 + 7:b0 + 8], op0=ALU.subtract, op1=ALU.mult),
                                 reads=[Ysb.d(), st2.d()], writes=[yln.d()])
                        self.tr(ps[4].ap[:, 384:384 + n], yln.ap[:n, :], n, reads=[yln.d()], writes=[ps[4].d()])
                        S.op("dve", lambda e: e.tensor_scalar(out=og.ap[:, :n], in0=ps[4].ap[:, 384:384 + n], scalar1=pvc(I_LNW, hp), scalar2=pvc(I_LNB, hp),
                                                              op0=ALU.mult, op1=ALU.add),
                             reads=[ps[4].d(), self.pv.d()], writes=[og.d()])
                        S.op("pool", lambda e: e.tensor_tensor(out=og.ap[:, :n], in0=og.ap[:, :n], in1=F["bon"].ap[:, tl], op=ALU.add),
                             reads=[og.d()] + D_("bon"), writes=[og.d()])
                        S.op("pool", lambda e: e.tensor_tensor(out=ogb.ap[:, :n], in0=og.ap[:, :n], in1=F["g"].ap[:, tl], op=ALU.mult),
                             reads=[og.d()] + D_("g"), writes=[ogb.d()])
                        for half in range(2):
                            po = ps[5 + half]
                            self.mm(po.ap[:n, :], ogb.ap[:, :n], wo.ap[:, half * 512:(half + 1) * 512], True, True, reads=[ogb.d(), wo.d()], writes=[po.d()])
                            S.op("dve", lambda e: e.tensor_tensor(out=h.ap[:n, t, half * 512:(half + 1) * 512],
                                                                  in0=h.ap[:n, t, half * 512:(half + 1) * 512], in1=po.ap[:n, :], op=ALU.add),
                                 reads=[po.d(), h.d(t)], writes=[h.d(t)])


    def emit_ssd(self, j):
        nc, S, h, uT, cfg = self.nc, self.S, self.h, self.uT, self.cfg
        sin_d, sout_d, scst_d = self.dram["ssd_in"], self.dram["ssd_out"], self.dram["ssd_cst"]
        ident = self.cst.ap[:, 0:128]
        tri = self.cst.ap[:, 128:256]
        ones = self.cst.ap[:, 384:512]
        pvb = cfg["pv_ssd_conv"]
        cw = lambda jj, ch: self.pv.ap[:, pvb + jj * 32 + ch:pvb + jj * 32 + ch + 1]
        cb = lambda ch: self.pv.ap[:, pvb + 128 + ch:pvb + 128 + ch + 1]
        rvA = 0
        with ExitStack() as es:
            self.rv = self.load_rv(es, cfg["rv_ssd"], 64 + 4096)
            negm = self.sb(es, "snegm", [128, 4, 128], F32)
            S.dma("sp", negm.ap.rearrange("p h i -> p (h i)"), scst_d, writes=[negm.d()])
            win = self.sb(es, "swin", [128, 8, 772], BF16)
            wo = self.sb(es, "swo", [128, 2, 1024], BF16)
            Aneg = self.sb(es, "sA", [128, 32], F32)
            pc = self.sb(es, "spc", [128, 4, 515], F32)
            acc = self.sb(es, "sacc", [128, 512], F32)
            a4 = self.sb(es, "sa4", [128, 4, 512], F32)
            BTb = self.sb(es, "sBTb", [128, 512], BF16)
            CTb = self.sb(es, "sCTb", [128, 512], BF16)
            M = self.sb(es, "sM", [128, 256], F32)
            Mb = self.sb(es, "sMb", [128, 256], BF16)
            xs = self.sb(es, "sxs", [128, 256], F32)
            Btok = self.sb(es, "sBtok", [128, 128], BF16)
            dt = self.sb(es, "sdt", [128, 32], F32)
            trila = self.sb(es, "strila", [128, 4, 128], F32)
            dec = self.sb(es, "sdec", [128, 4, 128], F32)
            ecr = self.sb(es, "secr", [128, 4, 128], F32)
            Pm = self.sb(es, "sP", [128, 4, 128], BF16)
            CTs = self.sb(es, "sCTs", [128, 4, 128], BF16)
            vsb = self.sb(es, "sv", [128, 256], BF16)
            vh = self.sb(es, "svh", [128, 256], BF16)
            t1 = self.sb(es, "st1", [128, 256], F32)
            sz = self.sb(es, "ssz", [128, 256], F32)
            yn = self.sb(es, "syn", [128, 256], F32)
            ygT = self.sb(es, "sygT", [128, 2, 128], BF16)
            st = self.sb(es, "sst", [128, 8], F32)
            ps = self.ps
            S.op("pool", lambda e: e.memset(dt.ap, 0.0), writes=[dt.d()])
            S.op("act", lambda e: e.activation(out=Aneg.ap, in_=self.rv.ap[:, rvA + 32:rvA + 64], func=AF.Exp),
                 reads=[self.rv.d()], writes=[Aneg.d()])
            S.op("dve", lambda e: e.tensor_scalar(out=Aneg.ap, in0=Aneg.ap, scalar1=-1.0, scalar2=None, op0=ALU.mult),
                 reads=[Aneg.d()], writes=[Aneg.d()])
            for g in range(8):
                S.dma("pool", win.ap.rearrange("p c f -> p (c f)"), sin_d[j * 8 + g], writes=[win.d()])
                S.dma("pool", wo.ap.rearrange("p c f -> p (c f)"), sout_d[j * 8 + g], writes=[wo.d()])
                S.op("pool", lambda e: e.memset(M.ap, 0.0), writes=[M.d()])
                S.op("pool", lambda e: e.memset(Mb.ap, 0.0), writes=[Mb.d()])
                S.op("pool", lambda e: e.memset(pc.ap[:, :, 0:3], 0.0), writes=[pc.d()])
                chans = [2 * g, 2 * g + 1, 16 + g, 24 + g]
                dtb = self.rv.ap[:, rvA + g * 4:rvA + g * 4 + 4]
                Ag = Aneg.ap[:, g * 4:g * 4 + 4]
                dsk = self.rv.ap[:, rvA + 64 + g * 256:rvA + 64 + (g + 1) * 256]
                nwb = self.rv.ap[:, rvA + 64 + 2048 + g * 256:rvA + 64 + 2048 + (g + 1) * 256]
                prev_gn = None
                for gi, (g0, gn, gtiles) in enumerate(GROUPS):
                    if prev_gn is not None:
                        S.op("pool", lambda e: e.tensor_copy(out=pc.ap[:, :, 0:3], in_=pc.ap[:, :, prev_gn:prev_gn + 3]),
                             reads=[pc.d()], writes=[pc.d()])
                    prev_gn = gn
                    for a in range(4):
                        for dc in range(8):
                            self.mm(ps[a].ap[:, :gn], win.ap[:, dc, a * 128:(a + 1) * 128], uT.ap[:, dc, g0:g0 + gn], dc == 0, dc == 7,
                                    reads=[win.d(), uT.d(gi)], writes=[ps[a].d()])
                        S.op("act", lambda e: e.activation(out=pc.ap[:, a, 3:3 + gn], in_=ps[a].ap[:, :gn], func=AF.Copy),
                             reads=[ps[a].d()], writes=[pc.d()])
                    for a in range(4):
                        ch = chans[a]
                        S.op("dve", lambda e: e.tensor_scalar(out=acc.ap[:, :gn], in0=pc.ap[:, a, 0:gn], scalar1=cw(0, ch), scalar2=None,
                                                              op0=ALU.mult),
                             reads=[pc.d(), self.pv.d()], writes=[acc.d()])
                        for jj in range(1, 4):
                            S.op("dve", lambda e: e.scalar_tensor_tensor(out=acc.ap[:, :gn], in0=pc.ap[:, a, jj:jj + gn], scalar=cw(jj, ch),
                                                                          in1=acc.ap[:, :gn], op0=ALU.mult, op1=ALU.add),
                                 reads=[pc.d(), self.pv.d(), acc.d()], writes=[acc.d()])
                        S.op("act", lambda e: e.activation(out=a4.ap[:, a, :gn], in_=acc.ap[:, :gn], func=AF.Silu, bias=cb(ch)),
                             reads=[acc.d(), self.pv.d()], writes=[a4.d()])
                    if cfg.get("ssd_stage", 9) < 1:
                        continue
                    S.op("act", lambda e: e.activation(out=BTb.ap[:, :gn], in_=a4.ap[:, 2, :gn], func=AF.Copy), reads=[a4.d()], writes=[BTb.d()])
                    S.op("act", lambda e: e.activation(out=CTb.ap[:, :gn], in_=a4.ap[:, 3, :gn], func=AF.Copy), reads=[a4.d()], writes=[CTb.d()])
                    for ti, t in enumerate(gtiles):
                        c0, n = TILES[t]
                        lo = ti * 128
                        v3 = lambda ap: ap.rearrange("p (h i) -> p h i", h=4)
                        for dc in range(8):
                            self.mm(ps[4].ap[:n, 0:260], uT.ap[:, dc, c0:c0 + n], win.ap[:, dc, 512:772], dc == 0, dc == 7,
                                    reads=[win.d(), uT.d(gi)], writes=[ps[4].d()])
                        S.op("act", lambda e: e.activation(out=sz.ap[:n, :], in_=ps[4].ap[:n, 0:256], func=AF.Silu),
                             reads=[ps[4].d()], writes=[sz.d()])
                        S.op("dve", lambda e: e.tensor_tensor(out=dt.ap[:n, 0:4], in0=ps[4].ap[:n, 256:260], in1=dtb[:n, :], op=ALU.add),
                             reads=[ps[4].d(), self.rv.d()], writes=[dt.d()])
                        S.op("act", lambda e: e.activation(out=dt.ap[:n, 0:4], in_=dt.ap[:n, 0:4], func=AF.Exp), reads=[dt.d()], writes=[dt.d()])
                        S.op("act", lambda e: e.activation(out=dt.ap[:n, 0:4], in_=dt.ap[:n, 0:4], func=AF.Ln, bias=1.0),
                             reads=[dt.d()], writes=[dt.d()])
                        S.op("dve", lambda e: e.tensor_tensor(out=dt.ap[:n, 4:8], in0=dt.ap[:n, 0:4], in1=Ag[:n, :], op=ALU.mult),
                             reads=[dt.d(), Aneg.d()], writes=[dt.d()])
                        if cfg.get("ssd_stage", 9) < 2:
                            continue
                        for c in range(2):
                            self.tr(ps[5].ap[:n, c * 128:(c + 1) * 128], a4.ap[:, c, lo:lo + n], 128, reads=[a4.d()], writes=[ps[5].d()])
                        self.tr(ps[5].ap[:n, 256:384], a4.ap[:, 2, lo:lo + n], 128, reads=[a4.d()], writes=[ps[5].d()])
                        S.op("act", lambda e: e.activation(out=xs.ap[:n, :], in_=ps[5].ap[:n, 0:256], func=AF.Copy), reads=[ps[5].d()], writes=[xs.d()])
                        S.op("act", lambda e: e.activation(out=Btok.ap[:n, :], in_=ps[5].ap[:n, 256:384], func=AF.Copy),
                             reads=[ps[5].d()], writes=[Btok.d()])
                        if cfg.get("ssd_stage", 9) < 3:
                            continue
                        self.mm(ps[6].ap[:, 256:272], tri[:n, :], dt.ap[:n, 4:20], True, True, reads=[self.cst.d(), dt.d()], writes=[ps[6].d()])
                        S.op("dve", lambda e: e.tensor_scalar(out=dt.ap[:n, 8:12], in0=ps[6].ap[:n, 256:260], scalar1=-1.0, scalar2=None, op0=ALU.mult),
                             reads=[ps[6].d()], writes=[dt.d()])
                        S.op("dve", lambda e: e.tensor_tensor(out=trila.ap[:n, :, :], in0=tri[:n, :].unsqueeze(1).to_broadcast([n, 4, 128]),
                                                              in1=dt.ap[:n, 4:8].unsqueeze(2).to_broadcast([n, 4, 128]), op=ALU.mult),
                             reads=[self.cst.d(), dt.d()], writes=[trila.d()])
                        crow = v3(ps[7].ap)[:, :, :n]
                        self.mm(ps[7].ap, ones[:n, :], trila.ap[:n, :, :].rearrange("p h i -> p (h i)"), True, True,
                                reads=[self.cst.d(), trila.d()], writes=[ps[7].d()])
                        S.op("act", lambda e: e.activation(out=ecr.ap[:, :, :n], in_=crow, func=AF.Exp), reads=[ps[7].d()], writes=[ecr.d()])
                        S.op("dve", lambda e: e.tensor_tensor(out=dt.ap[:n, 12:16], in0=v3(ps[7].ap)[:n, :, n - 1], in1=dt.ap[:n, 8:12], op=ALU.add),
                             reads=[ps[7].d(), dt.d()], writes=[dt.d()])
                        S.op("act", lambda e: e.activation(out=dt.ap[:n, 12:16], in_=dt.ap[:n, 12:16], func=AF.Exp), reads=[dt.d()], writes=[dt.d()])
                        S.op("dve", lambda e: e.tensor_tensor(out=dt.ap[:n, 16:20], in0=dt.ap[:n, 12:16], in1=dt.ap[:n, 0:4], op=ALU.mult),
                             reads=[dt.d()], writes=[dt.d()])
                        if cfg.get("ssd_stage", 9) < 4:
                            continue
                        self.mm(ps[7].ap, ident[:n, :], negm.ap[:n, :, :].rearrange("p h i -> p (h i)"), False, True,
                                reads=[self.cst.d(), negm.d()], writes=[ps[7].d()])
                        for hh in range(4):
                            S.op("act", lambda e: e.activation(out=dec.ap[:n, hh, :n], in_=v3(ps[7].ap)[:n, hh, :n], func=AF.Exp,
                                                               bias=dt.ap[:n, 8 + hh:9 + hh]),
                                 reads=[ps[7].d(), dt.d()], writes=[dec.d()])
                        if cfg.get("ssd_stage", 9) < 5:
                            continue
                        self.mm(ps[6].ap[:n, :n], BTb.ap[:, lo:lo + n], CTb.ap[:, lo:lo + n], True, True, reads=[BTb.d(), CTb.d()], writes=[ps[6].d()])
                        S.op("dve", lambda e: e.tensor_tensor(out=Pm.ap[:n, :, :n], in0=ps[6].ap[:n, :n].unsqueeze(1).to_broadcast([n, 4, n]),
                                                              in1=dec.ap[:n, :, :n], op=ALU.mult),
                             reads=[ps[6].d(), dec.d()], writes=[Pm.d()])
                        S.op("pool", lambda e: e.tensor_tensor(out=CTs.ap[:, :, :n], in0=a4.ap[:, 3, lo:lo + n].unsqueeze(1).to_broadcast([128, 4, n]),
                                                               in1=ecr.ap[:, :, :n], op=ALU.mult),
                             reads=[a4.d(), ecr.d()], writes=[CTs.d()])
                        if cfg.get("ssd_stage", 9) < 6:
                            continue
                        x3 = xs.ap[:n, :].rearrange("p (h q) -> p h q", h=4)
                        S.op("dve", lambda e: e.tensor_tensor(out=vsb.ap[:n, :].rearrange("p (h q) -> p h q", h=4), in0=x3,
                                                              in1=dt.ap[:n, 0:4].unsqueeze(2).to_broadcast([n, 4, 64]), op=ALU.mult),
                             reads=[xs.d(), dt.d()], writes=[vsb.d()])
                        S.op("dve", lambda e: e.tensor_tensor(out=vh.ap[:n, :].rearrange("p (h q) -> p h q", h=4), in0=x3,
                                                              in1=dt.ap[:n, 16:20].unsqueeze(2).to_broadcast([n, 4, 64]), op=ALU.mult),
                             reads=[xs.d(), dt.d()], writes=[vh.d()])
                        for hh in range(4):
                            cs_ = slice(hh * 64, (hh + 1) * 64)
                            self.mm(ps[0].ap[:n, cs_], Pm.ap[:n, hh, :n], vsb.ap[:n, cs_], True, False, reads=[Pm.d(), vsb.d()], writes=[ps[0].d()])
                            self.mm(ps[0].ap[:n, cs_], CTs.ap[:, hh, :n], Mb.ap[:, cs_], False, True, reads=[CTs.d(), Mb.d()], writes=[ps[0].d()])
                        if cfg.get("ssd_stage", 9) < 7:
                            continue
                        self.mm(ps[1].ap[:, 0:256], Btok.ap[:n, :], vh.ap[:n, :], True, True, reads=[Btok.d(), vh.d()], writes=[ps[1].d()])
                        S.op("dve", lambda e: e.tensor_tensor(out=M.ap.rearrange("p (h q) -> p h q", h=4), in0=M.ap.rearrange("p (h q) -> p h q", h=4),
                                                              in1=ecr.ap[:, :, n - 1:n].to_broadcast([128, 4, 64]), op=ALU.mult),
                             reads=[M.d(), ecr.d()], writes=[M.d()])
                        S.op("dve", lambda e: e.tensor_tensor(out=M.ap, in0=M.ap, in1=ps[1].ap[:, 0:256], op=ALU.add),
                             reads=[M.d(), ps[1].d()], writes=[M.d()])
                        S.op("act", lambda e: e.activation(out=Mb.ap, in_=M.ap, func=AF.Copy), reads=[M.d()], writes=[Mb.d()])
                        if cfg.get("ssd_stage", 9) < 8:
                            continue
                        S.op("pool", lambda e: e.tensor_tensor(out=t1.ap[:n, :], in0=xs.ap[:n, :], in1=dsk[:n, :], op=ALU.mult),
                             reads=[xs.d(), self.rv.d()], writes=[t1.d()])
                        S.op("dve", lambda e: e.tensor_tensor(out=t1.ap[:n, :], in0=t1.ap[:n, :], in1=ps[0].ap[:n, 0:256], op=ALU.add),
                             reads=[t1.d(), ps[0].d()], writes=[t1.d()])
                        S.op("pool", lambda e: e.tensor_tensor(out=t1.ap[:n, :], in0=t1.ap[:n, :], in1=sz.ap[:n, :], op=ALU.mult),
                             reads=[t1.d(), sz.d()], writes=[t1.d()])
                        S.op("act", lambda e: e.activation(out=self.junk.ap[:n, 0:256], in_=t1.ap[:n, :], func=AF.Square, accum_out=st.ap[:n, 2:3]),
                             reads=[t1.d()], writes=[self.junk.d(), st.d()])
                        S.op("act", lambda e: e.activation(out=st.ap[:n, 3:4], in_=st.ap[:n, 2:3], func=AF.Sqrt, scale=1.0 / 256.0, bias=EPS),
                             reads=[st.d()], writes=[st.d()])
                        S.op("dve", lambda e: e.reciprocal(out=st.ap[:n, 4:5], in_=st.ap[:n, 3:4]), reads=[st.d()], writes=[st.d()])
                        S.op("dve", lambda e: e.scalar_tensor_tensor(out=yn.ap[:n, :], in0=t1.ap[:n, :], scalar=st.ap[:n, 4:5], in1=nwb[:n, :],
                                                                      op0=ALU.mult, op1=ALU.mult),
                             reads=[t1.d(), st.d(), self.rv.d()], writes=[yn.d()])
                        self.out_proj(yn, n, 2, ygT, wo, t, ps[2], (ps[3], ps[5]))


    def emit_gla(self, j):
        nc, S, h, uT, cfg = self.nc, self.S, self.h, self.uT, self.cfg
        gin_d, gout_d, gup_d = self.dram["gla_in"], self.dram["gla_out"], self.dram["gla_gup"]
        tri = self.cst.ap[:, 128:256]
        ones = self.cst.ap[:, 384:512]
        gb = self.pv.ap[:, cfg["pv_gla_bias"]:cfg["pv_gla_bias"] + 4]
        with ExitStack() as es:
            self.rv = self.load_rv(es, cfg["rv_gla_norm"], 256)
            nwb = self.rv.ap[:, 0:256]
            win = self.sb(es, "gwin", [128, 8, 784], BF16)
            wo = self.sb(es, "gwo", [128, 2, 1024], BF16)
            gup = self.sb(es, "gup", [16, 512], F32)
            negb = self.sb(es, "gnegb", [128, 4], F32)
            M = self.sb(es, "gM", [128, 256], F32)
            Mb = self.sb(es, "gMb", [128, 256], BF16)
            glr = self.sb(es, "gglr", [16, 512], F32)
            sp = self.sb(es, "gsp", [128, 512], F32)
            cum = self.sb(es, "gcum", [128, 512], F32)
            ec = self.sb(es, "gec", [128, 512], F32)
            en = self.sb(es, "gen", [128, 512], F32)
            qt = self.sb(es, "gqt", [128, 512], BF16)
            kt = self.sb(es, "gkt", [128, 512], BF16)
            khT = self.sb(es, "gkhT", [128, 128], F32)
            eh = self.sb(es, "geh", [128, 128], F32)
            khat = self.sb(es, "gkhat", [128, 128], BF16)
            vsb = self.sb(es, "gv", [128, 256], BF16)
            sr = self.sb(es, "gsr", [128, 256], F32)
            Pm = self.sb(es, "gP", [128, 128], BF16)
            yn = self.sb(es, "gyn", [128, 256], F32)
            ygT = self.sb(es, "gygT", [128, 2, 128], BF16)
            st = self.sb(es, "gst", [128, 8], F32)
            ps = self.ps
            S.dma("sp", gup.ap, gup_d[j], writes=[gup.d()])
            S.op("dve", lambda e: e.tensor_scalar(out=negb.ap, in0=gb, scalar1=-1.0, scalar2=None, op0=ALU.mult),
                 reads=[self.pv.d()], writes=[negb.d()])
            for hd in range(4):
                S.dma("pool", win.ap.rearrange("p c f -> p (c f)"), gin_d[j * 4 + hd], writes=[win.d()])
                S.dma("pool", wo.ap.rearrange("p c f -> p (c f)"), gout_d[j * 4 + hd], writes=[wo.d()])
                S.op("pool", lambda e: e.memset(M.ap, 0.0), writes=[M.d()])
                S.op("pool", lambda e: e.memset(Mb.ap, 0.0), writes=[Mb.d()])
                for gi, (g0, gn, gtiles) in enumerate(GROUPS):
                    for a, (o0, o1) in enumerate(((0, 128), (128, 256), (256, 272))):
                        m = o1 - o0
                        for dc in range(8):
                            self.mm(ps[a].ap[:m, :gn], win.ap[:, dc, o0:o1], uT.ap[:, dc, g0:g0 + gn], dc == 0, dc == 7,
                                    reads=[win.d(), uT.d(gi)], writes=[ps[a].d()])
                    S.op("act", lambda e: e.activation(out=glr.ap[:, :gn], in_=ps[2].ap[:16, :gn], func=AF.Copy),
                         reads=[ps[2].d()], writes=[glr.d()])
                    self.mm(ps[3].ap[:, :gn], gup.ap[:, hd * 128:(hd + 1) * 128], glr.ap[:, :gn], True, True,
                            reads=[gup.d(), glr.d()], writes=[ps[3].d()])
                    S.op("act", lambda e: e.activation(out=sp.ap[:, :gn], in_=ps[3].ap[:, :gn], func=AF.Exp, scale=-1.0,
                                                       bias=negb.ap[:, hd:hd + 1]),
                         reads=[ps[3].d(), negb.d()], writes=[sp.d()])
                    S.op("act", lambda e: e.activation(out=sp.ap[:, :gn], in_=sp.ap[:, :gn], func=AF.Ln, bias=1.0),
                         reads=[sp.d()], writes=[sp.d()])
                    for ti, t in enumerate(gtiles):
                        n = TILES[t][1]
                        lo = ti * 128
                        S.op("dve", lambda e: e.tensor_tensor_scan(out=cum.ap[:, lo:lo + n], data0=ones[:, :n], data1=sp.ap[:, lo:lo + n],
                                                                   initial=0.0, op0=ALU.mult, op1=ALU.add),
                             reads=[sp.d(), self.cst.d()], writes=[cum.d()])
                    S.op("act", lambda e: e.activation(out=ec.ap[:, :gn], in_=cum.ap[:, :gn], func=AF.Exp, scale=-1.0 / 16.0),
                         reads=[cum.d()], writes=[ec.d()])
                    S.op("act", lambda e: e.activation(out=en.ap[:, :gn], in_=cum.ap[:, :gn], func=AF.Exp, scale=1.0 / 16.0),
                         reads=[cum.d()], writes=[en.d()])
                    S.op("dve", lambda e: e.scalar_tensor_tensor(out=qt.ap[:, :gn], in0=ps[0].ap[:, :gn], scalar=128.0 ** -0.5,
                                                                  in1=ec.ap[:, :gn], op0=ALU.mult, op1=ALU.mult),
                         reads=[ps[0].d(), ec.d()], writes=[qt.d()])
                    S.op("dve", lambda e: e.tensor_tensor(out=kt.ap[:, :gn], in0=ps[1].ap[:, :gn], in1=en.ap[:, :gn], op=ALU.mult),
                         reads=[ps[1].d(), en.d()], writes=[kt.d()])
                    for ti, t in enumerate(gtiles):
                        c0, n = TILES[t]
                        lo = ti * 128
                        last = cum.ap[:, lo + n - 1:lo + n]
                        for dc in range(8):
                            self.mm(ps[4].ap[:n, :], uT.ap[:, dc, c0:c0 + n], win.ap[:, dc, 272:784], dc == 0, dc == 7,
                                    reads=[win.d(), uT.d(gi)], writes=[ps[4].d()])
                        S.op("act", lambda e: e.activation(out=vsb.ap[:n, :], in_=ps[4].ap[:n, 0:256], func=AF.Copy),
                             reads=[ps[4].d()], writes=[vsb.d()])
                        S.op("act", lambda e: e.activation(out=sr.ap[:n, :], in_=ps[4].ap[:n, 256:512], func=AF.Silu),
                             reads=[ps[4].d()], writes=[sr.d()])
                        self.mm(ps[5].ap[:n, :n], kt.ap[:, lo:lo + n], qt.ap[:, lo:lo + n], True, True,
                                reads=[kt.d(), qt.d()], writes=[ps[5].d()])
                        S.op("dve", lambda e: e.tensor_tensor(out=Pm.ap[:n, :n], in0=ps[5].ap[:n, :n], in1=tri[:n, :n], op=ALU.mult),
                             reads=[ps[5].d(), self.cst.d()], writes=[Pm.d()])
                        self.mm(ps[6].ap[:n, 0:256], Pm.ap[:n, :n], vsb.ap[:n, :], True, False, reads=[Pm.d(), vsb.d()], writes=[ps[6].d()])
                        self.mm(ps[6].ap[:n, 0:256], qt.ap[:, lo:lo + n], Mb.ap, False, True, reads=[qt.d(), Mb.d()], writes=[ps[6].d()])
                        S.op("dve", lambda e: e.tensor_scalar(out=st.ap[:, 0:1], in0=last, scalar1=-1.0 / 16.0, scalar2=None, op0=ALU.mult),
                             reads=[cum.d()], writes=[st.d()])
                        S.op("act", lambda e: e.activation(out=eh.ap[:, :n], in_=cum.ap[:, lo:lo + n], func=AF.Exp, scale=1.0 / 16.0,
                                                           bias=st.ap[:, 0:1]),
                             reads=[cum.d(), st.d()], writes=[eh.d()])
                        S.op("act", lambda e: e.activation(out=st.ap[:, 1:2], in_=last, func=AF.Exp, scale=-1.0 / 16.0),
                             reads=[cum.d()], writes=[st.d()])
                        S.op("dve", lambda e: e.tensor_tensor(out=khT.ap[:, :n], in0=ps[1].ap[:, lo:lo + n], in1=eh.ap[:, :n], op=ALU.mult),
                             reads=[ps[1].d(), eh.d()], writes=[khT.d()])
                        self.tr(ps[5].ap[:n, 128:256], khT.ap[:, :n], 128, reads=[khT.d()], writes=[ps[5].d()])
                        S.op("act", lambda e: e.activation(out=khat.ap[:n, :], in_=ps[5].ap[:n, 128:256], func=AF.Copy),
                             reads=[ps[5].d()], writes=[khat.d()])
                        self.mm(ps[7].ap[:, 0:256], khat.ap[:n, :], vsb.ap[:n, :], True, True, reads=[khat.d(), vsb.d()], writes=[ps[7].d()])
                        S.op("dve", lambda e: e.scalar_tensor_tensor(out=M.ap, in0=M.ap, scalar=st.ap[:, 1:2], in1=ps[7].ap[:, 0:256],
                                                                      op0=ALU.mult, op1=ALU.add),
                             reads=[M.d(), st.d(), ps[7].d()], writes=[M.d()])
                        S.op("act", lambda e: e.activation(out=Mb.ap, in_=M.ap, func=AF.Copy), reads=[M.d()], writes=[Mb.d()])
                        S.op("act", lambda e: e.activation(out=self.junk.ap[:n, 0:256], in_=ps[6].ap[:n, 0:256], func=AF.Square,
                                                           accum_out=st.ap[:n, 2:3]),
                             reads=[ps[6].d()], writes=[self.junk.d(), st.d()])
                        S.op("act", lambda e: e.activation(out=st.ap[:n, 3:4], in_=st.ap[:n, 2:3], func=AF.Sqrt, scale=1.0 / 256.0, bias=EPS),
                             reads=[st.d()], writes=[st.d()])
                        S.op("dve", lambda e: e.reciprocal(out=st.ap[:n, 4:5], in_=st.ap[:n, 3:4]), reads=[st.d()], writes=[st.d()])
                        S.op("dve", lambda e: e.scalar_tensor_tensor(out=yn.ap[:n, :], in0=ps[6].ap[:n, 0:256], scalar=st.ap[:n, 4:5],
                                                                      in1=nwb[:n, :], op0=ALU.mult, op1=ALU.mult),
                             reads=[ps[6].d(), st.d(), self.rv.d()], writes=[yn.d()])
                        S.op("pool", lambda e: e.tensor_tensor(out=yn.ap[:n, :], in0=yn.ap[:n, :], in1=sr.ap[:n, :], op=ALU.mult),
                             reads=[yn.d(), sr.d()], writes=[yn.d()])
                        self.out_proj(yn, n, 2, ygT, wo, t, ps[5], (ps[4], ps[7]))


    def emit_ret(self, j):
        nc, S, h, uT, cfg = self.nc, self.S, self.h, self.uT, self.cfg
        ret_in_d, ret_out_d, ret_cst_d = self.dram["ret_in"], self.dram["ret_out"], self.dram["ret_cst"]
        NRC = cfg["nretc"]
        with ExitStack() as es:
            rc = self.sb(es, "retc", [128, NRC], F32)
            S.dma("sp", rc.ap, ret_cst_d, writes=[rc.d()])
            decT = lambda hd: rc.ap[:, hd * 128:(hd + 1) * 128]
            rowpow = lambda hd: rc.ap[:, 512 + hd * 128:512 + (hd + 1) * 128]
            kdec = lambda hd, n: rc.ap[:, 1024 + (0 if n == 128 else 4) + hd:1024 + (0 if n == 128 else 4) + hd + 1]
            cosT = rc.ap[:, 1032:1032 + L]
            sinT = rc.ap[:, 1032 + L:1032 + 2 * L]
            win = self.sb(es, "rwin", [128, 8, 1536], BF16)
            wo = self.sb(es, "rwo", [128, 4, 1024], BF16)
            M = self.sb(es, "rM", [128, 2, 512], F32)
            Mb = self.sb(es, "rMb", [128, 2, 512], BF16)
            qT = self.sb(es, "rqT", [128, 2, 512], BF16)
            kT = self.sb(es, "rkT", [128, 2, 512], BF16)
            kf = self.sb(es, "rkf", [128, 2, 512], F32)
            tmp = [self.sb(es, f"rtmp{i}", [128, 512], F32) for i in range(2)]
            vsb = self.sb(es, "rv", [128, 512], BF16)
            sg = self.sb(es, "rsg", [128, 512], F32)
            Pm = self.sb(es, "rP", [128, 128], BF16)
            qs = self.sb(es, "rqs", [128, 2, 128], BF16)
            khat = self.sb(es, "rkhat", [128, 256], BF16)
            yn = self.sb(es, "ryn", [128, 512], F32)
            ygT = self.sb(es, "rygT", [128, 4, 128], BF16)
            st = self.sb(es, "rst", [128, 16], F32)
            ps = self.ps
            for hd in range(4):
                gam = 1.0 - 2.0 ** (-5.0 - hd)
                S.dma("pool", win.ap.rearrange("p c f -> p (c f)"), ret_in_d[j * 4 + hd], writes=[win.d()])
                S.dma("pool", wo.ap.rearrange("p c f -> p (c f)"), ret_out_d[j * 4 + hd], writes=[wo.d()])
                S.op("pool", lambda e: e.memset(M.ap, 0.0), writes=[M.d()])
                S.op("pool", lambda e: e.memset(Mb.ap, 0.0), writes=[Mb.d()])
                for gi, (g0, gn, gtiles) in enumerate(GROUPS):
                    for a in range(4):
                        for dc in range(8):
                            self.mm(ps[a].ap[:, :gn], win.ap[:, dc, a * 128:(a + 1) * 128], uT.ap[:, dc, g0:g0 + gn],
                                    dc == 0, dc == 7, reads=[win.d(), uT.d(gi)], writes=[ps[a].d()])
                    cs, sn = cosT[:, g0:g0 + gn], sinT[:, g0:g0 + gn]
                    for qk in range(2):
                        p1, p2 = ps[2 * qk], ps[2 * qk + 1]
                        sc = 1.0 if qk == 0 else 1.0 / 16.0
                        dstb = qT if qk == 0 else kT
                        t0, t1 = tmp
                        S.op("dve", lambda e: e.scalar_tensor_tensor(out=t0.ap[:, :gn], in0=p1.ap[:, :gn], scalar=sc, in1=cs,
                                                                      op0=ALU.mult, op1=ALU.mult),
                             reads=[p1.d(), rc.d()], writes=[t0.d()])
                        S.op("dve", lambda e: e.scalar_tensor_tensor(out=t1.ap[:, :gn], in0=p2.ap[:, :gn], scalar=sc, in1=sn,
                                                                      op0=ALU.mult, op1=ALU.mult),
                             reads=[p2.d(), rc.d()], writes=[t1.d()])
                        if qk == 0:
                            S.op("pool", lambda e: e.tensor_tensor(out=dstb.ap[:, 0, :gn], in0=t0.ap[:, :gn], in1=t1.ap[:, :gn], op=ALU.subtract),
                                 reads=[t0.d(), t1.d()], writes=[dstb.d()])
                        else:
                            S.op("pool", lambda e: e.tensor_tensor(out=kf.ap[:, 0, :gn], in0=t0.ap[:, :gn], in1=t1.ap[:, :gn], op=ALU.subtract),
                                 reads=[t0.d(), t1.d()], writes=[kf.d()])
                        S.op("dve", lambda e: e.scalar_tensor_tensor(out=t0.ap[:, :gn], in0=p1.ap[:, :gn], scalar=sc, in1=sn,
                                                                      op0=ALU.mult, op1=ALU.mult),
                             reads=[p1.d(), rc.d()], writes=[t0.d()])
                        S.op("dve", lambda e: e.scalar_tensor_tensor(out=t1.ap[:, :gn], in0=p2.ap[:, :gn], scalar=sc, in1=cs,
                                                                      op0=ALU.mult, op1=ALU.mult),
                             reads=[p2.d(), rc.d()], writes=[t1.d()])
                        if qk == 0:
                            S.op("pool", lambda e: e.tensor_tensor(out=dstb.ap[:, 1, :gn], in0=t0.ap[:, :gn], in1=t1.ap[:, :gn], op=ALU.add),
                                 reads=[t0.d(), t1.d()], writes=[dstb.d()])
                        else:
                            S.op("pool", lambda e: e.tensor_tensor(out=kf.ap[:, 1, :gn], in0=t0.ap[:, :gn], in1=t1.ap[:, :gn], op=ALU.add),
                                 reads=[t0.d(), t1.d()], writes=[kf.d()])
                            S.op("act", lambda e: e.activation(out=kT.ap[:, :, :gn], in_=kf.ap[:, :, :gn], func=AF.Copy),
                                 reads=[kf.d()], writes=[kT.d()])
                    for ti, t in enumerate(gtiles):
                        c0, n = TILES[t]
                        lo = ti * 128
                        for a, pb in ((0, ps[4]), (1, ps[5])):
                            for dc in range(8):
                                self.mm(pb.ap[:n, :], uT.ap[:, dc, c0:c0 + n], win.ap[:, dc, 512 + a * 512:1024 + a * 512],
                                        dc == 0, dc == 7, reads=[win.d(), uT.d(gi)], writes=[pb.d()])
                        S.op("act", lambda e: e.activation(out=vsb.ap[:n, :], in_=ps[4].ap[:n, :], func=AF.Copy),
                             reads=[ps[4].d()], writes=[vsb.d()])
                        S.op("act", lambda e: e.activation(out=sg.ap[:n, :], in_=ps[5].ap[:n, :], func=AF.Silu),
                             reads=[ps[5].d()], writes=[sg.d()])
                        for c in range(2):
                            self.mm(ps[6].ap[:n, :n], kT.ap[:, c, lo:lo + n], qT.ap[:, c, lo:lo + n], c == 0, c == 1,
                                    reads=[kT.d(), qT.d()], writes=[ps[6].d()])
                        S.op("dve", lambda e: e.tensor_tensor(out=Pm.ap[:n, :n], in0=ps[6].ap[:n, :n], in1=decT(hd)[:n, :n], op=ALU.mult),
                             reads=[ps[6].d(), rc.d()], writes=[Pm.d()])
                        S.op("pool", lambda e: e.tensor_tensor(out=qs.ap[:, :, :n], in0=qT.ap[:, :, lo:lo + n],
                                                               in1=rowpow(hd)[:, :n].unsqueeze(1).to_broadcast([128, 2, n]), op=ALU.mult),
                             reads=[qT.d(), rc.d()], writes=[qs.d()])
                        self.mm(ps[7].ap[:n, :], Pm.ap[:n, :n], vsb.ap[:n, :], True, False, reads=[Pm.d(), vsb.d()], writes=[ps[7].d()])
                        for c in range(2):
                            self.mm(ps[7].ap[:n, :], qs.ap[:, c, :n], Mb.ap[:, c, :], False, c == 1,
                                    reads=[qs.d(), Mb.d()], writes=[ps[7].d()])
                        for c in range(2):
                            self.tr(ps[6].ap[:n, c * 128:(c + 1) * 128], kf.ap[:, c, lo:lo + n], 128, reads=[kf.d()], writes=[ps[6].d()])
                        S.op("dve", lambda e: e.tensor_scalar(out=khat.ap[:n, :], in0=ps[6].ap[:n, 0:256], scalar1=kdec(hd, n)[:n, :],
                                                              scalar2=None, op0=ALU.mult),
                             reads=[ps[6].d(), rc.d()], writes=[khat.d()])
                        for c in range(2):
                            self.mm(ps[4 + c].ap[:, :], khat.ap[:n, c * 128:(c + 1) * 128], vsb.ap[:n, :], True, True,
                                    reads=[khat.d(), vsb.d()], writes=[ps[4 + c].d()])
                            S.op("dve", lambda e: e.scalar_tensor_tensor(out=M.ap[:, c, :], in0=M.ap[:, c, :], scalar=float(gam ** n),
                                                                          in1=ps[4 + c].ap[:, :], op0=ALU.mult, op1=ALU.add),
                                 reads=[M.d(), ps[4 + c].d()], writes=[M.d()])
                        S.op("act", lambda e: e.activation(out=Mb.ap, in_=M.ap, func=AF.Copy), reads=[M.d()], writes=[Mb.d()])
                        S.op("dve", lambda e: e.bn_stats(out=st.ap[:n, 0:6], in_=ps[7].ap[:n, :]), reads=[ps[7].d()], writes=[st.d()])
                        S.op("dve", lambda e: e.bn_aggr(out=st.ap[:n, 6:8], in_=st.ap[:n, 0:6]), reads=[st.d()], writes=[st.d()])
                        S.op("act", lambda e: e.activation(out=st.ap[:n, 8:9], in_=st.ap[:n, 7:8], func=AF.Sqrt, bias=EPS),
                             reads=[st.d()], writes=[st.d()])
                        S.op("dve", lambda e: e.reciprocal(out=st.ap[:n, 9:10], in_=st.ap[:n, 8:9]), reads=[st.d()], writes=[st.d()])
                        S.op("dve", lambda e: e.tensor_scalar(out=yn.ap[:n, :], in0=ps[7].ap[:n, :], scalar1=st.ap[:n, 6:7],
                                                              scalar2=st.ap[:n, 9:10], op0=ALU.subtract, op1=ALU.mult),
                             reads=[ps[7].d(), st.d()], writes=[yn.d()])
                        S.op("pool", lambda e: e.tensor_tensor(out=yn.ap[:n, :], in0=yn.ap[:n, :], in1=sg.ap[:n, :], op=ALU.mult),
                             reads=[yn.d(), sg.d()], writes=[yn.d()])
                        self.out_proj(yn, n, 4, ygT, wo, t, ps[6], (ps[4], ps[5]))

    def out_proj(self, y, n, nch, yT, wo, t, ptr, pouts):
        S, h = self.S, self.h
        for c in range(nch):
            self.tr(ptr.ap[:, c * 128:c * 128 + n], y.ap[:n, c * 128:(c + 1) * 128], n, reads=[y.d()], writes=[ptr.d()])
        S.op("act", lambda e: e.activation(out=yT.ap[:, 0:nch, :n], in_=ptr.ap[:, 0:nch * 128].rearrange("p (c n) -> p c n", c=nch)[:, :, :n],
                                           func=AF.Copy),
             reads=[ptr.d()], writes=[yT.d()])
        for half in range(2):
            po = pouts[half]
            for c in range(nch):
                self.mm(po.ap[:n, :], yT.ap[:, c, :n], wo.ap[:, c, half * 512:(half + 1) * 512], c == 0, c == nch - 1,
                        reads=[yT.d(), wo.d()], writes=[po.d()])
            S.op("dve", lambda e: e.tensor_tensor(out=h.ap[:n, t, half * 512:(half + 1) * 512],
                                                  in0=h.ap[:n, t, half * 512:(half + 1) * 512], in1=po.ap[:n, :], op=ALU.add),
                 reads=[po.d(), h.d(t)], writes=[h.d(t)])


def host_pack(inp, cfg):
    f32 = np.float32
    shared = {}
    pv_cols = []

    def add_pv(name, arr2d):
        cfg[name] = sum(a.shape[1] for a in pv_cols)
        pv_cols.append(np.asarray(arr2d, f32))
    add_pv("pv_norm_mix", np.concatenate([vec_cols(inp["norm_mix"][i]) for i in range(DEPTH)], axis=1))
    add_pv("pv_norm_mlp", np.concatenate([vec_cols(inp["norm_mlp"][i]) for i in range(DEPTH)], axis=1))
    add_pv("pv_gla_bias", vec_cols(inp["gla_gate_bias"][0]))
    add_pv("pv_rwkv", np.concatenate([vec_cols(inp["rwkv_mu"][0][i]) for i in range(6)] + [vec_cols(inp[nm][0].reshape(-1)) for nm in
                                     ("rwkv_w0", "rwkv_a0", "rwkv_k_k", "rwkv_k_a", "rwkv_r_k", "rwkv_ln_w", "rwkv_ln_b")], axis=1))
    add_pv("pv_ssd_conv", np.concatenate([vec_cols(inp["m2_conv_w"][0][jj]) for jj in range(4)] + [vec_cols(inp["m2_conv_b"][0])], axis=1))
    shared["pvec"] = np.ascontiguousarray(np.concatenate(pv_cols, axis=1))
    cfg["npv"] = shared["pvec"].shape[1]
    rv = []

    def add_rv(name, v):
        cfg[name] = sum(a.shape[0] for a in rv)
        rv.append(np.asarray(v, f32).reshape(-1))
    add_rv("rv_norm_final", inp["norm_final"])
    add_rv("rv_gla_norm", inp["gla_norm_w"][0])
    add_rv("rv_ssd", np.concatenate([inp["m2_dt_bias"][0], inp["m2_a_log"][0], np.repeat(inp["m2_d"][0], 64), inp["m2_norm_w"][0]]))
    shared["rvec"] = np.ascontiguousarray(np.concatenate(rv)[None, :])
    cfg["nrv"] = shared["rvec"].shape[1]
    ii = np.arange(128)
    cst = [np.eye(128, dtype=f32), (ii[:, None] <= ii[None, :]).astype(f32), (ii[:, None] < ii[None, :]).astype(f32), np.ones((128, 128), f32)]
    shared["cst"] = np.ascontiguousarray(np.concatenate(cst, axis=1))
    cfg["ncst"] = shared["cst"].shape[1]
    if cfg["mlps"]:
        w_in = inp["mlp_w_in"]
        w_out = inp["mlp_w_out"]
        shared["mlp_in"] = np.stack([pack_rows(w_in[l][:, fb * 512:(fb + 1) * 512]) for l in range(DEPTH) for fb in range(8)])
        shared["mlp_out"] = np.stack([pack_rows(w_out[l][fb * 512:(fb + 1) * 512, :]) for l in range(DEPTH) for fb in range(8)])
    if 0 in cfg["mixers"]:
        ii = np.arange(128)
        same = (ii[:, None] // 64) == (ii[None, :] // 64)
        su = ((ii[:, None] < ii[None, :]) & same).astype(f32)
        sl = ((ii[:, None] > ii[None, :]) & same).astype(f32)
        iu = ((ii[:, None] <= ii[None, :]) & same).astype(f32)
        shared["rw_cst"] = np.ascontiguousarray(np.concatenate([su, sl, su, sl, su, su, iu, iu, iu, iu, same.astype(f32)], axis=1))
        shared["rw_lora"] = pack_rows(np.concatenate([inp["rwkv_w_lora_a"][0], inp["rwkv_a_lora_a"][0], inp["rwkv_g_lora_a"][0]], axis=1))
        pairs, wos, bwas, bgs = [], [], [], []
        for hp in range(8):
            pc = slice(hp * 128, (hp + 1) * 128)
            pairs.append(pack_rows(np.concatenate([inp["rwkv_w_r"][0][:, pc], inp["rwkv_w_k"][0][:, pc], inp["rwkv_w_v"][0][:, pc]], axis=1)))
            wos.append(np.ascontiguousarray(inp["rwkv_w_o"][0][pc, :]))
            bw = np.zeros((128, 256), f32)
            bw[0:64, 0:128] = inp["rwkv_w_lora_b"][0][:, pc]
            bw[64:128, 128:256] = inp["rwkv_a_lora_b"][0][:, pc]
            bwas.append(bw)
            bg = np.zeros((128, 256), f32)
            bg[:, 0:128] = inp["rwkv_g_lora_b"][0][0:128, pc]
            bg[0:32, 128:256] = inp["rwkv_g_lora_b"][0][128:160, pc]
            bgs.append(bg)
        shared["rw_pair"] = np.stack(pairs)
        shared["rw_wo"] = np.stack(wos)
        shared["rw_bwa"] = np.ascontiguousarray(np.stack(bwas))
        shared["rw_bg"] = np.stack(bgs)
    if 1 in cfg["mixers"]:
        W = inp["m2_in_proj"][0]
        ins = []
        for g in range(8):
            cols = np.concatenate([2048 + np.arange(g * 256, (g + 1) * 256), 4096 + np.arange(g * 128, (g + 1) * 128),
                                   5120 + np.arange(g * 128, (g + 1) * 128), np.arange(g * 256, (g + 1) * 256),
                                   6144 + np.arange(g * 4, (g + 1) * 4)])
            ins.append(pack_rows(W[:, cols]))
        shared["ssd_in"] = np.stack(ins)
        shared["ssd_out"] = np.stack([pack_rows(inp["m2_out_proj"][0][g * 256:(g + 1) * 256, :]) for g in range(8)])
        ii = np.arange(128)
        nm = np.where(ii[:, None] <= ii[None, :], 0.0, -30000.0).astype(f32)
        shared["ssd_cst"] = np.ascontiguousarray(np.tile(nm, (1, 4)))
    if 2 in cfg["mixers"]:
        W = inp["gla_in_proj"][0]
        ins = []
        for hd in range(4):
            cols = np.concatenate([np.arange(hd * 128, (hd + 1) * 128), 512 + np.arange(hd * 128, (hd + 1) * 128),
                                   3072 + np.arange(16), 1024 + np.arange(hd * 256, (hd + 1) * 256),
                                   2048 + np.arange(hd * 256, (hd + 1) * 256)])
            ins.append(pack_rows(W[:, cols]))
        shared["gla_in"] = np.stack(ins)
        shared["gla_out"] = np.stack([pack_rows(inp["gla_out_proj"][0][hd * 256:(hd + 1) * 256, :]) for hd in range(4)])
        shared["gla_gup"] = np.ascontiguousarray(inp["gla_gate_up"]).astype(f32)
    if 3 in cfg["mixers"]:
        W = inp["ret_in_proj"][0]
        ins = []
        for hd in range(4):
            cols = np.concatenate([np.arange(hd * 256, (hd + 1) * 256), 1024 + np.arange(hd * 256, (hd + 1) * 256),
                                   2048 + np.arange(hd * 512, (hd + 1) * 512), 4096 + np.arange(hd * 512, (hd + 1) * 512)])
            ins.append(pack_rows(W[:, cols]))
        shared["ret_in"] = np.stack(ins)
        shared["ret_out"] = np.stack([pack_rows(inp["ret_out_proj"][0][hd * 512:(hd + 1) * 512, :]) for hd in range(4)])
        ii = np.arange(128)
        dec, rowp, kd128, kd16 = [], [], [], []
        for hd in range(4):
            lg = np.log1p(-np.exp2(-5.0 - hd))
            diff = (ii[None, :] - ii[:, None]).astype(np.float64)
            dec.append(np.where(diff >= 0, np.exp(lg * diff), 0.0))
            rowp.append(np.broadcast_to(np.exp(lg * (ii + 1.0))[None, :], (128, 128)))
            kd128.append(np.exp(lg * (127.0 - ii)))
            kd16.append(np.exp(lg * (15.0 - ii)))
        inv_freq = (1.0 / (10000.0 ** np.linspace(0.0, 1.0, 128, dtype=f32))).astype(f32)
        ang = (np.arange(L, dtype=f32)[None, :] * inv_freq[:, None]).astype(f32).astype(np.float64)
        shared["ret_cst"] = np.ascontiguousarray(np.concatenate(
            dec + rowp + [np.stack(kd128, 1), np.stack(kd16, 1), np.cos(ang), np.sin(ang)], axis=1).astype(f32))
        cfg["nretc"] = shared["ret_cst"].shape[1]
    return shared


def run(inputs, cfg):
    inp = {k: np.asarray(v) for k, v in inputs.items()}
    shared = host_pack(inp, cfg)
    b = Builder(cfg)
    nc = b.build()
    in_maps = []
    for c in range(8):
        m = dict(shared)
        m["x"] = np.ascontiguousarray(inp["x"][c])
        m["meta"] = np.ascontiguousarray(inp["meta_tokens"])
        in_maps.append(m)
    ncores = cfg.get("ncores", 8)
    in_maps = in_maps[:ncores]
    res = run_bass_kernel_spmd(nc, in_maps, core_ids=list(range(ncores)))
    out = np.stack([np.asarray(r["out"]) for r in res.results]).astype(np.float32)
    if cfg.get("debug"):
        return out, np.stack([np.asarray(r["dbg"]) for r in res.results])
    return out


def kernel(**inputs):
    cfg = {"mixers": [0, 1, 2, 3], "mlps": [0, 1, 2, 3]}
    return run(inputs, cfg)
```

```python
import math
import threading
from contextlib import ExitStack
import numpy as np
import concourse.bass as bass
import concourse.mybir as mybir
from concourse.alu_op_type import AluOpType as ALU
from concourse.bass_utils import run_bass_kernel_spmd

F32 = mybir.dt.float32
BF16 = mybir.dt.bfloat16
AF = mybir.ActivationFunctionType

D = 1024
SEQ = 2048
NMETA = 16
L = SEQ + NMETA
DEPTH = 4
DFF = 4096
EPS = 1e-5
NT = 17
TILES = [(0, 16)] + [(16 + 128 * i, 128) for i in range(16)]
GROUPS = [(0, 16, [0])] + [(16 + 512 * g, 512, [1 + 4 * g + j for j in range(4)]) for g in range(4)]


class Dep:
    __slots__ = ("w", "r", "excl")

    def __init__(self, excl=False):
        self.w = None
        self.r = {}
        self.excl = excl


class Sched:
    N_DMA_SEMS = 12

    def __init__(self, nc):
        self.nc = nc
        self.eng = {"pe": nc.tensor, "dve": nc.vector, "act": nc.scalar, "pool": nc.gpsimd, "sp": nc.sync}
        self.sems = {}
        self.count = {}
        self.known = {e: {} for e in self.eng}
        for e in self.eng:
            self.sems[e] = nc.alloc_semaphore("s_" + e)
            self.count[e] = 0
        self.dma_sems = {}
        self.dma_i = {}
        for q in ("sp", "pool"):
            self.dma_sems[q] = [nc.alloc_semaphore(f"d_{q}{i}") for i in range(self.N_DMA_SEMS)]
            self.dma_i[q] = 0
        self.n_inst = 0

    def _sem(self, key):
        if isinstance(key, str):
            return self.sems[key]
        q, i = key
        return self.dma_sems[q][i]

    def _wait(self, e, key, val):
        if self.known[e].get(key, 0) >= val:
            return
        self.eng[e].wait_ge(self._sem(key), val)
        self.known[e][key] = val

    def _collect(self, e, reads, writes):
        need = {}

        def add(k, v):
            if need.get(k, 0) < v:
                need[k] = v
        for d in reads:
            if d.w is not None:
                add(*d.w)
            if d.excl:
                for k, v in d.r.items():
                    if k != e:
                        add(k, v)
        for d in writes:
            if d.w is not None and d.w[0] != e:
                add(*d.w)
            for k, v in d.r.items():
                if k != e:
                    add(k, v)
        for k, v in need.items():
            self._wait(e, k, v)

    def _mark(self, tok, reads, writes):
        k, v = tok
        for d in reads:
            if d.r.get(k, 0) < v:
                d.r[k] = v
        for d in writes:
            d.w = tok
            d.r = {}

    def op(self, e, fn, reads=(), writes=()):
        IL.tick()
        self._collect(e, reads, writes)
        inst = fn(self.eng[e])
        self.count[e] += 1
        inst.then_inc(self.sems[e], 1)
        self._mark((e, self.count[e]), reads, writes)
        self.n_inst += 1
        return inst

    def dma(self, q, out, in_, reads=(), writes=(), **kw):
        IL.tick()
        i = self.dma_i[q]
        self.dma_i[q] += 1
        slot = i % self.N_DMA_SEMS
        rnd = i // self.N_DMA_SEMS
        key = (q, slot)
        if rnd > 0:
            self._wait(q, key, 16 * rnd)
        self._collect(q, reads, writes)
        inst = self.eng[q].dma_start(out=out, in_=in_, **kw)
        inst.then_inc(self.dma_sems[q][slot], 16)
        self._mark((key, 16 * (rnd + 1)), reads, writes)
        self.n_inst += 1
        return inst

    def barrier(self):
        for e in self.eng:
            for e2 in self.eng:
                if e2 != e and self.count[e2] > 0:
                    self._wait(e, e2, self.count[e2])
            for q in self.dma_sems:
                n = self.dma_i[q]
                for slot in range(min(n, self.N_DMA_SEMS)):
                    last_rnd = (n - 1 - slot) // self.N_DMA_SEMS
                    self._wait(e, (q, slot), 16 * (last_rnd + 1))


class Interleaver:
    def __init__(self):
        self.in_run = False
        self.local = threading.local()

    def run(self, fns, weights=None):
        fns = [f for f in fns if f is not None]
        if len(fns) == 1 or self.in_run:
            for f in fns:
                f()
            return
        n = len(fns)
        self.sems = [threading.Semaphore(0) for _ in range(n)]
        self.alive = [True] * n
        self.weights = weights or [1] * n
        self.cnt = [0] * n
        self.exc = None
        self.done = threading.Semaphore(0)

        def wrap(i, fn):
            self.sems[i].acquire()
            self.local.idx = i
            try:
                if self.exc is None:
                    fn()
            except BaseException as ex:
                self.exc = ex
            self.alive[i] = False
            j = self._next(i)
            if j is None:
                self.done.release()
            else:
                self.sems[j].release()
        ths = [threading.Thread(target=wrap, args=(i, f)) for i, f in enumerate(fns)]
        self.in_run = True
        for th in ths:
            th.start()
        self.sems[0].release()
        self.done.acquire()
        for th in ths:
            th.join()
        self.in_run = False
        if self.exc is not None:
            raise self.exc

    def _next(self, i):
        n = len(self.sems)
        for k in range(1, n):
            j = (i + k) % n
            if self.alive[j]:
                return j
        return None

    def tick(self):
        if not self.in_run:
            return
        i = getattr(self.local, "idx", None)
        if i is None:
            return
        if self.exc is not None:
            raise RuntimeError("sibling stream failed")
        self.cnt[i] += 1
        if self.cnt[i] % self.weights[i]:
            return
        j = self._next(i)
        if j is None:
            return
        self.sems[j].release()
        self.sems[i].acquire()


    def wait(self, cond):
        if not self.in_run:
            assert cond()
            return
        i = self.local.idx
        spins = 0
        while not cond():
            if self.exc is not None:
                raise RuntimeError("sibling stream failed")
            j = self._next(i)
            assert j is not None, "interleaver deadlock"
            spins += 1
            assert spins < 10_000_000, "interleaver livelock"
            self.sems[j].release()
            self.sems[i].acquire()


IL = Interleaver()


def run_pipeline(ntiles, stageA, stageB, nA, nring):
    assert nring >= nA + 1
    doneA = [False] * ntiles
    doneB = [False] * ntiles

    def a_stream(i):
        for t in range(i, ntiles, nA):
            IL.wait(lambda: t - nring < 0 or doneB[t - nring])
            stageA(t, i, t % nring)
            doneA[t] = True

    def b_stream():
        for t in range(ntiles):
            IL.wait(lambda: doneA[t])
            stageB(t, t % nring)
            doneB[t] = True
    IL.run([b_stream] + [(lambda i=i: a_stream(i)) for i in range(nA)])


class T:
    def __init__(self, ap, excl=False):
        self.ap = ap
        self.deps = {}
        self.excl = excl

    def d(self, key=None):
        if key not in self.deps:
            self.deps[key] = Dep(self.excl)
        return self.deps[key]


def pack_rows(w):
    K, Fd = w.shape
    kc = K // 128
    return np.ascontiguousarray(w.reshape(kc, 128, Fd).transpose(1, 0, 2).reshape(128, kc * Fd))


def vec_cols(v):
    return np.ascontiguousarray(v.reshape(-1, 128).T)


class Builder:
    def __init__(self, cfg):
        self.cfg = cfg
        self.nc = bass.Bass("TRN2", target_bir_lowering=False)
        self.S = Sched(self.nc)
        self.dram = {}

    def din(self, name, shape):
        self.dram[name] = self.nc.dram_tensor(name, list(shape), F32, kind="ExternalInput").ap()
        return self.dram[name]

    def sb(self, es, name, shape, dt):
        self.uid = getattr(self, "uid", 0) + 1
        return T(es.enter_context(self.nc.sbuf_tensor(f"sb{self.uid}_{name}", list(shape), dt))[:])

    def mm(self, out, lhsT, rhs, start, stop, reads, writes):
        return self.S.op("pe", lambda e: e.matmul(out, lhsT=lhsT, rhs=rhs, start=start, stop=stop),
                         reads=reads, writes=writes)

    def tr(self, out, in_, n, reads, writes):
        idn = self.ident
        return self.S.op("pe", lambda e: e.transpose(out, in_, idn.ap[:n, :n]),
                         reads=list(reads) + [idn.d()], writes=writes)

    def build(self):
        nc, S, cfg = self.nc, self.S, self.cfg
        x_d = self.din("x", [SEQ, D])
        meta_d = self.din("meta", [NMETA, D])
        pv_d = self.din("pvec", [128, cfg["npv"]])
        rv_d = self.din("rvec", [1, cfg["nrv"]])
        cst_d = self.din("cst", [128, cfg["ncst"]])
        mlp_in_d = mlp_out_d = None
        if cfg["mlps"]:
            mlp_in_d = self.din("mlp_in", [DEPTH * 8, 128, 8 * 512])
            mlp_out_d = self.din("mlp_out", [DEPTH * 8, 128, 4 * 1024])
        if 0 in cfg["mixers"]:
            self.din("rw_cst", [128, 1408])
            self.din("rw_lora", [128, 8 * 288])
            self.din("rw_pair", [8, 128, 8 * 384])
            self.din("rw_wo", [8, 128, 1024])
            self.din("rw_bwa", [8, 128, 256])
            self.din("rw_bg", [8, 128, 256])
        if 1 in cfg["mixers"]:
            self.din("ssd_in", [8, 128, 8 * 772])
            self.din("ssd_out", [8, 128, 2 * 1024])
            self.din("ssd_cst", [128, 512])
        if 2 in cfg["mixers"]:
            self.din("gla_in", [4, 128, 8 * 784])
            self.din("gla_out", [4, 128, 2 * 1024])
            self.din("gla_gup", [1, 16, 512])
        if 3 in cfg["mixers"]:
            self.din("ret_in", [4, 128, 8 * 1536])
            self.din("ret_out", [4, 128, 4 * 1024])
            self.din("ret_cst", [128, cfg["nretc"]])
        out_d = nc.dram_tensor("out", [SEQ, D], F32, kind="ExternalOutput").ap()
        dbg_d = None
        if cfg.get("debug"):
            dbg_d = nc.dram_tensor("dbg", [DEPTH * 2, L, D], F32, kind="ExternalOutput").ap()

        with ExitStack() as es:
            self.h = self.sb(es, "h", [128, NT, D], F32)
            self.uT = self.sb(es, "uT", [128, 8, L], BF16)
            self.pv = self.sb(es, "pv", [128, cfg["npv"]], F32)
            self.cst = self.sb(es, "cst", [128, cfg["ncst"]], F32)
            self.ident = T(self.cst.ap[:, 0:128])
            self.ident.deps = self.cst.deps
            self.stat = self.sb(es, "stat", [128, 8], F32)
            self.ps = [T(es.enter_context(nc.psum_tensor(f"ps{i}", [128, 512], F32))[:], excl=True) for i in range(8)]
            h, uT = self.h, self.uT

            S.dma("sp", self.cst.ap, cst_d, writes=[self.cst.d()])
            S.dma("sp", self.pv.ap, pv_d, writes=[self.pv.d()])
            S.dma("sp", h.ap[0:16, 0, :], meta_d, writes=[h.d(0)])
            for g in range(4):
                S.dma("sp", h.ap[:, 1 + 4 * g:5 + 4 * g, :],
                      x_d[512 * g:512 * (g + 1), :].rearrange("(t p) d -> p t d", p=128),
                      writes=[h.d(1 + 4 * g + j) for j in range(4)])

            for layer in range(DEPTH):
                if layer in cfg["mixers"]:
                    self.emit_norm(self.pv.ap[:, cfg["pv_norm_mix"] + 8 * layer: cfg["pv_norm_mix"] + 8 * layer + 8])
                    [self.emit_rwkv, self.emit_ssd, self.emit_gla, self.emit_ret][layer % 4](layer // 4)
                    S.barrier()
                if dbg_d is not None:
                    self.emit_dump(dbg_d[2 * layer])
                if layer in cfg["mlps"]:
                    self.emit_norm(self.pv.ap[:, cfg["pv_norm_mlp"] + 8 * layer: cfg["pv_norm_mlp"] + 8 * layer + 8])
                    self.emit_mlp(layer, mlp_in_d, mlp_out_d)
                    S.barrier()
                if dbg_d is not None:
                    self.emit_dump(dbg_d[2 * layer + 1])

            self.emit_final(out_d)
        return nc

    def emit_dump(self, dst):
        S, h = self.S, self.h
        S.dma("sp", dst[0:16, :], h.ap[0:16, 0, :], reads=[h.d(0)])
        for t in range(1, NT):
            c0 = TILES[t][0]
            S.dma("sp", dst[c0:c0 + 128, :], h.ap[:, t, :], reads=[h.d(t)])

    def rstd_of(self, t, n):
        S, h, stat, junk = self.S, self.h, self.stat, self.junk
        S.op("dve", lambda e: e.scalar_tensor_tensor(out=junk.ap[:n, :], in0=h.ap[:n, t, :], scalar=1.0 / D,
                                                      in1=h.ap[:n, t, :], op0=ALU.mult, op1=ALU.mult,
                                                      accum_out=stat.ap[:n, 1:2]),
             reads=[h.d(t)], writes=[junk.d(), stat.d()])
        S.op("act", lambda e: e.activation(out=stat.ap[:n, 2:3], in_=stat.ap[:n, 1:2], func=AF.Sqrt, bias=EPS),
             reads=[stat.d()], writes=[stat.d()])
        S.op("dve", lambda e: e.reciprocal(out=stat.ap[:n, 0:1], in_=stat.ap[:n, 2:3]),
             reads=[stat.d()], writes=[stat.d()])

    def emit_norm(self, wcols):
        S, h, uT, stat = self.S, self.h, self.uT, self.stat
        es = ExitStack()
        self.junk = self.sb(es, "junk", [128, D], F32)
        un = self.sb(es, "un", [128, D], F32)
        for t, (c0, n) in enumerate(TILES):
            gi = 0 if t == 0 else 1 + (t - 1) // 4
            self.rstd_of(t, n)
            S.op("act", lambda e: e.activation(out=un.ap[:n, :], in_=h.ap[:n, t, :], func=AF.Copy,
                                               scale=stat.ap[:n, 0:1]),
                 reads=[h.d(t), stat.d()], writes=[un.d()])
            for half in range(2):
                pb = self.ps[half]
                pv3 = pb.ap.rearrange("p (j n) -> p j n", j=4)
                for j in range(4):
                    dc = half * 4 + j
                    self.tr(pv3[:, j, :n], un.ap[:n, dc * 128:(dc + 1) * 128], n, reads=[un.d()], writes=[pb.d()])
                S.op("dve", lambda e: e.tensor_tensor(out=uT.ap[:, half * 4:half * 4 + 4, c0:c0 + n],
                                                      in0=pv3[:, :, :n],
                                                      in1=wcols[:, half * 4:half * 4 + 4].unsqueeze(2).to_broadcast([128, 4, n]),
                                                      op=ALU.mult),
                     reads=[pb.d(), self.pv.d()], writes=[uT.d(gi)])
        S.barrier()
        es.close()

    def load_rv(self, es, off, n):
        t = self.sb(es, "rv", [128, n], F32)
        self.S.dma("sp", t.ap, self.dram["rvec"][:, off:off + n].partition_broadcast(128), writes=[t.d()])
        return t

    def emit_final(self, out_d):
        S, h, stat, cfg = self.S, self.h, self.stat, self.cfg
        es = ExitStack()
        self.junk = self.sb(es, "junk", [128, D], F32)
        self.rv = self.load_rv(es, cfg["rv_norm_final"], D)
        wb = self.rv.ap[:, 0:D]
        for t in range(1, NT):
            c0, n = TILES[t]
            self.rstd_of(t, n)
            S.op("dve", lambda e: e.scalar_tensor_tensor(out=h.ap[:, t, :], in0=h.ap[:, t, :], scalar=stat.ap[:, 0:1],
                                                          in1=wb, op0=ALU.mult, op1=ALU.mult),
                 reads=[h.d(t), stat.d(), self.rv.d()], writes=[h.d(t)])
            S.dma("sp", out_d[c0 - 16:c0 - 16 + 128, :], h.ap[:, t, :], reads=[h.d(t)])
        S.barrier()
        es.close()

    def emit_mlp(self, layer, mlp_in_d, mlp_out_d):
        nc, S, h, uT = self.nc, self.S, self.h, self.uT
        with ExitStack() as es:
            win = [self.sb(es, f"win{i}", [128, 8, 512], BF16) for i in range(2)]
            wout = [self.sb(es, f"wout{i}", [128, 4, 1024], BF16) for i in range(2)]
            hT = [self.sb(es, f"hT{i}", [128, 4, 512], BF16) for i in range(2)]
            rl = [self.sb(es, f"rl{i}", [128, 512], F32) for i in range(2)]
            cnt = 0
            for fb in range(8):
                wi, wo = win[fb % 2], wout[fb % 2]
                S.dma("pool", wi.ap.rearrange("p c f -> p (c f)"), mlp_in_d[layer * 8 + fb], writes=[wi.d()])
                S.dma("pool", wo.ap.rearrange("p c f -> p (c f)"), mlp_out_d[layer * 8 + fb], writes=[wo.d()])
                for gi, (g0, gn, gtiles) in enumerate(GROUPS):
                    hb = hT[cnt % 2]
                    for fc in range(4):
                        ph = self.ps[fc % 2]
                        for dc in range(8):
                            self.mm(ph.ap[:, :gn], wi.ap[:, dc, fc * 128:(fc + 1) * 128], uT.ap[:, dc, g0:g0 + gn],
                                    dc == 0, dc == 7, reads=[wi.d(), uT.d(gi)], writes=[ph.d()])
                        r = rl[fc % 2]
                        S.op("act", lambda e: e.activation(out=r.ap[:, :gn], in_=ph.ap[:, :gn], func=AF.Relu),
                             reads=[ph.d()], writes=[r.d()])
                        S.op("dve", lambda e: e.tensor_tensor(out=hb.ap[:, fc, :gn], in0=r.ap[:, :gn], in1=r.ap[:, :gn],
                                                              op=ALU.mult),
                             reads=[r.d()], writes=[hb.d()])
                    for ti, t in enumerate(gtiles):
                        n = TILES[t][1]
                        for half in range(2):
                            po = self.ps[2 + (ti * 2 + half) % 4]
                            for fc in range(4):
                                self.mm(po.ap[:n, :], hb.ap[:, fc, ti * 128:ti * 128 + n],
                                        wo.ap[:, fc, half * 512:(half + 1) * 512], fc == 0, fc == 3,
                                        reads=[hb.d(), wo.d()], writes=[po.d()])
                            S.op("dve", lambda e: e.tensor_tensor(out=h.ap[:n, t, half * 512:(half + 1) * 512],
                                                                  in0=h.ap[:n, t, half * 512:(half + 1) * 512],
                                                                  in1=po.ap[:n, :], op=ALU.add),
                                 reads=[po.d(), h.d(t)], writes=[h.d(t)])
                    cnt += 1


    def emit_rwkv(self, j):
        nc, S, h, uT, cfg = self.nc, self.S, self.h, self.uT, self.cfg
        dr = self.dram
        C0 = 0.6065306597126334
        GN_EPS = 64e-5
        pvb = cfg["pv_rwkv"]
        pvc = lambda idx, c: self.pv.ap[:, pvb + idx * 8 + c:pvb + idx * 8 + c + 1]
        mu = lambda i: self.pv.ap[:, pvb + i * 8:pvb + i * 8 + 8]
        I_W0, I_A0, I_KK, I_KA, I_RK, I_LNW, I_LNB = 6, 7, 8, 9, 10, 11, 12
        ident = self.cst.ap[:, 0:128]
        ps = self.ps
        RG = [(0, 16, [0])] + [(16 + 256 * g, 256, [1 + 2 * g, 2 + 2 * g]) for g in range(8)]
        NA, NR = 3, 4
        with ExitStack() as es:
            hid = self.sb(es, "hid", [128, 3, L], BF16)
            rc = self.sb(es, "rwc", [128, 1408], F32)
            S.dma("sp", rc.ap, dr["rw_cst"], writes=[rc.d()])
            mskA = rc.ap[:, 0:512].rearrange("p (a i) -> p a i", a=4)
            mskB = rc.ap[:, 512:1024].rearrange("p (a i) -> p a i", a=4)
            mskC = rc.ap[:, 1024:1280].rearrange("p (a i) -> p a i", a=2)
            blk = rc.ap[:, 1280:1408]
            nhalf = self.sb(es, "nhalf", [128, 128], F32)
            hb = self.sb(es, "hb", [128, 16], F32)
            S.op("pool", lambda e: e.memset(nhalf.ap, -0.5), writes=[nhalf.d()])
            S.op("pool", lambda e: e.memset(hid.ap[:, 2, :], 0.0), writes=[hid.d()])
            S.op("dve", lambda e: e.tensor_scalar(out=hb.ap, in0=self.pv.ap[:, pvb + I_W0 * 8:pvb + I_W0 * 8 + 16], scalar1=-1.0, scalar2=None, op0=ALU.mult),
                 reads=[self.pv.d()], writes=[hb.d()])

            def shift_diff(dst, c0, n, wr):
                a = 0
                if c0 == 0:
                    S.op("dve", lambda e: e.tensor_scalar(out=dst.ap[:, :, 0:1], in0=uT.ap[:, :, 0:1], scalar1=-1.0, scalar2=None, op0=ALU.mult),
                         reads=[uT.d(0)], writes=wr)
                    a = 1
                if n > a:
                    S.op("dve", lambda e: e.tensor_tensor(out=dst.ap[:, :, a:n], in0=uT.ap[:, :, c0 + a - 1:c0 + n - 1], in1=uT.ap[:, :, c0 + a:c0 + n],
                                                          op=ALU.subtract),
                         reads=[uT.d(g) for g in range(5)], writes=wr)
            with ExitStack() as es2:
                lw_p = self.sb(es2, "lwp", [128, 8, 288], BF16)
                lw_m = self.sb(es2, "lwm", [128, 8, 288], BF16)
                stg = self.sb(es2, "lstg", [128, 8, 288], F32)
                xxg = self.sb(es2, "xxg", [128, 8, 512], BF16)
                S.dma("pool", lw_p.ap.rearrange("p c f -> p (c f)"), dr["rw_lora"], writes=[lw_p.d()])
                S.dma("sp", stg.ap.rearrange("p c f -> p (c f)"), dr["rw_lora"], writes=[stg.d()])
                for (c0_, c1_, mi) in ((0, 64, 1), (64, 128, 4), (128, 288, 5)):
                    S.op("dve", lambda e: e.tensor_tensor(out=lw_m.ap[:, :, c0_:c1_], in0=stg.ap[:, :, c0_:c1_],
                                                          in1=mu(mi).unsqueeze(2).to_broadcast([128, 8, c1_ - c0_]), op=ALU.mult),
                         reads=[stg.d(), self.pv.d()], writes=[lw_m.d()])
                for gi, (g0, gn, gtiles) in enumerate(GROUPS):
                    shift_diff(xxg, g0, gn, [xxg.d()])
                    for a, (o0, o1) in enumerate(((0, 128), (128, 256), (256, 288))):
                        m = o1 - o0
                        for dc in range(8):
                            self.mm(ps[a].ap[:m, :gn], lw_p.ap[:, dc, o0:o1], uT.ap[:, dc, g0:g0 + gn], dc == 0, False,
                                    reads=[lw_p.d(), uT.d(gi)], writes=[ps[a].d()])
                        for dc in range(8):
                            self.mm(ps[a].ap[:m, :gn], lw_m.ap[:, dc, o0:o1], xxg.ap[:, dc, 0:gn], False, dc == 7,
                                    reads=[lw_m.d(), xxg.d()], writes=[ps[a].d()])
                    S.op("act", lambda e: e.activation(out=hid.ap[0:64, 0, g0:g0 + gn], in_=ps[0].ap[0:64, :gn], func=AF.Tanh),
                         reads=[ps[0].d()], writes=[hid.d()])
                    S.op("act", lambda e: e.activation(out=hid.ap[64:128, 0, g0:g0 + gn], in_=ps[0].ap[64:128, :gn], func=AF.Copy),
                         reads=[ps[0].d()], writes=[hid.d()])
                    S.op("act", lambda e: e.activation(out=hid.ap[:, 1, g0:g0 + gn], in_=ps[1].ap[:, :gn], func=AF.Tanh, scale=0.5),
                         reads=[ps[1].d()], writes=[hid.d()])
                    S.op("act", lambda e: e.activation(out=hid.ap[0:32, 2, g0:g0 + gn], in_=ps[2].ap[0:32, :gn], func=AF.Tanh, scale=0.5),
                         reads=[ps[2].d()], writes=[hid.d()])
                    S.op("dve", lambda e: e.tensor_scalar(out=hid.ap[:, 1, g0:g0 + gn], in0=hid.ap[:, 1, g0:g0 + gn], scalar1=0.5, scalar2=0.5,
                                                          op0=ALU.mult, op1=ALU.add), reads=[hid.d()], writes=[hid.d()])
                    S.op("dve", lambda e: e.tensor_scalar(out=hid.ap[0:32, 2, g0:g0 + gn], in0=hid.ap[0:32, 2, g0:g0 + gn], scalar1=0.5, scalar2=0.5,
                                                          op0=ALU.mult, op1=ALU.add), reads=[hid.d()], writes=[hid.d()])
                S.barrier()
            wp_p = self.sb(es, "wpp", [128, 8, 384], BF16)
            wp_m = self.sb(es, "wpm", [128, 8, 384], BF16)
            stg = self.sb(es, "wstg", [128, 8, 128], F32)
            wo = self.sb(es, "rwo", [128, 1024], BF16)
            Bwa = self.sb(es, "Bwa", [128, 256], BF16)
            Bg = self.sb(es, "Bg", [128, 256], BF16)
            M = self.sb(es, "wM", [128, 64], F32)
            Mb = self.sb(es, "wMb", [128, 64], BF16)
            Ysb = self.sb(es, "Ysb", [128, 128], F32)
            yln = self.sb(es, "yln", [128, 128], F32)
            og = self.sb(es, "og", [128, 128], F32)
            ogb = self.sb(es, "ogb", [128, 128], BF16)
            st2 = self.sb(es, "wst2", [128, 24], F32)
            AP_ = []
            for ai in range(NA):
                Fd = {nm: self.sb(es, f"w{nm}{ai}", [128, 128], F32) for nm in
                      ("r", "k", "v", "sg", "a", "kk", "tmp", "kmod", "cum", "ec", "en", "ecx", "bv")}
                Fd["eh"], Fd["bh"], Fd["kh"] = Fd["ec"], Fd["en"], Fd["ecx"]
                AP_.append({"F": Fd, "bt": self.sb(es, f"bt{ai}", [128, 128], BF16), "kt": self.sb(es, f"kt{ai}", [128, 128], BF16),
                            "A1": self.sb(es, f"A1{ai}", [128, 4, 128], BF16), "X2": self.sb(es, f"X2{ai}", [128, 4, 128], BF16),
                            "xx": self.sb(es, f"xx{ai}", [128, 8, 128], BF16), "bx": ps[2 + 2 * ai], "by": ps[3 + 2 * ai]})
            PB = []
            for par in range(NR):
                pb = {"at0": self.sb(es, f"at0{par}", [128, 128], BF16), "at1": self.sb(es, f"at1{par}", [128, 128], BF16),
                      "rt0": self.sb(es, f"rt0{par}", [128, 128], BF16), "rt1": self.sb(es, f"rt1{par}", [128, 128], BF16),
                      "bon": self.sb(es, f"bon{par}", [128, 128], F32), "g": self.sb(es, f"g{par}", [128, 128], F32),
                      "st": self.sb(es, f"st{par}", [128, 8], F32),
                      "tok0": self.sb(es, f"tok0{par}", [128, 3, 128], BF16), "tok1": self.sb(es, f"tok1{par}", [128, 3, 128], BF16),
                      "P0": self.sb(es, f"P0{par}", [128, 128], BF16), "P1": self.sb(es, f"P1{par}", [128, 128], BF16),
                      "U0": self.sb(es, f"U0{par}", [128, 128], BF16), "U1": self.sb(es, f"U1{par}", [128, 128], BF16),
                      "A2": self.sb(es, f"A2{par}", [128, 4, 128], BF16), "A3": self.sb(es, f"A3{par}", [128, 2, 128], BF16),
                      "TT": self.sb(es, f"TT{par}", [128, 4, 128], BF16)}
                for nm in ("at0", "at1", "rt0", "rt1", "tok0", "tok1", "P0", "P1", "U0", "U1"):
                    S.op("pool", lambda e: e.memset(pb[nm].ap, 0.0), writes=[pb[nm].d()])
                PB.append(pb)
            ones = self.cst.ap[:, 384:512]

            def stageA(hp, t, ai, B_):
                c0, n = TILES[t]
                ugi = 0 if t == 0 else 1 + (t - 1) // 4
                gs = slice(c0, c0 + n)
                P_ = AP_[ai]
                F, bx, by, A1, X2, xx = P_["F"], P_["bx"], P_["by"], P_["A1"], P_["X2"], P_["xx"]
                btk = {"bt": P_["bt"], "kt": P_["kt"]}
                A = lambda nm: F[nm].ap[:, :n]
                D_ = lambda *nms: [F[n_].d() for n_ in nms]
                shift_diff(xx, c0, n, [xx.d()])
                for i in range(3):
                    for dc in range(8):
                        self.mm(bx.ap[:, i * 128:i * 128 + n], wp_p.ap[:, dc, i * 128:(i + 1) * 128], uT.ap[:, dc, gs], dc == 0, False,
                                reads=[wp_p.d(), uT.d(ugi)], writes=[bx.d()])
                    for dc in range(8):
                        self.mm(bx.ap[:, i * 128:i * 128 + n], wp_m.ap[:, dc, i * 128:(i + 1) * 128], xx.ap[:, dc, 0:n], False, dc == 7,
                                reads=[wp_m.d(), xx.d()], writes=[bx.d()])
                self.mm(bx.ap[:, 384:384 + n], Bg.ap[:, 0:128], hid.ap[:, 1, gs], True, False, reads=[Bg.d(), hid.d()], writes=[bx.d()])
                self.mm(bx.ap[:, 384:384 + n], Bg.ap[:, 128:256], hid.ap[:, 2, gs], False, True, reads=[Bg.d(), hid.d()], writes=[bx.d()])
                self.mm(by.ap[:, 0:n], Bwa.ap[:, 0:128], hid.ap[:, 0, gs], True, True, reads=[Bwa.d(), hid.d()], writes=[by.d()])
                self.mm(by.ap[:, 128:128 + n], Bwa.ap[:, 128:256], hid.ap[:, 0, gs], True, True, reads=[Bwa.d(), hid.d()], writes=[by.d()])
                S.op("act", lambda e: e.activation(out=A("r"), in_=bx.ap[:, 0:n], func=AF.Copy), reads=[bx.d()], writes=D_("r"))
                S.op("act", lambda e: e.activation(out=A("k"), in_=bx.ap[:, 128:128 + n], func=AF.Copy), reads=[bx.d()], writes=D_("k"))
                S.op("act", lambda e: e.activation(out=A("v"), in_=bx.ap[:, 256:256 + n], func=AF.Copy), reads=[bx.d()], writes=D_("v"))
                S.op("act", lambda e: e.activation(out=B_["g"].ap[:, :n], in_=bx.ap[:, 384:384 + n], func=AF.Copy), reads=[bx.d()], writes=[B_["g"].d()])
                for nm, off, hc in (("sg", 0, hp), ("a", 128, 8 + hp)):
                    S.op("act", lambda e: e.activation(out=A(nm), in_=by.ap[:, off:off + n], func=AF.Exp, scale=-1.0, bias=hb.ap[:, hc:hc + 1]),
                         reads=[by.d(), hb.d()], writes=D_(nm))
                    S.op("act", lambda e: e.activation(out=A(nm), in_=A(nm), func=AF.Ln, bias=1.0), reads=D_(nm), writes=D_(nm))
                    S.op("act", lambda e: e.activation(out=A(nm), in_=A(nm), func=AF.Exp, scale=-1.0), reads=D_(nm), writes=D_(nm))
                S.op("dve", lambda e: e.tensor_scalar(out=A("kk"), in0=A("k"), scalar1=pvc(I_KK, hp), scalar2=None, op0=ALU.mult),
                     reads=D_("k") + [self.pv.d()], writes=D_("kk"))
                S.op("pool", lambda e: e.tensor_tensor(out=A("tmp"), in0=A("kk"), in1=A("kk"), op=ALU.mult), reads=D_("kk"), writes=D_("tmp"))
                self.mm(by.ap[:, 256:256 + n], blk, A("tmp"), True, True, reads=[rc.d()] + D_("tmp"), writes=[by.d()])
                S.op("dve", lambda e: e.tensor_scalar(out=A("tmp"), in0=by.ap[:, 256:256 + n], scalar1=1e-24, scalar2=None, op0=ALU.max),
                     reads=[by.d()], writes=D_("tmp"))
                S.op("act", lambda e: e.activation(out=A("tmp"), in_=A("tmp"), func=AF.Ln), reads=D_("tmp"), writes=D_("tmp"))
                S.op("act", lambda e: e.activation(out=A("tmp"), in_=A("tmp"), func=AF.Exp, scale=-0.5), reads=D_("tmp"), writes=D_("tmp"))
                S.op("dve", lambda e: e.tensor_tensor(out=A("kk"), in0=A("kk"), in1=A("tmp"), op=ALU.mult), reads=D_("kk", "tmp"), writes=D_("kk"))
                S.op("dve", lambda e: e.tensor_scalar(out=A("tmp"), in0=A("a"), scalar1=-1.0, scalar2=pvc(I_KA, hp), op0=ALU.add, op1=ALU.mult),
                     reads=D_("a") + [self.pv.d()], writes=D_("tmp"))
                S.op("dve", lambda e: e.scalar_tensor_tensor(out=A("kmod"), in0=A("tmp"), scalar=1.0, in1=A("k"), op0=ALU.add, op1=ALU.mult),
                     reads=D_("tmp", "k"), writes=D_("kmod"))
                S.op("dve", lambda e: e.scalar_tensor_tensor(out=A("tmp"), in0=A("r"), scalar=pvc(I_RK, hp), in1=A("kmod"), op0=ALU.mult, op1=ALU.mult),
                     reads=D_("r", "kmod") + [self.pv.d()], writes=D_("tmp"))
                self.mm(by.ap[:, 384:384 + n], blk, A("tmp"), True, True, reads=[rc.d()] + D_("tmp"), writes=[by.d()])
                S.op("dve", lambda e: e.tensor_tensor(out=B_["bon"].ap[:, :n], in0=by.ap[:, 384:384 + n], in1=A("v"), op=ALU.mult),
                     reads=[by.d()] + D_("v"), writes=[B_["bon"].d()])
                S.op("pool", lambda e: e.tensor_tensor(out=A("bv"), in0=A("kk"), in1=A("a"), op=ALU.mult), reads=D_("kk", "a"), writes=D_("bv"))
                tchunks = ((0, 16),) if n == 16 else ((0, 64), (64, 64))
                for (o, m) in tchunks:
                    S.op("dve", lambda e: e.tensor_tensor_scan(out=F["cum"].ap[:, o:o + m], data0=ones[:, :m], data1=F["sg"].ap[:, o:o + m],
                                                               initial=0.0, op0=ALU.mult, op1=ALU.add),
                         reads=D_("sg") + [self.cst.d()], writes=D_("cum"))
                S.op("act", lambda e: e.activation(out=A("ec"), in_=A("cum"), func=AF.Exp, scale=-C0), reads=D_("cum"), writes=D_("ec"))
                S.op("act", lambda e: e.activation(out=A("en"), in_=A("cum"), func=AF.Exp, scale=C0), reads=D_("cum"), writes=D_("en"))
                S.op("dve", lambda e: e.tensor_tensor(out=A("ecx"), in0=A("cum"), in1=A("sg"), op=ALU.subtract), reads=D_("cum", "sg"), writes=D_("ecx"))
                S.op("act", lambda e: e.activation(out=A("ecx"), in_=A("ecx"), func=AF.Exp, scale=-C0), reads=D_("ecx"), writes=D_("ecx"))
                for hd in range(2):
                    hs = slice(hd * 64, hd * 64 + 64)
                    S.op("dve", lambda e: e.scalar_tensor_tensor(out=B_[f"at{hd}"].ap[hs, :n], in0=F["kk"].ap[hs, :n], scalar=-1.0,
                                                                  in1=F["ecx"].ap[hs, :n], op0=ALU.mult, op1=ALU.mult),
                         reads=D_("kk", "ecx"), writes=[B_[f"at{hd}"].d()])
                    S.op("dve", lambda e: e.tensor_tensor(out=B_[f"rt{hd}"].ap[hs, :n], in0=F["r"].ap[hs, :n], in1=F["ec"].ap[hs, :n], op=ALU.mult),
                         reads=D_("r", "ec"), writes=[B_[f"rt{hd}"].d()])
                S.op("pool", lambda e: e.tensor_tensor(out=btk["bt"].ap[:, :n], in0=A("bv"), in1=A("en"), op=ALU.mult), reads=D_("bv", "en"), writes=[btk["bt"].d()])
                S.op("pool", lambda e: e.tensor_tensor(out=btk["kt"].ap[:, :n], in0=A("kmod"), in1=A("en"), op=ALU.mult), reads=D_("kmod", "en"), writes=[btk["kt"].d()])
                stp = B_["st"]
                for ci, (o, m) in enumerate(tchunks):
                    last = F["cum"].ap[:, o + m - 1:o + m]
                    S.op("dve", lambda e: e.tensor_scalar(out=stp.ap[:, ci:ci + 1], in0=last, scalar1=-C0, scalar2=None, op0=ALU.mult),
                         reads=D_("cum"), writes=[stp.d()])
                    S.op("act", lambda e: e.activation(out=F["eh"].ap[:, o:o + m], in_=F["cum"].ap[:, o:o + m], func=AF.Exp, scale=C0,
                                                       bias=stp.ap[:, ci:ci + 1]),
                         reads=D_("cum") + [stp.d()], writes=D_("eh"))
                    S.op("act", lambda e: e.activation(out=stp.ap[:, 2 + ci:3 + ci], in_=last, func=AF.Exp, scale=-C0), reads=D_("cum"), writes=[stp.d()])
                S.op("pool", lambda e: e.tensor_tensor(out=A("bh"), in0=A("bv"), in1=A("eh"), op=ALU.mult), reads=D_("bv", "eh"), writes=D_("bh"))
                S.op("pool", lambda e: e.tensor_tensor(out=A("kh"), in0=A("kmod"), in1=A("eh"), op=ALU.mult), reads=D_("kmod", "eh"), writes=D_("kh"))
                for q, nm in enumerate(("v", "bh", "kh")):
                    self.tr(bx.ap[:n, q * 128:(q + 1) * 128], F[nm].ap[:, :n], 128, reads=D_(nm), writes=[bx.d()])
                for c, (o, m) in enumerate(tchunks):
                    tk = B_[f"tok{c}"]
                    S.op("act", lambda e: e.activation(out=tk.ap[o:o + m, :, :], in_=bx.ap[o:o + m, 0:384].rearrange("p (q c) -> p q c", q=3), func=AF.Copy),
                         reads=[bx.d()], writes=[tk.d()])
                bA = by.ap.rearrange("p (a i) -> p a i", a=4)
                bB = bx.ap.rearrange("p (a i) -> p a i", a=4)
                A2, A3, TT = B_["A2"], B_["A3"], B_["TT"]
                ops_ = []
                for hd in range(2):
                    at, bt, kt, rt = B_[f"at{hd}"].ap[:, :n], btk["bt"].ap[:, :n], btk["kt"].ap[:, :n], B_[f"rt{hd}"].ap[:, :n]
                    rd = [B_[f"at{hd}"].d(), btk["bt"].d(), btk["kt"].d(), B_[f"rt{hd}"].d()]
                    ops_.append((at, bt, kt, rt, rd))
                    self.mm(bA[:n, 2 * hd, :n], bt, at, True, True, reads=rd, writes=[by.d()])
                    self.mm(bA[:n, 2 * hd + 1, :n], at, bt, True, True, reads=rd, writes=[by.d()])
                    self.mm(bB[:n, hd, :n], kt, at, True, True, reads=rd, writes=[bx.d()])
                    self.mm(bB[:n, 2 + hd, :n], bt, rt, True, True, reads=rd, writes=[bx.d()])
                S.op("dve", lambda e: e.tensor_tensor(out=A1.ap[:n, :, :n], in0=bA[:n, :, :n], in1=mskA[:n, :, :n], op=ALU.mult),
                     reads=[by.d(), rc.d()], writes=[A1.d()])
                S.op("dve", lambda e: e.tensor_tensor(out=A2.ap[:n, :, :n], in0=bB[:n, :, :n], in1=mskB[:n, :, :n], op=ALU.mult),
                     reads=[bx.d(), rc.d()], writes=[A2.d()])
                for hd in range(2):
                    at, bt, kt, rt, rd = ops_[hd]
                    self.mm(bA[:n, hd, :n], kt, rt, True, True, reads=rd, writes=[by.d()])
                S.op("dve", lambda e: e.tensor_tensor(out=A3.ap[:n, :, :n], in0=bA[:n, 0:2, :n], in1=mskC[:n, :, :n], op=ALU.mult),
                     reads=[by.d(), rc.d()], writes=[A3.d()])
                S.op("dve", lambda e: e.tensor_tensor(out=TT.ap[:n, :, :n], in0=A1.ap[:n, :, :n],
                                                      in1=ident[:n, :n].unsqueeze(1).to_broadcast([n, 4, n]), op=ALU.add),
                     reads=[A1.d(), self.cst.d()], writes=[TT.d()])
                Xc = A1
                nlev = 5 if n == 128 else 3
                for lev in range(nlev):
                    pq, pq3 = bx, bB
                    for hd in range(2):
                        Xm, Ym = Xc.ap[:n, 2 * hd, :n], Xc.ap[:n, 2 * hd + 1, :n]
                        self.mm(pq3[:n, 2 * hd, :n], Ym, Xm, True, True, reads=[Xc.d()], writes=[pq.d()])
                        self.mm(pq3[:n, 2 * hd + 1, :n], Xm, Ym, True, True, reads=[Xc.d()], writes=[pq.d()])
                    S.op("act", lambda e: e.activation(out=X2.ap[:n, :, :n], in_=pq3[:n, :, :n], func=AF.Copy), reads=[pq.d()], writes=[X2.d()])
                    Xc = X2
                    pr, pr3 = by, bA
                    for hd in range(2):
                        T_ = TT.ap[:n, 2 * hd + 1, :n]
                        self.mm(pr3[:n, 2 * hd, :n], T_, X2.ap[:n, 2 * hd, :n], True, True, reads=[TT.d(), X2.d()], writes=[pr.d()])
                        self.mm(pr3[:n, 2 * hd + 1, :n], X2.ap[:n, 2 * hd, :n], T_, True, True, reads=[TT.d(), X2.d()], writes=[pr.d()])
                    S.op("dve", lambda e: e.tensor_tensor(out=TT.ap[:n, :, :n], in0=TT.ap[:n, :, :n], in1=pr3[:n, :, :n], op=ALU.add),
                         reads=[TT.d(), pr.d()], writes=[TT.d()])

            def stageB(hp, t, B_):
                c0, n = TILES[t]
                tl = slice(0, n)
                tchunks = ((0, 16),) if n == 16 else ((0, 64), (64, 64))
                A2, A3, TT, stp = B_["A2"], B_["A3"], B_["TT"], B_["st"]
                b0, b1 = ps[0], ps[1]
                for c, (o, m) in enumerate(tchunks):
                    cs = slice(o, o + m)
                    tk, Pc, Uc = B_[f"tok{c}"], B_[f"P{c}"], B_[f"U{c}"]
                    for hd in range(2):
                        hs = slice(hd * 64, hd * 64 + 64)
                        self.mm(b0.ap[:n, hs], B_[f"at{hd}"].ap[:, tl], Mb.ap[:, :], True, False, reads=[B_[f"at{hd}"].d(), Mb.d()], writes=[b0.d()])
                        self.mm(b0.ap[:n, hs], A2.ap[:n, hd, :n], tk.ap[:n, 0, hs], False, True, reads=[A2.d(), tk.d()], writes=[b0.d()])
                    S.op("act", lambda e: e.activation(out=Pc.ap[cs, :], in_=b0.ap[cs, 0:128], func=AF.Copy), reads=[b0.d()], writes=[Pc.d()])
                    for hd in range(2):
                        hs = slice(hd * 64, hd * 64 + 64)
                        self.mm(b1.ap[:n, hs], TT.ap[:n, 2 * hd, :n], Pc.ap[:n, hs], True, True, reads=[TT.d(), Pc.d()], writes=[b1.d()])
                    S.op("dve", lambda e: e.tensor_copy(out=Uc.ap[cs, :], in_=b1.ap[cs, 0:128]), reads=[b1.d()], writes=[Uc.d()])
                    for hd in range(2):
                        hs = slice(128 + hd * 64, 128 + hd * 64 + 64)
                        hv = slice(hd * 64, hd * 64 + 64)
                        self.mm(b0.ap[:n, hs], B_[f"rt{hd}"].ap[:, tl], Mb.ap[:, :], True, False, reads=[B_[f"rt{hd}"].d(), Mb.d()], writes=[b0.d()])
                        self.mm(b0.ap[:n, hs], A2.ap[:n, 2 + hd, :n], Uc.ap[:n, hv], False, False, reads=[A2.d(), Uc.d()], writes=[b0.d()])
                        self.mm(b0.ap[:n, hs], A3.ap[:n, hd, :n], tk.ap[:n, 0, hv], False, True, reads=[A3.d(), tk.d()], writes=[b0.d()])
                    S.op("act", lambda e: e.activation(out=Ysb.ap[cs, :], in_=b0.ap[cs, 128:256], func=AF.Copy), reads=[b0.d()], writes=[Ysb.d()])
                    self.mm(b1.ap[:, 128:256], tk.ap[:n, 1, :], Uc.ap[:n, :], True, False, reads=[tk.d(), Uc.d()], writes=[b1.d()])
                    self.mm(b1.ap[:, 128:256], tk.ap[:n, 2, :], tk.ap[:n, 0, :], False, True, reads=[tk.d()], writes=[b1.d()])
                    for hd in range(2):
                        hs = slice(hd * 64, hd * 64 + 64)
                        S.op("dve", lambda e: e.scalar_tensor_tensor(out=M.ap[hs, :], in0=M.ap[hs, :], scalar=stp.ap[hs, 2 + c:3 + c],
                                                                      in1=b1.ap[hs, 128 + hd * 64:128 + hd * 64 + 64], op0=ALU.mult, op1=ALU.add),
                             reads=[M.d(), stp.d(), b1.d()], writes=[M.d()])
                    S.op("act", lambda e: e.activation(out=Mb.ap, in_=M.ap, func=AF.Copy), reads=[M.d()], writes=[Mb.d()])
                for hd in range(2):
                    hs = slice(hd * 64, hd * 64 + 64)
                    q0 = hd * 12
                    S.op("dve", lambda e: e.bn_stats(out=st2.ap[:n, q0:q0 + 6], in_=Ysb.ap[:n, hs]), reads=[Ysb.d()], writes=[st2.d()])
                    S.op("dve", lambda e: e.bn_aggr(out=st2.ap[:n, q0 + 6:q0 + 8], in_=st2.ap[:n, q0:q0 + 6]), reads=[st2.d()], writes=[st2.d()])
                    S.op("act", lambda e: e.activation(out=st2.ap[:n, q0 + 7:q0 + 8], in_=st2.ap[:n, q0 + 7:q0 + 8], func=AF.Ln, bias=GN_EPS),
                         reads=[st2.d()], writes=[st2.d()])
                    S.op("act", lambda e: e.activation(out=st2.ap[:n, q0 + 7:q0 + 8], in_=st2.ap[:n, q0 + 7:q0 + 8], func=AF.Exp, scale=-0.5),
                         reads=[st2.d()], writes=[st2.d()])
                    S.op("dve", lambda e: e.tensor_scalar(out=yln.ap[:n, hs], in0=Ysb.ap[:n, hs], scalar1=st2.ap[:n, q0 + 6:q0 + 7],
                                                          scalar2=st2.ap[:n, q0 + 7:q0 + 8], op0=ALU.subtract, op1=ALU.mult),
                         reads=[Ysb.d(), st2.d()], writes=[yln.d()])
                self.tr(b0.ap[:, 256:256 + n], yln.ap[:n, :], n, reads=[yln.d()], writes=[b0.d()])
                S.op("dve", lambda e: e.tensor_scalar(out=og.ap[:, :n], in0=b0.ap[:, 256:256 + n], scalar1=pvc(I_LNW, hp), scalar2=pvc(I_LNB, hp),
                                                      op0=ALU.mult, op1=ALU.add),
                     reads=[b0.d(), self.pv.d()], writes=[og.d()])
                S.op("pool", lambda e: e.tensor_tensor(out=og.ap[:, :n], in0=og.ap[:, :n], in1=B_["bon"].ap[:, :n], op=ALU.add),
                     reads=[og.d(), B_["bon"].d()], writes=[og.d()])
                S.op("pool", lambda e: e.tensor_tensor(out=ogb.ap[:, :n], in0=og.ap[:, :n], in1=B_["g"].ap[:, :n], op=ALU.mult),
                     reads=[og.d(), B_["g"].d()], writes=[ogb.d()])
                for half in range(2):
                    po = b1 if half == 0 else b0
                    self.mm(po.ap[:n, :], ogb.ap[:, :n], wo.ap[:, half * 512:(half + 1) * 512], True, True, reads=[ogb.d(), wo.d()], writes=[po.d()])
                    S.op("dve", lambda e: e.tensor_tensor(out=h.ap[:n, t, half * 512:(half + 1) * 512],
                                                          in0=h.ap[:n, t, half * 512:(half + 1) * 512], in1=po.ap[:n, :], op=ALU.add),
                         reads=[po.d(), h.d(t)], writes=[h.d(t)])

            for hp in range(8):
                S.dma("pool", wp_p.ap.rearrange("p c f -> p (c f)"), dr["rw_pair"][hp], writes=[wp_p.d()])
                for i, mi in enumerate((0, 2, 3)):
                    S.dma("sp", stg.ap, dr["rw_pair"][hp].rearrange("p (c f) -> p c f", c=8)[:, :, i * 128:(i + 1) * 128], writes=[stg.d()])
                    S.op("dve", lambda e: e.tensor_tensor(out=wp_m.ap[:, :, i * 128:(i + 1) * 128], in0=stg.ap,
                                                          in1=mu(mi).unsqueeze(2).to_broadcast([128, 8, 128]), op=ALU.mult),
                         reads=[stg.d(), self.pv.d()], writes=[wp_m.d()])
                S.dma("pool", wo.ap, dr["rw_wo"][hp], writes=[wo.d()])
                S.dma("pool", Bwa.ap, dr["rw_bwa"][hp], writes=[Bwa.d()])
                S.dma("pool", Bg.ap, dr["rw_bg"][hp], writes=[Bg.d()])
                S.op("pool", lambda e: e.memset(M.ap, 0.0), writes=[M.d()])
                S.op("pool", lambda e: e.memset(Mb.ap, 0.0), writes=[Mb.d()])
                run_pipeline(NT, lambda t, ai, slot: stageA(hp, t, ai, PB[slot]), lambda t, slot: stageB(hp, t, PB[slot]), NA, NR)

    def emit_ssd(self, j):
        nc, S, h, uT, cfg = self.nc, self.S, self.h, self.uT, self.cfg
        sin_d, sout_d, scst_d = self.dram["ssd_in"], self.dram["ssd_out"], self.dram["ssd_cst"]
        ident = self.cst.ap[:, 0:128]
        tri = self.cst.ap[:, 128:256]
        ones = self.cst.ap[:, 384:512]
        pvb = cfg["pv_ssd_conv"]
        cw = lambda jj, ch: self.pv.ap[:, pvb + jj * 32 + ch:pvb + jj * 32 + ch + 1]
        cb = lambda ch: self.pv.ap[:, pvb + 128 + ch:pvb + 128 + ch + 1]
        rvA = 0
        NA, NR = 3, 4
        ps = self.ps
        v3 = lambda ap: ap.rearrange("p (h i) -> p h i", h=4)
        with ExitStack() as es:
            self.rv = self.load_rv(es, cfg["rv_ssd"], 96 + 2048)
            self.junk = self.sb(es, "junk", [128, 256], F32)
            negm = self.sb(es, "snegm", [128, 4, 128], F32)
            S.dma("sp", negm.ap.rearrange("p h i -> p (h i)"), scst_d, writes=[negm.d()])
            win = self.sb(es, "swin", [128, 8, 772], BF16)
            wo = self.sb(es, "swo", [128, 2, 1024], BF16)
            Aneg = self.sb(es, "sA", [128, 32], F32)
            M = self.sb(es, "sM", [128, 256], F32)
            Mb = self.sb(es, "sMb", [128, 256], BF16)
            yn = self.sb(es, "syn", [128, 256], F32)
            ygT = self.sb(es, "sygT", [128, 2, 128], BF16)
            st = self.sb(es, "sst", [128, 8], F32)
            AP_ = []
            for ai in range(NA):
                d = {"pc": self.sb(es, f"spc{ai}", [128, 4, 131], F32), "acc": self.sb(es, f"sacc{ai}", [128, 4, 128], F32),
                     "a4": self.sb(es, f"sa4{ai}", [128, 4, 128], F32), "sig": self.sb(es, f"ssig{ai}", [128, 4, 128], F32),
                     "BTb": self.sb(es, f"sBTb{ai}", [128, 128], BF16),
                     "CTb": self.sb(es, f"sCTb{ai}", [128, 128], BF16), "xs": self.sb(es, f"sxs{ai}", [128, 256], F32),
                     "dt": self.sb(es, f"sdt{ai}", [128, 32], F32), "trila": self.sb(es, f"strila{ai}", [128, 4, 128], F32),
                     "dec": self.sb(es, f"sdec{ai}", [128, 4, 128], F32), "ecr": self.sb(es, f"secr{ai}", [128, 4, 128], F32),
                     "bx": ps[2 + 2 * ai], "by": ps[3 + 2 * ai]}
                S.op("pool", lambda e: e.memset(d["dt"].ap, 0.0), writes=[d["dt"].d()])
                AP_.append(d)
            RB = []
            for r_ in range(NR):
                RB.append({"sz": self.sb(es, f"ssz{r_}", [128, 256], F32), "Btok": self.sb(es, f"sBtok{r_}", [128, 128], BF16),
                           "el": self.sb(es, f"sel{r_}", [128, 4], F32), "Pm": self.sb(es, f"sP{r_}", [128, 4, 128], BF16),
                           "CTs": self.sb(es, f"sCTs{r_}", [128, 4, 128], BF16), "vsb": self.sb(es, f"sv{r_}", [128, 256], BF16),
                           "vh": self.sb(es, f"svh{r_}", [128, 256], BF16), "t1": self.sb(es, f"st1{r_}", [128, 256], F32)})
            S.op("act", lambda e: e.activation(out=Aneg.ap, in_=self.rv.ap[:, rvA + 32:rvA + 64], func=AF.Exp),
                 reads=[self.rv.d()], writes=[Aneg.d()])
            S.op("dve", lambda e: e.tensor_scalar(out=Aneg.ap, in0=Aneg.ap, scalar1=-1.0, scalar2=None, op0=ALU.mult),
                 reads=[Aneg.d()], writes=[Aneg.d()])

            def sig_from(dst, src, rd, wr, eng_reads_psum=False):
                S.op("act", lambda e: e.activation(out=dst, in_=src, func=AF.Exp, scale=-1.0), reads=rd, writes=wr)
                S.op("act", lambda e: e.activation(out=dst, in_=dst, func=AF.Ln, bias=1.0), reads=wr, writes=wr)
                S.op("act", lambda e: e.activation(out=dst, in_=dst, func=AF.Exp, scale=-1.0), reads=wr, writes=wr)

            def stageA(g, t, ai, R_):
                c0, n = TILES[t]
                ugi = 0 if t == 0 else 1 + (t - 1) // 4
                ugp = 0 if t <= 1 else 1 + (t - 2) // 4
                P_ = AP_[ai]
                pc, acc, a4, sig, BTb, CTb, xs, dt, trila, dec, ecr, bx, by = (P_[k] for k in
                    ("pc", "acc", "a4", "sig", "BTb", "CTb", "xs", "dt", "trila", "dec", "ecr", "bx", "by"))
                chans = [2 * g, 2 * g + 1, 16 + g, 24 + g]
                dtb = self.rv.ap[:, rvA + g * 4:rvA + g * 4 + 4]
                Ag = Aneg.ap[:, g * 4:g * 4 + 4]
                dsk = self.rv.ap[:, rvA + 64 + g * 4:rvA + 64 + g * 4 + 4]
                if c0 == 0:
                    S.op("pool", lambda e: e.memset(pc.ap[:, :, 0:3], 0.0), writes=[pc.d()])
                    src0, dst0, w_ = 0, 3, n
                else:
                    src0, dst0, w_ = c0 - 3, 0, n + 3
                for a in range(4):
                    pb, off = (bx, by)[a // 2], (a % 2) * 256
                    for dc in range(8):
                        self.mm(pb.ap[:, off:off + w_], win.ap[:, dc, a * 128:(a + 1) * 128], uT.ap[:, dc, src0:src0 + w_], dc == 0, dc == 7,
                                reads=[win.d(), uT.d(ugi), uT.d(ugp)], writes=[pb.d()])
                for hf, pb in enumerate((bx, by)):
                    S.op("act", lambda e: e.activation(out=pc.ap[:, 2 * hf:2 * hf + 2, dst0:dst0 + w_],
                                                       in_=pb.ap.rearrange("p (a i) -> p a i", a=2)[:, :, 0:w_], func=AF.Copy),
                         reads=[pb.d()], writes=[pc.d()])
                for a in range(4):
                    ch = chans[a]
                    S.op("dve", lambda e: e.tensor_scalar(out=acc.ap[:, a, :n], in0=pc.ap[:, a, 0:n], scalar1=cw(0, ch), scalar2=cb(ch),
                                                          op0=ALU.mult, op1=ALU.add),
                         reads=[pc.d(), self.pv.d()], writes=[acc.d()])
                    for jj in range(1, 4):
                        S.op("dve", lambda e: e.scalar_tensor_tensor(out=acc.ap[:, a, :n], in0=pc.ap[:, a, jj:jj + n], scalar=cw(jj, ch),
                                                                      in1=acc.ap[:, a, :n], op0=ALU.mult, op1=ALU.add),
                             reads=[pc.d(), self.pv.d(), acc.d()], writes=[acc.d()])
                sig_from(sig.ap[:, :, :n], acc.ap[:, :, :n], [acc.d()], [sig.d()])
                S.op("pool", lambda e: e.tensor_tensor(out=a4.ap[:, :, :n], in0=acc.ap[:, :, :n], in1=sig.ap[:, :, :n], op=ALU.mult),
                     reads=[acc.d(), sig.d()], writes=[a4.d()])
                S.op("act", lambda e: e.activation(out=BTb.ap[:, :n], in_=a4.ap[:, 2, :n], func=AF.Copy), reads=[a4.d()], writes=[BTb.d()])
                S.op("act", lambda e: e.activation(out=CTb.ap[:, :n], in_=a4.ap[:, 3, :n], func=AF.Copy), reads=[a4.d()], writes=[CTb.d()])
                for dc in range(8):
                    self.mm(bx.ap[:n, 0:260], uT.ap[:, dc, c0:c0 + n], win.ap[:, dc, 512:772], dc == 0, dc == 7,
                            reads=[win.d(), uT.d(ugi)], writes=[bx.d()])
                sz = R_["sz"]
                sig_from(sz.ap[:n, :], bx.ap[:n, 0:256], [bx.d()], [sz.d()])
                S.op("dve", lambda e: e.tensor_tensor(out=sz.ap[:n, :], in0=sz.ap[:n, :], in1=bx.ap[:n, 0:256], op=ALU.mult),
                     reads=[sz.d(), bx.d()], writes=[sz.d()])
                S.op("dve", lambda e: e.tensor_tensor(out=dt.ap[:n, 0:4], in0=bx.ap[:n, 256:260], in1=dtb[:n, :], op=ALU.add),
                     reads=[bx.d(), self.rv.d()], writes=[dt.d()])
                S.op("act", lambda e: e.activation(out=dt.ap[:n, 0:4], in_=dt.ap[:n, 0:4], func=AF.Exp), reads=[dt.d()], writes=[dt.d()])
                S.op("act", lambda e: e.activation(out=dt.ap[:n, 0:4], in_=dt.ap[:n, 0:4], func=AF.Ln, bias=1.0),
                     reads=[dt.d()], writes=[dt.d()])
                S.op("dve", lambda e: e.tensor_tensor(out=dt.ap[:n, 4:8], in0=dt.ap[:n, 0:4], in1=Ag[:n, :], op=ALU.mult),
                     reads=[dt.d(), Aneg.d()], writes=[dt.d()])
                for c in range(2):
                    self.tr(by.ap[:n, c * 128:(c + 1) * 128], a4.ap[:, c, :n], 128, reads=[a4.d()], writes=[by.d()])
                self.tr(by.ap[:n, 256:384], a4.ap[:, 2, :n], 128, reads=[a4.d()], writes=[by.d()])
                S.op("act", lambda e: e.activation(out=xs.ap[:n, :], in_=by.ap[:n, 0:256], func=AF.Copy), reads=[by.d()], writes=[xs.d()])
                S.op("act", lambda e: e.activation(out=R_["Btok"].ap[:n, :], in_=by.ap[:n, 256:384], func=AF.Copy),
                     reads=[by.d()], writes=[R_["Btok"].d()])
                self.mm(bx.ap[:, 384:400], tri[:n, :], dt.ap[:n, 4:20], True, True, reads=[self.cst.d(), dt.d()], writes=[bx.d()])
                S.op("dve", lambda e: e.tensor_scalar(out=dt.ap[:n, 8:12], in0=bx.ap[:n, 384:388], scalar1=-1.0, scalar2=None, op0=ALU.mult),
                     reads=[bx.d()], writes=[dt.d()])
                S.op("dve", lambda e: e.tensor_tensor(out=trila.ap[:n, :, :], in0=tri[:n, :].unsqueeze(1).to_broadcast([n, 4, 128]),
                                                      in1=dt.ap[:n, 4:8].unsqueeze(2).to_broadcast([n, 4, 128]), op=ALU.mult),
                     reads=[self.cst.d(), dt.d()], writes=[trila.d()])
                crow = v3(by.ap)[:, :, :n]
                self.mm(by.ap, ones[:n, :], trila.ap[:n, :, :].rearrange("p h i -> p (h i)"), True, True,
                        reads=[self.cst.d(), trila.d()], writes=[by.d()])
                S.op("act", lambda e: e.activation(out=ecr.ap[:, :, :n], in_=crow, func=AF.Exp), reads=[by.d()], writes=[ecr.d()])
                S.op("pool", lambda e: e.tensor_copy(out=R_["el"].ap, in_=ecr.ap[:, :, n - 1]), reads=[ecr.d()], writes=[R_["el"].d()])
                S.op("dve", lambda e: e.tensor_tensor(out=dt.ap[:n, 12:16], in0=v3(by.ap)[:n, :, n - 1], in1=dt.ap[:n, 8:12], op=ALU.add),
                     reads=[by.d(), dt.d()], writes=[dt.d()])
                S.op("act", lambda e: e.activation(out=dt.ap[:n, 12:16], in_=dt.ap[:n, 12:16], func=AF.Exp), reads=[dt.d()], writes=[dt.d()])
                S.op("dve", lambda e: e.tensor_tensor(out=dt.ap[:n, 16:20], in0=dt.ap[:n, 12:16], in1=dt.ap[:n, 0:4], op=ALU.mult),
                     reads=[dt.d()], writes=[dt.d()])
                self.mm(by.ap, ident[:n, :], negm.ap[:n, :, :].rearrange("p h i -> p (h i)"), False, True,
                        reads=[self.cst.d(), negm.d()], writes=[by.d()])
                for hh in range(4):
                    S.op("act", lambda e: e.activation(out=dec.ap[:n, hh, :n], in_=v3(by.ap)[:n, hh, :n], func=AF.Exp,
                                                       bias=dt.ap[:n, 8 + hh:9 + hh]),
                         reads=[by.d(), dt.d()], writes=[dec.d()])
                self.mm(bx.ap[:n, :n], BTb.ap[:, :n], CTb.ap[:, :n], True, True, reads=[BTb.d(), CTb.d()], writes=[bx.d()])
                S.op("dve", lambda e: e.tensor_tensor(out=R_["Pm"].ap[:n, :, :n], in0=bx.ap[:n, :n].unsqueeze(1).to_broadcast([n, 4, n]),
                                                      in1=dec.ap[:n, :, :n], op=ALU.mult),
                     reads=[bx.d(), dec.d()], writes=[R_["Pm"].d()])
                S.op("pool", lambda e: e.tensor_tensor(out=R_["CTs"].ap[:, :, :n], in0=a4.ap[:, 3, :n].unsqueeze(1).to_broadcast([128, 4, n]),
                                                       in1=ecr.ap[:, :, :n], op=ALU.mult),
                     reads=[a4.d(), ecr.d()], writes=[R_["CTs"].d()])
                x3 = xs.ap[:n, :].rearrange("p (h q) -> p h q", h=4)
                S.op("dve", lambda e: e.tensor_tensor(out=R_["vsb"].ap[:n, :].rearrange("p (h q) -> p h q", h=4), in0=x3,
                                                      in1=dt.ap[:n, 0:4].unsqueeze(2).to_broadcast([n, 4, 64]), op=ALU.mult),
                     reads=[xs.d(), dt.d()], writes=[R_["vsb"].d()])
                S.op("dve", lambda e: e.tensor_tensor(out=R_["vh"].ap[:n, :].rearrange("p (h q) -> p h q", h=4), in0=x3,
                                                      in1=dt.ap[:n, 16:20].unsqueeze(2).to_broadcast([n, 4, 64]), op=ALU.mult),
                     reads=[xs.d(), dt.d()], writes=[R_["vh"].d()])
                S.op("dve", lambda e: e.tensor_tensor(out=R_["t1"].ap[:n, :].rearrange("p (h q) -> p h q", h=4), in0=x3,
                                                      in1=dsk[:n, :].unsqueeze(2).to_broadcast([n, 4, 64]), op=ALU.mult),
                     reads=[xs.d(), self.rv.d()], writes=[R_["t1"].d()])

            def stageB(g, t, R_):
                c0, n = TILES[t]
                b0, b1 = ps[0], ps[1]
                nwb = self.rv.ap[:, rvA + 96 + g * 256:rvA + 96 + (g + 1) * 256]
                Pm, CTs, vsb, vh, t1, sz, Btok, el = (R_[k] for k in ("Pm", "CTs", "vsb", "vh", "t1", "sz", "Btok", "el"))
                for hh in range(4):
                    cs_ = slice(hh * 64, (hh + 1) * 64)
                    self.mm(b0.ap[:n, cs_], Pm.ap[:n, hh, :n], vsb.ap[:n, cs_], True, False, reads=[Pm.d(), vsb.d()], writes=[b0.d()])
                    self.mm(b0.ap[:n, cs_], CTs.ap[:, hh, :n], Mb.ap[:, cs_], False, True, reads=[CTs.d(), Mb.d()], writes=[b0.d()])
                self.mm(b1.ap[:, 0:256], Btok.ap[:n, :], vh.ap[:n, :], True, True, reads=[Btok.d(), vh.d()], writes=[b1.d()])
                S.op("dve", lambda e: e.tensor_tensor(out=M.ap.rearrange("p (h q) -> p h q", h=4), in0=M.ap.rearrange("p (h q) -> p h q", h=4),
                                                      in1=el.ap.unsqueeze(2).to_broadcast([128, 4, 64]), op=ALU.mult),
                     reads=[M.d(), el.d()], writes=[M.d()])
                S.op("dve", lambda e: e.tensor_tensor(out=M.ap, in0=M.ap, in1=b1.ap[:, 0:256], op=ALU.add),
                     reads=[M.d(), b1.d()], writes=[M.d()])
                S.op("act", lambda e: e.activation(out=Mb.ap, in_=M.ap, func=AF.Copy), reads=[M.d()], writes=[Mb.d()])
                S.op("dve", lambda e: e.tensor_tensor(out=t1.ap[:n, :], in0=t1.ap[:n, :], in1=b0.ap[:n, 0:256], op=ALU.add),
                     reads=[t1.d(), b0.d()], writes=[t1.d()])
                S.op("pool", lambda e: e.tensor_tensor(out=t1.ap[:n, :], in0=t1.ap[:n, :], in1=sz.ap[:n, :], op=ALU.mult),
                     reads=[t1.d(), sz.d()], writes=[t1.d()])
                S.op("act", lambda e: e.activation(out=self.junk.ap[:n, 0:256], in_=t1.ap[:n, :], func=AF.Square, accum_out=st.ap[:n, 2:3]),
                     reads=[t1.d()], writes=[self.junk.d(), st.d()])
                S.op("act", lambda e: e.activation(out=st.ap[:n, 3:4], in_=st.ap[:n, 2:3], func=AF.Ln, scale=1.0 / 256.0, bias=EPS),
                     reads=[st.d()], writes=[st.d()])
                S.op("act", lambda e: e.activation(out=st.ap[:n, 4:5], in_=st.ap[:n, 3:4], func=AF.Exp, scale=-0.5), reads=[st.d()], writes=[st.d()])
                S.op("dve", lambda e: e.scalar_tensor_tensor(out=yn.ap[:n, :], in0=t1.ap[:n, :], scalar=st.ap[:n, 4:5], in1=nwb[:n, :],
                                                              op0=ALU.mult, op1=ALU.mult),
                     reads=[t1.d(), st.d(), self.rv.d()], writes=[yn.d()])
                self.out_proj(yn, n, 2, ygT, wo, t, b1, (b0, b1))

            for g in range(8):
                S.dma("pool", win.ap.rearrange("p c f -> p (c f)"), sin_d[j * 8 + g], writes=[win.d()])
                S.dma("pool", wo.ap.rearrange("p c f -> p (c f)"), sout_d[j * 8 + g], writes=[wo.d()])
                S.op("pool", lambda e: e.memset(M.ap, 0.0), writes=[M.d()])
                S.op("pool", lambda e: e.memset(Mb.ap, 0.0), writes=[Mb.d()])
                run_pipeline(NT, lambda t, ai, slot: stageA(g, t, ai, RB[slot]), lambda t, slot: stageB(g, t, RB[slot]), NA, NR)

    def emit_gla(self, j):
        nc, S, h, uT, cfg = self.nc, self.S, self.h, self.uT, self.cfg
        gin_d, gout_d, gup_d = self.dram["gla_in"], self.dram["gla_out"], self.dram["gla_gup"]
        tri = self.cst.ap[:, 128:256]
        ones = self.cst.ap[:, 384:512]
        gb = self.pv.ap[:, cfg["pv_gla_bias"]:cfg["pv_gla_bias"] + 4]
        with ExitStack() as es:
            self.rv = self.load_rv(es, cfg["rv_gla_norm"], 256)
            self.junk = self.sb(es, "junk", [128, 256], F32)
            nwb = self.rv.ap[:, 0:256]
            win = self.sb(es, "gwin", [128, 8, 784], BF16)
            wo = self.sb(es, "gwo", [128, 2, 1024], BF16)
            gup = self.sb(es, "gup", [16, 512], F32)
            negb = self.sb(es, "gnegb", [128, 4], F32)
            M = self.sb(es, "gM", [128, 256], F32)
            Mb = self.sb(es, "gMb", [128, 256], BF16)
            glr = self.sb(es, "gglr", [16, 512], F32)
            sp = self.sb(es, "gsp", [128, 512], F32)
            cum = self.sb(es, "gcum", [128, 512], F32)
            ec = self.sb(es, "gec", [128, 512], F32)
            en = self.sb(es, "gen", [128, 512], F32)
            qt = self.sb(es, "gqt", [128, 512], BF16)
            kt = self.sb(es, "gkt", [128, 512], BF16)
            khT = self.sb(es, "gkhT", [128, 128], F32)
            eh = self.sb(es, "geh", [128, 128], F32)
            khat = self.sb(es, "gkhat", [128, 128], BF16)
            vsb = self.sb(es, "gv", [128, 256], BF16)
            sr = self.sb(es, "gsr", [128, 256], F32)
            Pm = self.sb(es, "gP", [128, 128], BF16)
            yn = self.sb(es, "gyn", [128, 256], F32)
            ygT = self.sb(es, "gygT", [128, 2, 128], BF16)
            st = self.sb(es, "gst", [128, 8], F32)
            ps = self.ps
            S.dma("sp", gup.ap, gup_d[j], writes=[gup.d()])
            S.op("dve", lambda e: e.tensor_scalar(out=negb.ap, in0=gb, scalar1=-1.0, scalar2=None, op0=ALU.mult),
                 reads=[self.pv.d()], writes=[negb.d()])
            for hd in range(4):
                S.dma("pool", win.ap.rearrange("p c f -> p (c f)"), gin_d[j * 4 + hd], writes=[win.d()])
                S.dma("pool", wo.ap.rearrange("p c f -> p (c f)"), gout_d[j * 4 + hd], writes=[wo.d()])
                S.op("pool", lambda e: e.memset(M.ap, 0.0), writes=[M.d()])
                S.op("pool", lambda e: e.memset(Mb.ap, 0.0), writes=[Mb.d()])
                for gi, (g0, gn, gtiles) in enumerate(GROUPS):
                    for a, (o0, o1) in enumerate(((0, 128), (128, 256), (256, 272))):
                        m = o1 - o0
                        for dc in range(8):
                            self.mm(ps[a].ap[:m, :gn], win.ap[:, dc, o0:o1], uT.ap[:, dc, g0:g0 + gn], dc == 0, dc == 7,
                                    reads=[win.d(), uT.d(gi)], writes=[ps[a].d()])
                    S.op("act", lambda e: e.activation(out=glr.ap[:, :gn], in_=ps[2].ap[:16, :gn], func=AF.Copy),
                         reads=[ps[2].d()], writes=[glr.d()])
                    self.mm(ps[3].ap[:, :gn], gup.ap[:, hd * 128:(hd + 1) * 128], glr.ap[:, :gn], True, True,
                            reads=[gup.d(), glr.d()], writes=[ps[3].d()])
                    S.op("act", lambda e: e.activation(out=sp.ap[:, :gn], in_=ps[3].ap[:, :gn], func=AF.Exp, scale=-1.0,
                                                       bias=negb.ap[:, hd:hd + 1]),
                         reads=[ps[3].d(), negb.d()], writes=[sp.d()])
                    S.op("act", lambda e: e.activation(out=sp.ap[:, :gn], in_=sp.ap[:, :gn], func=AF.Ln, bias=1.0),
                         reads=[sp.d()], writes=[sp.d()])
                    for ti, t in enumerate(gtiles):
                        n = TILES[t][1]
                        lo = ti * 128
                        S.op("dve", lambda e: e.tensor_tensor_scan(out=cum.ap[:, lo:lo + n], data0=ones[:, :n], data1=sp.ap[:, lo:lo + n],
                                                                   initial=0.0, op0=ALU.mult, op1=ALU.add),
                             reads=[sp.d(), self.cst.d()], writes=[cum.d()])
                    S.op("act", lambda e: e.activation(out=ec.ap[:, :gn], in_=cum.ap[:, :gn], func=AF.Exp, scale=-1.0 / 16.0),
                         reads=[cum.d()], writes=[ec.d()])
                    S.op("act", lambda e: e.activation(out=en.ap[:, :gn], in_=cum.ap[:, :gn], func=AF.Exp, scale=1.0 / 16.0),
                         reads=[cum.d()], writes=[en.d()])
                    S.op("dve", lambda e: e.scalar_tensor_tensor(out=qt.ap[:, :gn], in0=ps[0].ap[:, :gn], scalar=128.0 ** -0.5,
                                                                  in1=ec.ap[:, :gn], op0=ALU.mult, op1=ALU.mult),
                         reads=[ps[0].d(), ec.d()], writes=[qt.d()])
                    S.op("dve", lambda e: e.tensor_tensor(out=kt.ap[:, :gn], in0=ps[1].ap[:, :gn], in1=en.ap[:, :gn], op=ALU.mult),
                         reads=[ps[1].d(), en.d()], writes=[kt.d()])
                    for ti, t in enumerate(gtiles):
                        c0, n = TILES[t]
                        lo = ti * 128
                        last = cum.ap[:, lo + n - 1:lo + n]
                        for dc in range(8):
                            self.mm(ps[4].ap[:n, :], uT.ap[:, dc, c0:c0 + n], win.ap[:, dc, 272:784], dc == 0, dc == 7,
                                    reads=[win.d(), uT.d(gi)], writes=[ps[4].d()])
                        S.op("act", lambda e: e.activation(out=vsb.ap[:n, :], in_=ps[4].ap[:n, 0:256], func=AF.Copy),
                             reads=[ps[4].d()], writes=[vsb.d()])
                        S.op("act", lambda e: e.activation(out=sr.ap[:n, :], in_=ps[4].ap[:n, 256:512], func=AF.Silu),
                             reads=[ps[4].d()], writes=[sr.d()])
                        self.mm(ps[5].ap[:n, :n], kt.ap[:, lo:lo + n], qt.ap[:, lo:lo + n], True, True,
                                reads=[kt.d(), qt.d()], writes=[ps[5].d()])
                        S.op("dve", lambda e: e.tensor_tensor(out=Pm.ap[:n, :n], in0=ps[5].ap[:n, :n], in1=tri[:n, :n], op=ALU.mult),
                             reads=[ps[5].d(), self.cst.d()], writes=[Pm.d()])
                        self.mm(ps[6].ap[:n, 0:256], Pm.ap[:n, :n], vsb.ap[:n, :], True, False, reads=[Pm.d(), vsb.d()], writes=[ps[6].d()])
                        self.mm(ps[6].ap[:n, 0:256], qt.ap[:, lo:lo + n], Mb.ap, False, True, reads=[qt.d(), Mb.d()], writes=[ps[6].d()])
                        S.op("dve", lambda e: e.tensor_scalar(out=st.ap[:, 0:1], in0=last, scalar1=-1.0 / 16.0, scalar2=None, op0=ALU.mult),
                             reads=[cum.d()], writes=[st.d()])
                        S.op("act", lambda e: e.activation(out=eh.ap[:, :n], in_=cum.ap[:, lo:lo + n], func=AF.Exp, scale=1.0 / 16.0,
                                                           bias=st.ap[:, 0:1]),
                             reads=[cum.d(), st.d()], writes=[eh.d()])
                        S.op("act", lambda e: e.activation(out=st.ap[:, 1:2], in_=last, func=AF.Exp, scale=-1.0 / 16.0),
                             reads=[cum.d()], writes=[st.d()])
                        S.op("dve", lambda e: e.tensor_tensor(out=khT.ap[:, :n], in0=ps[1].ap[:, lo:lo + n], in1=eh.ap[:, :n], op=ALU.mult),
                             reads=[ps[1].d(), eh.d()], writes=[khT.d()])
                        self.tr(ps[5].ap[:n, 128:256], khT.ap[:, :n], 128, reads=[khT.d()], writes=[ps[5].d()])
                        S.op("act", lambda e: e.activation(out=khat.ap[:n, :], in_=ps[5].ap[:n, 128:256], func=AF.Copy),
                             reads=[ps[5].d()], writes=[khat.d()])
                        self.mm(ps[7].ap[:, 0:256], khat.ap[:n, :], vsb.ap[:n, :], True, True, reads=[khat.d(), vsb.d()], writes=[ps[7].d()])
                        S.op("dve", lambda e: e.scalar_tensor_tensor(out=M.ap, in0=M.ap, scalar=st.ap[:, 1:2], in1=ps[7].ap[:, 0:256],
                                                                      op0=ALU.mult, op1=ALU.add),
                             reads=[M.d(), st.d(), ps[7].d()], writes=[M.d()])
                        S.op("act", lambda e: e.activation(out=Mb.ap, in_=M.ap, func=AF.Copy), reads=[M.d()], writes=[Mb.d()])
                        S.op("act", lambda e: e.activation(out=self.junk.ap[:n, 0:256], in_=ps[6].ap[:n, 0:256], func=AF.Square,
                                                           accum_out=st.ap[:n, 2:3]),
                             reads=[ps[6].d()], writes=[self.junk.d(), st.d()])
                        S.op("act", lambda e: e.activation(out=st.ap[:n, 3:4], in_=st.ap[:n, 2:3], func=AF.Sqrt, scale=1.0 / 256.0, bias=EPS),
                             reads=[st.d()], writes=[st.d()])
                        S.op("dve", lambda e: e.reciprocal(out=st.ap[:n, 4:5], in_=st.ap[:n, 3:4]), reads=[st.d()], writes=[st.d()])
                        S.op("dve", lambda e: e.scalar_tensor_tensor(out=yn.ap[:n, :], in0=ps[6].ap[:n, 0:256], scalar=st.ap[:n, 4:5],
                                                                      in1=nwb[:n, :], op0=ALU.mult, op1=ALU.mult),
                             reads=[ps[6].d(), st.d(), self.rv.d()], writes=[yn.d()])
                        S.op("pool", lambda e: e.tensor_tensor(out=yn.ap[:n, :], in0=yn.ap[:n, :], in1=sr.ap[:n, :], op=ALU.mult),
                             reads=[yn.d(), sr.d()], writes=[yn.d()])
                        self.out_proj(yn, n, 2, ygT, wo, t, ps[5], (ps[4], ps[7]))


    def emit_ret(self, j):
        nc, S, h, uT, cfg = self.nc, self.S, self.h, self.uT, self.cfg
        ret_in_d, ret_out_d, ret_cst_d = self.dram["ret_in"], self.dram["ret_out"], self.dram["ret_cst"]
        NRC = cfg["nretc"]
        with ExitStack() as es:
            rc = self.sb(es, "retc", [128, NRC], F32)
            S.dma("sp", rc.ap, ret_cst_d, writes=[rc.d()])
            decT = lambda hd: rc.ap[:, hd * 128:(hd + 1) * 128]
            rowpow = lambda hd: rc.ap[:, 512 + hd * 128:512 + (hd + 1) * 128]
            kdec = lambda hd, n: rc.ap[:, 1024 + (0 if n == 128 else 4) + hd:1024 + (0 if n == 128 else 4) + hd + 1]
            cosT = rc.ap[:, 1032:1032 + L]
            sinT = rc.ap[:, 1032 + L:1032 + 2 * L]
            win = self.sb(es, "rwin", [128, 8, 1536], BF16)
            wo = self.sb(es, "rwo", [128, 4, 1024], BF16)
            M = self.sb(es, "rM", [128, 2, 512], F32)
            Mb = self.sb(es, "rMb", [128, 2, 512], BF16)
            qT = self.sb(es, "rqT", [128, 2, 512], BF16)
            kT = self.sb(es, "rkT", [128, 2, 512], BF16)
            kf = self.sb(es, "rkf", [128, 2, 512], F32)
            tmp = [self.sb(es, f"rtmp{i}", [128, 512], F32) for i in range(2)]
            vsb = self.sb(es, "rv", [128, 512], BF16)
            sg = self.sb(es, "rsg", [128, 512], F32)
            Pm = self.sb(es, "rP", [128, 128], BF16)
            qs = self.sb(es, "rqs", [128, 2, 128], BF16)
            khat = self.sb(es, "rkhat", [128, 256], BF16)
            yn = self.sb(es, "ryn", [128, 512], F32)
            ygT = self.sb(es, "rygT", [128, 4, 128], BF16)
            st = self.sb(es, "rst", [128, 16], F32)
            ps = self.ps
            for hd in range(4):
                gam = 1.0 - 2.0 ** (-5.0 - hd)
                S.dma("pool", win.ap.rearrange("p c f -> p (c f)"), ret_in_d[j * 4 + hd], writes=[win.d()])
                S.dma("pool", wo.ap.rearrange("p c f -> p (c f)"), ret_out_d[j * 4 + hd], writes=[wo.d()])
                S.op("pool", lambda e: e.memset(M.ap, 0.0), writes=[M.d()])
                S.op("pool", lambda e: e.memset(Mb.ap, 0.0), writes=[Mb.d()])
                for gi, (g0, gn, gtiles) in enumerate(GROUPS):
                    for a in range(4):
                        for dc in range(8):
                            self.mm(ps[a].ap[:, :gn], win.ap[:, dc, a * 128:(a + 1) * 128], uT.ap[:, dc, g0:g0 + gn],
                                    dc == 0, dc == 7, reads=[win.d(), uT.d(gi)], writes=[ps[a].d()])
                    cs, sn = cosT[:, g0:g0 + gn], sinT[:, g0:g0 + gn]
                    for qk in range(2):
                        p1, p2 = ps[2 * qk], ps[2 * qk + 1]
                        sc = 1.0 if qk == 0 else 1.0 / 16.0
                        dstb = qT if qk == 0 else kT
                        t0, t1 = tmp
                        S.op("dve", lambda e: e.scalar_tensor_tensor(out=t0.ap[:, :gn], in0=p1.ap[:, :gn], scalar=sc, in1=cs,
                                                                      op0=ALU.mult, op1=ALU.mult),
                             reads=[p1.d(), rc.d()], writes=[t0.d()])
                        S.op("dve", lambda e: e.scalar_tensor_tensor(out=t1.ap[:, :gn], in0=p2.ap[:, :gn], scalar=sc, in1=sn,
                                                                      op0=ALU.mult, op1=ALU.mult),
                             reads=[p2.d(), rc.d()], writes=[t1.d()])
                        if qk == 0:
                            S.op("pool", lambda e: e.tensor_tensor(out=dstb.ap[:, 0, :gn], in0=t0.ap[:, :gn], in1=t1.ap[:, :gn], op=ALU.subtract),
                                 reads=[t0.d(), t1.d()], writes=[dstb.d()])
                        else:
                            S.op("pool", lambda e: e.tensor_tensor(out=kf.ap[:, 0, :gn], in0=t0.ap[:, :gn], in1=t1.ap[:, :gn], op=ALU.subtract),
                                 reads=[t0.d(), t1.d()], writes=[kf.d()])
                        S.op("dve", lambda e: e.scalar_tensor_tensor(out=t0.ap[:, :gn], in0=p1.ap[:, :gn], scalar=sc, in1=sn,
                                                                      op0=ALU.mult, op1=ALU.mult),
                             reads=[p1.d(), rc.d()], writes=[t0.d()])
                        S.op("dve", lambda e: e.scalar_tensor_tensor(out=t1.ap[:, :gn], in0=p2.ap[:, :gn], scalar=sc, in1=cs,
                                                                      op0=ALU.mult, op1=ALU.mult),
                             reads=[p2.d(), rc.d()], writes=[t1.d()])
                        if qk == 0:
                            S.op("pool", lambda e: e.tensor_tensor(out=dstb.ap[:, 1, :gn], in0=t0.ap[:, :gn], in1=t1.ap[:, :gn], op=ALU.add),
                                 reads=[t0.d(), t1.d()], writes=[dstb.d()])
                        else:
                            S.op("pool", lambda e: e.tensor_tensor(out=kf.ap[:, 1, :gn], in0=t0.ap[:, :gn], in1=t1.ap[:, :gn], op=ALU.add),
                                 reads=[t0.d(), t1.d()], writes=[kf.d()])
                            S.op("act", lambda e: e.activation(out=kT.ap[:, :, :gn], in_=kf.ap[:, :, :gn], func=AF.Copy),
                                 reads=[kf.d()], writes=[kT.d()])
                    for ti, t in enumerate(gtiles):
                        c0, n = TILES[t]
                        lo = ti * 128
                        for a, pb in ((0, ps[4]), (1, ps[5])):
                            for dc in range(8):
                                self.mm(pb.ap[:n, :], uT.ap[:, dc, c0:c0 + n], win.ap[:, dc, 512 + a * 512:1024 + a * 512],
                                        dc == 0, dc == 7, reads=[win.d(), uT.d(gi)], writes=[pb.d()])
                        S.op("act", lambda e: e.activation(out=vsb.ap[:n, :], in_=ps[4].ap[:n, :], func=AF.Copy),
                             reads=[ps[4].d()], writes=[vsb.d()])
                        S.op("act", lambda e: e.activation(out=sg.ap[:n, :], in_=ps[5].ap[:n, :], func=AF.Silu),
                             reads=[ps[5].d()], writes=[sg.d()])
                        for c in range(2):
                            self.mm(ps[6].ap[:n, :n], kT.ap[:, c, lo:lo + n], qT.ap[:, c, lo:lo + n], c == 0, c == 1,
                                    reads=[kT.d(), qT.d()], writes=[ps[6].d()])
                        S.op("dve", lambda e: e.tensor_tensor(out=Pm.ap[:n, :n], in0=ps[6].ap[:n, :n], in1=decT(hd)[:n, :n], op=ALU.mult),
                             reads=[ps[6].d(), rc.d()], writes=[Pm.d()])
                        S.op("pool", lambda e: e.tensor_tensor(out=qs.ap[:, :, :n], in0=qT.ap[:, :, lo:lo + n],
                                                               in1=rowpow(hd)[:, :n].unsqueeze(1).to_broadcast([128, 2, n]), op=ALU.mult),
                             reads=[qT.d(), rc.d()], writes=[qs.d()])
                        self.mm(ps[7].ap[:n, :], Pm.ap[:n, :n], vsb.ap[:n, :], True, False, reads=[Pm.d(), vsb.d()], writes=[ps[7].d()])
                        for c in range(2):
                            self.mm(ps[7].ap[:n, :], qs.ap[:, c, :n], Mb.ap[:, c, :], False, c == 1,
                                    reads=[qs.d(), Mb.d()], writes=[ps[7].d()])
                        for c in range(2):
                            self.tr(ps[6].ap[:n, c * 128:(c + 1) * 128], kf.ap[:, c, lo:lo + n], 128, reads=[kf.d()], writes=[ps[6].d()])
                        S.op("dve", lambda e: e.tensor_scalar(out=khat.ap[:n, :], in0=ps[6].ap[:n, 0:256], scalar1=kdec(hd, n)[:n, :],
                                                              scalar2=None, op0=ALU.mult),
                             reads=[ps[6].d(), rc.d()], writes=[khat.d()])
                        for c in range(2):
                            self.mm(ps[4 + c].ap[:, :], khat.ap[:n, c * 128:(c + 1) * 128], vsb.ap[:n, :], True, True,
                                    reads=[khat.d(), vsb.d()], writes=[ps[4 + c].d()])
                            S.op("dve", lambda e: e.scalar_tensor_tensor(out=M.ap[:, c, :], in0=M.ap[:, c, :], scalar=float(gam ** n),
                                                                          in1=ps[4 + c].ap[:, :], op0=ALU.mult, op1=ALU.add),
                                 reads=[M.d(), ps[4 + c].d()], writes=[M.d()])
                        S.op("act", lambda e: e.activation(out=Mb.ap, in_=M.ap, func=AF.Copy), reads=[M.d()], writes=[Mb.d()])
                        S.op("dve", lambda e: e.bn_stats(out=st.ap[:n, 0:6], in_=ps[7].ap[:n, :]), reads=[ps[7].d()], writes=[st.d()])
                        S.op("dve", lambda e: e.bn_aggr(out=st.ap[:n, 6:8], in_=st.ap[:n, 0:6]), reads=[st.d()], writes=[st.d()])
                        S.op("act", lambda e: e.activation(out=st.ap[:n, 8:9], in_=st.ap[:n, 7:8], func=AF.Sqrt, bias=EPS),
                             reads=[st.d()], writes=[st.d()])
                        S.op("dve", lambda e: e.reciprocal(out=st.ap[:n, 9:10], in_=st.ap[:n, 8:9]), reads=[st.d()], writes=[st.d()])
                        S.op("dve", lambda e: e.tensor_scalar(out=yn.ap[:n, :], in0=ps[7].ap[:n, :], scalar1=st.ap[:n, 6:7],
                                                              scalar2=st.ap[:n, 9:10], op0=ALU.subtract, op1=ALU.mult),
                             reads=[ps[7].d(), st.d()], writes=[yn.d()])
                        S.op("pool", lambda e: e.tensor_tensor(out=yn.ap[:n, :], in0=yn.ap[:n, :], in1=sg.ap[:n, :], op=ALU.mult),
                             reads=[yn.d(), sg.d()], writes=[yn.d()])
                        self.out_proj(yn, n, 4, ygT, wo, t, ps[6], (ps[4], ps[5]))

    def out_proj(self, y, n, nch, yT, wo, t, ptr, pouts):
        S, h = self.S, self.h
        for c in range(nch):
            self.tr(ptr.ap[:, c * 128:c * 128 + n], y.ap[:n, c * 128:(c + 1) * 128], n, reads=[y.d()], writes=[ptr.d()])
        S.op("act", lambda e: e.activation(out=yT.ap[:, 0:nch, :n], in_=ptr.ap[:, 0:nch * 128].rearrange("p (c n) -> p c n", c=nch)[:, :, :n],
                                           func=AF.Copy),
             reads=[ptr.d()], writes=[yT.d()])
        for half in range(2):
            po = pouts[half]
            for c in range(nch):
                self.mm(po.ap[:n, :], yT.ap[:, c, :n], wo.ap[:, c, half * 512:(half + 1) * 512], c == 0, c == nch - 1,
                        reads=[yT.d(), wo.d()], writes=[po.d()])
            S.op("dve", lambda e: e.tensor_tensor(out=h.ap[:n, t, half * 512:(half + 1) * 512],
                                                  in0=h.ap[:n, t, half * 512:(half + 1) * 512], in1=po.ap[:n, :], op=ALU.add),
                 reads=[po.d(), h.d(t)], writes=[h.d(t)])


def host_pack(inp, cfg):
    f32 = np.float32
    shared = {}
    pv_cols = []

    def add_pv(name, arr2d):
        cfg[name] = sum(a.shape[1] for a in pv_cols)
        pv_cols.append(np.asarray(arr2d, f32))
    add_pv("pv_norm_mix", np.concatenate([vec_cols(inp["norm_mix"][i]) for i in range(DEPTH)], axis=1))
    add_pv("pv_norm_mlp", np.concatenate([vec_cols(inp["norm_mlp"][i]) for i in range(DEPTH)], axis=1))
    add_pv("pv_gla_bias", vec_cols(inp["gla_gate_bias"][0]))
    add_pv("pv_rwkv", np.concatenate([vec_cols(inp["rwkv_mu"][0][i]) for i in range(6)] + [vec_cols(inp[nm][0].reshape(-1)) for nm in
                                     ("rwkv_w0", "rwkv_a0", "rwkv_k_k", "rwkv_k_a", "rwkv_r_k", "rwkv_ln_w", "rwkv_ln_b")], axis=1))
    add_pv("pv_ssd_conv", np.concatenate([vec_cols(inp["m2_conv_w"][0][jj]) for jj in range(4)] + [vec_cols(inp["m2_conv_b"][0])], axis=1))
    shared["pvec"] = np.ascontiguousarray(np.concatenate(pv_cols, axis=1))
    cfg["npv"] = shared["pvec"].shape[1]
    rv = []

    def add_rv(name, v):
        cfg[name] = sum(a.shape[0] for a in rv)
        rv.append(np.asarray(v, f32).reshape(-1))
    add_rv("rv_norm_final", inp["norm_final"])
    add_rv("rv_gla_norm", inp["gla_norm_w"][0])
    add_rv("rv_ssd", np.concatenate([inp["m2_dt_bias"][0], inp["m2_a_log"][0], inp["m2_d"][0], inp["m2_norm_w"][0]]))
    shared["rvec"] = np.ascontiguousarray(np.concatenate(rv)[None, :])
    cfg["nrv"] = shared["rvec"].shape[1]
    ii = np.arange(128)
    cst = [np.eye(128, dtype=f32), (ii[:, None] <= ii[None, :]).astype(f32), (ii[:, None] < ii[None, :]).astype(f32), np.ones((128, 128), f32)]
    shared["cst"] = np.ascontiguousarray(np.concatenate(cst, axis=1))
    cfg["ncst"] = shared["cst"].shape[1]
    if cfg["mlps"]:
        w_in = inp["mlp_w_in"]
        w_out = inp["mlp_w_out"]
        shared["mlp_in"] = np.stack([pack_rows(w_in[l][:, fb * 512:(fb + 1) * 512]) for l in range(DEPTH) for fb in range(8)])
        shared["mlp_out"] = np.stack([pack_rows(w_out[l][fb * 512:(fb + 1) * 512, :]) for l in range(DEPTH) for fb in range(8)])
    if 0 in cfg["mixers"]:
        ii = np.arange(128)
        same = (ii[:, None] // 64) == (ii[None, :] // 64)
        su = ((ii[:, None] < ii[None, :]) & same).astype(f32)
        sl = ((ii[:, None] > ii[None, :]) & same).astype(f32)
        iu = ((ii[:, None] <= ii[None, :]) & same).astype(f32)
        shared["rw_cst"] = np.ascontiguousarray(np.concatenate([su, sl, su, sl, su, su, iu, iu, iu, iu, same.astype(f32)], axis=1))
        shared["rw_lora"] = pack_rows(np.concatenate([inp["rwkv_w_lora_a"][0], inp["rwkv_a_lora_a"][0], inp["rwkv_g_lora_a"][0]], axis=1))
        pairs, wos, bwas, bgs = [], [], [], []
        for hp in range(8):
            pc = slice(hp * 128, (hp + 1) * 128)
            pairs.append(pack_rows(np.concatenate([inp["rwkv_w_r"][0][:, pc], inp["rwkv_w_k"][0][:, pc], inp["rwkv_w_v"][0][:, pc]], axis=1)))
            wos.append(np.ascontiguousarray(inp["rwkv_w_o"][0][pc, :]))
            bw = np.zeros((128, 256), f32)
            bw[0:64, 0:128] = inp["rwkv_w_lora_b"][0][:, pc]
            bw[64:128, 128:256] = inp["rwkv_a_lora_b"][0][:, pc]
            bwas.append(bw)
            bg = np.zeros((128, 256), f32)
            bg[:, 0:128] = inp["rwkv_g_lora_b"][0][0:128, pc]
            bg[0:32, 128:256] = inp["rwkv_g_lora_b"][0][128:160, pc]
            bgs.append(bg)
        shared["rw_pair"] = np.stack(pairs)
        shared["rw_wo"] = np.stack(wos)
        shared["rw_bwa"] = np.ascontiguousarray(np.stack(bwas))
        shared["rw_bg"] = np.stack(bgs)
    if 1 in cfg["mixers"]:
        W = inp["m2_in_proj"][0]
        ins = []
        for g in range(8):
            cols = np.concatenate([2048 + np.arange(g * 256, (g + 1) * 256), 4096 + np.arange(g * 128, (g + 1) * 128),
                                   5120 + np.arange(g * 128, (g + 1) * 128), np.arange(g * 256, (g + 1) * 256),
                                   6144 + np.arange(g * 4, (g + 1) * 4)])
            ins.append(pack_rows(W[:, cols]))
        shared["ssd_in"] = np.stack(ins)
        shared["ssd_out"] = np.stack([pack_rows(inp["m2_out_proj"][0][g * 256:(g + 1) * 256, :]) for g in range(8)])
        ii = np.arange(128)
        nm = np.where(ii[:, None] <= ii[None, :], 0.0, -30000.0).astype(f32)
        shared["ssd_cst"] = np.ascontiguousarray(np.tile(nm, (1, 4)))
    if 2 in cfg["mixers"]:
        W = inp["gla_in_proj"][0]
        ins = []
        for hd in range(4):
            cols = np.concatenate([np.arange(hd * 128, (hd + 1) * 128), 512 + np.arange(hd * 128, (hd + 1) * 128),
                                   3072 + np.arange(16), 1024 + np.arange(hd * 256, (hd + 1) * 256),
                                   2048 + np.arange(hd * 256, (hd + 1) * 256)])
            ins.append(pack_rows(W[:, cols]))
        shared["gla_in"] = np.stack(ins)
        shared["gla_out"] = np.stack([pack_rows(inp["gla_out_proj"][0][hd * 256:(hd + 1) * 256, :]) for hd in range(4)])
        shared["gla_gup"] = np.ascontiguousarray(inp["gla_gate_up"]).astype(f32)
    if 3 in cfg["mixers"]:
        W = inp["ret_in_proj"][0]
        ins = []
        for hd in range(4):
            cols = np.concatenate([np.arange(hd * 256, (hd + 1) * 256), 1024 + np.arange(hd * 256, (hd + 1) * 256),
                                   2048 + np.arange(hd * 512, (hd + 1) * 512), 4096 + np.arange(hd * 512, (hd + 1) * 512)])
            ins.append(pack_rows(W[:, cols]))
        shared["ret_in"] = np.stack(ins)
        shared["ret_out"] = np.stack([pack_rows(inp["ret_out_proj"][0][hd * 512:(hd + 1) * 512, :]) for hd in range(4)])
        ii = np.arange(128)
        dec, rowp, kd128, kd16 = [], [], [], []
        for hd in range(4):
            lg = np.log1p(-np.exp2(-5.0 - hd))
            diff = (ii[None, :] - ii[:, None]).astype(np.float64)
            dec.append(np.where(diff >= 0, np.exp(lg * diff), 0.0))
            rowp.append(np.broadcast_to(np.exp(lg * (ii + 1.0))[None, :], (128, 128)))
            kd128.append(np.exp(lg * (127.0 - ii)))
            kd16.append(np.exp(lg * (15.0 - ii)))
        inv_freq = (1.0 / (10000.0 ** np.linspace(0.0, 1.0, 128, dtype=f32))).astype(f32)
        ang = (np.arange(L, dtype=f32)[None, :] * inv_freq[:, None]).astype(f32).astype(np.float64)
        shared["ret_cst"] = np.ascontiguousarray(np.concatenate(
            dec + rowp + [np.stack(kd128, 1), np.stack(kd16, 1), np.cos(ang), np.sin(ang)], axis=1).astype(f32))
        cfg["nretc"] = shared["ret_cst"].shape[1]
    return shared


def run(inputs, cfg):
    inp = {k: np.asarray(v) for k, v in inputs.items()}
    shared = host_pack(inp, cfg)
    b = Builder(cfg)
    nc = b.build()
    in_maps = []
    for c in range(8):
        m = dict(shared)
        m["x"] = np.ascontiguousarray(inp["x"][c])
        m["meta"] = np.ascontiguousarray(inp["meta_tokens"])
        in_maps.append(m)
    ncores = cfg.get("ncores", 8)
    in_maps = in_maps[:ncores]
    res = run_bass_kernel_spmd(nc, in_maps, core_ids=list(range(ncores)))
    out = np.stack([np.asarray(r["out"]) for r in res.results]).astype(np.float32)
    if cfg.get("debug"):
        return out, np.stack([np.asarray(r["dbg"]) for r in res.results])
    return out


def kernel(**inputs):
    cfg = {"mixers": [0, 1, 2, 3], "mlps": [0, 1, 2, 3]}
    return run(inputs, cfg)
```

```python
import math
import threading
from contextlib import ExitStack
import numpy as np
import concourse.bass as bass
import concourse.mybir as mybir
from concourse.alu_op_type import AluOpType as ALU
from concourse.bass_utils import run_bass_kernel_spmd

F32 = mybir.dt.float32
BF16 = mybir.dt.bfloat16
AF = mybir.ActivationFunctionType

D = 1024
SEQ = 2048
NMETA = 16
L = SEQ + NMETA
DEPTH = 4
DFF = 4096
EPS = 1e-5
NT = 17
TILES = [(0, 16)] + [(16 + 128 * i, 128) for i in range(16)]
GROUPS = [(0, 16, [0])] + [(16 + 512 * g, 512, [1 + 4 * g + j for j in range(4)]) for g in range(4)]


class Dep:
    __slots__ = ("w", "r", "excl", "tw", "tr")

    def __init__(self, excl=False):
        self.w = None
        self.r = {}
        self.excl = excl
        self.tw = 0.0
        self.tr = 0.0


class Sched:
    N_DMA_SEMS = 12

    def __init__(self, nc):
        self.nc = nc
        self.eng = {"pe": nc.tensor, "dve": nc.vector, "act": nc.scalar, "pool": nc.gpsimd, "sp": nc.sync}
        self.sems = {}
        self.count = {}
        self.known = {e: {} for e in self.eng}
        for e in self.eng:
            self.sems[e] = nc.alloc_semaphore("s_" + e)
            self.count[e] = 0
        self.dma_sems = {}
        self.dma_i = {}
        for q in ("sp", "pool"):
            self.dma_sems[q] = [nc.alloc_semaphore(f"d_{q}{i}") for i in range(self.N_DMA_SEMS)]
            self.dma_i[q] = 0
        self.n_inst = 0
        self.clock = {}

    def _sem(self, key):
        if isinstance(key, str):
            return self.sems[key]
        q, i = key
        return self.dma_sems[q][i]

    def _wait(self, e, key, val):
        if self.known[e].get(key, 0) >= val:
            return
        self.eng[e].wait_ge(self._sem(key), val)
        self.known[e][key] = val

    def _collect(self, e, reads, writes):
        need = {}

        def add(k, v):
            if need.get(k, 0) < v:
                need[k] = v
        for d in reads:
            if d.w is not None:
                add(*d.w)
            if d.excl:
                for k, v in d.r.items():
                    if k != e:
                        add(k, v)
        for d in writes:
            if d.w is not None and d.w[0] != e:
                add(*d.w)
            for k, v in d.r.items():
                if k != e:
                    add(k, v)
        for k, v in need.items():
            self._wait(e, k, v)

    def _mark(self, tok, reads, writes):
        k, v = tok
        for d in reads:
            if d.r.get(k, 0) < v:
                d.r[k] = v
        for d in writes:
            d.w = tok
            d.r = {}

    COST = {"pe": 0.12, "dve": 0.38, "act": 0.36, "pool": 0.5, "sp": 0.3}
    LAT = 0.3

    class _Rec:
        out = None

        def __getattr__(self, name):
            def f(*a, **k):
                self.out = k.get("out", a[0] if a else None)
                return self
            return f

    def _cost(self, e, fn):
        try:
            r = Sched._Rec()
            fn(r)
            n = float(r.out.free_size())
        except Exception:
            return self.COST[e]
        if e == "pe":
            return 0.06 + 0.0004 * n
        if e == "dve":
            return 0.09 + n / 960.0
        if e == "act":
            return 0.2 + n / 1400.0
        if e == "pool":
            return 0.25 + n / 600.0
        return self.COST[e]

    def _est(self, e, reads, writes):
        ready = 0.0
        for d in reads:
            if d.tw > ready:
                ready = d.tw
            if d.excl and d.tr > ready:
                ready = d.tr
        for d in writes:
            if d.tw > ready:
                ready = d.tw
            if d.tr > ready:
                ready = d.tr
        return max(ready + self.LAT, self.clock.get(e, 0.0))

    def _advance(self, e, reads, writes, cost):
        fin = self._est(e, reads, writes) + cost
        self.clock[e] = fin
        for d in reads:
            if d.tr < fin:
                d.tr = fin
        for d in writes:
            d.tw = fin
            d.tr = 0.0

    def op(self, e, fn, reads=(), writes=()):
        cost = self._cost(e, fn) if IL.in_run else self.COST[e]
        IL.arbitrate(lambda: self._est(e, reads, writes))
        self._advance(e, reads, writes, cost)
        self._collect(e, reads, writes)
        inst = fn(self.eng[e])
        self.count[e] += 1
        inst.then_inc(self.sems[e], 1)
        self._mark((e, self.count[e]), reads, writes)
        self.n_inst += 1
        return inst

    def dma(self, q, out, in_, reads=(), writes=(), **kw):
        IL.arbitrate(lambda: self._est(q, reads, writes))
        est = self._est(q, reads, writes)
        self.clock[q] = est + 0.3
        fin = est + 20.0
        for d in reads:
            if d.tr < fin:
                d.tr = fin
        for d in writes:
            d.tw = fin
            d.tr = 0.0
        i = self.dma_i[q]
        self.dma_i[q] += 1
        slot = i % self.N_DMA_SEMS
        rnd = i // self.N_DMA_SEMS
        key = (q, slot)
        if rnd > 0:
            self._wait(q, key, 16 * rnd)
        self._collect(q, reads, writes)
        inst = self.eng[q].dma_start(out=out, in_=in_, **kw)
        inst.then_inc(self.dma_sems[q][slot], 16)
        self._mark((key, 16 * (rnd + 1)), reads, writes)
        self.n_inst += 1
        return inst

    def barrier(self):
        for e in self.eng:
            for e2 in self.eng:
                if e2 != e and self.count[e2] > 0:
                    self._wait(e, e2, self.count[e2])
            for q in self.dma_sems:
                n = self.dma_i[q]
                for slot in range(min(n, self.N_DMA_SEMS)):
                    last_rnd = (n - 1 - slot) // self.N_DMA_SEMS
                    self._wait(e, (q, slot), 16 * (last_rnd + 1))


class Interleaver:
    def __init__(self):
        self.in_run = False
        self.local = threading.local()

    def run(self, fns, weights=None):
        fns = [f for f in fns if f is not None]
        if len(fns) == 1 or self.in_run:
            for f in fns:
                f()
            return
        n = len(fns)
        self.sems = [threading.Semaphore(0) for _ in range(n)]
        self.alive = [True] * n
        self.pending = [None] * n
        self.waiting = [None] * n
        self.cnt = [0] * n
        self.exc = None
        self.done = threading.Semaphore(0)

        def wrap(i, fn):
            self.sems[i].acquire()
            self.local.idx = i
            try:
                if self.exc is None:
                    fn()
            except BaseException as ex:
                if self.exc is None:
                    self.exc = ex
            self.alive[i] = False
            self.pending[i] = None
            self.waiting[i] = None
            self.local.idx = None
            self._dispatch(i, finished=True)
        ths = [threading.Thread(target=wrap, args=(i, f)) for i, f in enumerate(fns)]
        self.in_run = True
        for th in ths:
            th.start()
        self.sems[0].release()
        self.done.acquire()
        for th in ths:
            th.join()
        self.in_run = False
        if self.exc is not None:
            raise self.exc

    def _pick(self):
        n = len(self.sems)
        if self.exc is not None:
            for j in range(n):
                if self.alive[j]:
                    return j
            return None
        for j in range(n):
            if self.alive[j] and self.waiting[j] is not None and self.waiting[j]():
                return j
        for j in range(n):
            if self.alive[j] and self.pending[j] is None and self.waiting[j] is None:
                return j
        best, bt = None, None
        for j in range(n):
            if self.alive[j] and self.pending[j] is not None:
                tj = self.pending[j]()
                if bt is None or tj < bt:
                    best, bt = j, tj
        if best is None and any(self.alive):
            self.exc = self.exc or RuntimeError("interleaver deadlock")
            for j in range(n):
                if self.alive[j]:
                    return j
        return best

    def _dispatch(self, i, finished=False):
        j = self._pick()
        if j is None:
            if finished:
                self.done.release()
            return
        if j == i and not finished:
            return
        self.sems[j].release()
        if not finished:
            self.sems[i].acquire()

    def arbitrate(self, est):
        if not self.in_run:
            return
        i = getattr(self.local, "idx", None)
        if i is None:
            return
        if self.exc is not None:
            raise RuntimeError("sibling stream failed")
        self.cnt[i] += 1
        self.pending[i] = est
        self._dispatch(i)
        self.pending[i] = None
        if self.exc is not None:
            raise RuntimeError("sibling stream failed")

    def tick(self):
        self.arbitrate(lambda: 0.0)

    def wait(self, cond):
        if not self.in_run:
            assert cond()
            return
        i = self.local.idx
        while not cond():
            if self.exc is not None:
                raise RuntimeError("sibling stream failed")
            self.waiting[i] = cond
            self._dispatch(i)
            self.waiting[i] = None
        if self.exc is not None:
            raise RuntimeError("sibling stream failed")


IL = Interleaver()


def run_pipeline(ntiles, stageA, stageB, nA, nring, stageC=None, stagger=0, group=None, hook=None):
    assert nring >= nA + 1
    doneA = [False] * ntiles
    doneB = [False] * ntiles
    doneC = [False] * ntiles if stageC is not None else doneB
    hooked = {}

    base = 2 if stageC is not None else 1

    def a_stream(i):
        if i > 0 and stagger:
            IL.wait(lambda: IL.cnt[base + i - 1] >= stagger or not IL.alive[base + i - 1])
        for t in range(i, ntiles, nA):
            if group is not None:
                g = t // group
                if t % group == 0:
                    IL.wait(lambda: all(doneA[:t]))
                    hook(g)
                    hooked[g] = True
                else:
                    IL.wait(lambda: hooked.get(g, False))
            IL.wait(lambda: t - nring < 0 or doneC[t - nring])
            stageA(t, i, t % nring)
            doneA[t] = True

    def b_stream():
        for t in range(ntiles):
            IL.wait(lambda: doneA[t])
            stageB(t, t % nring)
            doneB[t] = True
    def c_stream():
        for t in range(ntiles):
            IL.wait(lambda: doneB[t])
            stageC(t, t % nring)
            doneC[t] = True
    IL.run([b_stream] + ([c_stream] if stageC is not None else []) + [(lambda i=i: a_stream(i)) for i in range(nA)])


class T:
    def __init__(self, ap, excl=False):
        self.ap = ap
        self.deps = {}
        self.excl = excl

    def d(self, key=None):
        if key not in self.deps:
            self.deps[key] = Dep(self.excl)
        return self.deps[key]


def pack_rows(w):
    K, Fd = w.shape
    kc = K // 128
    return np.ascontiguousarray(w.reshape(kc, 128, Fd).transpose(1, 0, 2).reshape(128, kc * Fd))


def vec_cols(v):
    return np.ascontiguousarray(v.reshape(-1, 128).T)


class Builder:
    def __init__(self, cfg):
        self.cfg = cfg
        self.nc = bass.Bass("TRN2", target_bir_lowering=False)
        self.S = Sched(self.nc)
        self.dram = {}

    def din(self, name, shape):
        self.dram[name] = self.nc.dram_tensor(name, list(shape), F32, kind="ExternalInput").ap()
        return self.dram[name]

    def sb(self, es, name, shape, dt):
        self.uid = getattr(self, "uid", 0) + 1
        return T(es.enter_context(self.nc.sbuf_tensor(f"sb{self.uid}_{name}", list(shape), dt))[:])

    def mm(self, out, lhsT, rhs, start, stop, reads, writes):
        return self.S.op("pe", lambda e: e.matmul(out, lhsT=lhsT, rhs=rhs, start=start, stop=stop),
                         reads=reads, writes=writes)

    def tr(self, out, in_, n, reads, writes):
        idn = self.ident
        return self.S.op("pe", lambda e: e.transpose(out, in_, idn.ap[:n, :n]),
                         reads=list(reads) + [idn.d()], writes=writes)

    def build(self):
        nc, S, cfg = self.nc, self.S, self.cfg
        x_d = self.din("x", [SEQ, D])
        meta_d = self.din("meta", [NMETA, D])
        pv_d = self.din("pvec", [128, cfg["npv"]])
        rv_d = self.din("rvec", [1, cfg["nrv"]])
        cst_d = self.din("cst", [128, cfg["ncst"]])
        mlp_in_d = mlp_out_d = None
        if cfg["mlps"]:
            mlp_in_d = self.din("mlp_in", [DEPTH * 8, 128, 8 * 512])
            mlp_out_d = self.din("mlp_out", [DEPTH * 8, 128, 4 * 1024])
        if 0 in cfg["mixers"]:
            self.din("rw_cst", [128, 512])
            self.din("rw_lora", [128, 8 * 288])
            self.din("rw_pair", [8, 128, 8 * 384])
            self.din("rw_wo", [8, 128, 1024])
            self.din("rw_bwa", [8, 128, 256])
            self.din("rw_bg", [8, 128, 256])
        if 1 in cfg["mixers"]:
            self.din("ssd_in", [8, 128, 8 * 772])
            self.din("ssd_out", [8, 128, 2 * 1024])
            self.din("ssd_cst", [128, 512])
        if 2 in cfg["mixers"]:
            self.din("gla_in", [4, 128, 8 * 784])
            self.din("gla_out", [4, 128, 2 * 1024])
            self.din("gla_gup", [1, 16, 512])
        if 3 in cfg["mixers"]:
            self.din("ret_in", [4, 128, 8 * 1536])
            self.din("ret_out", [4, 128, 4 * 1024])
            self.din("ret_cst", [128, cfg["nretc"]])
        out_d = nc.dram_tensor("out", [SEQ, D], F32, kind="ExternalOutput").ap()
        dbg_d = None
        if cfg.get("debug"):
            dbg_d = nc.dram_tensor("dbg", [DEPTH * 2, L, D], F32, kind="ExternalOutput").ap()

        with ExitStack() as es:
            self.h = self.sb(es, "h", [128, NT, D], F32)
            self.uT = self.sb(es, "uT", [128, 8, L], BF16)
            self.pv = self.sb(es, "pv", [128, cfg["npv"]], F32)
            self.cst = self.sb(es, "cst", [128, cfg["ncst"]], F32)
            self.ident = T(self.cst.ap[:, 0:128])
            self.ident.deps = self.cst.deps
            self.stat = self.sb(es, "stat", [128, 8], F32)
            self.ps = [T(es.enter_context(nc.psum_tensor(f"ps{i}", [128, 512], F32))[:], excl=True) for i in range(8)]
            h, uT = self.h, self.uT

            S.dma("sp", self.cst.ap, cst_d, writes=[self.cst.d()])
            S.dma("sp", self.pv.ap, pv_d, writes=[self.pv.d()])
            S.dma("sp", h.ap[0:16, 0, :], meta_d, writes=[h.d(0)])
            for g in range(4):
                S.dma("sp", h.ap[:, 1 + 4 * g:5 + 4 * g, :],
                      x_d[512 * g:512 * (g + 1), :].rearrange("(t p) d -> p t d", p=128),
                      writes=[h.d(1 + 4 * g + j) for j in range(4)])

            for layer in range(DEPTH):
                if layer in cfg["mixers"]:
                    nw = self.pv.ap[:, cfg["pv_norm_mix"] + 8 * layer: cfg["pv_norm_mix"] + 8 * layer + 8]
                    if layer % 4 in (2, 3):
                        [None, None, self.emit_gla, self.emit_ret][layer % 4](layer // 4, nw)
                    else:
                        self.emit_norm(nw)
                        [self.emit_rwkv, self.emit_ssd][layer % 4](layer // 4)
                    S.barrier()
                if dbg_d is not None:
                    self.emit_dump(dbg_d[2 * layer])
                if layer in cfg["mlps"]:
                    self.emit_mlp(layer, mlp_in_d, mlp_out_d, self.pv.ap[:, cfg["pv_norm_mlp"] + 8 * layer: cfg["pv_norm_mlp"] + 8 * layer + 8])
                    S.barrier()
                if dbg_d is not None:
                    self.emit_dump(dbg_d[2 * layer + 1])

            self.emit_final(out_d)
        return nc

    def emit_dump(self, dst):
        S, h = self.S, self.h
        S.dma("sp", dst[0:16, :], h.ap[0:16, 0, :], reads=[h.d(0)])
        for t in range(1, NT):
            c0 = TILES[t][0]
            S.dma("sp", dst[c0:c0 + 128, :], h.ap[:, t, :], reads=[h.d(t)])

    def rstd_of(self, t, n):
        S, h, stat, junk = self.S, self.h, self.stat, self.junk
        S.op("dve", lambda e: e.scalar_tensor_tensor(out=junk.ap[:n, :], in0=h.ap[:n, t, :], scalar=1.0 / D,
                                                      in1=h.ap[:n, t, :], op0=ALU.mult, op1=ALU.mult,
                                                      accum_out=stat.ap[:n, 1:2]),
             reads=[h.d(t)], writes=[junk.d(), stat.d()])
        S.op("act", lambda e: e.activation(out=stat.ap[:n, 2:3], in_=stat.ap[:n, 1:2], func=AF.Sqrt, bias=EPS),
             reads=[stat.d()], writes=[stat.d()])
        S.op("dve", lambda e: e.reciprocal(out=stat.ap[:n, 0:1], in_=stat.ap[:n, 2:3]),
             reads=[stat.d()], writes=[stat.d()])

    def emit_norm(self, wcols, ns=3):
        S, h, uT = self.S, self.h, self.uT
        es = ExitStack()
        uns = [self.sb(es, f"un{i}", [128, D], F32) for i in range(ns)]
        stats = [self.sb(es, f"nst{i}", [128, 8], F32) for i in range(ns)]

        def stream(i):
            un, stat = uns[i], stats[i]
            for t in range(i, NT, ns):
                c0, n = TILES[t]
                gi = 0 if t == 0 else 1 + (t - 1) // 4
                S.op("dve", lambda e: e.scalar_tensor_tensor(out=un.ap[:n, :], in0=h.ap[:n, t, :], scalar=1.0 / D,
                                                              in1=h.ap[:n, t, :], op0=ALU.mult, op1=ALU.mult,
                                                              accum_out=stat.ap[:n, 1:2]),
                     reads=[h.d(t)], writes=[un.d(), stat.d()])
                S.op("act", lambda e: e.activation(out=stat.ap[:n, 2:3], in_=stat.ap[:n, 1:2], func=AF.Sqrt, bias=EPS),
                     reads=[stat.d()], writes=[stat.d()])
                S.op("dve", lambda e: e.reciprocal(out=stat.ap[:n, 0:1], in_=stat.ap[:n, 2:3]),
                     reads=[stat.d()], writes=[stat.d()])
                S.op("act", lambda e: e.activation(out=un.ap[:n, :], in_=h.ap[:n, t, :], func=AF.Copy,
                                                   scale=stat.ap[:n, 0:1]),
                     reads=[h.d(t), stat.d()], writes=[un.d()])
                for half in range(2):
                    pb = self.ps[2 * i + half]
                    pv3 = pb.ap.rearrange("p (j n) -> p j n", j=4)
                    for j in range(4):
                        dc = half * 4 + j
                        self.tr(pv3[:, j, :n], un.ap[:n, dc * 128:(dc + 1) * 128], n, reads=[un.d()], writes=[pb.d()])
                    S.op("dve", lambda e: e.tensor_tensor(out=uT.ap[:, half * 4:half * 4 + 4, c0:c0 + n],
                                                          in0=pv3[:, :, :n],
                                                          in1=wcols[:, half * 4:half * 4 + 4].unsqueeze(2).to_broadcast([128, 4, n]),
                                                          op=ALU.mult),
                         reads=[pb.d(), self.pv.d()], writes=[uT.d(gi)])
        IL.run([(lambda i=i: stream(i)) for i in range(ns)])
        S.barrier()
        es.close()

    def load_rv(self, es, off, n):
        t = self.sb(es, "rv", [128, n], F32)
        self.S.dma("sp", t.ap, self.dram["rvec"][:, off:off + n].partition_broadcast(128), writes=[t.d()])
        return t

    def emit_final(self, out_d):
        S, h, stat, cfg = self.S, self.h, self.stat, self.cfg
        es = ExitStack()
        self.junk = self.sb(es, "junk", [128, D], F32)
        self.rv = self.load_rv(es, cfg["rv_norm_final"], D)
        wb = self.rv.ap[:, 0:D]
        for t in range(1, NT):
            c0, n = TILES[t]
            self.rstd_of(t, n)
            S.op("dve", lambda e: e.scalar_tensor_tensor(out=h.ap[:, t, :], in0=h.ap[:, t, :], scalar=stat.ap[:, 0:1],
                                                          in1=wb, op0=ALU.mult, op1=ALU.mult),
                 reads=[h.d(t), stat.d(), self.rv.d()], writes=[h.d(t)])
            S.dma("sp", out_d[c0 - 16:c0 - 16 + 128, :], h.ap[:, t, :], reads=[h.d(t)])
        S.barrier()
        es.close()

    def emit_mlp(self, layer, mlp_in_d, mlp_out_d, norm_w):
        nc, S, h, uT = self.nc, self.S, self.h, self.uT
        with ExitStack() as es:
            win = [self.sb(es, f"win{i}", [128, 8, 512], BF16) for i in range(2)]
            wout = [self.sb(es, f"wout{i}", [128, 4, 1024], BF16) for i in range(2)]
            hT = [self.sb(es, f"hT{i}", [128, 4, 512], BF16) for i in range(2)]
            rl = [self.sb(es, f"rl{i}", [128, 512], F32) for i in range(2)]

            def load_in(fb):
                S.dma("pool", win[fb % 2].ap.rearrange("p c f -> p (c f)"), mlp_in_d[layer * 8 + fb], writes=[win[fb % 2].d()])

            def load_out(fb):
                S.dma("pool", wout[fb % 2].ap.rearrange("p c f -> p (c f)"), mlp_out_d[layer * 8 + fb], writes=[wout[fb % 2].d()])
            for fb in range(2):
                load_in(fb)
                load_out(fb)
            self.emit_norm(norm_w)
            NG = len(GROUPS)

            def hidden(u, ai, slot):
                fb, gi = divmod(u, NG)
                g0, gn, gtiles = GROUPS[gi]
                wi, hb = win[fb % 2], hT[slot]
                if gi == 0 and 1 <= fb < 7:
                    load_in(fb + 1)
                for fc in range(4):
                    ph = self.ps[fc % 2]
                    for dc in range(8):
                        self.mm(ph.ap[:, :gn], wi.ap[:, dc, fc * 128:(fc + 1) * 128], uT.ap[:, dc, g0:g0 + gn],
                                dc == 0, dc == 7, reads=[wi.d(), uT.d(gi)], writes=[ph.d()])
                    r = rl[fc % 2]
                    S.op("act", lambda e: e.activation(out=r.ap[:, :gn], in_=ph.ap[:, :gn], func=AF.Relu),
                         reads=[ph.d()], writes=[r.d()])
                    S.op("dve", lambda e: e.tensor_tensor(out=hb.ap[:, fc, :gn], in0=r.ap[:, :gn], in1=r.ap[:, :gn],
                                                          op=ALU.mult),
                         reads=[r.d()], writes=[hb.d()])

            def output(u, slot):
                fb, gi = divmod(u, NG)
                g0, gn, gtiles = GROUPS[gi]
                wo, hb = wout[fb % 2], hT[slot]
                if gi == 0 and 1 <= fb < 7:
                    load_out(fb + 1)
                for ti, t in enumerate(gtiles):
                    n = TILES[t][1]
                    for half in range(2):
                        po = self.ps[2 + (ti * 2 + half) % 4]
                        for fc in range(4):
                            self.mm(po.ap[:n, :], hb.ap[:, fc, ti * 128:ti * 128 + n],
                                    wo.ap[:, fc, half * 512:(half + 1) * 512], fc == 0, fc == 3,
                                    reads=[hb.d(), wo.d()], writes=[po.d()])
                        S.op("dve", lambda e: e.tensor_tensor(out=h.ap[:n, t, half * 512:(half + 1) * 512],
                                                              in0=h.ap[:n, t, half * 512:(half + 1) * 512],
                                                              in1=po.ap[:n, :], op=ALU.add),
                             reads=[po.d(), h.d(t)], writes=[h.d(t)])
            run_pipeline(8 * NG, hidden, output, 1, 2)

    def emit_rwkv(self, j):
        nc, S, h, uT, cfg = self.nc, self.S, self.h, self.uT, self.cfg
        dr = self.dram
        C0 = 0.6065306597126334
        GN_EPS = 64e-5
        pvb = cfg["pv_rwkv"]
        pvc = lambda idx, c: self.pv.ap[:, pvb + idx * 8 + c:pvb + idx * 8 + c + 1]
        mu = lambda i: self.pv.ap[:, pvb + i * 8:pvb + i * 8 + 8]
        I_W0, I_A0, I_KK, I_KA, I_RK, I_LNW, I_LNB = 6, 7, 8, 9, 10, 11, 12
        ident = self.cst.ap[:, 0:128]
        ps = self.ps
        RG = [(0, 16, [0])] + [(16 + 256 * g, 256, [1 + 2 * g, 2 + 2 * g]) for g in range(8)]
        NA, NR = 3, 5
        with ExitStack() as es:
            hid = self.sb(es, "hid", [128, 3, L], BF16)
            rc = self.sb(es, "rwc", [128, 512], F32)
            S.dma("sp", rc.ap, dr["rw_cst"], writes=[rc.d()])
            rc3 = rc.ap.rearrange("p (w i) -> p w i", w=4)
            blk = rc.ap[:, 384:512]
            hb = self.sb(es, "hb", [128, 16], F32)
            S.op("pool", lambda e: e.memset(hid.ap[:, 2, :], 0.0), writes=[hid.d()])
            S.op("dve", lambda e: e.tensor_scalar(out=hb.ap, in0=self.pv.ap[:, pvb + I_W0 * 8:pvb + I_W0 * 8 + 16], scalar1=-1.0, scalar2=None, op0=ALU.mult),
                 reads=[self.pv.d()], writes=[hb.d()])

            def shift_diff(dst, c0, n, wr):
                a = 0
                if c0 == 0:
                    S.op("dve", lambda e: e.tensor_scalar(out=dst.ap[:, :, 0:1], in0=uT.ap[:, :, 0:1], scalar1=-1.0, scalar2=None, op0=ALU.mult),
                         reads=[uT.d(0)], writes=wr)
                    a = 1
                if n > a:
                    S.op("dve", lambda e: e.tensor_tensor(out=dst.ap[:, :, a:n], in0=uT.ap[:, :, c0 + a - 1:c0 + n - 1], in1=uT.ap[:, :, c0 + a:c0 + n],
                                                          op=ALU.subtract),
                         reads=[uT.d(g) for g in range(5)], writes=wr)
            with ExitStack() as es2:
                lw_p = self.sb(es2, "lwp", [128, 8, 288], BF16)
                lw_m = self.sb(es2, "lwm", [128, 8, 288], BF16)
                stg = self.sb(es2, "lstg", [128, 8, 288], F32)
                xxg = self.sb(es2, "xxg", [128, 8, 512], BF16)
                S.dma("pool", lw_p.ap.rearrange("p c f -> p (c f)"), dr["rw_lora"], writes=[lw_p.d()])
                S.dma("sp", stg.ap.rearrange("p c f -> p (c f)"), dr["rw_lora"], writes=[stg.d()])
                for (c0_, c1_, mi) in ((0, 64, 1), (64, 128, 4), (128, 288, 5)):
                    S.op("dve", lambda e: e.tensor_tensor(out=lw_m.ap[:, :, c0_:c1_], in0=stg.ap[:, :, c0_:c1_],
                                                          in1=mu(mi).unsqueeze(2).to_broadcast([128, 8, c1_ - c0_]), op=ALU.mult),
                         reads=[stg.d(), self.pv.d()], writes=[lw_m.d()])
                for gi, (g0, gn, gtiles) in enumerate(GROUPS):
                    shift_diff(xxg, g0, gn, [xxg.d()])
                    for a, (o0, o1) in enumerate(((0, 128), (128, 256), (256, 288))):
                        m = o1 - o0
                        for dc in range(8):
                            self.mm(ps[a].ap[:m, :gn], lw_p.ap[:, dc, o0:o1], uT.ap[:, dc, g0:g0 + gn], dc == 0, False,
                                    reads=[lw_p.d(), uT.d(gi)], writes=[ps[a].d()])
                        for dc in range(8):
                            self.mm(ps[a].ap[:m, :gn], lw_m.ap[:, dc, o0:o1], xxg.ap[:, dc, 0:gn], False, dc == 7,
                                    reads=[lw_m.d(), xxg.d()], writes=[ps[a].d()])
                    S.op("act", lambda e: e.activation(out=hid.ap[0:64, 0, g0:g0 + gn], in_=ps[0].ap[0:64, :gn], func=AF.Tanh),
                         reads=[ps[0].d()], writes=[hid.d()])
                    S.op("act", lambda e: e.activation(out=hid.ap[64:128, 0, g0:g0 + gn], in_=ps[0].ap[64:128, :gn], func=AF.Copy),
                         reads=[ps[0].d()], writes=[hid.d()])
                    S.op("act", lambda e: e.activation(out=hid.ap[:, 1, g0:g0 + gn], in_=ps[1].ap[:, :gn], func=AF.Tanh, scale=0.5),
                         reads=[ps[1].d()], writes=[hid.d()])
                    S.op("act", lambda e: e.activation(out=hid.ap[0:32, 2, g0:g0 + gn], in_=ps[2].ap[0:32, :gn], func=AF.Tanh, scale=0.5),
                         reads=[ps[2].d()], writes=[hid.d()])
                    S.op("dve", lambda e: e.tensor_scalar(out=hid.ap[:, 1, g0:g0 + gn], in0=hid.ap[:, 1, g0:g0 + gn], scalar1=0.5, scalar2=0.5,
                                                          op0=ALU.mult, op1=ALU.add), reads=[hid.d()], writes=[hid.d()])
                    S.op("dve", lambda e: e.tensor_scalar(out=hid.ap[0:32, 2, g0:g0 + gn], in0=hid.ap[0:32, 2, g0:g0 + gn], scalar1=0.5, scalar2=0.5,
                                                          op0=ALU.mult, op1=ALU.add), reads=[hid.d()], writes=[hid.d()])
                S.barrier()
            wp_p = self.sb(es, "wpp", [128, 8, 384], BF16)
            wp_m = self.sb(es, "wpm", [128, 8, 384], BF16)
            stg = self.sb(es, "wstg", [128, 4, 128], F32)
            wos = [self.sb(es, f"rwo{i}", [128, 1024], BF16) for i in range(2)]
            Bwa = self.sb(es, "Bwa", [128, 256], BF16)
            Bg = self.sb(es, "Bg", [128, 256], BF16)
            M = self.sb(es, "wM", [128, 64], F32)
            Mb = self.sb(es, "wMb", [128, 64], BF16)
            yln = self.sb(es, "yln", [128, 128], F32)
            og = self.sb(es, "og", [128, 128], F32)
            ogb = self.sb(es, "ogb", [128, 128], BF16)
            st2 = self.sb(es, "wst2", [128, 24], F32)
            AP_ = []
            for ai in range(NA):
                Fd = {nm: self.sb(es, f"w{nm}{ai}", [128, 128], F32) for nm in
                      ("r", "k", "v", "sg", "a", "kk", "tmp", "kmod", "cum", "ec", "en", "ecx")}
                Fd["eh"], Fd["bh"], Fd["kh"] = Fd["ec"], Fd["en"], Fd["ecx"]
                Fd["bv"] = Fd["a"]
                AP_.append({"F": Fd, "bt": self.sb(es, f"bt{ai}", [128, 128], BF16), "kt": self.sb(es, f"kt{ai}", [128, 128], BF16),
                            "A1": self.sb(es, f"A1{ai}", [128, 4, 128], BF16), "X2": self.sb(es, f"X2{ai}", [128, 4, 128], BF16),
                            "xx": self.sb(es, f"xx{ai}", [128, 8, 128], BF16), "bx": ps[2 + 2 * ai], "by": ps[3 + 2 * ai]})
            PB = []
            for par in range(NR):
                pb = {"at0": self.sb(es, f"at0{par}", [128, 128], BF16), "at1": self.sb(es, f"at1{par}", [128, 128], BF16),
                      "rt0": self.sb(es, f"rt0{par}", [128, 128], BF16), "rt1": self.sb(es, f"rt1{par}", [128, 128], BF16),
                      "bon": self.sb(es, f"bon{par}", [128, 128], F32), "g": self.sb(es, f"g{par}", [128, 128], F32),
                      "st": self.sb(es, f"st{par}", [128, 8], F32),
                      "tok0": self.sb(es, f"tok0{par}", [128, 3, 128], BF16), "tok1": self.sb(es, f"tok1{par}", [128, 3, 128], BF16),
                      "P0": self.sb(es, f"P0{par}", [128, 128], BF16), "P1": self.sb(es, f"P1{par}", [128, 128], BF16),
                      "U0": self.sb(es, f"U0{par}", [128, 128], BF16), "U1": self.sb(es, f"U1{par}", [128, 128], BF16),
                      "A2": self.sb(es, f"A2{par}", [128, 4, 128], BF16), "A3": self.sb(es, f"A3{par}", [128, 2, 128], BF16),
                      "TT": self.sb(es, f"TT{par}", [128, 4, 128], BF16), "Ysb": self.sb(es, f"Ysb{par}", [128, 128], F32)}
                for nm in ("at0", "at1", "rt0", "rt1", "tok0", "tok1", "P0", "P1", "U0", "U1"):
                    S.op("pool", lambda e: e.memset(pb[nm].ap, 0.0), writes=[pb[nm].d()])
                PB.append(pb)
            ones = self.cst.ap[:, 384:512]

            def stageA(hp, t, ai, B_):
                c0, n = TILES[t]
                ugi = 0 if t == 0 else 1 + (t - 1) // 4
                gs = slice(c0, c0 + n)
                P_ = AP_[ai]
                F, bx, by, A1, X2, xx = P_["F"], P_["bx"], P_["by"], P_["A1"], P_["X2"], P_["xx"]
                btk = {"bt": P_["bt"], "kt": P_["kt"]}
                A = lambda nm: F[nm].ap[:, :n]
                D_ = lambda *nms: [F[n_].d() for n_ in nms]
                shift_diff(xx, c0, n, [xx.d()])
                for i in range(3):
                    for dc in range(8):
                        self.mm(bx.ap[:, i * 128:i * 128 + n], wp_p.ap[:, dc, i * 128:(i + 1) * 128], uT.ap[:, dc, gs], dc == 0, False,
                                reads=[wp_p.d(), uT.d(ugi)], writes=[bx.d()])
                    for dc in range(8):
                        self.mm(bx.ap[:, i * 128:i * 128 + n], wp_m.ap[:, dc, i * 128:(i + 1) * 128], xx.ap[:, dc, 0:n], False, dc == 7,
                                reads=[wp_m.d(), xx.d()], writes=[bx.d()])
                self.mm(bx.ap[:, 384:384 + n], Bg.ap[:, 0:128], hid.ap[:, 1, gs], True, False, reads=[Bg.d(), hid.d()], writes=[bx.d()])
                self.mm(bx.ap[:, 384:384 + n], Bg.ap[:, 128:256], hid.ap[:, 2, gs], False, True, reads=[Bg.d(), hid.d()], writes=[bx.d()])
                self.mm(by.ap[:, 0:n], Bwa.ap[:, 0:128], hid.ap[:, 0, gs], True, True, reads=[Bwa.d(), hid.d()], writes=[by.d()])
                self.mm(by.ap[:, 128:128 + n], Bwa.ap[:, 128:256], hid.ap[:, 0, gs], True, True, reads=[Bwa.d(), hid.d()], writes=[by.d()])
                S.op("act", lambda e: e.activation(out=A("r"), in_=bx.ap[:, 0:n], func=AF.Copy), reads=[bx.d()], writes=D_("r"))
                S.op("act", lambda e: e.activation(out=A("k"), in_=bx.ap[:, 128:128 + n], func=AF.Copy), reads=[bx.d()], writes=D_("k"))
                S.op("act", lambda e: e.activation(out=A("v"), in_=bx.ap[:, 256:256 + n], func=AF.Copy), reads=[bx.d()], writes=D_("v"))
                S.op("act", lambda e: e.activation(out=B_["g"].ap[:, :n], in_=bx.ap[:, 384:384 + n], func=AF.Copy), reads=[bx.d()], writes=[B_["g"].d()])
                for nm, off, hc in (("sg", 0, hp), ("a", 128, 8 + hp)):
                    S.op("act", lambda e: e.activation(out=A(nm), in_=by.ap[:, off:off + n], func=AF.Exp, scale=-1.0, bias=hb.ap[:, hc:hc + 1]),
                         reads=[by.d(), hb.d()], writes=D_(nm))
                    S.op("act", lambda e: e.activation(out=A(nm), in_=A(nm), func=AF.Ln, bias=1.0), reads=D_(nm), writes=D_(nm))
                    S.op("act", lambda e: e.activation(out=A(nm), in_=A(nm), func=AF.Exp, scale=-1.0), reads=D_(nm), writes=D_(nm))
                S.op("dve", lambda e: e.tensor_scalar(out=A("kk"), in0=A("k"), scalar1=pvc(I_KK, hp), scalar2=None, op0=ALU.mult),
                     reads=D_("k") + [self.pv.d()], writes=D_("kk"))
                S.op("pool", lambda e: e.tensor_tensor(out=A("tmp"), in0=A("kk"), in1=A("kk"), op=ALU.mult), reads=D_("kk"), writes=D_("tmp"))
                self.mm(by.ap[:, 256:256 + n], blk, A("tmp"), True, True, reads=[rc.d()] + D_("tmp"), writes=[by.d()])
                S.op("dve", lambda e: e.tensor_scalar(out=A("tmp"), in0=by.ap[:, 256:256 + n], scalar1=1e-24, scalar2=None, op0=ALU.max),
                     reads=[by.d()], writes=D_("tmp"))
                S.op("act", lambda e: e.activation(out=A("tmp"), in_=A("tmp"), func=AF.Ln), reads=D_("tmp"), writes=D_("tmp"))
                S.op("act", lambda e: e.activation(out=A("tmp"), in_=A("tmp"), func=AF.Exp, scale=-0.5), reads=D_("tmp"), writes=D_("tmp"))
                S.op("dve", lambda e: e.tensor_tensor(out=A("kk"), in0=A("kk"), in1=A("tmp"), op=ALU.mult), reads=D_("kk", "tmp"), writes=D_("kk"))
                S.op("dve", lambda e: e.tensor_scalar(out=A("tmp"), in0=A("a"), scalar1=-1.0, scalar2=pvc(I_KA, hp), op0=ALU.add, op1=ALU.mult),
                     reads=D_("a") + [self.pv.d()], writes=D_("tmp"))
                S.op("dve", lambda e: e.scalar_tensor_tensor(out=A("kmod"), in0=A("tmp"), scalar=1.0, in1=A("k"), op0=ALU.add, op1=ALU.mult),
                     reads=D_("tmp", "k"), writes=D_("kmod"))
                S.op("dve", lambda e: e.scalar_tensor_tensor(out=A("tmp"), in0=A("r"), scalar=pvc(I_RK, hp), in1=A("kmod"), op0=ALU.mult, op1=ALU.mult),
                     reads=D_("r", "kmod") + [self.pv.d()], writes=D_("tmp"))
                self.mm(by.ap[:, 384:384 + n], blk, A("tmp"), True, True, reads=[rc.d()] + D_("tmp"), writes=[by.d()])
                S.op("dve", lambda e: e.tensor_tensor(out=B_["bon"].ap[:, :n], in0=by.ap[:, 384:384 + n], in1=A("v"), op=ALU.mult),
                     reads=[by.d()] + D_("v"), writes=[B_["bon"].d()])
                S.op("pool", lambda e: e.tensor_tensor(out=A("bv"), in0=A("kk"), in1=A("a"), op=ALU.mult), reads=D_("kk", "a"), writes=D_("bv"))
                tchunks = ((0, 16),) if n == 16 else ((0, 64), (64, 64))
                for (o, m) in tchunks:
                    S.op("dve", lambda e: e.tensor_tensor_scan(out=F["cum"].ap[:, o:o + m], data0=ones[:, :m], data1=F["sg"].ap[:, o:o + m],
                                                               initial=0.0, op0=ALU.mult, op1=ALU.add),
                         reads=D_("sg") + [self.cst.d()], writes=D_("cum"))
                S.op("act", lambda e: e.activation(out=A("ec"), in_=A("cum"), func=AF.Exp, scale=-C0), reads=D_("cum"), writes=D_("ec"))
                S.op("act", lambda e: e.activation(out=A("en"), in_=A("cum"), func=AF.Exp, scale=C0), reads=D_("cum"), writes=D_("en"))
                S.op("dve", lambda e: e.tensor_tensor(out=A("ecx"), in0=A("cum"), in1=A("sg"), op=ALU.subtract), reads=D_("cum", "sg"), writes=D_("ecx"))
                S.op("act", lambda e: e.activation(out=A("ecx"), in_=A("ecx"), func=AF.Exp, scale=-C0), reads=D_("ecx"), writes=D_("ecx"))
                for hd in range(2):
                    hs = slice(hd * 64, hd * 64 + 64)
                    S.op("dve", lambda e: e.scalar_tensor_tensor(out=B_[f"at{hd}"].ap[hs, :n], in0=F["kk"].ap[hs, :n], scalar=-1.0,
                                                                  in1=F["ecx"].ap[hs, :n], op0=ALU.mult, op1=ALU.mult),
                         reads=D_("kk", "ecx"), writes=[B_[f"at{hd}"].d()])
                    S.op("dve", lambda e: e.tensor_tensor(out=B_[f"rt{hd}"].ap[hs, :n], in0=F["r"].ap[hs, :n], in1=F["ec"].ap[hs, :n], op=ALU.mult),
                         reads=D_("r", "ec"), writes=[B_[f"rt{hd}"].d()])
                S.op("pool", lambda e: e.tensor_tensor(out=btk["bt"].ap[:, :n], in0=A("bv"), in1=A("en"), op=ALU.mult), reads=D_("bv", "en"), writes=[btk["bt"].d()])
                S.op("pool", lambda e: e.tensor_tensor(out=btk["kt"].ap[:, :n], in0=A("kmod"), in1=A("en"), op=ALU.mult), reads=D_("kmod", "en"), writes=[btk["kt"].d()])
                stp = B_["st"]
                for ci, (o, m) in enumerate(tchunks):
                    last = F["cum"].ap[:, o + m - 1:o + m]
                    S.op("dve", lambda e: e.tensor_scalar(out=stp.ap[:, ci:ci + 1], in0=last, scalar1=-C0, scalar2=None, op0=ALU.mult),
                         reads=D_("cum"), writes=[stp.d()])
                    S.op("act", lambda e: e.activation(out=F["eh"].ap[:, o:o + m], in_=F["cum"].ap[:, o:o + m], func=AF.Exp, scale=C0,
                                                       bias=stp.ap[:, ci:ci + 1]),
                         reads=D_("cum") + [stp.d()], writes=D_("eh"))
                    S.op("act", lambda e: e.activation(out=stp.ap[:, 2 + ci:3 + ci], in_=last, func=AF.Exp, scale=-C0), reads=D_("cum"), writes=[stp.d()])
                S.op("pool", lambda e: e.tensor_tensor(out=A("bh"), in0=A("bv"), in1=A("eh"), op=ALU.mult), reads=D_("bv", "eh"), writes=D_("bh"))
                S.op("pool", lambda e: e.tensor_tensor(out=A("kh"), in0=A("kmod"), in1=A("eh"), op=ALU.mult), reads=D_("kmod", "eh"), writes=D_("kh"))
                for q, nm in enumerate(("v", "bh", "kh")):
                    self.tr(bx.ap[:n, q * 128:(q + 1) * 128], F[nm].ap[:, :n], 128, reads=D_(nm), writes=[bx.d()])
                for c, (o, m) in enumerate(tchunks):
                    tk = B_[f"tok{c}"]
                    S.op("act", lambda e: e.activation(out=tk.ap[o:o + m, :, :], in_=bx.ap[o:o + m, 0:384].rearrange("p (q c) -> p q c", q=3), func=AF.Copy),
                         reads=[bx.d()], writes=[tk.d()])
                bA = by.ap.rearrange("p (a i) -> p a i", a=4)
                bB = bx.ap.rearrange("p (a i) -> p a i", a=4)
                A2, A3, TT = B_["A2"], B_["A3"], B_["TT"]
                ops_ = []
                for hd in range(2):
                    at, bt, kt, rt = B_[f"at{hd}"].ap[:, :n], btk["bt"].ap[:, :n], btk["kt"].ap[:, :n], B_[f"rt{hd}"].ap[:, :n]
                    rd = [B_[f"at{hd}"].d(), btk["bt"].d(), btk["kt"].d(), B_[f"rt{hd}"].d()]
                    ops_.append((at, bt, kt, rt, rd))
                    self.mm(bA[:n, 2 * hd, :n], bt, at, True, True, reads=rd, writes=[by.d()])
                    self.mm(bA[:n, 2 * hd + 1, :n], at, bt, True, True, reads=rd, writes=[by.d()])
                    self.mm(bB[:n, hd, :n], kt, at, True, True, reads=rd, writes=[bx.d()])
                    self.mm(bB[:n, 2 + hd, :n], bt, rt, True, True, reads=rd, writes=[bx.d()])
                v4 = lambda ap3: ap3.rearrange("p (a b) i -> p a b i", a=2)
                S.op("dve", lambda e: e.tensor_tensor(out=v4(A1.ap)[:n, :, :, :n], in0=v4(bA)[:n, :, :, :n],
                                                      in1=rc3[:n, 0:2, :n].unsqueeze(1).to_broadcast([n, 2, 2, n]), op=ALU.mult),
                     reads=[by.d(), rc.d()], writes=[A1.d()])
                S.op("dve", lambda e: e.tensor_tensor(out=v4(A2.ap)[:n, :, :, :n], in0=v4(bB)[:n, :, :, :n],
                                                      in1=rc3[:n, 0:4:2, :n].unsqueeze(2).to_broadcast([n, 2, 2, n]), op=ALU.mult),
                     reads=[bx.d(), rc.d()], writes=[A2.d()])
                for hd in range(2):
                    at, bt, kt, rt, rd = ops_[hd]
                    self.mm(bA[:n, hd, :n], kt, rt, True, True, reads=rd, writes=[by.d()])
                S.op("dve", lambda e: e.tensor_tensor(out=A3.ap[:n, :, :n], in0=bA[:n, 0:2, :n],
                                                      in1=rc3[:n, 2:3, :n].to_broadcast([n, 2, n]), op=ALU.mult),
                     reads=[by.d(), rc.d()], writes=[A3.d()])
                S.op("dve", lambda e: e.tensor_tensor(out=TT.ap[:n, :, :n], in0=A1.ap[:n, :, :n],
                                                      in1=ident[:n, :n].unsqueeze(1).to_broadcast([n, 4, n]), op=ALU.add),
                     reads=[A1.d(), self.cst.d()], writes=[TT.d()])
                Xc = A1
                nlev = 5 if n == 128 else 3
                for lev in range(nlev):
                    pq, pq3 = bx, bB
                    for hd in range(2):
                        Xm, Ym = Xc.ap[:n, 2 * hd, :n], Xc.ap[:n, 2 * hd + 1, :n]
                        self.mm(pq3[:n, 2 * hd, :n], Ym, Xm, True, True, reads=[Xc.d()], writes=[pq.d()])
                        self.mm(pq3[:n, 2 * hd + 1, :n], Xm, Ym, True, True, reads=[Xc.d()], writes=[pq.d()])
                    S.op("act", lambda e: e.activation(out=X2.ap[:n, :, :n], in_=pq3[:n, :, :n], func=AF.Copy), reads=[pq.d()], writes=[X2.d()])
                    Xc = X2
                    pr, pr3 = by, bA
                    for hd in range(2):
                        T_ = TT.ap[:n, 2 * hd + 1, :n]
                        self.mm(pr3[:n, 2 * hd, :n], T_, X2.ap[:n, 2 * hd, :n], True, True, reads=[TT.d(), X2.d()], writes=[pr.d()])
                        self.mm(pr3[:n, 2 * hd + 1, :n], X2.ap[:n, 2 * hd, :n], T_, True, True, reads=[TT.d(), X2.d()], writes=[pr.d()])
                    S.op("dve", lambda e: e.tensor_tensor(out=TT.ap[:n, :, :n], in0=TT.ap[:n, :, :n], in1=pr3[:n, :, :n], op=ALU.add),
                         reads=[TT.d(), pr.d()], writes=[TT.d()])

            def stageB(hp, t, B_):
                c0, n = TILES[t]
                tl = slice(0, n)
                tchunks = ((0, 16),) if n == 16 else ((0, 64), (64, 64))
                A2, A3, TT, stp, Ysb = B_["A2"], B_["A3"], B_["TT"], B_["st"], B_["Ysb"]
                b0 = ps[0]
                if t == 0:
                    S.op("pool", lambda e: e.memset(M.ap, 0.0), writes=[M.d()])
                    S.op("pool", lambda e: e.memset(Mb.ap, 0.0), writes=[Mb.d()])
                for c, (o, m) in enumerate(tchunks):
                    cs = slice(o, o + m)
                    tk, Pc, Uc = B_[f"tok{c}"], B_[f"P{c}"], B_[f"U{c}"]
                    for hd in range(2):
                        hs = slice(hd * 64, hd * 64 + 64)
                        self.mm(b0.ap[:n, hs], B_[f"at{hd}"].ap[:, tl], Mb.ap[:, :], True, False, reads=[B_[f"at{hd}"].d(), Mb.d()], writes=[b0.d()])
                        self.mm(b0.ap[:n, hs], A2.ap[:n, hd, :n], tk.ap[:n, 0, hs], False, True, reads=[A2.d(), tk.d()], writes=[b0.d()])
                    S.op("act", lambda e: e.activation(out=Pc.ap[cs, :], in_=b0.ap[cs, 0:128], func=AF.Copy), reads=[b0.d()], writes=[Pc.d()])
                    for hd in range(2):
                        hs = slice(hd * 64, hd * 64 + 64)
                        self.mm(b0.ap[:n, 128 + hd * 64:128 + hd * 64 + 64], TT.ap[:n, 2 * hd, :n], Pc.ap[:n, hs], True, True, reads=[TT.d(), Pc.d()], writes=[b0.d()])
                    S.op("dve", lambda e: e.tensor_copy(out=Uc.ap[cs, :], in_=b0.ap[cs, 128:256]), reads=[b0.d()], writes=[Uc.d()])
                    self.mm(b0.ap[:, 384:512], tk.ap[:n, 1, :], Uc.ap[:n, :], True, False, reads=[tk.d(), Uc.d()], writes=[b0.d()])
                    self.mm(b0.ap[:, 384:512], tk.ap[:n, 2, :], tk.ap[:n, 0, :], False, True, reads=[tk.d()], writes=[b0.d()])
                    for hd in range(2):
                        hs = slice(256 + hd * 64, 256 + hd * 64 + 64)
                        hv = slice(hd * 64, hd * 64 + 64)
                        self.mm(b0.ap[:n, hs], B_[f"rt{hd}"].ap[:, tl], Mb.ap[:, :], True, False, reads=[B_[f"rt{hd}"].d(), Mb.d()], writes=[b0.d()])
                        self.mm(b0.ap[:n, hs], A2.ap[:n, 2 + hd, :n], Uc.ap[:n, hv], False, False, reads=[A2.d(), Uc.d()], writes=[b0.d()])
                        self.mm(b0.ap[:n, hs], A3.ap[:n, hd, :n], tk.ap[:n, 0, hv], False, True, reads=[A3.d(), tk.d()], writes=[b0.d()])
                    for hd in range(2):
                        hs = slice(hd * 64, hd * 64 + 64)
                        S.op("dve", lambda e: e.scalar_tensor_tensor(out=Mb.ap[hs, :], in0=M.ap[hs, :], scalar=stp.ap[hs, 2 + c:3 + c],
                                                                      in1=b0.ap[hs, 384 + hd * 64:384 + hd * 64 + 64], op0=ALU.mult, op1=ALU.add),
                             reads=[M.d(), stp.d(), b0.d()], writes=[Mb.d()])
                    for hd in range(2):
                        hs = slice(hd * 64, hd * 64 + 64)
                        S.op("dve", lambda e: e.scalar_tensor_tensor(out=M.ap[hs, :], in0=M.ap[hs, :], scalar=stp.ap[hs, 2 + c:3 + c],
                                                                      in1=b0.ap[hs, 384 + hd * 64:384 + hd * 64 + 64], op0=ALU.mult, op1=ALU.add),
                             reads=[M.d(), stp.d(), b0.d()], writes=[M.d()])
                    S.op("act", lambda e: e.activation(out=Ysb.ap[cs, :], in_=b0.ap[cs, 256:384], func=AF.Copy), reads=[b0.d()], writes=[Ysb.d()])

            def stageC(hp, t, B_):
                c0, n = TILES[t]
                Ysb = B_["Ysb"]
                b1 = ps[1]
                wo = wos[hp % 2]
                for hd in range(2):
                    hs = slice(hd * 64, hd * 64 + 64)
                    q0 = hd * 12
                    S.op("dve", lambda e: e.bn_stats(out=st2.ap[:n, q0:q0 + 6], in_=Ysb.ap[:n, hs]), reads=[Ysb.d()], writes=[st2.d()])
                    S.op("dve", lambda e: e.bn_aggr(out=st2.ap[:n, q0 + 6:q0 + 8], in_=st2.ap[:n, q0:q0 + 6]), reads=[st2.d()], writes=[st2.d()])
                    S.op("act", lambda e: e.activation(out=st2.ap[:n, q0 + 7:q0 + 8], in_=st2.ap[:n, q0 + 7:q0 + 8], func=AF.Ln, bias=GN_EPS),
                         reads=[st2.d()], writes=[st2.d()])
                    S.op("act", lambda e: e.activation(out=st2.ap[:n, q0 + 7:q0 + 8], in_=st2.ap[:n, q0 + 7:q0 + 8], func=AF.Exp, scale=-0.5),
                         reads=[st2.d()], writes=[st2.d()])
                    S.op("dve", lambda e: e.tensor_scalar(out=yln.ap[:n, hs], in0=Ysb.ap[:n, hs], scalar1=st2.ap[:n, q0 + 6:q0 + 7],
                                                          scalar2=st2.ap[:n, q0 + 7:q0 + 8], op0=ALU.subtract, op1=ALU.mult),
                         reads=[Ysb.d(), st2.d()], writes=[yln.d()])
                self.tr(b1.ap[:, 0:n], yln.ap[:n, :], n, reads=[yln.d()], writes=[b1.d()])
                S.op("dve", lambda e: e.tensor_scalar(out=og.ap[:, :n], in0=b1.ap[:, 0:n], scalar1=pvc(I_LNW, hp), scalar2=pvc(I_LNB, hp),
                                                      op0=ALU.mult, op1=ALU.add),
                     reads=[b1.d(), self.pv.d()], writes=[og.d()])
                S.op("pool", lambda e: e.tensor_tensor(out=og.ap[:, :n], in0=og.ap[:, :n], in1=B_["bon"].ap[:, :n], op=ALU.add),
                     reads=[og.d(), B_["bon"].d()], writes=[og.d()])
                S.op("pool", lambda e: e.tensor_tensor(out=ogb.ap[:, :n], in0=og.ap[:, :n], in1=B_["g"].ap[:, :n], op=ALU.mult),
                     reads=[og.d(), B_["g"].d()], writes=[ogb.d()])
                for half in range(2):
                    self.mm(b1.ap[:n, :], ogb.ap[:, :n], wo.ap[:, half * 512:(half + 1) * 512], True, True, reads=[ogb.d(), wo.d()], writes=[b1.d()])
                    S.op("dve", lambda e: e.tensor_tensor(out=h.ap[:n, t, half * 512:(half + 1) * 512],
                                                          in0=h.ap[:n, t, half * 512:(half + 1) * 512], in1=b1.ap[:n, :], op=ALU.add),
                         reads=[b1.d(), h.d(t)], writes=[h.d(t)])

            def hook(hp):
                S.dma("pool", wp_p.ap.rearrange("p c f -> p (c f)"), dr["rw_pair"][hp], writes=[wp_p.d()])
                for i, mi in enumerate((0, 2, 3)):
                    for hf in range(2):
                        S.dma("sp", stg.ap, dr["rw_pair"][hp].rearrange("p (c f) -> p c f", c=8)[:, 4 * hf:4 * hf + 4, i * 128:(i + 1) * 128], writes=[stg.d()])
                        S.op("dve", lambda e: e.tensor_tensor(out=wp_m.ap[:, 4 * hf:4 * hf + 4, i * 128:(i + 1) * 128], in0=stg.ap,
                                                              in1=mu(mi)[:, 4 * hf:4 * hf + 4].unsqueeze(2).to_broadcast([128, 4, 128]), op=ALU.mult),
                             reads=[stg.d(), self.pv.d()], writes=[wp_m.d()])
                S.dma("pool", wos[hp % 2].ap, dr["rw_wo"][hp], writes=[wos[hp % 2].d()])
                S.dma("pool", Bwa.ap, dr["rw_bwa"][hp], writes=[Bwa.d()])
                S.dma("pool", Bg.ap, dr["rw_bg"][hp], writes=[Bg.d()])
            run_pipeline(8 * NT, lambda u, ai, slot: stageA(u // NT, u % NT, ai, PB[slot]), lambda u, slot: stageB(u // NT, u % NT, PB[slot]),
                         NA, NR, stageC=lambda u, slot: stageC(u // NT, u % NT, PB[slot]), stagger=cfg.get("rw_stagger", 80),
                         group=NT, hook=hook)

    def emit_ssd(self, j):
        nc, S, h, uT, cfg = self.nc, self.S, self.h, self.uT, self.cfg
        sin_d, sout_d, scst_d = self.dram["ssd_in"], self.dram["ssd_out"], self.dram["ssd_cst"]
        ident = self.cst.ap[:, 0:128]
        tri = self.cst.ap[:, 128:256]
        ones = self.cst.ap[:, 384:512]
        pvb = cfg["pv_ssd_conv"]
        cw = lambda jj, ch: self.pv.ap[:, pvb + jj * 32 + ch:pvb + jj * 32 + ch + 1]
        cb = lambda ch: self.pv.ap[:, pvb + 128 + ch:pvb + 128 + ch + 1]
        rvA = 0
        NA, NR = 3, 4
        ps = self.ps
        v3 = lambda ap: ap.rearrange("p (h i) -> p h i", h=4)
        with ExitStack() as es:
            self.rv = self.load_rv(es, cfg["rv_ssd"], 96 + 2048)
            self.junk = self.sb(es, "junk", [128, 256], F32)
            negm = self.sb(es, "snegm", [128, 4, 128], F32)
            S.dma("sp", negm.ap.rearrange("p h i -> p (h i)"), scst_d, writes=[negm.d()])
            win = self.sb(es, "swin", [128, 8, 772], BF16)
            wos = [self.sb(es, f"swo{i}", [128, 2, 1024], BF16) for i in range(2)]
            Aneg = self.sb(es, "sA", [128, 32], F32)
            M = self.sb(es, "sM", [128, 256], F32)
            Mb = self.sb(es, "sMb", [128, 256], BF16)
            yn = self.sb(es, "syn", [128, 256], F32)
            ygT = self.sb(es, "sygT", [128, 2, 128], BF16)
            st = self.sb(es, "sst", [128, 8], F32)
            AP_ = []
            for ai in range(NA):
                d = {"pc": self.sb(es, f"spc{ai}", [128, 4, 131], F32), "acc": self.sb(es, f"sacc{ai}", [128, 4, 128], F32),
                     "a4": self.sb(es, f"sa4{ai}", [128, 4, 128], F32), "sig": self.sb(es, f"ssig{ai}", [128, 4, 128], F32),
                     "BTb": self.sb(es, f"sBTb{ai}", [128, 128], BF16),
                     "CTb": self.sb(es, f"sCTb{ai}", [128, 128], BF16), "xs": self.sb(es, f"sxs{ai}", [128, 256], F32),
                     "dt": self.sb(es, f"sdt{ai}", [128, 32], F32), "trila": self.sb(es, f"strila{ai}", [128, 4, 128], F32),
                     "dec": self.sb(es, f"sdec{ai}", [128, 4, 128], F32), "ecr": self.sb(es, f"secr{ai}", [128, 4, 128], F32),
                     "bx": ps[2 + 2 * ai], "by": ps[3 + 2 * ai]}
                S.op("pool", lambda e: e.memset(d["dt"].ap, 0.0), writes=[d["dt"].d()])
                AP_.append(d)
            RB = []
            for r_ in range(NR):
                RB.append({"sz": self.sb(es, f"ssz{r_}", [128, 256], F32), "Btok": self.sb(es, f"sBtok{r_}", [128, 128], BF16),
                           "el": self.sb(es, f"sel{r_}", [128, 4], F32), "Pm": self.sb(es, f"sP{r_}", [128, 4, 128], BF16),
                           "CTs": self.sb(es, f"sCTs{r_}", [128, 4, 128], BF16), "vsb": self.sb(es, f"sv{r_}", [128, 256], BF16),
                           "vh": self.sb(es, f"svh{r_}", [128, 256], BF16), "t1": self.sb(es, f"st1{r_}", [128, 256], F32)})
            S.op("act", lambda e: e.activation(out=Aneg.ap, in_=self.rv.ap[:, rvA + 32:rvA + 64], func=AF.Exp),
                 reads=[self.rv.d()], writes=[Aneg.d()])
            S.op("dve", lambda e: e.tensor_scalar(out=Aneg.ap, in0=Aneg.ap, scalar1=-1.0, scalar2=None, op0=ALU.mult),
                 reads=[Aneg.d()], writes=[Aneg.d()])

            def sig_from(dst, src, rd, wr, eng_reads_psum=False):
                S.op("act", lambda e: e.activation(out=dst, in_=src, func=AF.Exp, scale=-1.0), reads=rd, writes=wr)
                S.op("act", lambda e: e.activation(out=dst, in_=dst, func=AF.Ln, bias=1.0), reads=wr, writes=wr)
                S.op("act", lambda e: e.activation(out=dst, in_=dst, func=AF.Exp, scale=-1.0), reads=wr, writes=wr)

            def stageA(g, t, ai, R_):
                c0, n = TILES[t]
                ugi = 0 if t == 0 else 1 + (t - 1) // 4
                ugp = 0 if t <= 1 else 1 + (t - 2) // 4
                P_ = AP_[ai]
                pc, acc, a4, sig, BTb, CTb, xs, dt, trila, dec, ecr, bx, by = (P_[k] for k in
                    ("pc", "acc", "a4", "sig", "BTb", "CTb", "xs", "dt", "trila", "dec", "ecr", "bx", "by"))
                chans = [2 * g, 2 * g + 1, 16 + g, 24 + g]
                dtb = self.rv.ap[:, rvA + g * 4:rvA + g * 4 + 4]
                Ag = Aneg.ap[:, g * 4:g * 4 + 4]
                dsk = self.rv.ap[:, rvA + 64 + g * 4:rvA + 64 + g * 4 + 4]
                if c0 == 0:
                    S.op("pool", lambda e: e.memset(pc.ap[:, :, 0:3], 0.0), writes=[pc.d()])
                    src0, dst0, w_ = 0, 3, n
                else:
                    src0, dst0, w_ = c0 - 3, 0, n + 3
                for a in range(4):
                    pb, off = (bx, by)[a // 2], (a % 2) * 256
                    for dc in range(8):
                        self.mm(pb.ap[:, off:off + w_], win.ap[:, dc, a * 128:(a + 1) * 128], uT.ap[:, dc, src0:src0 + w_], dc == 0, dc == 7,
                                reads=[win.d(), uT.d(ugi), uT.d(ugp)], writes=[pb.d()])
                for hf, pb in enumerate((bx, by)):
                    S.op("act", lambda e: e.activation(out=pc.ap[:, 2 * hf:2 * hf + 2, dst0:dst0 + w_],
                                                       in_=pb.ap.rearrange("p (a i) -> p a i", a=2)[:, :, 0:w_], func=AF.Copy),
                         reads=[pb.d()], writes=[pc.d()])
                for a in range(4):
                    ch = chans[a]
                    S.op("dve", lambda e: e.tensor_scalar(out=acc.ap[:, a, :n], in0=pc.ap[:, a, 0:n], scalar1=cw(0, ch), scalar2=cb(ch),
                                                          op0=ALU.mult, op1=ALU.add),
                         reads=[pc.d(), self.pv.d()], writes=[acc.d()])
                    for jj in range(1, 4):
                        S.op("dve", lambda e: e.scalar_tensor_tensor(out=acc.ap[:, a, :n], in0=pc.ap[:, a, jj:jj + n], scalar=cw(jj, ch),
                                                                      in1=acc.ap[:, a, :n], op0=ALU.mult, op1=ALU.add),
                             reads=[pc.d(), self.pv.d(), acc.d()], writes=[acc.d()])
                sig_from(sig.ap[:, :, :n], acc.ap[:, :, :n], [acc.d()], [sig.d()])
                S.op("pool", lambda e: e.tensor_tensor(out=a4.ap[:, :, :n], in0=acc.ap[:, :, :n], in1=sig.ap[:, :, :n], op=ALU.mult),
                     reads=[acc.d(), sig.d()], writes=[a4.d()])
                S.op("act", lambda e: e.activation(out=BTb.ap[:, :n], in_=a4.ap[:, 2, :n], func=AF.Copy), reads=[a4.d()], writes=[BTb.d()])
                S.op("act", lambda e: e.activation(out=CTb.ap[:, :n], in_=a4.ap[:, 3, :n], func=AF.Copy), reads=[a4.d()], writes=[CTb.d()])
                for dc in range(8):
                    self.mm(bx.ap[:n, 0:260], uT.ap[:, dc, c0:c0 + n], win.ap[:, dc, 512:772], dc == 0, dc == 7,
                            reads=[win.d(), uT.d(ugi)], writes=[bx.d()])
                sz = R_["sz"]
                sig_from(sz.ap[:n, :], bx.ap[:n, 0:256], [bx.d()], [sz.d()])
                S.op("dve", lambda e: e.tensor_tensor(out=sz.ap[:n, :], in0=sz.ap[:n, :], in1=bx.ap[:n, 0:256], op=ALU.mult),
                     reads=[sz.d(), bx.d()], writes=[sz.d()])
                S.op("dve", lambda e: e.tensor_tensor(out=dt.ap[:n, 0:4], in0=bx.ap[:n, 256:260], in1=dtb[:n, :], op=ALU.add),
                     reads=[bx.d(), self.rv.d()], writes=[dt.d()])
                S.op("act", lambda e: e.activation(out=dt.ap[:n, 0:4], in_=dt.ap[:n, 0:4], func=AF.Exp), reads=[dt.d()], writes=[dt.d()])
                S.op("act", lambda e: e.activation(out=dt.ap[:n, 0:4], in_=dt.ap[:n, 0:4], func=AF.Ln, bias=1.0),
                     reads=[dt.d()], writes=[dt.d()])
                S.op("dve", lambda e: e.tensor_tensor(out=dt.ap[:n, 4:8], in0=dt.ap[:n, 0:4], in1=Ag[:n, :], op=ALU.mult),
                     reads=[dt.d(), Aneg.d()], writes=[dt.d()])
                for c in range(2):
                    self.tr(by.ap[:n, c * 128:(c + 1) * 128], a4.ap[:, c, :n], 128, reads=[a4.d()], writes=[by.d()])
                self.tr(by.ap[:n, 256:384], a4.ap[:, 2, :n], 128, reads=[a4.d()], writes=[by.d()])
                S.op("act", lambda e: e.activation(out=xs.ap[:n, :], in_=by.ap[:n, 0:256], func=AF.Copy), reads=[by.d()], writes=[xs.d()])
                S.op("act", lambda e: e.activation(out=R_["Btok"].ap[:n, :], in_=by.ap[:n, 256:384], func=AF.Copy),
                     reads=[by.d()], writes=[R_["Btok"].d()])
                self.mm(bx.ap[:, 384:400], tri[:n, :], dt.ap[:n, 4:20], True, True, reads=[self.cst.d(), dt.d()], writes=[bx.d()])
                S.op("dve", lambda e: e.tensor_scalar(out=dt.ap[:n, 8:12], in0=bx.ap[:n, 384:388], scalar1=-1.0, scalar2=None, op0=ALU.mult),
                     reads=[bx.d()], writes=[dt.d()])
                S.op("dve", lambda e: e.tensor_tensor(out=trila.ap[:n, :, :], in0=tri[:n, :].unsqueeze(1).to_broadcast([n, 4, 128]),
                                                      in1=dt.ap[:n, 4:8].unsqueeze(2).to_broadcast([n, 4, 128]), op=ALU.mult),
                     reads=[self.cst.d(), dt.d()], writes=[trila.d()])
                crow = v3(by.ap)[:, :, :n]
                self.mm(by.ap, ones[:n, :], trila.ap[:n, :, :].rearrange("p h i -> p (h i)"), True, True,
                        reads=[self.cst.d(), trila.d()], writes=[by.d()])
                S.op("act", lambda e: e.activation(out=ecr.ap[:, :, :n], in_=crow, func=AF.Exp), reads=[by.d()], writes=[ecr.d()])
                S.op("pool", lambda e: e.tensor_copy(out=R_["el"].ap, in_=ecr.ap[:, :, n - 1]), reads=[ecr.d()], writes=[R_["el"].d()])
                S.op("dve", lambda e: e.tensor_tensor(out=dt.ap[:n, 12:16], in0=v3(by.ap)[:n, :, n - 1], in1=dt.ap[:n, 8:12], op=ALU.add),
                     reads=[by.d(), dt.d()], writes=[dt.d()])
                S.op("act", lambda e: e.activation(out=dt.ap[:n, 12:16], in_=dt.ap[:n, 12:16], func=AF.Exp), reads=[dt.d()], writes=[dt.d()])
                S.op("dve", lambda e: e.tensor_tensor(out=dt.ap[:n, 16:20], in0=dt.ap[:n, 12:16], in1=dt.ap[:n, 0:4], op=ALU.mult),
                     reads=[dt.d()], writes=[dt.d()])
                self.mm(by.ap, ident[:n, :], negm.ap[:n, :, :].rearrange("p h i -> p (h i)"), False, True,
                        reads=[self.cst.d(), negm.d()], writes=[by.d()])
                for hh in range(4):
                    S.op("act", lambda e: e.activation(out=dec.ap[:n, hh, :n], in_=v3(by.ap)[:n, hh, :n], func=AF.Exp,
                                                       bias=dt.ap[:n, 8 + hh:9 + hh]),
                         reads=[by.d(), dt.d()], writes=[dec.d()])
                self.mm(bx.ap[:n, :n], BTb.ap[:, :n], CTb.ap[:, :n], True, True, reads=[BTb.d(), CTb.d()], writes=[bx.d()])
                S.op("dve", lambda e: e.tensor_tensor(out=R_["Pm"].ap[:n, :, :n], in0=bx.ap[:n, :n].unsqueeze(1).to_broadcast([n, 4, n]),
                                                      in1=dec.ap[:n, :, :n], op=ALU.mult),
                     reads=[bx.d(), dec.d()], writes=[R_["Pm"].d()])
                S.op("pool", lambda e: e.tensor_tensor(out=R_["CTs"].ap[:, :, :n], in0=a4.ap[:, 3, :n].unsqueeze(1).to_broadcast([128, 4, n]),
                                                       in1=ecr.ap[:, :, :n], op=ALU.mult),
                     reads=[a4.d(), ecr.d()], writes=[R_["CTs"].d()])
                x3 = xs.ap[:n, :].rearrange("p (h q) -> p h q", h=4)
                S.op("dve", lambda e: e.tensor_tensor(out=R_["vsb"].ap[:n, :].rearrange("p (h q) -> p h q", h=4), in0=x3,
                                                      in1=dt.ap[:n, 0:4].unsqueeze(2).to_broadcast([n, 4, 64]), op=ALU.mult),
                     reads=[xs.d(), dt.d()], writes=[R_["vsb"].d()])
                S.op("dve", lambda e: e.tensor_tensor(out=R_["vh"].ap[:n, :].rearrange("p (h q) -> p h q", h=4), in0=x3,
                                                      in1=dt.ap[:n, 16:20].unsqueeze(2).to_broadcast([n, 4, 64]), op=ALU.mult),
                     reads=[xs.d(), dt.d()], writes=[R_["vh"].d()])
                S.op("dve", lambda e: e.tensor_tensor(out=R_["t1"].ap[:n, :].rearrange("p (h q) -> p h q", h=4), in0=x3,
                                                      in1=dsk[:n, :].unsqueeze(2).to_broadcast([n, 4, 64]), op=ALU.mult),
                     reads=[xs.d(), self.rv.d()], writes=[R_["t1"].d()])

            def stageB(g, t, R_):
                c0, n = TILES[t]
                b0, b1 = ps[0], ps[1]
                wo = wos[g % 2]
                if t == 0:
                    S.op("pool", lambda e: e.memset(M.ap, 0.0), writes=[M.d()])
                    S.op("pool", lambda e: e.memset(Mb.ap, 0.0), writes=[Mb.d()])
                nwb = self.rv.ap[:, rvA + 96 + g * 256:rvA + 96 + (g + 1) * 256]
                Pm, CTs, vsb, vh, t1, sz, Btok, el = (R_[k] for k in ("Pm", "CTs", "vsb", "vh", "t1", "sz", "Btok", "el"))
                for hh in range(4):
                    cs_ = slice(hh * 64, (hh + 1) * 64)
                    self.mm(b0.ap[:n, cs_], Pm.ap[:n, hh, :n], vsb.ap[:n, cs_], True, False, reads=[Pm.d(), vsb.d()], writes=[b0.d()])
                    self.mm(b0.ap[:n, cs_], CTs.ap[:, hh, :n], Mb.ap[:, cs_], False, True, reads=[CTs.d(), Mb.d()], writes=[b0.d()])
                self.mm(b1.ap[:, 0:256], Btok.ap[:n, :], vh.ap[:n, :], True, True, reads=[Btok.d(), vh.d()], writes=[b1.d()])
                S.op("dve", lambda e: e.tensor_tensor(out=M.ap.rearrange("p (h q) -> p h q", h=4), in0=M.ap.rearrange("p (h q) -> p h q", h=4),
                                                      in1=el.ap.unsqueeze(2).to_broadcast([128, 4, 64]), op=ALU.mult),
                     reads=[M.d(), el.d()], writes=[M.d()])
                S.op("dve", lambda e: e.tensor_tensor(out=M.ap, in0=M.ap, in1=b1.ap[:, 0:256], op=ALU.add),
                     reads=[M.d(), b1.d()], writes=[M.d()])
                S.op("act", lambda e: e.activation(out=Mb.ap, in_=M.ap, func=AF.Copy), reads=[M.d()], writes=[Mb.d()])
                S.op("dve", lambda e: e.tensor_tensor(out=t1.ap[:n, :], in0=t1.ap[:n, :], in1=b0.ap[:n, 0:256], op=ALU.add),
                     reads=[t1.d(), b0.d()], writes=[t1.d()])
                S.op("pool", lambda e: e.tensor_tensor(out=t1.ap[:n, :], in0=t1.ap[:n, :], in1=sz.ap[:n, :], op=ALU.mult),
                     reads=[t1.d(), sz.d()], writes=[t1.d()])
                S.op("act", lambda e: e.activation(out=self.junk.ap[:n, 0:256], in_=t1.ap[:n, :], func=AF.Square, accum_out=st.ap[:n, 2:3]),
                     reads=[t1.d()], writes=[self.junk.d(), st.d()])
                S.op("act", lambda e: e.activation(out=st.ap[:n, 3:4], in_=st.ap[:n, 2:3], func=AF.Ln, scale=1.0 / 256.0, bias=EPS),
                     reads=[st.d()], writes=[st.d()])
                S.op("act", lambda e: e.activation(out=st.ap[:n, 4:5], in_=st.ap[:n, 3:4], func=AF.Exp, scale=-0.5), reads=[st.d()], writes=[st.d()])
                S.op("dve", lambda e: e.scalar_tensor_tensor(out=yn.ap[:n, :], in0=t1.ap[:n, :], scalar=st.ap[:n, 4:5], in1=nwb[:n, :],
                                                              op0=ALU.mult, op1=ALU.mult),
                     reads=[t1.d(), st.d(), self.rv.d()], writes=[yn.d()])
                self.out_proj(yn, n, 2, ygT, wo, t, b1, (b0, b1))

            def hook(g):
                S.dma("pool", win.ap.rearrange("p c f -> p (c f)"), sin_d[j * 8 + g], writes=[win.d()])
                S.dma("pool", wos[g % 2].ap.rearrange("p c f -> p (c f)"), sout_d[j * 8 + g], writes=[wos[g % 2].d()])
            run_pipeline(8 * NT, lambda u, ai, slot: stageA(u // NT, u % NT, ai, RB[slot]), lambda u, slot: stageB(u // NT, u % NT, RB[slot]),
                         NA, NR, stagger=cfg.get("ssd_stagger", 45), group=NT, hook=hook)

    def emit_gla(self, j, norm_w):
        nc, S, h, uT, cfg = self.nc, self.S, self.h, self.uT, self.cfg
        gup_d = self.dram["gla_gup"]
        gb = self.pv.ap[:, cfg["pv_gla_bias"]:cfg["pv_gla_bias"] + 4]
        with ExitStack() as es:
            self.rv = self.load_rv(es, cfg["rv_gla_norm"], 256)
            nwb = self.rv.ap[:, 0:256]
            gup = self.sb(es, "gup", [16, 512], F32)
            negb = self.sb(es, "gnegb", [128, 4], F32)
            S.dma("sp", gup.ap, gup_d[j], writes=[gup.d()])
            S.op("dve", lambda e: e.tensor_scalar(out=negb.ap, in0=gb, scalar1=-1.0, scalar2=None, op0=ALU.mult),
                 reads=[self.pv.d()], writes=[negb.d()])
            bufsets = [self.gla_bufs(es, i) for i in range(2)]
            for i in range(2):
                self.gla_load(j, i, bufsets[i])
            self.emit_norm(norm_w)
            for hd0 in (0, 2):
                if hd0:
                    for i in range(2):
                        self.gla_load(j, hd0 + i, bufsets[i])
                IL.run([(lambda hd=hd0 + i, bs=bufsets[i], bk=[self.ps[4 * i + k] for k in range(4)]:
                         self.gla_head(j, hd, bs, bk, gup, negb, nwb)) for i in range(2)])

    def gla_bufs(self, es, i):
        sb = lambda nm, shape, dt: self.sb(es, f"{nm}{i}", shape, dt)
        return dict(win=sb("gwin", [128, 8, 784], BF16), wo=sb("gwo", [128, 2, 1024], BF16), M=sb("gM", [128, 256], F32),
                    Mb=sb("gMb", [128, 256], BF16), glr=sb("gglr", [16, 512], F32), sp=sb("gsp", [128, 512], F32),
                    cum=sb("gcum", [128, 512], F32), ec=sb("gec", [128, 512], F32), en=sb("gen", [128, 512], F32),
                    qt=sb("gqt", [128, 512], BF16), kt=sb("gkt", [128, 512], BF16), khT=sb("gkhT", [128, 128], F32),
                    eh=sb("geh", [128, 128], F32), khat=sb("gkhat", [128, 128], BF16), vsb=sb("gv", [128, 256], BF16),
                    sr=sb("gsr", [128, 256], F32), Pm=sb("gP", [128, 128], BF16), yn=sb("gyn", [128, 256], F32),
                    ygT=sb("gygT", [128, 2, 128], BF16), st=sb("gst", [128, 8], F32), junk=sb("gjunk", [128, 256], F32))

    def gla_load(self, j, hd, B_):
        S = self.S
        S.dma("pool", B_["win"].ap.rearrange("p c f -> p (c f)"), self.dram["gla_in"][j * 4 + hd], writes=[B_["win"].d()])
        S.dma("pool", B_["wo"].ap.rearrange("p c f -> p (c f)"), self.dram["gla_out"][j * 4 + hd], writes=[B_["wo"].d()])

    def gla_head(self, j, hd, B_, bk, gup, negb, nwb):
        nc, S, h, uT, cfg = self.nc, self.S, self.h, self.uT, self.cfg
        gin_d, gout_d = self.dram["gla_in"], self.dram["gla_out"]
        tri = self.cst.ap[:, 128:256]
        ones = self.cst.ap[:, 384:512]
        win, wo, M, Mb, glr, sp, cum, ec, en, qt, kt, khT, eh, khat, vsb, sr, Pm, yn, ygT, st, junk = (B_[k] for k in
            ("win", "wo", "M", "Mb", "glr", "sp", "cum", "ec", "en", "qt", "kt", "khT", "eh", "khat", "vsb", "sr", "Pm", "yn", "ygT", "st", "junk"))
        ps = {0: bk[0], 1: bk[1], 2: bk[2], 3: bk[2], 4: bk[2], 5: bk[3], 6: bk[0], 7: bk[3]}
        S.op("pool", lambda e: e.memset(M.ap, 0.0), writes=[M.d()])
        S.op("pool", lambda e: e.memset(Mb.ap, 0.0), writes=[Mb.d()])
        for gi, (g0, gn, gtiles) in enumerate(GROUPS):
            for a, (o0, o1) in enumerate(((0, 128), (128, 256), (256, 272))):
                m = o1 - o0
                for dc in range(8):
                    self.mm(ps[a].ap[:m, :gn], win.ap[:, dc, o0:o1], uT.ap[:, dc, g0:g0 + gn], dc == 0, dc == 7,
                            reads=[win.d(), uT.d(gi)], writes=[ps[a].d()])
            S.op("act", lambda e: e.activation(out=glr.ap[:, :gn], in_=ps[2].ap[:16, :gn], func=AF.Copy),
                 reads=[ps[2].d()], writes=[glr.d()])
            self.mm(ps[3].ap[:, :gn], gup.ap[:, hd * 128:(hd + 1) * 128], glr.ap[:, :gn], True, True,
                    reads=[gup.d(), glr.d()], writes=[ps[3].d()])
            S.op("act", lambda e: e.activation(out=sp.ap[:, :gn], in_=ps[3].ap[:, :gn], func=AF.Exp, scale=-1.0,
                                               bias=negb.ap[:, hd:hd + 1]),
                 reads=[ps[3].d(), negb.d()], writes=[sp.d()])
            S.op("act", lambda e: e.activation(out=sp.ap[:, :gn], in_=sp.ap[:, :gn], func=AF.Ln, bias=1.0),
                 reads=[sp.d()], writes=[sp.d()])
            for ti, t in enumerate(gtiles):
                n = TILES[t][1]
                lo = ti * 128
                S.op("dve", lambda e: e.tensor_tensor_scan(out=cum.ap[:, lo:lo + n], data0=ones[:, :n], data1=sp.ap[:, lo:lo + n],
                                                           initial=0.0, op0=ALU.mult, op1=ALU.add),
                     reads=[sp.d(), self.cst.d()], writes=[cum.d()])
            S.op("act", lambda e: e.activation(out=ec.ap[:, :gn], in_=cum.ap[:, :gn], func=AF.Exp, scale=-1.0 / 16.0),
                 reads=[cum.d()], writes=[ec.d()])
            S.op("act", lambda e: e.activation(out=en.ap[:, :gn], in_=cum.ap[:, :gn], func=AF.Exp, scale=1.0 / 16.0),
                 reads=[cum.d()], writes=[en.d()])
            S.op("dve", lambda e: e.scalar_tensor_tensor(out=qt.ap[:, :gn], in0=ps[0].ap[:, :gn], scalar=128.0 ** -0.5,
                                                          in1=ec.ap[:, :gn], op0=ALU.mult, op1=ALU.mult),
                 reads=[ps[0].d(), ec.d()], writes=[qt.d()])
            S.op("dve", lambda e: e.tensor_tensor(out=kt.ap[:, :gn], in0=ps[1].ap[:, :gn], in1=en.ap[:, :gn], op=ALU.mult),
                 reads=[ps[1].d(), en.d()], writes=[kt.d()])
            for ti, t in enumerate(gtiles):
                c0, n = TILES[t]
                lo = ti * 128
                last = cum.ap[:, lo + n - 1:lo + n]
                for dc in range(8):
                    self.mm(ps[4].ap[:n, :], uT.ap[:, dc, c0:c0 + n], win.ap[:, dc, 272:784], dc == 0, dc == 7,
                            reads=[win.d(), uT.d(gi)], writes=[ps[4].d()])
                S.op("act", lambda e: e.activation(out=vsb.ap[:n, :], in_=ps[4].ap[:n, 0:256], func=AF.Copy),
                     reads=[ps[4].d()], writes=[vsb.d()])
                S.op("act", lambda e: e.activation(out=sr.ap[:n, :], in_=ps[4].ap[:n, 256:512], func=AF.Silu),
                     reads=[ps[4].d()], writes=[sr.d()])
                self.mm(ps[5].ap[:n, :n], kt.ap[:, lo:lo + n], qt.ap[:, lo:lo + n], True, True,
                        reads=[kt.d(), qt.d()], writes=[ps[5].d()])
                S.op("dve", lambda e: e.tensor_tensor(out=Pm.ap[:n, :n], in0=ps[5].ap[:n, :n], in1=tri[:n, :n], op=ALU.mult),
                     reads=[ps[5].d(), self.cst.d()], writes=[Pm.d()])
                self.mm(ps[6].ap[:n, 0:256], Pm.ap[:n, :n], vsb.ap[:n, :], True, False, reads=[Pm.d(), vsb.d()], writes=[ps[6].d()])
                self.mm(ps[6].ap[:n, 0:256], qt.ap[:, lo:lo + n], Mb.ap, False, True, reads=[qt.d(), Mb.d()], writes=[ps[6].d()])
                S.op("dve", lambda e: e.tensor_scalar(out=st.ap[:, 0:1], in0=last, scalar1=-1.0 / 16.0, scalar2=None, op0=ALU.mult),
                     reads=[cum.d()], writes=[st.d()])
                S.op("act", lambda e: e.activation(out=eh.ap[:, :n], in_=cum.ap[:, lo:lo + n], func=AF.Exp, scale=1.0 / 16.0,
                                                   bias=st.ap[:, 0:1]),
                     reads=[cum.d(), st.d()], writes=[eh.d()])
                S.op("act", lambda e: e.activation(out=st.ap[:, 1:2], in_=last, func=AF.Exp, scale=-1.0 / 16.0),
                     reads=[cum.d()], writes=[st.d()])
                S.op("dve", lambda e: e.tensor_tensor(out=khT.ap[:, :n], in0=ps[1].ap[:, lo:lo + n], in1=eh.ap[:, :n], op=ALU.mult),
                     reads=[ps[1].d(), eh.d()], writes=[khT.d()])
                self.tr(ps[5].ap[:n, 128:256], khT.ap[:, :n], 128, reads=[khT.d()], writes=[ps[5].d()])
                S.op("act", lambda e: e.activation(out=khat.ap[:n, :], in_=ps[5].ap[:n, 128:256], func=AF.Copy),
                     reads=[ps[5].d()], writes=[khat.d()])
                self.mm(ps[7].ap[:, 0:256], khat.ap[:n, :], vsb.ap[:n, :], True, True, reads=[khat.d(), vsb.d()], writes=[ps[7].d()])
                S.op("dve", lambda e: e.scalar_tensor_tensor(out=M.ap, in0=M.ap, scalar=st.ap[:, 1:2], in1=ps[7].ap[:, 0:256],
                                                              op0=ALU.mult, op1=ALU.add),
                     reads=[M.d(), st.d(), ps[7].d()], writes=[M.d()])
                S.op("act", lambda e: e.activation(out=Mb.ap, in_=M.ap, func=AF.Copy), reads=[M.d()], writes=[Mb.d()])
                S.op("act", lambda e: e.activation(out=junk.ap[:n, 0:256], in_=ps[6].ap[:n, 0:256], func=AF.Square,
                                                   accum_out=st.ap[:n, 2:3]),
                     reads=[ps[6].d()], writes=[junk.d(), st.d()])
                S.op("act", lambda e: e.activation(out=st.ap[:n, 3:4], in_=st.ap[:n, 2:3], func=AF.Sqrt, scale=1.0 / 256.0, bias=EPS),
                     reads=[st.d()], writes=[st.d()])
                S.op("dve", lambda e: e.reciprocal(out=st.ap[:n, 4:5], in_=st.ap[:n, 3:4]), reads=[st.d()], writes=[st.d()])
                S.op("dve", lambda e: e.scalar_tensor_tensor(out=yn.ap[:n, :], in0=ps[6].ap[:n, 0:256], scalar=st.ap[:n, 4:5],
                                                              in1=nwb[:n, :], op0=ALU.mult, op1=ALU.mult),
                     reads=[ps[6].d(), st.d(), self.rv.d()], writes=[yn.d()])
                S.op("pool", lambda e: e.tensor_tensor(out=yn.ap[:n, :], in0=yn.ap[:n, :], in1=sr.ap[:n, :], op=ALU.mult),
                     reads=[yn.d(), sr.d()], writes=[yn.d()])
                self.out_proj(yn, n, 2, ygT, wo, t, ps[5], (ps[4], ps[7]))


    def emit_ret(self, j, norm_w):
        nc, S, h, uT, cfg = self.nc, self.S, self.h, self.uT, self.cfg
        ret_in_d, ret_out_d, ret_cst_d = self.dram["ret_in"], self.dram["ret_out"], self.dram["ret_cst"]
        NRC = cfg["nretc"]
        with ExitStack() as es:
            rc = self.sb(es, "retc", [128, NRC], F32)
            S.dma("sp", rc.ap, ret_cst_d, writes=[rc.d()])
            decT = lambda hd: rc.ap[:, hd * 128:(hd + 1) * 128]
            rowpow = lambda hd: rc.ap[:, 512 + hd * 128:512 + (hd + 1) * 128]
            kdec = lambda hd, n: rc.ap[:, 1024 + (0 if n == 128 else 4) + hd:1024 + (0 if n == 128 else 4) + hd + 1]
            cosT = rc.ap[:, 1032:1032 + L]
            sinT = rc.ap[:, 1032 + L:1032 + 2 * L]
            win = self.sb(es, "rwin", [128, 8, 1536], BF16)
            wo = self.sb(es, "rwo", [128, 4, 1024], BF16)
            M = self.sb(es, "rM", [128, 2, 512], F32)
            Mb = self.sb(es, "rMb", [128, 2, 512], BF16)
            qT = self.sb(es, "rqT", [128, 2, 512], BF16)
            kT = self.sb(es, "rkT", [128, 2, 512], BF16)
            kf = self.sb(es, "rkf", [128, 2, 512], F32)
            tmp = [self.sb(es, f"rtmp{i}", [128, 512], F32) for i in range(2)]
            yn = self.sb(es, "ryn", [128, 512], F32)
            ygT = self.sb(es, "rygT", [128, 4, 128], BF16)
            st = self.sb(es, "rst", [128, 16], F32)
            RB = [dict(vsb=self.sb(es, f"rv{i}", [128, 512], BF16), sg=self.sb(es, f"rsg{i}", [128, 512], F32),
                       Pm=self.sb(es, f"rP{i}", [128, 128], BF16), qs=self.sb(es, f"rqs{i}", [128, 2, 128], BF16),
                       khat=self.sb(es, f"rkhat{i}", [128, 256], BF16)) for i in range(2)]
            ps = self.ps
            tile_pos = {}
            for gi, (g0, gn, gtiles) in enumerate(GROUPS):
                for ti, t in enumerate(gtiles):
                    tile_pos[t] = (gi, g0, gn, ti)

            def stageA(hd, t, R_):
                gi, g0, gn, ti = tile_pos[t]
                c0, n = TILES[t]
                lo = ti * 128
                vsb, sg, Pm, qs, khat = (R_[k] for k in ("vsb", "sg", "Pm", "qs", "khat"))
                if ti == 0:
                    for a in range(4):
                        for dc in range(8):
                            self.mm(ps[a].ap[:, :gn], win.ap[:, dc, a * 128:(a + 1) * 128], uT.ap[:, dc, g0:g0 + gn],
                                    dc == 0, dc == 7, reads=[win.d(), uT.d(gi)], writes=[ps[a].d()])
                    cs, sn = cosT[:, g0:g0 + gn], sinT[:, g0:g0 + gn]
                    for qk in range(2):
                        p1, p2 = ps[2 * qk], ps[2 * qk + 1]
                        sc = 1.0 if qk == 0 else 1.0 / 16.0
                        dst = qT if qk == 0 else kf
                        t0, t1 = tmp
                        S.op("dve", lambda e: e.scalar_tensor_tensor(out=t0.ap[:, :gn], in0=p1.ap[:, :gn], scalar=sc, in1=cs,
                                                                      op0=ALU.mult, op1=ALU.mult),
                             reads=[p1.d(), rc.d()], writes=[t0.d()])
                        S.op("dve", lambda e: e.scalar_tensor_tensor(out=t1.ap[:, :gn], in0=p2.ap[:, :gn], scalar=sc, in1=sn,
                                                                      op0=ALU.mult, op1=ALU.mult),
                             reads=[p2.d(), rc.d()], writes=[t1.d()])
                        S.op("pool", lambda e: e.tensor_tensor(out=dst.ap[:, 0, :gn], in0=t0.ap[:, :gn], in1=t1.ap[:, :gn], op=ALU.subtract),
                             reads=[t0.d(), t1.d()], writes=[dst.d()])
                        S.op("dve", lambda e: e.scalar_tensor_tensor(out=t0.ap[:, :gn], in0=p1.ap[:, :gn], scalar=sc, in1=sn,
                                                                      op0=ALU.mult, op1=ALU.mult),
                             reads=[p1.d(), rc.d()], writes=[t0.d()])
                        S.op("dve", lambda e: e.scalar_tensor_tensor(out=t1.ap[:, :gn], in0=p2.ap[:, :gn], scalar=sc, in1=cs,
                                                                      op0=ALU.mult, op1=ALU.mult),
                             reads=[p2.d(), rc.d()], writes=[t1.d()])
                        S.op("pool", lambda e: e.tensor_tensor(out=dst.ap[:, 1, :gn], in0=t0.ap[:, :gn], in1=t1.ap[:, :gn], op=ALU.add),
                             reads=[t0.d(), t1.d()], writes=[dst.d()])
                    S.op("act", lambda e: e.activation(out=kT.ap[:, :, :gn], in_=kf.ap[:, :, :gn], func=AF.Copy),
                         reads=[kf.d()], writes=[kT.d()])
                for a, pb in ((0, ps[4]), (1, ps[5])):
                    for dc in range(8):
                        self.mm(pb.ap[:n, :], uT.ap[:, dc, c0:c0 + n], win.ap[:, dc, 512 + a * 512:1024 + a * 512],
                                dc == 0, dc == 7, reads=[win.d(), uT.d(gi)], writes=[pb.d()])
                S.op("act", lambda e: e.activation(out=vsb.ap[:n, :], in_=ps[4].ap[:n, :], func=AF.Copy),
                     reads=[ps[4].d()], writes=[vsb.d()])
                S.op("act", lambda e: e.activation(out=sg.ap[:n, :], in_=ps[5].ap[:n, :], func=AF.Silu),
                     reads=[ps[5].d()], writes=[sg.d()])
                for c in range(2):
                    self.mm(ps[4].ap[:n, :n], kT.ap[:, c, lo:lo + n], qT.ap[:, c, lo:lo + n], c == 0, c == 1,
                            reads=[kT.d(), qT.d()], writes=[ps[4].d()])
                S.op("dve", lambda e: e.tensor_tensor(out=Pm.ap[:n, :n], in0=ps[4].ap[:n, :n], in1=decT(hd)[:n, :n], op=ALU.mult),
                     reads=[ps[4].d(), rc.d()], writes=[Pm.d()])
                S.op("pool", lambda e: e.tensor_tensor(out=qs.ap[:, :, :n], in0=qT.ap[:, :, lo:lo + n],
                                                       in1=rowpow(hd)[:, :n].unsqueeze(1).to_broadcast([128, 2, n]), op=ALU.mult),
                     reads=[qT.d(), rc.d()], writes=[qs.d()])
                for c in range(2):
                    self.tr(ps[5].ap[:n, c * 128:(c + 1) * 128], kf.ap[:, c, lo:lo + n], 128, reads=[kf.d()], writes=[ps[5].d()])
                S.op("dve", lambda e: e.tensor_scalar(out=khat.ap[:n, :], in0=ps[5].ap[:n, 0:256], scalar1=kdec(hd, n)[:n, :],
                                                      scalar2=None, op0=ALU.mult),
                     reads=[ps[5].d(), rc.d()], writes=[khat.d()])

            def stageB(hd, t, R_):
                c0, n = TILES[t]
                gam = 1.0 - 2.0 ** (-5.0 - hd)
                vsb, sg, Pm, qs, khat = (R_[k] for k in ("vsb", "sg", "Pm", "qs", "khat"))
                self.mm(ps[6].ap[:n, :], Pm.ap[:n, :n], vsb.ap[:n, :], True, False, reads=[Pm.d(), vsb.d()], writes=[ps[6].d()])
                for c in range(2):
                    self.mm(ps[6].ap[:n, :], qs.ap[:, c, :n], Mb.ap[:, c, :], False, c == 1,
                            reads=[qs.d(), Mb.d()], writes=[ps[6].d()])
                for c in range(2):
                    self.mm(ps[7].ap[:, :], khat.ap[:n, c * 128:(c + 1) * 128], vsb.ap[:n, :], True, True,
                            reads=[khat.d(), vsb.d()], writes=[ps[7].d()])
                    S.op("dve", lambda e: e.scalar_tensor_tensor(out=M.ap[:, c, :], in0=M.ap[:, c, :], scalar=float(gam ** n),
                                                                  in1=ps[7].ap[:, :], op0=ALU.mult, op1=ALU.add),
                         reads=[M.d(), ps[7].d()], writes=[M.d()])
                S.op("act", lambda e: e.activation(out=Mb.ap, in_=M.ap, func=AF.Copy), reads=[M.d()], writes=[Mb.d()])
                S.op("dve", lambda e: e.bn_stats(out=st.ap[:n, 0:6], in_=ps[6].ap[:n, :]), reads=[ps[6].d()], writes=[st.d()])
                S.op("dve", lambda e: e.bn_aggr(out=st.ap[:n, 6:8], in_=st.ap[:n, 0:6]), reads=[st.d()], writes=[st.d()])
                S.op("act", lambda e: e.activation(out=st.ap[:n, 8:9], in_=st.ap[:n, 7:8], func=AF.Sqrt, bias=EPS),
                     reads=[st.d()], writes=[st.d()])
                S.op("dve", lambda e: e.reciprocal(out=st.ap[:n, 9:10], in_=st.ap[:n, 8:9]), reads=[st.d()], writes=[st.d()])
                S.op("dve", lambda e: e.tensor_scalar(out=yn.ap[:n, :], in0=ps[6].ap[:n, :], scalar1=st.ap[:n, 6:7],
                                                      scalar2=st.ap[:n, 9:10], op0=ALU.subtract, op1=ALU.mult),
                     reads=[ps[6].d(), st.d()], writes=[yn.d()])
                S.op("pool", lambda e: e.tensor_tensor(out=yn.ap[:n, :], in0=yn.ap[:n, :], in1=sg.ap[:n, :], op=ALU.mult),
                     reads=[yn.d(), sg.d()], writes=[yn.d()])
                self.out_proj(yn, n, 4, ygT, wo, t, ps[6], (ps[7], ps[6]))

            for hd in range(4):
                S.dma("pool", win.ap.rearrange("p c f -> p (c f)"), ret_in_d[j * 4 + hd], writes=[win.d()])
                S.dma("pool", wo.ap.rearrange("p c f -> p (c f)"), ret_out_d[j * 4 + hd], writes=[wo.d()])
                if hd == 0:
                    self.emit_norm(norm_w)
                S.op("pool", lambda e: e.memset(M.ap, 0.0), writes=[M.d()])
                S.op("pool", lambda e: e.memset(Mb.ap, 0.0), writes=[Mb.d()])
                stageA(hd, 0, RB[0])
                for t in range(NT):
                    fns = [lambda: stageB(hd, t, RB[t % 2])]
                    if t + 1 < NT:
                        fns.append(lambda: stageA(hd, t + 1, RB[(t + 1) % 2]))
                    IL.run(fns)

    def out_proj(self, y, n, nch, yT, wo, t, ptr, pouts):
        S, h = self.S, self.h
        for c in range(nch):
            self.tr(ptr.ap[:, c * 128:c * 128 + n], y.ap[:n, c * 128:(c + 1) * 128], n, reads=[y.d()], writes=[ptr.d()])
        S.op("act", lambda e: e.activation(out=yT.ap[:, 0:nch, :n], in_=ptr.ap[:, 0:nch * 128].rearrange("p (c n) -> p c n", c=nch)[:, :, :n],
                                           func=AF.Copy),
             reads=[ptr.d()], writes=[yT.d()])
        for half in range(2):
            po = pouts[half]
            for c in range(nch):
                self.mm(po.ap[:n, :], yT.ap[:, c, :n], wo.ap[:, c, half * 512:(half + 1) * 512], c == 0, c == nch - 1,
                        reads=[yT.d(), wo.d()], writes=[po.d()])
            S.op("dve", lambda e: e.tensor_tensor(out=h.ap[:n, t, half * 512:(half + 1) * 512],
                                                  in0=h.ap[:n, t, half * 512:(half + 1) * 512], in1=po.ap[:n, :], op=ALU.add),
                 reads=[po.d(), h.d(t)], writes=[h.d(t)])


def host_pack(inp, cfg):
    f32 = np.float32
    shared = {}
    pv_cols = []

    def add_pv(name, arr2d):
        cfg[name] = sum(a.shape[1] for a in pv_cols)
        pv_cols.append(np.asarray(arr2d, f32))
    add_pv("pv_norm_mix", np.concatenate([vec_cols(inp["norm_mix"][i]) for i in range(DEPTH)], axis=1))
    add_pv("pv_norm_mlp", np.concatenate([vec_cols(inp["norm_mlp"][i]) for i in range(DEPTH)], axis=1))
    add_pv("pv_gla_bias", vec_cols(inp["gla_gate_bias"][0]))
    add_pv("pv_rwkv", np.concatenate([vec_cols(inp["rwkv_mu"][0][i]) for i in range(6)] + [vec_cols(inp[nm][0].reshape(-1)) for nm in
                                     ("rwkv_w0", "rwkv_a0", "rwkv_k_k", "rwkv_k_a", "rwkv_r_k", "rwkv_ln_w", "rwkv_ln_b")], axis=1))
    add_pv("pv_ssd_conv", np.concatenate([vec_cols(inp["m2_conv_w"][0][jj]) for jj in range(4)] + [vec_cols(inp["m2_conv_b"][0])], axis=1))
    shared["pvec"] = np.ascontiguousarray(np.concatenate(pv_cols, axis=1))
    cfg["npv"] = shared["pvec"].shape[1]
    rv = []

    def add_rv(name, v):
        cfg[name] = sum(a.shape[0] for a in rv)
        rv.append(np.asarray(v, f32).reshape(-1))
    add_rv("rv_norm_final", inp["norm_final"])
    add_rv("rv_gla_norm", inp["gla_norm_w"][0])
    add_rv("rv_ssd", np.concatenate([inp["m2_dt_bias"][0], inp["m2_a_log"][0], inp["m2_d"][0], inp["m2_norm_w"][0]]))
    shared["rvec"] = np.ascontiguousarray(np.concatenate(rv)[None, :])
    cfg["nrv"] = shared["rvec"].shape[1]
    ii = np.arange(128)
    cst = [np.eye(128, dtype=f32), (ii[:, None] <= ii[None, :]).astype(f32), (ii[:, None] < ii[None, :]).astype(f32), np.ones((128, 128), f32)]
    shared["cst"] = np.ascontiguousarray(np.concatenate(cst, axis=1))
    cfg["ncst"] = shared["cst"].shape[1]
    if cfg["mlps"]:
        w_in = inp["mlp_w_in"]
        w_out = inp["mlp_w_out"]
        shared["mlp_in"] = np.stack([pack_rows(w_in[l][:, fb * 512:(fb + 1) * 512]) for l in range(DEPTH) for fb in range(8)])
        shared["mlp_out"] = np.stack([pack_rows(w_out[l][fb * 512:(fb + 1) * 512, :]) for l in range(DEPTH) for fb in range(8)])
    if 0 in cfg["mixers"]:
        ii = np.arange(128)
        same = (ii[:, None] // 64) == (ii[None, :] // 64)
        su = ((ii[:, None] < ii[None, :]) & same).astype(f32)
        sl = ((ii[:, None] > ii[None, :]) & same).astype(f32)
        iu = ((ii[:, None] <= ii[None, :]) & same).astype(f32)
        shared["rw_cst"] = np.ascontiguousarray(np.concatenate([su, sl, iu, same.astype(f32)], axis=1))
        shared["rw_lora"] = pack_rows(np.concatenate([inp["rwkv_w_lora_a"][0], inp["rwkv_a_lora_a"][0], inp["rwkv_g_lora_a"][0]], axis=1))
        pairs, wos, bwas, bgs = [], [], [], []
        for hp in range(8):
            pc = slice(hp * 128, (hp + 1) * 128)
            pairs.append(pack_rows(np.concatenate([inp["rwkv_w_r"][0][:, pc], inp["rwkv_w_k"][0][:, pc], inp["rwkv_w_v"][0][:, pc]], axis=1)))
            wos.append(np.ascontiguousarray(inp["rwkv_w_o"][0][pc, :]))
            bw = np.zeros((128, 256), f32)
            bw[0:64, 0:128] = inp["rwkv_w_lora_b"][0][:, pc]
            bw[64:128, 128:256] = inp["rwkv_a_lora_b"][0][:, pc]
            bwas.append(bw)
            bg = np.zeros((128, 256), f32)
            bg[:, 0:128] = inp["rwkv_g_lora_b"][0][0:128, pc]
            bg[0:32, 128:256] = inp["rwkv_g_lora_b"][0][128:160, pc]
            bgs.append(bg)
        shared["rw_pair"] = np.stack(pairs)
        shared["rw_wo"] = np.stack(wos)
        shared["rw_bwa"] = np.ascontiguousarray(np.stack(bwas))
        shared["rw_bg"] = np.stack(bgs)
    if 1 in cfg["mixers"]:
        W = inp["m2_in_proj"][0]
        ins = []
        for g in range(8):
            cols = np.concatenate([2048 + np.arange(g * 256, (g + 1) * 256), 4096 + np.arange(g * 128, (g + 1) * 128),
                                   5120 + np.arange(g * 128, (g + 1) * 128), np.arange(g * 256, (g + 1) * 256),
                                   6144 + np.arange(g * 4, (g + 1) * 4)])
            ins.append(pack_rows(W[:, cols]))
        shared["ssd_in"] = np.stack(ins)
        shared["ssd_out"] = np.stack([pack_rows(inp["m2_out_proj"][0][g * 256:(g + 1) * 256, :]) for g in range(8)])
        ii = np.arange(128)
        nm = np.where(ii[:, None] <= ii[None, :], 0.0, -30000.0).astype(f32)
        shared["ssd_cst"] = np.ascontiguousarray(np.tile(nm, (1, 4)))
    if 2 in cfg["mixers"]:
        W = inp["gla_in_proj"][0]
        ins = []
        for hd in range(4):
            cols = np.concatenate([np.arange(hd * 128, (hd + 1) * 128), 512 + np.arange(hd * 128, (hd + 1) * 128),
                                   3072 + np.arange(16), 1024 + np.arange(hd * 256, (hd + 1) * 256),
                                   2048 + np.arange(hd * 256, (hd + 1) * 256)])
            ins.append(pack_rows(W[:, cols]))
        shared["gla_in"] = np.stack(ins)
        shared["gla_out"] = np.stack([pack_rows(inp["gla_out_proj"][0][hd * 256:(hd + 1) * 256, :]) for hd in range(4)])
        shared["gla_gup"] = np.ascontiguousarray(inp["gla_gate_up"]).astype(f32)
    if 3 in cfg["mixers"]:
        W = inp["ret_in_proj"][0]
        ins = []
        for hd in range(4):
            cols = np.concatenate([np.arange(hd * 256, (hd + 1) * 256), 1024 + np.arange(hd * 256, (hd + 1) * 256),
                                   2048 + np.arange(hd * 512, (hd + 1) * 512), 4096 + np.arange(hd * 512, (hd + 1) * 512)])
            ins.append(pack_rows(W[:, cols]))
        shared["ret_in"] = np.stack(ins)
        shared["ret_out"] = np.stack([pack_rows(inp["ret_out_proj"][0][hd * 512:(hd + 1) * 512, :]) for hd in range(4)])
        ii = np.arange(128)
        dec, rowp, kd128, kd16 = [], [], [], []
        for hd in range(4):
            lg = np.log1p(-np.exp2(-5.0 - hd))
            diff = (ii[None, :] - ii[:, None]).astype(np.float64)
            dec.append(np.where(diff >= 0, np.exp(lg * diff), 0.0))
            rowp.append(np.broadcast_to(np.exp(lg * (ii + 1.0))[None, :], (128, 128)))
            kd128.append(np.exp(lg * (127.0 - ii)))
            kd16.append(np.exp(lg * (15.0 - ii)))
        inv_freq = (1.0 / (10000.0 ** np.linspace(0.0, 1.0, 128, dtype=f32))).astype(f32)
        ang = (np.arange(L, dtype=f32)[None, :] * inv_freq[:, None]).astype(f32).astype(np.float64)
        shared["ret_cst"] = np.ascontiguousarray(np.concatenate(
            dec + rowp + [np.stack(kd128, 1), np.stack(kd16, 1), np.cos(ang), np.sin(ang)], axis=1).astype(f32))
        cfg["nretc"] = shared["ret_cst"].shape[1]
    return shared


def run(inputs, cfg):
    inp = {k: np.asarray(v) for k, v in inputs.items()}
    shared = host_pack(inp, cfg)
    b = Builder(cfg)
    nc = b.build()
    in_maps = []
    for c in range(8):
        m = dict(shared)
        m["x"] = np.ascontiguousarray(inp["x"][c])
        m["meta"] = np.ascontiguousarray(inp["meta_tokens"])
        in_maps.append(m)
    ncores = cfg.get("ncores", 8)
    in_maps = in_maps[:ncores]
    res = run_bass_kernel_spmd(nc, in_maps, core_ids=list(range(ncores)))
    out = np.stack([np.asarray(r["out"]) for r in res.results]).astype(np.float32)
    if cfg.get("debug"):
        return out, np.stack([np.asarray(r["dbg"]) for r in res.results])
    return out


def kernel(**inputs):
    cfg = {"mixers": [0, 1, 2, 3], "mlps": [0, 1, 2, 3]}
    return run(inputs, cfg)
```

```python
import math
import threading
from contextlib import ExitStack
import numpy as np
import concourse.bass as bass
import concourse.mybir as mybir
from concourse.alu_op_type import AluOpType as ALU
from concourse.bass_utils import run_bass_kernel_spmd

F32 = mybir.dt.float32
BF16 = mybir.dt.bfloat16
AF = mybir.ActivationFunctionType

D = 1024
SEQ = 2048
NMETA = 16
L = SEQ + NMETA
DEPTH = 4
DFF = 4096
EPS = 1e-5
NT = 17
TILES = [(0, 16)] + [(16 + 128 * i, 128) for i in range(16)]
GROUPS = [(0, 16, [0])] + [(16 + 512 * g, 512, [1 + 4 * g + j for j in range(4)]) for g in range(4)]


class Dep:
    __slots__ = ("w", "r", "excl", "tw", "tr")

    def __init__(self, excl=False):
        self.w = None
        self.r = {}
        self.excl = excl
        self.tw = 0.0
        self.tr = 0.0


class Sched:
    N_DMA_SEMS = 12

    def __init__(self, nc):
        self.nc = nc
        self.eng = {"pe": nc.tensor, "dve": nc.vector, "act": nc.scalar, "pool": nc.gpsimd, "sp": nc.sync}
        self.sems = {}
        self.count = {}
        self.known = {e: {} for e in self.eng}
        for e in self.eng:
            self.sems[e] = nc.alloc_semaphore("s_" + e)
            self.count[e] = 0
        self.dma_sems = {}
        self.dma_i = {}
        for q in ("sp", "pool"):
            self.dma_sems[q] = [nc.alloc_semaphore(f"d_{q}{i}") for i in range(self.N_DMA_SEMS)]
            self.dma_i[q] = 0
        self.n_inst = 0
        self.clock = {}

    def _sem(self, key):
        if isinstance(key, str):
            return self.sems[key]
        q, i = key
        return self.dma_sems[q][i]

    def _wait(self, e, key, val):
        if self.known[e].get(key, 0) >= val:
            return
        self.eng[e].wait_ge(self._sem(key), val)
        self.known[e][key] = val

    def _collect(self, e, reads, writes):
        need = {}

        def add(k, v):
            if need.get(k, 0) < v:
                need[k] = v
        for d in reads:
            if d.w is not None:
                add(*d.w)
            if d.excl:
                for k, v in d.r.items():
                    if k != e:
                        add(k, v)
        for d in writes:
            if d.w is not None and d.w[0] != e:
                add(*d.w)
            for k, v in d.r.items():
                if k != e:
                    add(k, v)
        for k, v in need.items():
            self._wait(e, k, v)

    def _mark(self, tok, reads, writes):
        k, v = tok
        for d in reads:
            if d.r.get(k, 0) < v:
                d.r[k] = v
        for d in writes:
            d.w = tok
            d.r = {}

    COST = {"pe": 0.12, "dve": 0.38, "act": 0.36, "pool": 0.5, "sp": 0.3}
    LAT = 0.3

    class _Rec:
        out = None

        def __getattr__(self, name):
            def f(*a, **k):
                self.out = k.get("out", a[0] if a else None)
                return self
            return f

    def _cost(self, e, fn):
        try:
            r = Sched._Rec()
            fn(r)
            n = float(r.out.free_size())
        except Exception:
            return self.COST[e]
        if e == "pe":
            return 0.06 + 0.0004 * n
        if e == "dve":
            return 0.09 + n / 960.0
        if e == "act":
            return 0.2 + n / 1400.0
        if e == "pool":
            return 0.25 + n / 600.0
        return self.COST[e]

    def _est(self, e, reads, writes):
        ready = 0.0
        for d in reads:
            if d.tw > ready:
                ready = d.tw
            if d.excl and d.tr > ready:
                ready = d.tr
        for d in writes:
            if d.tw > ready:
                ready = d.tw
            if d.tr > ready:
                ready = d.tr
        return max(ready + self.LAT, self.clock.get(e, 0.0))

    def _advance(self, e, reads, writes, cost):
        fin = self._est(e, reads, writes) + cost
        self.clock[e] = fin
        for d in reads:
            if d.tr < fin:
                d.tr = fin
        for d in writes:
            d.tw = fin
            d.tr = 0.0

    def op(self, e, fn, reads=(), writes=()):
        cost = self._cost(e, fn) if IL.in_run else self.COST[e]
        IL.arbitrate(lambda: self._est(e, reads, writes))
        self._advance(e, reads, writes, cost)
        self._collect(e, reads, writes)
        inst = fn(self.eng[e])
        self.count[e] += 1
        inst.then_inc(self.sems[e], 1)
        self._mark((e, self.count[e]), reads, writes)
        self.n_inst += 1
        return inst

    def dma(self, q, out, in_, reads=(), writes=(), **kw):
        IL.arbitrate(lambda: self._est(q, reads, writes))
        est = self._est(q, reads, writes)
        self.clock[q] = est + 0.3
        fin = est + 20.0
        for d in reads:
            if d.tr < fin:
                d.tr = fin
        for d in writes:
            d.tw = fin
            d.tr = 0.0
        i = self.dma_i[q]
        self.dma_i[q] += 1
        slot = i % self.N_DMA_SEMS
        rnd = i // self.N_DMA_SEMS
        key = (q, slot)
        if rnd > 0:
            self._wait(q, key, 16 * rnd)
        self._collect(q, reads, writes)
        inst = self.eng[q].dma_start(out=out, in_=in_, **kw)
        inst.then_inc(self.dma_sems[q][slot], 16)
        self._mark((key, 16 * (rnd + 1)), reads, writes)
        self.n_inst += 1
        return inst

    def barrier(self):
        for e in self.eng:
            for e2 in self.eng:
                if e2 != e and self.count[e2] > 0:
                    self._wait(e, e2, self.count[e2])
            for q in self.dma_sems:
                n = self.dma_i[q]
                for slot in range(min(n, self.N_DMA_SEMS)):
                    last_rnd = (n - 1 - slot) // self.N_DMA_SEMS
                    self._wait(e, (q, slot), 16 * (last_rnd + 1))


class Interleaver:
    def __init__(self):
        self.in_run = False
        self.local = threading.local()

    def run(self, fns, weights=None):
        fns = [f for f in fns if f is not None]
        if len(fns) == 1 or self.in_run:
            for f in fns:
                f()
            return
        n = len(fns)
        self.sems = [threading.Semaphore(0) for _ in range(n)]
        self.alive = [True] * n
        self.pending = [None] * n
        self.waiting = [None] * n
        self.cnt = [0] * n
        self.exc = None
        self.done = threading.Semaphore(0)

        def wrap(i, fn):
            self.sems[i].acquire()
            self.local.idx = i
            try:
                if self.exc is None:
                    fn()
            except BaseException as ex:
                if self.exc is None:
                    self.exc = ex
            self.alive[i] = False
            self.pending[i] = None
            self.waiting[i] = None
            self.local.idx = None
            self._dispatch(i, finished=True)
        ths = [threading.Thread(target=wrap, args=(i, f)) for i, f in enumerate(fns)]
        self.in_run = True
        for th in ths:
            th.start()
        self.sems[0].release()
        self.done.acquire()
        for th in ths:
            th.join()
        self.in_run = False
        if self.exc is not None:
            raise self.exc

    def _pick(self):
        n = len(self.sems)
        if self.exc is not None:
            for j in range(n):
                if self.alive[j]:
                    return j
            return None
        for j in range(n):
            if self.alive[j] and self.waiting[j] is not None and self.waiting[j]():
                return j
        for j in range(n):
            if self.alive[j] and self.pending[j] is None and self.waiting[j] is None:
                return j
        best, bt = None, None
        for j in range(n):
            if self.alive[j] and self.pending[j] is not None:
                tj = self.pending[j]()
                if bt is None or tj < bt:
                    best, bt = j, tj
        if best is None and any(self.alive):
            self.exc = self.exc or RuntimeError("interleaver deadlock")
            for j in range(n):
                if self.alive[j]:
                    return j
        return best

    def _dispatch(self, i, finished=False):
        j = self._pick()
        if j is None:
            if finished:
                self.done.release()
            return
        if j == i and not finished:
            return
        self.sems[j].release()
        if not finished:
            self.sems[i].acquire()

    def arbitrate(self, est):
        if not self.in_run:
            return
        i = getattr(self.local, "idx", None)
        if i is None:
            return
        if self.exc is not None:
            raise RuntimeError("sibling stream failed")
        self.cnt[i] += 1
        self.pending[i] = est
        self._dispatch(i)
        self.pending[i] = None
        if self.exc is not None:
            raise RuntimeError("sibling stream failed")

    def tick(self):
        self.arbitrate(lambda: 0.0)

    def wait(self, cond):
        if not self.in_run:
            assert cond()
            return
        i = self.local.idx
        while not cond():
            if self.exc is not None:
                raise RuntimeError("sibling stream failed")
            self.waiting[i] = cond
            self._dispatch(i)
            self.waiting[i] = None
        if self.exc is not None:
            raise RuntimeError("sibling stream failed")


IL = Interleaver()


def run_pipeline(ntiles, stageA, stageB, nA, nring, stageC=None, stagger=0, group=None, hook=None):
    assert nring >= nA + 1
    doneA = [False] * ntiles
    doneB = [False] * ntiles
    doneC = [False] * ntiles if stageC is not None else doneB
    hooked = {}

    base = 2 if stageC is not None else 1

    def a_stream(i):
        if i > 0 and stagger:
            IL.wait(lambda: IL.cnt[base + i - 1] >= stagger or not IL.alive[base + i - 1])
        for t in range(i, ntiles, nA):
            if group is not None:
                g = t // group
                if t % group == 0:
                    IL.wait(lambda: all(doneA[:t]))
                    hook(g)
                    hooked[g] = True
                else:
                    IL.wait(lambda: hooked.get(g, False))
            IL.wait(lambda: t - nring < 0 or doneC[t - nring])
            stageA(t, i, t % nring)
            doneA[t] = True

    def b_stream():
        for t in range(ntiles):
            IL.wait(lambda: doneA[t])
            stageB(t, t % nring)
            doneB[t] = True
    def c_stream():
        for t in range(ntiles):
            IL.wait(lambda: doneB[t])
            stageC(t, t % nring)
            doneC[t] = True
    IL.run([b_stream] + ([c_stream] if stageC is not None else []) + [(lambda i=i: a_stream(i)) for i in range(nA)])


class T:
    def __init__(self, ap, excl=False):
        self.ap = ap
        self.deps = {}
        self.excl = excl

    def d(self, key=None):
        if key not in self.deps:
            self.deps[key] = Dep(self.excl)
        return self.deps[key]


def pack_rows(w):
    K, Fd = w.shape
    kc = K // 128
    return np.ascontiguousarray(w.reshape(kc, 128, Fd).transpose(1, 0, 2).reshape(128, kc * Fd))


def vec_cols(v):
    return np.ascontiguousarray(v.reshape(-1, 128).T)


class Builder:
    def __init__(self, cfg):
        self.cfg = cfg
        self.nc = bass.Bass("TRN2", target_bir_lowering=False)
        self.S = Sched(self.nc)
        self.dram = {}

    def din(self, name, shape):
        self.dram[name] = self.nc.dram_tensor(name, list(shape), F32, kind="ExternalInput").ap()
        return self.dram[name]

    def sb(self, es, name, shape, dt):
        self.uid = getattr(self, "uid", 0) + 1
        return T(es.enter_context(self.nc.sbuf_tensor(f"sb{self.uid}_{name}", list(shape), dt))[:])

    def mm(self, out, lhsT, rhs, start, stop, reads, writes):
        return self.S.op("pe", lambda e: e.matmul(out, lhsT=lhsT, rhs=rhs, start=start, stop=stop),
                         reads=reads, writes=writes)

    def tr(self, out, in_, n, reads, writes):
        idn = self.ident
        return self.S.op("pe", lambda e: e.transpose(out, in_, idn.ap[:n, :n]),
                         reads=list(reads) + [idn.d()], writes=writes)

    def build(self):
        nc, S, cfg = self.nc, self.S, self.cfg
        x_d = self.din("x", [SEQ, D])
        meta_d = self.din("meta", [NMETA, D])
        pv_d = self.din("pvec", [128, cfg["npv"]])
        rv_d = self.din("rvec", [1, cfg["nrv"]])
        cst_d = self.din("cst", [128, cfg["ncst"]])
        mlp_in_d = mlp_out_d = None
        if cfg["mlps"]:
            mlp_in_d = self.din("mlp_in", [DEPTH * 8, 128, 8 * 512])
            mlp_out_d = self.din("mlp_out", [DEPTH * 8, 128, 4 * 1024])
        if 0 in cfg["mixers"]:
            self.din("rw_cst", [128, 512])
            self.din("rw_lora", [128, 8 * 288])
            self.din("rw_pair", [8, 128, 8 * 384])
            self.din("rw_wo", [8, 128, 1024])
            self.din("rw_bwa", [8, 128, 256])
            self.din("rw_bg", [8, 128, 256])
        if 1 in cfg["mixers"]:
            self.din("ssd_in", [8, 128, 8 * 772])
            self.din("ssd_out", [8, 128, 2 * 1024])
            self.din("ssd_cst", [128, 512])
        if 2 in cfg["mixers"]:
            self.din("gla_in", [4, 128, 8 * 784])
            self.din("gla_out", [4, 128, 2 * 1024])
            self.din("gla_gup", [1, 16, 512])
        if 3 in cfg["mixers"]:
            self.din("ret_in", [4, 128, 8 * 1536])
            self.din("ret_out", [4, 128, 4 * 1024])
            self.din("ret_cst", [128, cfg["nretc"]])
        out_d = nc.dram_tensor("out", [SEQ, D], F32, kind="ExternalOutput").ap()
        dbg_d = None
        if cfg.get("debug"):
            dbg_d = nc.dram_tensor("dbg", [DEPTH * 2, L, D], F32, kind="ExternalOutput").ap()

        with ExitStack() as es:
            self.h = self.sb(es, "h", [128, NT, D], F32)
            self.uT = self.sb(es, "uT", [128, 8, L], BF16)
            self.pv = self.sb(es, "pv", [128, cfg["npv"]], F32)
            self.cst = self.sb(es, "cst", [128, cfg["ncst"]], F32)
            self.ident = T(self.cst.ap[:, 0:128])
            self.ident.deps = self.cst.deps
            self.stat = self.sb(es, "stat", [128, 8], F32)
            self.ps = [T(es.enter_context(nc.psum_tensor(f"ps{i}", [128, 512], F32))[:], excl=True) for i in range(8)]
            h, uT = self.h, self.uT

            S.dma("sp", self.cst.ap, cst_d, writes=[self.cst.d()])
            S.dma("sp", self.pv.ap, pv_d, writes=[self.pv.d()])
            S.dma("sp", h.ap[0:16, 0, :], meta_d, writes=[h.d(0)])
            for g in range(4):
                S.dma("sp", h.ap[:, 1 + 4 * g:5 + 4 * g, :],
                      x_d[512 * g:512 * (g + 1), :].rearrange("(t p) d -> p t d", p=128),
                      writes=[h.d(1 + 4 * g + j) for j in range(4)])

            for layer in range(DEPTH):
                if layer in cfg["mixers"]:
                    nw = self.pv.ap[:, cfg["pv_norm_mix"] + 8 * layer: cfg["pv_norm_mix"] + 8 * layer + 8]
                    if layer % 4 in (2, 3):
                        [None, None, self.emit_gla, self.emit_ret][layer % 4](layer // 4, nw)
                    else:
                        self.emit_norm(nw)
                        [self.emit_rwkv, self.emit_ssd][layer % 4](layer // 4)
                    S.barrier()
                if dbg_d is not None:
                    self.emit_dump(dbg_d[2 * layer])
                if layer in cfg["mlps"]:
                    self.emit_mlp(layer, mlp_in_d, mlp_out_d, self.pv.ap[:, cfg["pv_norm_mlp"] + 8 * layer: cfg["pv_norm_mlp"] + 8 * layer + 8])
                    S.barrier()
                if dbg_d is not None:
                    self.emit_dump(dbg_d[2 * layer + 1])

            self.emit_final(out_d)
        return nc

    def emit_dump(self, dst):
        S, h = self.S, self.h
        S.dma("sp", dst[0:16, :], h.ap[0:16, 0, :], reads=[h.d(0)])
        for t in range(1, NT):
            c0 = TILES[t][0]
            S.dma("sp", dst[c0:c0 + 128, :], h.ap[:, t, :], reads=[h.d(t)])

    def rstd_of(self, t, n):
        S, h, stat, junk = self.S, self.h, self.stat, self.junk
        S.op("dve", lambda e: e.scalar_tensor_tensor(out=junk.ap[:n, :], in0=h.ap[:n, t, :], scalar=1.0 / D,
                                                      in1=h.ap[:n, t, :], op0=ALU.mult, op1=ALU.mult,
                                                      accum_out=stat.ap[:n, 1:2]),
             reads=[h.d(t)], writes=[junk.d(), stat.d()])
        S.op("act", lambda e: e.activation(out=stat.ap[:n, 2:3], in_=stat.ap[:n, 1:2], func=AF.Sqrt, bias=EPS),
             reads=[stat.d()], writes=[stat.d()])
        S.op("dve", lambda e: e.reciprocal(out=stat.ap[:n, 0:1], in_=stat.ap[:n, 2:3]),
             reads=[stat.d()], writes=[stat.d()])

    def emit_norm(self, wcols, ns=3):
        S, h, uT = self.S, self.h, self.uT
        es = ExitStack()
        uns = [self.sb(es, f"un{i}", [128, D], F32) for i in range(ns)]
        stats = [self.sb(es, f"nst{i}", [128, 8], F32) for i in range(ns)]

        def stream(i):
            un, stat = uns[i], stats[i]
            for t in range(i, NT, ns):
                c0, n = TILES[t]
                gi = 0 if t == 0 else 1 + (t - 1) // 4
                S.op("dve", lambda e: e.scalar_tensor_tensor(out=un.ap[:n, :], in0=h.ap[:n, t, :], scalar=1.0 / D,
                                                              in1=h.ap[:n, t, :], op0=ALU.mult, op1=ALU.mult,
                                                              accum_out=stat.ap[:n, 1:2]),
                     reads=[h.d(t)], writes=[un.d(), stat.d()])
                S.op("act", lambda e: e.activation(out=stat.ap[:n, 2:3], in_=stat.ap[:n, 1:2], func=AF.Sqrt, bias=EPS),
                     reads=[stat.d()], writes=[stat.d()])
                S.op("dve", lambda e: e.reciprocal(out=stat.ap[:n, 0:1], in_=stat.ap[:n, 2:3]),
                     reads=[stat.d()], writes=[stat.d()])
                S.op("act", lambda e: e.activation(out=un.ap[:n, :], in_=h.ap[:n, t, :], func=AF.Copy,
                                                   scale=stat.ap[:n, 0:1]),
                     reads=[h.d(t), stat.d()], writes=[un.d()])
                for half in range(2):
                    pb = self.ps[2 * i + half]
                    pv3 = pb.ap.rearrange("p (j n) -> p j n", j=4)
                    for j in range(4):
                        dc = half * 4 + j
                        self.tr(pv3[:, j, :n], un.ap[:n, dc * 128:(dc + 1) * 128], n, reads=[un.d()], writes=[pb.d()])
                    S.op("dve", lambda e: e.tensor_tensor(out=uT.ap[:, half * 4:half * 4 + 4, c0:c0 + n],
                                                          in0=pv3[:, :, :n],
                                                          in1=wcols[:, half * 4:half * 4 + 4].unsqueeze(2).to_broadcast([128, 4, n]),
                                                          op=ALU.mult),
                         reads=[pb.d(), self.pv.d()], writes=[uT.d(gi)])
        IL.run([(lambda i=i: stream(i)) for i in range(ns)])
        S.barrier()
        es.close()

    def load_rv(self, es, off, n):
        t = self.sb(es, "rv", [128, n], F32)
        self.S.dma("sp", t.ap, self.dram["rvec"][:, off:off + n].partition_broadcast(128), writes=[t.d()])
        return t

    def emit_final(self, out_d):
        S, h, stat, cfg = self.S, self.h, self.stat, self.cfg
        es = ExitStack()
        self.junk = self.sb(es, "junk", [128, D], F32)
        self.rv = self.load_rv(es, cfg["rv_norm_final"], D)
        wb = self.rv.ap[:, 0:D]
        for t in range(1, NT):
            c0, n = TILES[t]
            self.rstd_of(t, n)
            S.op("dve", lambda e: e.scalar_tensor_tensor(out=h.ap[:, t, :], in0=h.ap[:, t, :], scalar=stat.ap[:, 0:1],
                                                          in1=wb, op0=ALU.mult, op1=ALU.mult),
                 reads=[h.d(t), stat.d(), self.rv.d()], writes=[h.d(t)])
            S.dma("sp", out_d[c0 - 16:c0 - 16 + 128, :], h.ap[:, t, :], reads=[h.d(t)])
        S.barrier()
        es.close()

    def emit_mlp(self, layer, mlp_in_d, mlp_out_d, norm_w):
        nc, S, h, uT = self.nc, self.S, self.h, self.uT
        with ExitStack() as es:
            win = [self.sb(es, f"win{i}", [128, 8, 512], BF16) for i in range(2)]
            wout = [self.sb(es, f"wout{i}", [128, 4, 1024], BF16) for i in range(2)]
            hT = [self.sb(es, f"hT{i}", [128, 4, 512], BF16) for i in range(2)]
            rl = [self.sb(es, f"rl{i}", [128, 512], F32) for i in range(2)]

            def load_in(fb):
                S.dma("pool", win[fb % 2].ap.rearrange("p c f -> p (c f)"), mlp_in_d[layer * 8 + fb], writes=[win[fb % 2].d()])

            def load_out(fb):
                S.dma("pool", wout[fb % 2].ap.rearrange("p c f -> p (c f)"), mlp_out_d[layer * 8 + fb], writes=[wout[fb % 2].d()])
            for fb in range(2):
                load_in(fb)
                load_out(fb)
            self.emit_norm(norm_w)
            NG = len(GROUPS)

            def hidden(u, ai, slot):
                fb, gi = divmod(u, NG)
                g0, gn, gtiles = GROUPS[gi]
                wi, hb = win[fb % 2], hT[slot]
                if gi == 0 and 1 <= fb < 7:
                    load_in(fb + 1)
                for fc in range(4):
                    ph = self.ps[fc % 2]
                    for dc in range(8):
                        self.mm(ph.ap[:, :gn], wi.ap[:, dc, fc * 128:(fc + 1) * 128], uT.ap[:, dc, g0:g0 + gn],
                                dc == 0, dc == 7, reads=[wi.d(), uT.d(gi)], writes=[ph.d()])
                    r = rl[fc % 2]
                    S.op("act", lambda e: e.activation(out=r.ap[:, :gn], in_=ph.ap[:, :gn], func=AF.Relu),
                         reads=[ph.d()], writes=[r.d()])
                    S.op("dve", lambda e: e.tensor_tensor(out=hb.ap[:, fc, :gn], in0=r.ap[:, :gn], in1=r.ap[:, :gn],
                                                          op=ALU.mult),
                         reads=[r.d()], writes=[hb.d()])

            def output(u, slot):
                fb, gi = divmod(u, NG)
                g0, gn, gtiles = GROUPS[gi]
                wo, hb = wout[fb % 2], hT[slot]
                if gi == 0 and 1 <= fb < 7:
                    load_out(fb + 1)
                for ti, t in enumerate(gtiles):
                    n = TILES[t][1]
                    for half in range(2):
                        po = self.ps[2 + (ti * 2 + half) % 4]
                        for fc in range(4):
                            self.mm(po.ap[:n, :], hb.ap[:, fc, ti * 128:ti * 128 + n],
                                    wo.ap[:, fc, half * 512:(half + 1) * 512], fc == 0, fc == 3,
                                    reads=[hb.d(), wo.d()], writes=[po.d()])
                        S.op("dve", lambda e: e.tensor_tensor(out=h.ap[:n, t, half * 512:(half + 1) * 512],
                                                              in0=h.ap[:n, t, half * 512:(half + 1) * 512],
                                                              in1=po.ap[:n, :], op=ALU.add),
                             reads=[po.d(), h.d(t)], writes=[h.d(t)])
            run_pipeline(8 * NG, hidden, output, 1, 2)

    def emit_rwkv(self, j):
        nc, S, h, uT, cfg = self.nc, self.S, self.h, self.uT, self.cfg
        dr = self.dram
        C0 = 0.6065306597126334
        GN_EPS = 64e-5
        pvb = cfg["pv_rwkv"]
        pvc = lambda idx, c: self.pv.ap[:, pvb + idx * 8 + c:pvb + idx * 8 + c + 1]
        mu = lambda i: self.pv.ap[:, pvb + i * 8:pvb + i * 8 + 8]
        I_W0, I_A0, I_KK, I_KA, I_RK, I_LNW, I_LNB = 6, 7, 8, 9, 10, 11, 12
        ident = self.cst.ap[:, 0:128]
        ps = self.ps
        RG = [(0, 16, [0])] + [(16 + 256 * g, 256, [1 + 2 * g, 2 + 2 * g]) for g in range(8)]
        NA, NR = 3, 5
        with ExitStack() as es:
            hid = self.sb(es, "hid", [128, 3, L], BF16)
            rc = self.sb(es, "rwc", [128, 512], F32)
            S.dma("sp", rc.ap, dr["rw_cst"], writes=[rc.d()])
            rc3 = rc.ap.rearrange("p (w i) -> p w i", w=4)
            blk = rc.ap[:, 384:512]
            hb = self.sb(es, "hb", [128, 16], F32)
            S.op("pool", lambda e: e.memset(hid.ap[:, 2, :], 0.0), writes=[hid.d()])
            S.op("dve", lambda e: e.tensor_scalar(out=hb.ap, in0=self.pv.ap[:, pvb + I_W0 * 8:pvb + I_W0 * 8 + 16], scalar1=-1.0, scalar2=None, op0=ALU.mult),
                 reads=[self.pv.d()], writes=[hb.d()])

            def shift_diff(dst, c0, n, wr):
                a = 0
                if c0 == 0:
                    S.op("dve", lambda e: e.tensor_scalar(out=dst.ap[:, :, 0:1], in0=uT.ap[:, :, 0:1], scalar1=-1.0, scalar2=None, op0=ALU.mult),
                         reads=[uT.d(0)], writes=wr)
                    a = 1
                if n > a:
                    S.op("dve", lambda e: e.tensor_tensor(out=dst.ap[:, :, a:n], in0=uT.ap[:, :, c0 + a - 1:c0 + n - 1], in1=uT.ap[:, :, c0 + a:c0 + n],
                                                          op=ALU.subtract),
                         reads=[uT.d(g) for g in range(5)], writes=wr)
            with ExitStack() as es2:
                lw_p = self.sb(es2, "lwp", [128, 8, 288], BF16)
                lw_m = self.sb(es2, "lwm", [128, 8, 288], BF16)
                stg = self.sb(es2, "lstg", [128, 8, 288], F32)
                xxg = self.sb(es2, "xxg", [128, 8, 512], BF16)
                S.dma("pool", lw_p.ap.rearrange("p c f -> p (c f)"), dr["rw_lora"], writes=[lw_p.d()])
                S.dma("sp", stg.ap.rearrange("p c f -> p (c f)"), dr["rw_lora"], writes=[stg.d()])
                for (c0_, c1_, mi) in ((0, 64, 1), (64, 128, 4), (128, 288, 5)):
                    S.op("dve", lambda e: e.tensor_tensor(out=lw_m.ap[:, :, c0_:c1_], in0=stg.ap[:, :, c0_:c1_],
                                                          in1=mu(mi).unsqueeze(2).to_broadcast([128, 8, c1_ - c0_]), op=ALU.mult),
                         reads=[stg.d(), self.pv.d()], writes=[lw_m.d()])
                for gi, (g0, gn, gtiles) in enumerate(GROUPS):
                    shift_diff(xxg, g0, gn, [xxg.d()])
                    for a, (o0, o1) in enumerate(((0, 128), (128, 256), (256, 288))):
                        m = o1 - o0
                        for dc in range(8):
                            self.mm(ps[a].ap[:m, :gn], lw_p.ap[:, dc, o0:o1], uT.ap[:, dc, g0:g0 + gn], dc == 0, False,
                                    reads=[lw_p.d(), uT.d(gi)], writes=[ps[a].d()])
                        for dc in range(8):
                            self.mm(ps[a].ap[:m, :gn], lw_m.ap[:, dc, o0:o1], xxg.ap[:, dc, 0:gn], False, dc == 7,
                                    reads=[lw_m.d(), xxg.d()], writes=[ps[a].d()])
                    S.op("act", lambda e: e.activation(out=hid.ap[0:64, 0, g0:g0 + gn], in_=ps[0].ap[0:64, :gn], func=AF.Tanh),
                         reads=[ps[0].d()], writes=[hid.d()])
                    S.op("act", lambda e: e.activation(out=hid.ap[64:128, 0, g0:g0 + gn], in_=ps[0].ap[64:128, :gn], func=AF.Copy),
                         reads=[ps[0].d()], writes=[hid.d()])
                    S.op("act", lambda e: e.activation(out=hid.ap[:, 1, g0:g0 + gn], in_=ps[1].ap[:, :gn], func=AF.Tanh, scale=0.5),
                         reads=[ps[1].d()], writes=[hid.d()])
                    S.op("act", lambda e: e.activation(out=hid.ap[0:32, 2, g0:g0 + gn], in_=ps[2].ap[0:32, :gn], func=AF.Tanh, scale=0.5),
                         reads=[ps[2].d()], writes=[hid.d()])
                    S.op("dve", lambda e: e.tensor_scalar(out=hid.ap[:, 1, g0:g0 + gn], in0=hid.ap[:, 1, g0:g0 + gn], scalar1=0.5, scalar2=0.5,
                                                          op0=ALU.mult, op1=ALU.add), reads=[hid.d()], writes=[hid.d()])
                    S.op("dve", lambda e: e.tensor_scalar(out=hid.ap[0:32, 2, g0:g0 + gn], in0=hid.ap[0:32, 2, g0:g0 + gn], scalar1=0.5, scalar2=0.5,
                                                          op0=ALU.mult, op1=ALU.add), reads=[hid.d()], writes=[hid.d()])
                S.barrier()
            wp_p = self.sb(es, "wpp", [128, 8, 384], BF16)
            wp_m = self.sb(es, "wpm", [128, 8, 384], BF16)
            stg = self.sb(es, "wstg", [128, 4, 128], F32)
            wos = [self.sb(es, f"rwo{i}", [128, 1024], BF16) for i in range(2)]
            Bwa = self.sb(es, "Bwa", [128, 256], BF16)
            Bg = self.sb(es, "Bg", [128, 256], BF16)
            M = self.sb(es, "wM", [128, 64], F32)
            Mb = self.sb(es, "wMb", [128, 64], BF16)
            yln = self.sb(es, "yln", [128, 128], F32)
            og = self.sb(es, "og", [128, 128], F32)
            ogb = self.sb(es, "ogb", [128, 128], BF16)
            st2 = self.sb(es, "wst2", [128, 24], F32)
            AP_ = []
            for ai in range(NA):
                Fd = {nm: self.sb(es, f"w{nm}{ai}", [128, 128], F32) for nm in
                      ("r", "k", "v", "sg", "a", "kk", "tmp", "kmod", "cum", "ec", "en", "ecx")}
                Fd["eh"], Fd["bh"], Fd["kh"] = Fd["ec"], Fd["en"], Fd["ecx"]
                Fd["bv"] = Fd["a"]
                AP_.append({"F": Fd, "bt": self.sb(es, f"bt{ai}", [128, 128], BF16), "kt": self.sb(es, f"kt{ai}", [128, 128], BF16),
                            "A1": self.sb(es, f"A1{ai}", [128, 4, 128], BF16), "X2": self.sb(es, f"X2{ai}", [128, 4, 128], BF16),
                            "xx": self.sb(es, f"xx{ai}", [128, 8, 128], BF16), "bx": ps[2 + 2 * ai], "by": ps[3 + 2 * ai]})
            PB = []
            for par in range(NR):
                pb = {"at0": self.sb(es, f"at0{par}", [128, 128], BF16), "at1": self.sb(es, f"at1{par}", [128, 128], BF16),
                      "rt0": self.sb(es, f"rt0{par}", [128, 128], BF16), "rt1": self.sb(es, f"rt1{par}", [128, 128], BF16),
                      "bon": self.sb(es, f"bon{par}", [128, 128], F32), "g": self.sb(es, f"g{par}", [128, 128], F32),
                      "st": self.sb(es, f"st{par}", [128, 8], F32),
                      "tok0": self.sb(es, f"tok0{par}", [128, 3, 128], BF16), "tok1": self.sb(es, f"tok1{par}", [128, 3, 128], BF16),
                      "P0": self.sb(es, f"P0{par}", [128, 128], BF16), "P1": self.sb(es, f"P1{par}", [128, 128], BF16),
                      "U0": self.sb(es, f"U0{par}", [128, 128], BF16), "U1": self.sb(es, f"U1{par}", [128, 128], BF16),
                      "A2": self.sb(es, f"A2{par}", [128, 4, 128], BF16), "A3": self.sb(es, f"A3{par}", [128, 2, 128], BF16),
                      "TT": self.sb(es, f"TT{par}", [128, 4, 128], BF16), "Ysb": self.sb(es, f"Ysb{par}", [128, 128], F32)}
                for nm in ("at0", "at1", "rt0", "rt1", "tok0", "tok1", "P0", "P1", "U0", "U1"):
                    S.op("pool", lambda e: e.memset(pb[nm].ap, 0.0), writes=[pb[nm].d()])
                PB.append(pb)
            ones = self.cst.ap[:, 384:512]

            def stageA(hp, t, ai, B_):
                c0, n = TILES[t]
                ugi = 0 if t == 0 else 1 + (t - 1) // 4
                gs = slice(c0, c0 + n)
                P_ = AP_[ai]
                F, bx, by, A1, X2, xx = P_["F"], P_["bx"], P_["by"], P_["A1"], P_["X2"], P_["xx"]
                btk = {"bt": P_["bt"], "kt": P_["kt"]}
                A = lambda nm: F[nm].ap[:, :n]
                D_ = lambda *nms: [F[n_].d() for n_ in nms]
                shift_diff(xx, c0, n, [xx.d()])
                for i in range(3):
                    for dc in range(8):
                        self.mm(bx.ap[:, i * 128:i * 128 + n], wp_p.ap[:, dc, i * 128:(i + 1) * 128], uT.ap[:, dc, gs], dc == 0, False,
                                reads=[wp_p.d(), uT.d(ugi)], writes=[bx.d()])
                    for dc in range(8):
                        self.mm(bx.ap[:, i * 128:i * 128 + n], wp_m.ap[:, dc, i * 128:(i + 1) * 128], xx.ap[:, dc, 0:n], False, dc == 7,
                                reads=[wp_m.d(), xx.d()], writes=[bx.d()])
                self.mm(bx.ap[:, 384:384 + n], Bg.ap[:, 0:128], hid.ap[:, 1, gs], True, False, reads=[Bg.d(), hid.d()], writes=[bx.d()])
                self.mm(bx.ap[:, 384:384 + n], Bg.ap[:, 128:256], hid.ap[:, 2, gs], False, True, reads=[Bg.d(), hid.d()], writes=[bx.d()])
                self.mm(by.ap[:, 0:n], Bwa.ap[:, 0:128], hid.ap[:, 0, gs], True, True, reads=[Bwa.d(), hid.d()], writes=[by.d()])
                self.mm(by.ap[:, 128:128 + n], Bwa.ap[:, 128:256], hid.ap[:, 0, gs], True, True, reads=[Bwa.d(), hid.d()], writes=[by.d()])
                S.op("act", lambda e: e.activation(out=A("r"), in_=bx.ap[:, 0:n], func=AF.Copy), reads=[bx.d()], writes=D_("r"))
                S.op("act", lambda e: e.activation(out=A("k"), in_=bx.ap[:, 128:128 + n], func=AF.Copy), reads=[bx.d()], writes=D_("k"))
                S.op("act", lambda e: e.activation(out=A("v"), in_=bx.ap[:, 256:256 + n], func=AF.Copy), reads=[bx.d()], writes=D_("v"))
                S.op("act", lambda e: e.activation(out=B_["g"].ap[:, :n], in_=bx.ap[:, 384:384 + n], func=AF.Copy), reads=[bx.d()], writes=[B_["g"].d()])
                for nm, off, hc in (("sg", 0, hp), ("a", 128, 8 + hp)):
                    S.op("act", lambda e: e.activation(out=A(nm), in_=by.ap[:, off:off + n], func=AF.Exp, scale=-1.0, bias=hb.ap[:, hc:hc + 1]),
                         reads=[by.d(), hb.d()], writes=D_(nm))
                    S.op("act", lambda e: e.activation(out=A(nm), in_=A(nm), func=AF.Ln, bias=1.0), reads=D_(nm), writes=D_(nm))
                    S.op("act", lambda e: e.activation(out=A(nm), in_=A(nm), func=AF.Exp, scale=-1.0), reads=D_(nm), writes=D_(nm))
                S.op("dve", lambda e: e.tensor_scalar(out=A("kk"), in0=A("k"), scalar1=pvc(I_KK, hp), scalar2=None, op0=ALU.mult),
                     reads=D_("k") + [self.pv.d()], writes=D_("kk"))
                S.op("pool", lambda e: e.tensor_tensor(out=A("tmp"), in0=A("kk"), in1=A("kk"), op=ALU.mult), reads=D_("kk"), writes=D_("tmp"))
                self.mm(by.ap[:, 256:256 + n], blk, A("tmp"), True, True, reads=[rc.d()] + D_("tmp"), writes=[by.d()])
                S.op("dve", lambda e: e.tensor_scalar(out=A("tmp"), in0=by.ap[:, 256:256 + n], scalar1=1e-24, scalar2=None, op0=ALU.max),
                     reads=[by.d()], writes=D_("tmp"))
                S.op("act", lambda e: e.activation(out=A("tmp"), in_=A("tmp"), func=AF.Ln), reads=D_("tmp"), writes=D_("tmp"))
                S.op("act", lambda e: e.activation(out=A("tmp"), in_=A("tmp"), func=AF.Exp, scale=-0.5), reads=D_("tmp"), writes=D_("tmp"))
                S.op("dve", lambda e: e.tensor_tensor(out=A("kk"), in0=A("kk"), in1=A("tmp"), op=ALU.mult), reads=D_("kk", "tmp"), writes=D_("kk"))
                S.op("dve", lambda e: e.tensor_scalar(out=A("tmp"), in0=A("a"), scalar1=-1.0, scalar2=pvc(I_KA, hp), op0=ALU.add, op1=ALU.mult),
                     reads=D_("a") + [self.pv.d()], writes=D_("tmp"))
                S.op("dve", lambda e: e.scalar_tensor_tensor(out=A("kmod"), in0=A("tmp"), scalar=1.0, in1=A("k"), op0=ALU.add, op1=ALU.mult),
                     reads=D_("tmp", "k"), writes=D_("kmod"))
                S.op("dve", lambda e: e.scalar_tensor_tensor(out=A("tmp"), in0=A("r"), scalar=pvc(I_RK, hp), in1=A("kmod"), op0=ALU.mult, op1=ALU.mult),
                     reads=D_("r", "kmod") + [self.pv.d()], writes=D_("tmp"))
                self.mm(by.ap[:, 384:384 + n], blk, A("tmp"), True, True, reads=[rc.d()] + D_("tmp"), writes=[by.d()])
                S.op("dve", lambda e: e.tensor_tensor(out=B_["bon"].ap[:, :n], in0=by.ap[:, 384:384 + n], in1=A("v"), op=ALU.mult),
                     reads=[by.d()] + D_("v"), writes=[B_["bon"].d()])
                S.op("pool", lambda e: e.tensor_tensor(out=A("bv"), in0=A("kk"), in1=A("a"), op=ALU.mult), reads=D_("kk", "a"), writes=D_("bv"))
                tchunks = ((0, 16),) if n == 16 else ((0, 64), (64, 64))
                for (o, m) in tchunks:
                    S.op("dve", lambda e: e.tensor_tensor_scan(out=F["cum"].ap[:, o:o + m], data0=ones[:, :m], data1=F["sg"].ap[:, o:o + m],
                                                               initial=0.0, op0=ALU.mult, op1=ALU.add),
                         reads=D_("sg") + [self.cst.d()], writes=D_("cum"))
                S.op("act", lambda e: e.activation(out=A("ec"), in_=A("cum"), func=AF.Exp, scale=-C0), reads=D_("cum"), writes=D_("ec"))
                S.op("act", lambda e: e.activation(out=A("en"), in_=A("cum"), func=AF.Exp, scale=C0), reads=D_("cum"), writes=D_("en"))
                S.op("dve", lambda e: e.tensor_tensor(out=A("ecx"), in0=A("cum"), in1=A("sg"), op=ALU.subtract), reads=D_("cum", "sg"), writes=D_("ecx"))
                S.op("act", lambda e: e.activation(out=A("ecx"), in_=A("ecx"), func=AF.Exp, scale=-C0), reads=D_("ecx"), writes=D_("ecx"))
                for hd in range(2):
                    hs = slice(hd * 64, hd * 64 + 64)
                    S.op("dve", lambda e: e.scalar_tensor_tensor(out=B_[f"at{hd}"].ap[hs, :n], in0=F["kk"].ap[hs, :n], scalar=-1.0,
                                                                  in1=F["ecx"].ap[hs, :n], op0=ALU.mult, op1=ALU.mult),
                         reads=D_("kk", "ecx"), writes=[B_[f"at{hd}"].d()])
                    S.op("dve", lambda e: e.tensor_tensor(out=B_[f"rt{hd}"].ap[hs, :n], in0=F["r"].ap[hs, :n], in1=F["ec"].ap[hs, :n], op=ALU.mult),
                         reads=D_("r", "ec"), writes=[B_[f"rt{hd}"].d()])
                S.op("pool", lambda e: e.tensor_tensor(out=btk["bt"].ap[:, :n], in0=A("bv"), in1=A("en"), op=ALU.mult), reads=D_("bv", "en"), writes=[btk["bt"].d()])
                S.op("pool", lambda e: e.tensor_tensor(out=btk["kt"].ap[:, :n], in0=A("kmod"), in1=A("en"), op=ALU.mult), reads=D_("kmod", "en"), writes=[btk["kt"].d()])
                stp = B_["st"]
                for ci, (o, m) in enumerate(tchunks):
                    last = F["cum"].ap[:, o + m - 1:o + m]
                    S.op("dve", lambda e: e.tensor_scalar(out=stp.ap[:, ci:ci + 1], in0=last, scalar1=-C0, scalar2=None, op0=ALU.mult),
                         reads=D_("cum"), writes=[stp.d()])
                    S.op("act", lambda e: e.activation(out=F["eh"].ap[:, o:o + m], in_=F["cum"].ap[:, o:o + m], func=AF.Exp, scale=C0,
                                                       bias=stp.ap[:, ci:ci + 1]),
                         reads=D_("cum") + [stp.d()], writes=D_("eh"))
                    S.op("act", lambda e: e.activation(out=stp.ap[:, 2 + ci:3 + ci], in_=last, func=AF.Exp, scale=-C0), reads=D_("cum"), writes=[stp.d()])
                S.op("pool", lambda e: e.tensor_tensor(out=A("bh"), in0=A("bv"), in1=A("eh"), op=ALU.mult), reads=D_("bv", "eh"), writes=D_("bh"))
                S.op("pool", lambda e: e.tensor_tensor(out=A("kh"), in0=A("kmod"), in1=A("eh"), op=ALU.mult), reads=D_("kmod", "eh"), writes=D_("kh"))
                for q, nm in enumerate(("v", "bh", "kh")):
                    self.tr(bx.ap[:n, q * 128:(q + 1) * 128], F[nm].ap[:, :n], 128, reads=D_(nm), writes=[bx.d()])
                for c, (o, m) in enumerate(tchunks):
                    tk = B_[f"tok{c}"]
                    S.op("act", lambda e: e.activation(out=tk.ap[o:o + m, :, :], in_=bx.ap[o:o + m, 0:384].rearrange("p (q c) -> p q c", q=3), func=AF.Copy),
                         reads=[bx.d()], writes=[tk.d()])
                bA = by.ap.rearrange("p (a i) -> p a i", a=4)
                bB = bx.ap.rearrange("p (a i) -> p a i", a=4)
                A2, A3, TT = B_["A2"], B_["A3"], B_["TT"]
                ops_ = []
                for hd in range(2):
                    at, bt, kt, rt = B_[f"at{hd}"].ap[:, :n], btk["bt"].ap[:, :n], btk["kt"].ap[:, :n], B_[f"rt{hd}"].ap[:, :n]
                    rd = [B_[f"at{hd}"].d(), btk["bt"].d(), btk["kt"].d(), B_[f"rt{hd}"].d()]
                    ops_.append((at, bt, kt, rt, rd))
                    self.mm(bA[:n, 2 * hd, :n], bt, at, True, True, reads=rd, writes=[by.d()])
                    self.mm(bA[:n, 2 * hd + 1, :n], at, bt, True, True, reads=rd, writes=[by.d()])
                    self.mm(bB[:n, hd, :n], kt, at, True, True, reads=rd, writes=[bx.d()])
                    self.mm(bB[:n, 2 + hd, :n], bt, rt, True, True, reads=rd, writes=[bx.d()])
                v4 = lambda ap3: ap3.rearrange("p (a b) i -> p a b i", a=2)
                S.op("dve", lambda e: e.tensor_tensor(out=v4(A1.ap)[:n, :, :, :n], in0=v4(bA)[:n, :, :, :n],
                                                      in1=rc3[:n, 0:2, :n].unsqueeze(1).to_broadcast([n, 2, 2, n]), op=ALU.mult),
                     reads=[by.d(), rc.d()], writes=[A1.d()])
                S.op("dve", lambda e: e.tensor_tensor(out=v4(A2.ap)[:n, :, :, :n], in0=v4(bB)[:n, :, :, :n],
                                                      in1=rc3[:n, 0:4:2, :n].unsqueeze(2).to_broadcast([n, 2, 2, n]), op=ALU.mult),
                     reads=[bx.d(), rc.d()], writes=[A2.d()])
                for hd in range(2):
                    at, bt, kt, rt, rd = ops_[hd]
                    self.mm(bA[:n, hd, :n], kt, rt, True, True, reads=rd, writes=[by.d()])
                S.op("dve", lambda e: e.tensor_tensor(out=A3.ap[:n, :, :n], in0=bA[:n, 0:2, :n],
                                                      in1=rc3[:n, 2:3, :n].to_broadcast([n, 2, n]), op=ALU.mult),
                     reads=[by.d(), rc.d()], writes=[A3.d()])
                S.op("dve", lambda e: e.tensor_tensor(out=TT.ap[:n, :, :n], in0=A1.ap[:n, :, :n],
                                                      in1=ident[:n, :n].unsqueeze(1).to_broadcast([n, 4, n]), op=ALU.add),
                     reads=[A1.d(), self.cst.d()], writes=[TT.d()])
                Xc = A1
                nlev = 5 if n == 128 else 3
                for lev in range(nlev):
                    pq, pq3 = bx, bB
                    for hd in range(2):
                        Xm, Ym = Xc.ap[:n, 2 * hd, :n], Xc.ap[:n, 2 * hd + 1, :n]
                        self.mm(pq3[:n, 2 * hd, :n], Ym, Xm, True, True, reads=[Xc.d()], writes=[pq.d()])
                        self.mm(pq3[:n, 2 * hd + 1, :n], Xm, Ym, True, True, reads=[Xc.d()], writes=[pq.d()])
                    S.op("act", lambda e: e.activation(out=X2.ap[:n, :, :n], in_=pq3[:n, :, :n], func=AF.Copy), reads=[pq.d()], writes=[X2.d()])
                    Xc = X2
                    pr, pr3 = by, bA
                    for hd in range(2):
                        T_ = TT.ap[:n, 2 * hd + 1, :n]
                        self.mm(pr3[:n, 2 * hd, :n], T_, X2.ap[:n, 2 * hd, :n], True, True, reads=[TT.d(), X2.d()], writes=[pr.d()])
                        self.mm(pr3[:n, 2 * hd + 1, :n], X2.ap[:n, 2 * hd, :n], T_, True, True, reads=[TT.d(), X2.d()], writes=[pr.d()])
                    S.op("dve", lambda e: e.tensor_tensor(out=TT.ap[:n, :, :n], in0=TT.ap[:n, :, :n], in1=pr3[:n, :, :n], op=ALU.add),
                         reads=[TT.d(), pr.d()], writes=[TT.d()])

            def stageB(hp, t, B_):
                c0, n = TILES[t]
                tl = slice(0, n)
                tchunks = ((0, 16),) if n == 16 else ((0, 64), (64, 64))
                A2, A3, TT, stp, Ysb = B_["A2"], B_["A3"], B_["TT"], B_["st"], B_["Ysb"]
                b0 = ps[0]
                if t == 0:
                    S.op("pool", lambda e: e.memset(M.ap, 0.0), writes=[M.d()])
                    S.op("pool", lambda e: e.memset(Mb.ap, 0.0), writes=[Mb.d()])
                for c, (o, m) in enumerate(tchunks):
                    cs = slice(o, o + m)
                    tk, Pc, Uc = B_[f"tok{c}"], B_[f"P{c}"], B_[f"U{c}"]
                    for hd in range(2):
                        hs = slice(hd * 64, hd * 64 + 64)
                        self.mm(b0.ap[:n, hs], B_[f"at{hd}"].ap[:, tl], Mb.ap[:, :], True, False, reads=[B_[f"at{hd}"].d(), Mb.d()], writes=[b0.d()])
                        self.mm(b0.ap[:n, hs], A2.ap[:n, hd, :n], tk.ap[:n, 0, hs], False, True, reads=[A2.d(), tk.d()], writes=[b0.d()])
                    S.op("act", lambda e: e.activation(out=Pc.ap[cs, :], in_=b0.ap[cs, 0:128], func=AF.Copy), reads=[b0.d()], writes=[Pc.d()])
                    for hd in range(2):
                        hs = slice(hd * 64, hd * 64 + 64)
                        self.mm(b0.ap[:n, 128 + hd * 64:128 + hd * 64 + 64], TT.ap[:n, 2 * hd, :n], Pc.ap[:n, hs], True, True, reads=[TT.d(), Pc.d()], writes=[b0.d()])
                    S.op("dve", lambda e: e.tensor_copy(out=Uc.ap[cs, :], in_=b0.ap[cs, 128:256]), reads=[b0.d()], writes=[Uc.d()])
                    self.mm(b0.ap[:, 384:512], tk.ap[:n, 1, :], Uc.ap[:n, :], True, False, reads=[tk.d(), Uc.d()], writes=[b0.d()])
                    self.mm(b0.ap[:, 384:512], tk.ap[:n, 2, :], tk.ap[:n, 0, :], False, True, reads=[tk.d()], writes=[b0.d()])
                    for hd in range(2):
                        hs = slice(256 + hd * 64, 256 + hd * 64 + 64)
                        hv = slice(hd * 64, hd * 64 + 64)
                        self.mm(b0.ap[:n, hs], B_[f"rt{hd}"].ap[:, tl], Mb.ap[:, :], True, False, reads=[B_[f"rt{hd}"].d(), Mb.d()], writes=[b0.d()])
                        self.mm(b0.ap[:n, hs], A2.ap[:n, 2 + hd, :n], Uc.ap[:n, hv], False, False, reads=[A2.d(), Uc.d()], writes=[b0.d()])
                        self.mm(b0.ap[:n, hs], A3.ap[:n, hd, :n], tk.ap[:n, 0, hv], False, True, reads=[A3.d(), tk.d()], writes=[b0.d()])
                    for hd in range(2):
                        hs = slice(hd * 64, hd * 64 + 64)
                        S.op("dve", lambda e: e.scalar_tensor_tensor(out=Mb.ap[hs, :], in0=M.ap[hs, :], scalar=stp.ap[hs, 2 + c:3 + c],
                                                                      in1=b0.ap[hs, 384 + hd * 64:384 + hd * 64 + 64], op0=ALU.mult, op1=ALU.add),
                             reads=[M.d(), stp.d(), b0.d()], writes=[Mb.d()])
                    for hd in range(2):
                        hs = slice(hd * 64, hd * 64 + 64)
                        S.op("dve", lambda e: e.scalar_tensor_tensor(out=M.ap[hs, :], in0=M.ap[hs, :], scalar=stp.ap[hs, 2 + c:3 + c],
                                                                      in1=b0.ap[hs, 384 + hd * 64:384 + hd * 64 + 64], op0=ALU.mult, op1=ALU.add),
                             reads=[M.d(), stp.d(), b0.d()], writes=[M.d()])
                    S.op("act", lambda e: e.activation(out=Ysb.ap[cs, :], in_=b0.ap[cs, 256:384], func=AF.Copy), reads=[b0.d()], writes=[Ysb.d()])

            def stageC(hp, t, B_):
                c0, n = TILES[t]
                Ysb = B_["Ysb"]
                b1 = ps[1]
                wo = wos[hp % 2]
                for hd in range(2):
                    hs = slice(hd * 64, hd * 64 + 64)
                    q0 = hd * 12
                    S.op("dve", lambda e: e.bn_stats(out=st2.ap[:n, q0:q0 + 6], in_=Ysb.ap[:n, hs]), reads=[Ysb.d()], writes=[st2.d()])
                    S.op("dve", lambda e: e.bn_aggr(out=st2.ap[:n, q0 + 6:q0 + 8], in_=st2.ap[:n, q0:q0 + 6]), reads=[st2.d()], writes=[st2.d()])
                    S.op("act", lambda e: e.activation(out=st2.ap[:n, q0 + 7:q0 + 8], in_=st2.ap[:n, q0 + 7:q0 + 8], func=AF.Ln, bias=GN_EPS),
                         reads=[st2.d()], writes=[st2.d()])
                    S.op("act", lambda e: e.activation(out=st2.ap[:n, q0 + 7:q0 + 8], in_=st2.ap[:n, q0 + 7:q0 + 8], func=AF.Exp, scale=-0.5),
                         reads=[st2.d()], writes=[st2.d()])
                    S.op("dve", lambda e: e.tensor_scalar(out=yln.ap[:n, hs], in0=Ysb.ap[:n, hs], scalar1=st2.ap[:n, q0 + 6:q0 + 7],
                                                          scalar2=st2.ap[:n, q0 + 7:q0 + 8], op0=ALU.subtract, op1=ALU.mult),
                         reads=[Ysb.d(), st2.d()], writes=[yln.d()])
                self.tr(b1.ap[:, 0:n], yln.ap[:n, :], n, reads=[yln.d()], writes=[b1.d()])
                S.op("dve", lambda e: e.tensor_scalar(out=og.ap[:, :n], in0=b1.ap[:, 0:n], scalar1=pvc(I_LNW, hp), scalar2=pvc(I_LNB, hp),
                                                      op0=ALU.mult, op1=ALU.add),
                     reads=[b1.d(), self.pv.d()], writes=[og.d()])
                S.op("pool", lambda e: e.tensor_tensor(out=og.ap[:, :n], in0=og.ap[:, :n], in1=B_["bon"].ap[:, :n], op=ALU.add),
                     reads=[og.d(), B_["bon"].d()], writes=[og.d()])
                S.op("pool", lambda e: e.tensor_tensor(out=ogb.ap[:, :n], in0=og.ap[:, :n], in1=B_["g"].ap[:, :n], op=ALU.mult),
                     reads=[og.d(), B_["g"].d()], writes=[ogb.d()])
                for half in range(2):
                    self.mm(b1.ap[:n, :], ogb.ap[:, :n], wo.ap[:, half * 512:(half + 1) * 512], True, True, reads=[ogb.d(), wo.d()], writes=[b1.d()])
                    S.op("dve", lambda e: e.tensor_tensor(out=h.ap[:n, t, half * 512:(half + 1) * 512],
                                                          in0=h.ap[:n, t, half * 512:(half + 1) * 512], in1=b1.ap[:n, :], op=ALU.add),
                         reads=[b1.d(), h.d(t)], writes=[h.d(t)])

            def hook(hp):
                S.dma("pool", wp_p.ap.rearrange("p c f -> p (c f)"), dr["rw_pair"][hp], writes=[wp_p.d()])
                for i, mi in enumerate((0, 2, 3)):
                    for hf in range(2):
                        S.dma("sp", stg.ap, dr["rw_pair"][hp].rearrange("p (c f) -> p c f", c=8)[:, 4 * hf:4 * hf + 4, i * 128:(i + 1) * 128], writes=[stg.d()])
                        S.op("dve", lambda e: e.tensor_tensor(out=wp_m.ap[:, 4 * hf:4 * hf + 4, i * 128:(i + 1) * 128], in0=stg.ap,
                                                              in1=mu(mi)[:, 4 * hf:4 * hf + 4].unsqueeze(2).to_broadcast([128, 4, 128]), op=ALU.mult),
                             reads=[stg.d(), self.pv.d()], writes=[wp_m.d()])
                S.dma("pool", wos[hp % 2].ap, dr["rw_wo"][hp], writes=[wos[hp % 2].d()])
                S.dma("pool", Bwa.ap, dr["rw_bwa"][hp], writes=[Bwa.d()])
                S.dma("pool", Bg.ap, dr["rw_bg"][hp], writes=[Bg.d()])
            run_pipeline(8 * NT, lambda u, ai, slot: stageA(u // NT, u % NT, ai, PB[slot]), lambda u, slot: stageB(u // NT, u % NT, PB[slot]),
                         NA, NR, stageC=lambda u, slot: stageC(u // NT, u % NT, PB[slot]), stagger=cfg.get("rw_stagger", 0),
                         group=NT, hook=hook)

    def emit_ssd(self, j):
        nc, S, h, uT, cfg = self.nc, self.S, self.h, self.uT, self.cfg
        sin_d, sout_d, scst_d = self.dram["ssd_in"], self.dram["ssd_out"], self.dram["ssd_cst"]
        ident = self.cst.ap[:, 0:128]
        tri = self.cst.ap[:, 128:256]
        ones = self.cst.ap[:, 384:512]
        pvb = cfg["pv_ssd_conv"]
        cw = lambda jj, ch: self.pv.ap[:, pvb + jj * 32 + ch:pvb + jj * 32 + ch + 1]
        cb = lambda ch: self.pv.ap[:, pvb + 128 + ch:pvb + 128 + ch + 1]
        rvA = 0
        NA, NR = 3, 4
        ps = self.ps
        v3 = lambda ap: ap.rearrange("p (h i) -> p h i", h=4)
        with ExitStack() as es:
            self.rv = self.load_rv(es, cfg["rv_ssd"], 96 + 2048)
            self.junk = self.sb(es, "junk", [128, 256], F32)
            negm = self.sb(es, "snegm", [128, 4, 128], F32)
            S.dma("sp", negm.ap.rearrange("p h i -> p (h i)"), scst_d, writes=[negm.d()])
            win = self.sb(es, "swin", [128, 8, 772], BF16)
            wos = [self.sb(es, f"swo{i}", [128, 2, 1024], BF16) for i in range(2)]
            Aneg = self.sb(es, "sA", [128, 32], F32)
            M = self.sb(es, "sM", [128, 256], F32)
            Mb = self.sb(es, "sMb", [128, 256], BF16)
            yn = self.sb(es, "syn", [128, 256], F32)
            ygT = self.sb(es, "sygT", [128, 2, 128], BF16)
            st = self.sb(es, "sst", [128, 8], F32)
            AP_ = []
            for ai in range(NA):
                d = {"pc": self.sb(es, f"spc{ai}", [128, 4, 131], F32), "acc": self.sb(es, f"sacc{ai}", [128, 4, 128], F32),
                     "a4": self.sb(es, f"sa4{ai}", [128, 4, 128], F32), "sig": self.sb(es, f"ssig{ai}", [128, 4, 128], F32),
                     "BTb": self.sb(es, f"sBTb{ai}", [128, 128], BF16),
                     "CTb": self.sb(es, f"sCTb{ai}", [128, 128], BF16), "xs": self.sb(es, f"sxs{ai}", [128, 256], F32),
                     "dt": self.sb(es, f"sdt{ai}", [128, 32], F32), "trila": self.sb(es, f"strila{ai}", [128, 4, 128], F32),
                     "dec": self.sb(es, f"sdec{ai}", [128, 4, 128], F32), "ecr": self.sb(es, f"secr{ai}", [128, 4, 128], F32),
                     "bx": ps[2 + 2 * ai], "by": ps[3 + 2 * ai]}
                S.op("pool", lambda e: e.memset(d["dt"].ap, 0.0), writes=[d["dt"].d()])
                AP_.append(d)
            RB = []
            for r_ in range(NR):
                RB.append({"sz": self.sb(es, f"ssz{r_}", [128, 256], F32), "Btok": self.sb(es, f"sBtok{r_}", [128, 128], BF16),
                           "el": self.sb(es, f"sel{r_}", [128, 4], F32), "Pm": self.sb(es, f"sP{r_}", [128, 4, 128], BF16),
                           "CTs": self.sb(es, f"sCTs{r_}", [128, 4, 128], BF16), "vsb": self.sb(es, f"sv{r_}", [128, 256], BF16),
                           "vh": self.sb(es, f"svh{r_}", [128, 256], BF16), "t1": self.sb(es, f"st1{r_}", [128, 256], F32)})
            S.op("act", lambda e: e.activation(out=Aneg.ap, in_=self.rv.ap[:, rvA + 32:rvA + 64], func=AF.Exp),
                 reads=[self.rv.d()], writes=[Aneg.d()])
            S.op("dve", lambda e: e.tensor_scalar(out=Aneg.ap, in0=Aneg.ap, scalar1=-1.0, scalar2=None, op0=ALU.mult),
                 reads=[Aneg.d()], writes=[Aneg.d()])

            def sig_from(dst, src, rd, wr, eng_reads_psum=False):
                S.op("act", lambda e: e.activation(out=dst, in_=src, func=AF.Exp, scale=-1.0), reads=rd, writes=wr)
                S.op("act", lambda e: e.activation(out=dst, in_=dst, func=AF.Ln, bias=1.0), reads=wr, writes=wr)
                S.op("act", lambda e: e.activation(out=dst, in_=dst, func=AF.Exp, scale=-1.0), reads=wr, writes=wr)

            def stageA(g, t, ai, R_):
                c0, n = TILES[t]
                ugi = 0 if t == 0 else 1 + (t - 1) // 4
                ugp = 0 if t <= 1 else 1 + (t - 2) // 4
                P_ = AP_[ai]
                pc, acc, a4, sig, BTb, CTb, xs, dt, trila, dec, ecr, bx, by = (P_[k] for k in
                    ("pc", "acc", "a4", "sig", "BTb", "CTb", "xs", "dt", "trila", "dec", "ecr", "bx", "by"))
                chans = [2 * g, 2 * g + 1, 16 + g, 24 + g]
                dtb = self.rv.ap[:, rvA + g * 4:rvA + g * 4 + 4]
                Ag = Aneg.ap[:, g * 4:g * 4 + 4]
                dsk = self.rv.ap[:, rvA + 64 + g * 4:rvA + 64 + g * 4 + 4]
                if c0 == 0:
                    S.op("pool", lambda e: e.memset(pc.ap[:, :, 0:3], 0.0), writes=[pc.d()])
                    src0, dst0, w_ = 0, 3, n
                else:
                    src0, dst0, w_ = c0 - 3, 0, n + 3
                for a in range(4):
                    pb, off = (bx, by)[a // 2], (a % 2) * 256
                    for dc in range(8):
                        self.mm(pb.ap[:, off:off + w_], win.ap[:, dc, a * 128:(a + 1) * 128], uT.ap[:, dc, src0:src0 + w_], dc == 0, dc == 7,
                                reads=[win.d(), uT.d(ugi), uT.d(ugp)], writes=[pb.d()])
                for hf, pb in enumerate((bx, by)):
                    S.op("act", lambda e: e.activation(out=pc.ap[:, 2 * hf:2 * hf + 2, dst0:dst0 + w_],
                                                       in_=pb.ap.rearrange("p (a i) -> p a i", a=2)[:, :, 0:w_], func=AF.Copy),
                         reads=[pb.d()], writes=[pc.d()])
                for a in range(4):
                    ch = chans[a]
                    S.op("dve", lambda e: e.tensor_scalar(out=acc.ap[:, a, :n], in0=pc.ap[:, a, 0:n], scalar1=cw(0, ch), scalar2=cb(ch),
                                                          op0=ALU.mult, op1=ALU.add),
                         reads=[pc.d(), self.pv.d()], writes=[acc.d()])
                    for jj in range(1, 4):
                        S.op("dve", lambda e: e.scalar_tensor_tensor(out=acc.ap[:, a, :n], in0=pc.ap[:, a, jj:jj + n], scalar=cw(jj, ch),
                                                                      in1=acc.ap[:, a, :n], op0=ALU.mult, op1=ALU.add),
                             reads=[pc.d(), self.pv.d(), acc.d()], writes=[acc.d()])
                sig_from(sig.ap[:, :, :n], acc.ap[:, :, :n], [acc.d()], [sig.d()])
                S.op("pool", lambda e: e.tensor_tensor(out=a4.ap[:, :, :n], in0=acc.ap[:, :, :n], in1=sig.ap[:, :, :n], op=ALU.mult),
                     reads=[acc.d(), sig.d()], writes=[a4.d()])
                S.op("act", lambda e: e.activation(out=BTb.ap[:, :n], in_=a4.ap[:, 2, :n], func=AF.Copy), reads=[a4.d()], writes=[BTb.d()])
                S.op("act", lambda e: e.activation(out=CTb.ap[:, :n], in_=a4.ap[:, 3, :n], func=AF.Copy), reads=[a4.d()], writes=[CTb.d()])
                for dc in range(8):
                    self.mm(bx.ap[:n, 0:260], uT.ap[:, dc, c0:c0 + n], win.ap[:, dc, 512:772], dc == 0, dc == 7,
                            reads=[win.d(), uT.d(ugi)], writes=[bx.d()])
                sz = R_["sz"]
                sig_from(sz.ap[:n, :], bx.ap[:n, 0:256], [bx.d()], [sz.d()])
                S.op("dve", lambda e: e.tensor_tensor(out=sz.ap[:n, :], in0=sz.ap[:n, :], in1=bx.ap[:n, 0:256], op=ALU.mult),
                     reads=[sz.d(), bx.d()], writes=[sz.d()])
                S.op("dve", lambda e: e.tensor_tensor(out=dt.ap[:n, 0:4], in0=bx.ap[:n, 256:260], in1=dtb[:n, :], op=ALU.add),
                     reads=[bx.d(), self.rv.d()], writes=[dt.d()])
                S.op("act", lambda e: e.activation(out=dt.ap[:n, 0:4], in_=dt.ap[:n, 0:4], func=AF.Exp), reads=[dt.d()], writes=[dt.d()])
                S.op("act", lambda e: e.activation(out=dt.ap[:n, 0:4], in_=dt.ap[:n, 0:4], func=AF.Ln, bias=1.0),
                     reads=[dt.d()], writes=[dt.d()])
                S.op("dve", lambda e: e.tensor_tensor(out=dt.ap[:n, 4:8], in0=dt.ap[:n, 0:4], in1=Ag[:n, :], op=ALU.mult),
                     reads=[dt.d(), Aneg.d()], writes=[dt.d()])
                for c in range(2):
                    self.tr(by.ap[:n, c * 128:(c + 1) * 128], a4.ap[:, c, :n], 128, reads=[a4.d()], writes=[by.d()])
                self.tr(by.ap[:n, 256:384], a4.ap[:, 2, :n], 128, reads=[a4.d()], writes=[by.d()])
                S.op("act", lambda e: e.activation(out=xs.ap[:n, :], in_=by.ap[:n, 0:256], func=AF.Copy), reads=[by.d()], writes=[xs.d()])
                S.op("act", lambda e: e.activation(out=R_["Btok"].ap[:n, :], in_=by.ap[:n, 256:384], func=AF.Copy),
                     reads=[by.d()], writes=[R_["Btok"].d()])
                self.mm(bx.ap[:, 384:400], tri[:n, :], dt.ap[:n, 4:20], True, True, reads=[self.cst.d(), dt.d()], writes=[bx.d()])
                S.op("dve", lambda e: e.tensor_scalar(out=dt.ap[:n, 8:12], in0=bx.ap[:n, 384:388], scalar1=-1.0, scalar2=None, op0=ALU.mult),
                     reads=[bx.d()], writes=[dt.d()])
                S.op("dve", lambda e: e.tensor_tensor(out=trila.ap[:n, :, :], in0=tri[:n, :].unsqueeze(1).to_broadcast([n, 4, 128]),
                                                      in1=dt.ap[:n, 4:8].unsqueeze(2).to_broadcast([n, 4, 128]), op=ALU.mult),
                     reads=[self.cst.d(), dt.d()], writes=[trila.d()])
                crow = v3(by.ap)[:, :, :n]
                self.mm(by.ap, ones[:n, :], trila.ap[:n, :, :].rearrange("p h i -> p (h i)"), True, True,
                        reads=[self.cst.d(), trila.d()], writes=[by.d()])
                S.op("act", lambda e: e.activation(out=ecr.ap[:, :, :n], in_=crow, func=AF.Exp), reads=[by.d()], writes=[ecr.d()])
                S.op("pool", lambda e: e.tensor_copy(out=R_["el"].ap, in_=ecr.ap[:, :, n - 1]), reads=[ecr.d()], writes=[R_["el"].d()])
                S.op("dve", lambda e: e.tensor_tensor(out=dt.ap[:n, 12:16], in0=v3(by.ap)[:n, :, n - 1], in1=dt.ap[:n, 8:12], op=ALU.add),
                     reads=[by.d(), dt.d()], writes=[dt.d()])
                S.op("act", lambda e: e.activation(out=dt.ap[:n, 12:16], in_=dt.ap[:n, 12:16], func=AF.Exp), reads=[dt.d()], writes=[dt.d()])
                S.op("dve", lambda e: e.tensor_tensor(out=dt.ap[:n, 16:20], in0=dt.ap[:n, 12:16], in1=dt.ap[:n, 0:4], op=ALU.mult),
                     reads=[dt.d()], writes=[dt.d()])
                self.mm(by.ap, ident[:n, :], negm.ap[:n, :, :].rearrange("p h i -> p (h i)"), False, True,
                        reads=[self.cst.d(), negm.d()], writes=[by.d()])
                for hh in range(4):
                    S.op("act", lambda e: e.activation(out=dec.ap[:n, hh, :n], in_=v3(by.ap)[:n, hh, :n], func=AF.Exp,
                                                       bias=dt.ap[:n, 8 + hh:9 + hh]),
                         reads=[by.d(), dt.d()], writes=[dec.d()])
                self.mm(bx.ap[:n, :n], BTb.ap[:, :n], CTb.ap[:, :n], True, True, reads=[BTb.d(), CTb.d()], writes=[bx.d()])
                S.op("dve", lambda e: e.tensor_tensor(out=R_["Pm"].ap[:n, :, :n], in0=bx.ap[:n, :n].unsqueeze(1).to_broadcast([n, 4, n]),
                                                      in1=dec.ap[:n, :, :n], op=ALU.mult),
                     reads=[bx.d(), dec.d()], writes=[R_["Pm"].d()])
                S.op("pool", lambda e: e.tensor_tensor(out=R_["CTs"].ap[:, :, :n], in0=a4.ap[:, 3, :n].unsqueeze(1).to_broadcast([128, 4, n]),
                                                       in1=ecr.ap[:, :, :n], op=ALU.mult),
                     reads=[a4.d(), ecr.d()], writes=[R_["CTs"].d()])
                x3 = xs.ap[:n, :].rearrange("p (h q) -> p h q", h=4)
                S.op("dve", lambda e: e.tensor_tensor(out=R_["vsb"].ap[:n, :].rearrange("p (h q) -> p h q", h=4), in0=x3,
                                                      in1=dt.ap[:n, 0:4].unsqueeze(2).to_broadcast([n, 4, 64]), op=ALU.mult),
                     reads=[xs.d(), dt.d()], writes=[R_["vsb"].d()])
                S.op("dve", lambda e: e.tensor_tensor(out=R_["vh"].ap[:n, :].rearrange("p (h q) -> p h q", h=4), in0=x3,
                                                      in1=dt.ap[:n, 16:20].unsqueeze(2).to_broadcast([n, 4, 64]), op=ALU.mult),
                     reads=[xs.d(), dt.d()], writes=[R_["vh"].d()])
                S.op("dve", lambda e: e.tensor_tensor(out=R_["t1"].ap[:n, :].rearrange("p (h q) -> p h q", h=4), in0=x3,
                                                      in1=dsk[:n, :].unsqueeze(2).to_broadcast([n, 4, 64]), op=ALU.mult),
                     reads=[xs.d(), self.rv.d()], writes=[R_["t1"].d()])

            def stageB(g, t, R_):
                c0, n = TILES[t]
                b0, b1 = ps[0], ps[1]
                wo = wos[g % 2]
                if t == 0:
                    S.op("pool", lambda e: e.memset(M.ap, 0.0), writes=[M.d()])
                    S.op("pool", lambda e: e.memset(Mb.ap, 0.0), writes=[Mb.d()])
                nwb = self.rv.ap[:, rvA + 96 + g * 256:rvA + 96 + (g + 1) * 256]
                Pm, CTs, vsb, vh, t1, sz, Btok, el = (R_[k] for k in ("Pm", "CTs", "vsb", "vh", "t1", "sz", "Btok", "el"))
                for hh in range(4):
                    cs_ = slice(hh * 64, (hh + 1) * 64)
                    self.mm(b0.ap[:n, cs_], Pm.ap[:n, hh, :n], vsb.ap[:n, cs_], True, False, reads=[Pm.d(), vsb.d()], writes=[b0.d()])
                    self.mm(b0.ap[:n, cs_], CTs.ap[:, hh, :n], Mb.ap[:, cs_], False, True, reads=[CTs.d(), Mb.d()], writes=[b0.d()])
                self.mm(b1.ap[:, 0:256], Btok.ap[:n, :], vh.ap[:n, :], True, True, reads=[Btok.d(), vh.d()], writes=[b1.d()])
                S.op("dve", lambda e: e.tensor_tensor(out=M.ap.rearrange("p (h q) -> p h q", h=4), in0=M.ap.rearrange("p (h q) -> p h q", h=4),
                                                      in1=el.ap.unsqueeze(2).to_broadcast([128, 4, 64]), op=ALU.mult),
                     reads=[M.d(), el.d()], writes=[M.d()])
                S.op("dve", lambda e: e.tensor_tensor(out=M.ap, in0=M.ap, in1=b1.ap[:, 0:256], op=ALU.add),
                     reads=[M.d(), b1.d()], writes=[M.d()])
                S.op("act", lambda e: e.activation(out=Mb.ap, in_=M.ap, func=AF.Copy), reads=[M.d()], writes=[Mb.d()])
                S.op("dve", lambda e: e.tensor_tensor(out=t1.ap[:n, :], in0=t1.ap[:n, :], in1=b0.ap[:n, 0:256], op=ALU.add),
                     reads=[t1.d(), b0.d()], writes=[t1.d()])
                S.op("pool", lambda e: e.tensor_tensor(out=t1.ap[:n, :], in0=t1.ap[:n, :], in1=sz.ap[:n, :], op=ALU.mult),
                     reads=[t1.d(), sz.d()], writes=[t1.d()])
                S.op("act", lambda e: e.activation(out=self.junk.ap[:n, 0:256], in_=t1.ap[:n, :], func=AF.Square, accum_out=st.ap[:n, 2:3]),
                     reads=[t1.d()], writes=[self.junk.d(), st.d()])
                S.op("act", lambda e: e.activation(out=st.ap[:n, 3:4], in_=st.ap[:n, 2:3], func=AF.Ln, scale=1.0 / 256.0, bias=EPS),
                     reads=[st.d()], writes=[st.d()])
                S.op("act", lambda e: e.activation(out=st.ap[:n, 4:5], in_=st.ap[:n, 3:4], func=AF.Exp, scale=-0.5), reads=[st.d()], writes=[st.d()])
                S.op("dve", lambda e: e.scalar_tensor_tensor(out=yn.ap[:n, :], in0=t1.ap[:n, :], scalar=st.ap[:n, 4:5], in1=nwb[:n, :],
                                                              op0=ALU.mult, op1=ALU.mult),
                     reads=[t1.d(), st.d(), self.rv.d()], writes=[yn.d()])
                self.out_proj(yn, n, 2, ygT, wo, t, b1, (b0, b1))

            def hook(g):
                S.dma("pool", win.ap.rearrange("p c f -> p (c f)"), sin_d[j * 8 + g], writes=[win.d()])
                S.dma("pool", wos[g % 2].ap.rearrange("p c f -> p (c f)"), sout_d[j * 8 + g], writes=[wos[g % 2].d()])
            run_pipeline(8 * NT, lambda u, ai, slot: stageA(u // NT, u % NT, ai, RB[slot]), lambda u, slot: stageB(u // NT, u % NT, RB[slot]),
                         NA, NR, stagger=cfg.get("ssd_stagger", 0), group=NT, hook=hook)

    def emit_gla(self, j, norm_w):
        nc, S, h, uT, cfg = self.nc, self.S, self.h, self.uT, self.cfg
        gup_d = self.dram["gla_gup"]
        gb = self.pv.ap[:, cfg["pv_gla_bias"]:cfg["pv_gla_bias"] + 4]
        with ExitStack() as es:
            self.rv = self.load_rv(es, cfg["rv_gla_norm"], 256)
            nwb = self.rv.ap[:, 0:256]
            gup = self.sb(es, "gup", [16, 512], F32)
            negb = self.sb(es, "gnegb", [128, 4], F32)
            S.dma("sp", gup.ap, gup_d[j], writes=[gup.d()])
            S.op("dve", lambda e: e.tensor_scalar(out=negb.ap, in0=gb, scalar1=-1.0, scalar2=None, op0=ALU.mult),
                 reads=[self.pv.d()], writes=[negb.d()])
            bufsets = [self.gla_bufs(es, i) for i in range(2)]
            for i in range(2):
                self.gla_load(j, i, bufsets[i])
            self.emit_norm(norm_w)

            def gstream(i):
                bk = [self.ps[4 * i + k] for k in range(4)]
                self.gla_head(j, i, bufsets[i], bk, gup, negb, nwb)
                self.gla_load(j, i + 2, bufsets[i])
                self.gla_head(j, i + 2, bufsets[i], bk, gup, negb, nwb)
            IL.run([(lambda i=i: gstream(i)) for i in range(2)])

    def gla_bufs(self, es, i):
        sb = lambda nm, shape, dt: self.sb(es, f"{nm}{i}", shape, dt)
        return dict(win=sb("gwin", [128, 8, 784], BF16), wo=sb("gwo", [128, 2, 1024], BF16), M=sb("gM", [128, 256], F32),
                    Mb=sb("gMb", [128, 256], BF16), glr=sb("gglr", [16, 512], F32), sp=sb("gsp", [128, 512], F32),
                    cum=sb("gcum", [128, 512], F32), ec=sb("gec", [128, 512], F32), en=sb("gen", [128, 512], F32),
                    qt=sb("gqt", [128, 512], BF16), kt=sb("gkt", [128, 512], BF16), khT=sb("gkhT", [128, 128], F32),
                    eh=sb("geh", [128, 128], F32), khat=sb("gkhat", [128, 128], BF16), vsb=sb("gv", [128, 256], BF16),
                    sr=sb("gsr", [128, 256], F32), Pm=sb("gP", [128, 128], BF16), yn=sb("gyn", [128, 256], F32),
                    ygT=sb("gygT", [128, 2, 128], BF16), st=sb("gst", [128, 8], F32), junk=sb("gjunk", [128, 256], F32))

    def gla_load(self, j, hd, B_):
        S = self.S
        S.dma("pool", B_["win"].ap.rearrange("p c f -> p (c f)"), self.dram["gla_in"][j * 4 + hd], writes=[B_["win"].d()])
        S.dma("pool", B_["wo"].ap.rearrange("p c f -> p (c f)"), self.dram["gla_out"][j * 4 + hd], writes=[B_["wo"].d()])

    def gla_head(self, j, hd, B_, bk, gup, negb, nwb):
        nc, S, h, uT, cfg = self.nc, self.S, self.h, self.uT, self.cfg
        gin_d, gout_d = self.dram["gla_in"], self.dram["gla_out"]
        tri = self.cst.ap[:, 128:256]
        ones = self.cst.ap[:, 384:512]
        win, wo, M, Mb, glr, sp, cum, ec, en, qt, kt, khT, eh, khat, vsb, sr, Pm, yn, ygT, st, junk = (B_[k] for k in
            ("win", "wo", "M", "Mb", "glr", "sp", "cum", "ec", "en", "qt", "kt", "khT", "eh", "khat", "vsb", "sr", "Pm", "yn", "ygT", "st", "junk"))
        ps = {0: bk[0], 1: bk[1], 2: bk[2], 3: bk[2], 4: bk[2], 5: bk[3], 6: bk[0], 7: bk[3]}
        S.op("pool", lambda e: e.memset(M.ap, 0.0), writes=[M.d()])
        S.op("pool", lambda e: e.memset(Mb.ap, 0.0), writes=[Mb.d()])
        for gi, (g0, gn, gtiles) in enumerate(GROUPS):
            for a, (o0, o1) in enumerate(((0, 128), (128, 256), (256, 272))):
                m = o1 - o0
                for dc in range(8):
                    self.mm(ps[a].ap[:m, :gn], win.ap[:, dc, o0:o1], uT.ap[:, dc, g0:g0 + gn], dc == 0, dc == 7,
                            reads=[win.d(), uT.d(gi)], writes=[ps[a].d()])
            S.op("act", lambda e: e.activation(out=glr.ap[:, :gn], in_=ps[2].ap[:16, :gn], func=AF.Copy),
                 reads=[ps[2].d()], writes=[glr.d()])
            self.mm(ps[3].ap[:, :gn], gup.ap[:, hd * 128:(hd + 1) * 128], glr.ap[:, :gn], True, True,
                    reads=[gup.d(), glr.d()], writes=[ps[3].d()])
            S.op("act", lambda e: e.activation(out=sp.ap[:, :gn], in_=ps[3].ap[:, :gn], func=AF.Exp, scale=-1.0,
                                               bias=negb.ap[:, hd:hd + 1]),
                 reads=[ps[3].d(), negb.d()], writes=[sp.d()])
            S.op("act", lambda e: e.activation(out=sp.ap[:, :gn], in_=sp.ap[:, :gn], func=AF.Ln, bias=1.0),
                 reads=[sp.d()], writes=[sp.d()])
            for ti, t in enumerate(gtiles):
                n = TILES[t][1]
                lo = ti * 128
                S.op("dve", lambda e: e.tensor_tensor_scan(out=cum.ap[:, lo:lo + n], data0=ones[:, :n], data1=sp.ap[:, lo:lo + n],
                                                           initial=0.0, op0=ALU.mult, op1=ALU.add),
                     reads=[sp.d(), self.cst.d()], writes=[cum.d()])
            S.op("act", lambda e: e.activation(out=ec.ap[:, :gn], in_=cum.ap[:, :gn], func=AF.Exp, scale=-1.0 / 16.0),
                 reads=[cum.d()], writes=[ec.d()])
            S.op("act", lambda e: e.activation(out=en.ap[:, :gn], in_=cum.ap[:, :gn], func=AF.Exp, scale=1.0 / 16.0),
                 reads=[cum.d()], writes=[en.d()])
            S.op("dve", lambda e: e.scalar_tensor_tensor(out=qt.ap[:, :gn], in0=ps[0].ap[:, :gn], scalar=128.0 ** -0.5,
                                                          in1=ec.ap[:, :gn], op0=ALU.mult, op1=ALU.mult),
                 reads=[ps[0].d(), ec.d()], writes=[qt.d()])
            S.op("dve", lambda e: e.tensor_tensor(out=kt.ap[:, :gn], in0=ps[1].ap[:, :gn], in1=en.ap[:, :gn], op=ALU.mult),
                 reads=[ps[1].d(), en.d()], writes=[kt.d()])
            for ti, t in enumerate(gtiles):
                c0, n = TILES[t]
                lo = ti * 128
                last = cum.ap[:, lo + n - 1:lo + n]
                for dc in range(8):
                    self.mm(ps[4].ap[:n, :], uT.ap[:, dc, c0:c0 + n], win.ap[:, dc, 272:784], dc == 0, dc == 7,
                            reads=[win.d(), uT.d(gi)], writes=[ps[4].d()])
                S.op("act", lambda e: e.activation(out=vsb.ap[:n, :], in_=ps[4].ap[:n, 0:256], func=AF.Copy),
                     reads=[ps[4].d()], writes=[vsb.d()])
                S.op("act", lambda e: e.activation(out=sr.ap[:n, :], in_=ps[4].ap[:n, 256:512], func=AF.Exp, scale=-1.0),
                     reads=[ps[4].d()], writes=[sr.d()])
                S.op("act", lambda e: e.activation(out=sr.ap[:n, :], in_=sr.ap[:n, :], func=AF.Ln, bias=1.0), reads=[sr.d()], writes=[sr.d()])
                S.op("act", lambda e: e.activation(out=sr.ap[:n, :], in_=sr.ap[:n, :], func=AF.Exp, scale=-1.0), reads=[sr.d()], writes=[sr.d()])
                S.op("dve", lambda e: e.tensor_tensor(out=sr.ap[:n, :], in0=sr.ap[:n, :], in1=ps[4].ap[:n, 256:512], op=ALU.mult),
                     reads=[sr.d(), ps[4].d()], writes=[sr.d()])
                self.mm(ps[5].ap[:n, :n], kt.ap[:, lo:lo + n], qt.ap[:, lo:lo + n], True, True,
                        reads=[kt.d(), qt.d()], writes=[ps[5].d()])
                S.op("dve", lambda e: e.tensor_tensor(out=Pm.ap[:n, :n], in0=ps[5].ap[:n, :n], in1=tri[:n, :n], op=ALU.mult),
                     reads=[ps[5].d(), self.cst.d()], writes=[Pm.d()])
                self.mm(ps[6].ap[:n, 0:256], Pm.ap[:n, :n], vsb.ap[:n, :], True, False, reads=[Pm.d(), vsb.d()], writes=[ps[6].d()])
                self.mm(ps[6].ap[:n, 0:256], qt.ap[:, lo:lo + n], Mb.ap, False, True, reads=[qt.d(), Mb.d()], writes=[ps[6].d()])
                S.op("dve", lambda e: e.tensor_scalar(out=st.ap[:, 0:1], in0=last, scalar1=-1.0 / 16.0, scalar2=None, op0=ALU.mult),
                     reads=[cum.d()], writes=[st.d()])
                S.op("act", lambda e: e.activation(out=eh.ap[:, :n], in_=cum.ap[:, lo:lo + n], func=AF.Exp, scale=1.0 / 16.0,
                                                   bias=st.ap[:, 0:1]),
                     reads=[cum.d(), st.d()], writes=[eh.d()])
                S.op("act", lambda e: e.activation(out=st.ap[:, 1:2], in_=last, func=AF.Exp, scale=-1.0 / 16.0),
                     reads=[cum.d()], writes=[st.d()])
                S.op("dve", lambda e: e.tensor_tensor(out=khT.ap[:, :n], in0=ps[1].ap[:, lo:lo + n], in1=eh.ap[:, :n], op=ALU.mult),
                     reads=[ps[1].d(), eh.d()], writes=[khT.d()])
                self.tr(ps[5].ap[:n, 128:256], khT.ap[:, :n], 128, reads=[khT.d()], writes=[ps[5].d()])
                S.op("act", lambda e: e.activation(out=khat.ap[:n, :], in_=ps[5].ap[:n, 128:256], func=AF.Copy),
                     reads=[ps[5].d()], writes=[khat.d()])
                self.mm(ps[7].ap[:, 0:256], khat.ap[:n, :], vsb.ap[:n, :], True, True, reads=[khat.d(), vsb.d()], writes=[ps[7].d()])
                S.op("dve", lambda e: e.scalar_tensor_tensor(out=M.ap, in0=M.ap, scalar=st.ap[:, 1:2], in1=ps[7].ap[:, 0:256],
                                                              op0=ALU.mult, op1=ALU.add),
                     reads=[M.d(), st.d(), ps[7].d()], writes=[M.d()])
                S.op("act", lambda e: e.activation(out=Mb.ap, in_=M.ap, func=AF.Copy), reads=[M.d()], writes=[Mb.d()])
                S.op("act", lambda e: e.activation(out=junk.ap[:n, 0:256], in_=ps[6].ap[:n, 0:256], func=AF.Square,
                                                   accum_out=st.ap[:n, 2:3]),
                     reads=[ps[6].d()], writes=[junk.d(), st.d()])
                S.op("act", lambda e: e.activation(out=st.ap[:n, 3:4], in_=st.ap[:n, 2:3], func=AF.Ln, scale=1.0 / 256.0, bias=EPS),
                     reads=[st.d()], writes=[st.d()])
                S.op("act", lambda e: e.activation(out=st.ap[:n, 4:5], in_=st.ap[:n, 3:4], func=AF.Exp, scale=-0.5), reads=[st.d()], writes=[st.d()])
                S.op("dve", lambda e: e.scalar_tensor_tensor(out=yn.ap[:n, :], in0=ps[6].ap[:n, 0:256], scalar=st.ap[:n, 4:5],
                                                              in1=nwb[:n, :], op0=ALU.mult, op1=ALU.mult),
                     reads=[ps[6].d(), st.d(), self.rv.d()], writes=[yn.d()])
                S.op("pool", lambda e: e.tensor_tensor(out=yn.ap[:n, :], in0=yn.ap[:n, :], in1=sr.ap[:n, :], op=ALU.mult),
                     reads=[yn.d(), sr.d()], writes=[yn.d()])
                self.out_proj(yn, n, 2, ygT, wo, t, ps[5], (ps[4], ps[7]))


    def emit_ret(self, j, norm_w):
        nc, S, h, uT, cfg = self.nc, self.S, self.h, self.uT, self.cfg
        ret_in_d, ret_out_d, ret_cst_d = self.dram["ret_in"], self.dram["ret_out"], self.dram["ret_cst"]
        NRC = cfg["nretc"]
        with ExitStack() as es:
            rc = self.sb(es, "retc", [128, NRC], F32)
            S.dma("sp", rc.ap, ret_cst_d, writes=[rc.d()])
            decT = lambda hd: rc.ap[:, hd * 128:(hd + 1) * 128]
            rowpow = lambda hd: rc.ap[:, 512 + hd * 128:512 + (hd + 1) * 128]
            kdec = lambda hd, n: rc.ap[:, 1024 + (0 if n == 128 else 4) + hd:1024 + (0 if n == 128 else 4) + hd + 1]
            cosT = rc.ap[:, 1032:1032 + L]
            sinT = rc.ap[:, 1032 + L:1032 + 2 * L]
            win = self.sb(es, "rwin", [128, 8, 1536], BF16)
            wos = [self.sb(es, f"rwo{i}", [128, 4, 1024], BF16) for i in range(2)]
            M = self.sb(es, "rM", [128, 2, 512], F32)
            Mb = self.sb(es, "rMb", [128, 2, 512], BF16)
            qT = self.sb(es, "rqT", [128, 2, 512], BF16)
            kT = self.sb(es, "rkT", [128, 2, 512], BF16)
            kf = self.sb(es, "rkf", [128, 2, 512], F32)
            tmp = [self.sb(es, f"rtmp{i}", [128, 512], F32) for i in range(2)]
            yn = self.sb(es, "ryn", [128, 512], F32)
            ygT = self.sb(es, "rygT", [128, 4, 128], BF16)
            st = self.sb(es, "rst", [128, 16], F32)
            RB = [dict(vsb=self.sb(es, f"rv{i}", [128, 512], BF16), sg=self.sb(es, f"rsg{i}", [128, 512], F32),
                       Pm=self.sb(es, f"rP{i}", [128, 128], BF16), qs=self.sb(es, f"rqs{i}", [128, 2, 128], BF16),
                       khat=self.sb(es, f"rkhat{i}", [128, 256], BF16)) for i in range(2)]
            ps = self.ps
            tile_pos = {}
            for gi, (g0, gn, gtiles) in enumerate(GROUPS):
                for ti, t in enumerate(gtiles):
                    tile_pos[t] = (gi, g0, gn, ti)

            def stageA(hd, t, R_):
                gi, g0, gn, ti = tile_pos[t]
                c0, n = TILES[t]
                lo = ti * 128
                vsb, sg, Pm, qs, khat = (R_[k] for k in ("vsb", "sg", "Pm", "qs", "khat"))
                if ti == 0:
                    for a in range(4):
                        for dc in range(8):
                            self.mm(ps[a].ap[:, :gn], win.ap[:, dc, a * 128:(a + 1) * 128], uT.ap[:, dc, g0:g0 + gn],
                                    dc == 0, dc == 7, reads=[win.d(), uT.d(gi)], writes=[ps[a].d()])
                    cs, sn = cosT[:, g0:g0 + gn], sinT[:, g0:g0 + gn]
                    for qk in range(2):
                        p1, p2 = ps[2 * qk], ps[2 * qk + 1]
                        sc = 1.0 if qk == 0 else 1.0 / 16.0
                        dst = qT if qk == 0 else kf
                        t0, t1 = tmp
                        S.op("dve", lambda e: e.scalar_tensor_tensor(out=t0.ap[:, :gn], in0=p1.ap[:, :gn], scalar=sc, in1=cs,
                                                                      op0=ALU.mult, op1=ALU.mult),
                             reads=[p1.d(), rc.d()], writes=[t0.d()])
                        S.op("dve", lambda e: e.scalar_tensor_tensor(out=t1.ap[:, :gn], in0=p2.ap[:, :gn], scalar=sc, in1=sn,
                                                                      op0=ALU.mult, op1=ALU.mult),
                             reads=[p2.d(), rc.d()], writes=[t1.d()])
                        S.op("pool", lambda e: e.tensor_tensor(out=dst.ap[:, 0, :gn], in0=t0.ap[:, :gn], in1=t1.ap[:, :gn], op=ALU.subtract),
                             reads=[t0.d(), t1.d()], writes=[dst.d()])
                        S.op("dve", lambda e: e.scalar_tensor_tensor(out=t0.ap[:, :gn], in0=p1.ap[:, :gn], scalar=sc, in1=sn,
                                                                      op0=ALU.mult, op1=ALU.mult),
                             reads=[p1.d(), rc.d()], writes=[t0.d()])
                        S.op("dve", lambda e: e.scalar_tensor_tensor(out=t1.ap[:, :gn], in0=p2.ap[:, :gn], scalar=sc, in1=cs,
                                                                      op0=ALU.mult, op1=ALU.mult),
                             reads=[p2.d(), rc.d()], writes=[t1.d()])
                        S.op("pool", lambda e: e.tensor_tensor(out=dst.ap[:, 1, :gn], in0=t0.ap[:, :gn], in1=t1.ap[:, :gn], op=ALU.add),
                             reads=[t0.d(), t1.d()], writes=[dst.d()])
                    S.op("act", lambda e: e.activation(out=kT.ap[:, :, :gn], in_=kf.ap[:, :, :gn], func=AF.Copy),
                         reads=[kf.d()], writes=[kT.d()])
                for a, pb in ((0, ps[4]), (1, ps[5])):
                    for dc in range(8):
                        self.mm(pb.ap[:n, :], uT.ap[:, dc, c0:c0 + n], win.ap[:, dc, 512 + a * 512:1024 + a * 512],
                                dc == 0, dc == 7, reads=[win.d(), uT.d(gi)], writes=[pb.d()])
                S.op("act", lambda e: e.activation(out=vsb.ap[:n, :], in_=ps[4].ap[:n, :], func=AF.Copy),
                     reads=[ps[4].d()], writes=[vsb.d()])
                S.op("act", lambda e: e.activation(out=sg.ap[:n, :], in_=ps[5].ap[:n, :], func=AF.Exp, scale=-1.0),
                     reads=[ps[5].d()], writes=[sg.d()])
                S.op("act", lambda e: e.activation(out=sg.ap[:n, :], in_=sg.ap[:n, :], func=AF.Ln, bias=1.0), reads=[sg.d()], writes=[sg.d()])
                S.op("act", lambda e: e.activation(out=sg.ap[:n, :], in_=sg.ap[:n, :], func=AF.Exp, scale=-1.0), reads=[sg.d()], writes=[sg.d()])
                S.op("dve", lambda e: e.tensor_tensor(out=sg.ap[:n, :], in0=sg.ap[:n, :], in1=ps[5].ap[:n, :], op=ALU.mult),
                     reads=[sg.d(), ps[5].d()], writes=[sg.d()])
                for c in range(2):
                    self.mm(ps[4].ap[:n, :n], kT.ap[:, c, lo:lo + n], qT.ap[:, c, lo:lo + n], c == 0, c == 1,
                            reads=[kT.d(), qT.d()], writes=[ps[4].d()])
                S.op("dve", lambda e: e.tensor_tensor(out=Pm.ap[:n, :n], in0=ps[4].ap[:n, :n], in1=decT(hd)[:n, :n], op=ALU.mult),
                     reads=[ps[4].d(), rc.d()], writes=[Pm.d()])
                S.op("pool", lambda e: e.tensor_tensor(out=qs.ap[:, :, :n], in0=qT.ap[:, :, lo:lo + n],
                                                       in1=rowpow(hd)[:, :n].unsqueeze(1).to_broadcast([128, 2, n]), op=ALU.mult),
                     reads=[qT.d(), rc.d()], writes=[qs.d()])
                for c in range(2):
                    self.tr(ps[5].ap[:n, c * 128:(c + 1) * 128], kf.ap[:, c, lo:lo + n], 128, reads=[kf.d()], writes=[ps[5].d()])
                S.op("dve", lambda e: e.tensor_scalar(out=khat.ap[:n, :], in0=ps[5].ap[:n, 0:256], scalar1=kdec(hd, n)[:n, :],
                                                      scalar2=None, op0=ALU.mult),
                     reads=[ps[5].d(), rc.d()], writes=[khat.d()])

            def stageB(hd, t, R_):
                c0, n = TILES[t]
                gam = 1.0 - 2.0 ** (-5.0 - hd)
                wo = wos[hd % 2]
                if t == 0:
                    S.op("pool", lambda e: e.memset(M.ap, 0.0), writes=[M.d()])
                    S.op("pool", lambda e: e.memset(Mb.ap, 0.0), writes=[Mb.d()])
                vsb, sg, Pm, qs, khat = (R_[k] for k in ("vsb", "sg", "Pm", "qs", "khat"))
                self.mm(ps[6].ap[:n, :], Pm.ap[:n, :n], vsb.ap[:n, :], True, False, reads=[Pm.d(), vsb.d()], writes=[ps[6].d()])
                for c in range(2):
                    self.mm(ps[6].ap[:n, :], qs.ap[:, c, :n], Mb.ap[:, c, :], False, c == 1,
                            reads=[qs.d(), Mb.d()], writes=[ps[6].d()])
                for c in range(2):
                    self.mm(ps[7].ap[:, :], khat.ap[:n, c * 128:(c + 1) * 128], vsb.ap[:n, :], True, True,
                            reads=[khat.d(), vsb.d()], writes=[ps[7].d()])
                    S.op("dve", lambda e: e.scalar_tensor_tensor(out=M.ap[:, c, :], in0=M.ap[:, c, :], scalar=float(gam ** n),
                                                                  in1=ps[7].ap[:, :], op0=ALU.mult, op1=ALU.add),
                         reads=[M.d(), ps[7].d()], writes=[M.d()])
                S.op("act", lambda e: e.activation(out=Mb.ap, in_=M.ap, func=AF.Copy), reads=[M.d()], writes=[Mb.d()])
                S.op("dve", lambda e: e.bn_stats(out=st.ap[:n, 0:6], in_=ps[6].ap[:n, :]), reads=[ps[6].d()], writes=[st.d()])
                S.op("dve", lambda e: e.bn_aggr(out=st.ap[:n, 6:8], in_=st.ap[:n, 0:6]), reads=[st.d()], writes=[st.d()])
                S.op("act", lambda e: e.activation(out=st.ap[:n, 8:9], in_=st.ap[:n, 7:8], func=AF.Ln, bias=EPS),
                     reads=[st.d()], writes=[st.d()])
                S.op("act", lambda e: e.activation(out=st.ap[:n, 9:10], in_=st.ap[:n, 8:9], func=AF.Exp, scale=-0.5), reads=[st.d()], writes=[st.d()])
                S.op("dve", lambda e: e.tensor_scalar(out=yn.ap[:n, :], in0=ps[6].ap[:n, :], scalar1=st.ap[:n, 6:7],
                                                      scalar2=st.ap[:n, 9:10], op0=ALU.subtract, op1=ALU.mult),
                     reads=[ps[6].d(), st.d()], writes=[yn.d()])
                S.op("pool", lambda e: e.tensor_tensor(out=yn.ap[:n, :], in0=yn.ap[:n, :], in1=sg.ap[:n, :], op=ALU.mult),
                     reads=[yn.d(), sg.d()], writes=[yn.d()])
                self.out_proj(yn, n, 4, ygT, wo, t, ps[6], (ps[7], ps[6]))

            def hook(hd):
                S.dma("pool", win.ap.rearrange("p c f -> p (c f)"), ret_in_d[j * 4 + hd], writes=[win.d()])
                S.dma("pool", wos[hd % 2].ap.rearrange("p c f -> p (c f)"), ret_out_d[j * 4 + hd], writes=[wos[hd % 2].d()])
            hook(0)
            self.emit_norm(norm_w)
            run_pipeline(4 * NT, lambda u, ai, slot: stageA(u // NT, u % NT, RB[slot]), lambda u, slot: stageB(u // NT, u % NT, RB[slot]),
                         1, 2, group=NT, hook=lambda hd: hook(hd) if hd else None)

    def out_proj(self, y, n, nch, yT, wo, t, ptr, pouts):
        S, h = self.S, self.h
        for c in range(nch):
            self.tr(ptr.ap[:, c * 128:c * 128 + n], y.ap[:n, c * 128:(c + 1) * 128], n, reads=[y.d()], writes=[ptr.d()])
        S.op("act", lambda e: e.activation(out=yT.ap[:, 0:nch, :n], in_=ptr.ap[:, 0:nch * 128].rearrange("p (c n) -> p c n", c=nch)[:, :, :n],
                                           func=AF.Copy),
             reads=[ptr.d()], writes=[yT.d()])
        for half in range(2):
            po = pouts[half]
            for c in range(nch):
                self.mm(po.ap[:n, :], yT.ap[:, c, :n], wo.ap[:, c, half * 512:(half + 1) * 512], c == 0, c == nch - 1,
                        reads=[yT.d(), wo.d()], writes=[po.d()])
            S.op("dve", lambda e: e.tensor_tensor(out=h.ap[:n, t, half * 512:(half + 1) * 512],
                                                  in0=h.ap[:n, t, half * 512:(half + 1) * 512], in1=po.ap[:n, :], op=ALU.add),
                 reads=[po.d(), h.d(t)], writes=[h.d(t)])


def host_pack(inp, cfg):
    f32 = np.float32
    shared = {}
    pv_cols = []

    def add_pv(name, arr2d):
        cfg[name] = sum(a.shape[1] for a in pv_cols)
        pv_cols.append(np.asarray(arr2d, f32))
    add_pv("pv_norm_mix", np.concatenate([vec_cols(inp["norm_mix"][i]) for i in range(DEPTH)], axis=1))
    add_pv("pv_norm_mlp", np.concatenate([vec_cols(inp["norm_mlp"][i]) for i in range(DEPTH)], axis=1))
    add_pv("pv_gla_bias", vec_cols(inp["gla_gate_bias"][0]))
    add_pv("pv_rwkv", np.concatenate([vec_cols(inp["rwkv_mu"][0][i]) for i in range(6)] + [vec_cols(inp[nm][0].reshape(-1)) for nm in
                                     ("rwkv_w0", "rwkv_a0", "rwkv_k_k", "rwkv_k_a", "rwkv_r_k", "rwkv_ln_w", "rwkv_ln_b")], axis=1))
    add_pv("pv_ssd_conv", np.concatenate([vec_cols(inp["m2_conv_w"][0][jj]) for jj in range(4)] + [vec_cols(inp["m2_conv_b"][0])], axis=1))
    shared["pvec"] = np.ascontiguousarray(np.concatenate(pv_cols, axis=1))
    cfg["npv"] = shared["pvec"].shape[1]
    rv = []

    def add_rv(name, v):
        cfg[name] = sum(a.shape[0] for a in rv)
        rv.append(np.asarray(v, f32).reshape(-1))
    add_rv("rv_norm_final", inp["norm_final"])
    add_rv("rv_gla_norm", inp["gla_norm_w"][0])
    add_rv("rv_ssd", np.concatenate([inp["m2_dt_bias"][0], inp["m2_a_log"][0], inp["m2_d"][0], inp["m2_norm_w"][0]]))
    shared["rvec"] = np.ascontiguousarray(np.concatenate(rv)[None, :])
    cfg["nrv"] = shared["rvec"].shape[1]
    ii = np.arange(128)
    cst = [np.eye(128, dtype=f32), (ii[:, None] <= ii[None, :]).astype(f32), (ii[:, None] < ii[None, :]).astype(f32), np.ones((128, 128), f32)]
    shared["cst"] = np.ascontiguousarray(np.concatenate(cst, axis=1))
    cfg["ncst"] = shared["cst"].shape[1]
    if cfg["mlps"]:
        w_in = inp["mlp_w_in"]
        w_out = inp["mlp_w_out"]
        shared["mlp_in"] = np.stack([pack_rows(w_in[l][:, fb * 512:(fb + 1) * 512]) for l in range(DEPTH) for fb in range(8)])
        shared["mlp_out"] = np.stack([pack_rows(w_out[l][fb * 512:(fb + 1) * 512, :]) for l in range(DEPTH) for fb in range(8)])
    if 0 in cfg["mixers"]:
        ii = np.arange(128)
        same = (ii[:, None] // 64) == (ii[None, :] // 64)
        su = ((ii[:, None] < ii[None, :]) & same).astype(f32)
        sl = ((ii[:, None] > ii[None, :]) & same).astype(f32)
        iu = ((ii[:, None] <= ii[None, :]) & same).astype(f32)
        shared["rw_cst"] = np.ascontiguousarray(np.concatenate([su, sl, iu, same.astype(f32)], axis=1))
        shared["rw_lora"] = pack_rows(np.concatenate([inp["rwkv_w_lora_a"][0], inp["rwkv_a_lora_a"][0], inp["rwkv_g_lora_a"][0]], axis=1))
        pairs, wos, bwas, bgs = [], [], [], []
        for hp in range(8):
            pc = slice(hp * 128, (hp + 1) * 128)
            pairs.append(pack_rows(np.concatenate([inp["rwkv_w_r"][0][:, pc], inp["rwkv_w_k"][0][:, pc], inp["rwkv_w_v"][0][:, pc]], axis=1)))
            wos.append(np.ascontiguousarray(inp["rwkv_w_o"][0][pc, :]))
            bw = np.zeros((128, 256), f32)
            bw[0:64, 0:128] = inp["rwkv_w_lora_b"][0][:, pc]
            bw[64:128, 128:256] = inp["rwkv_a_lora_b"][0][:, pc]
            bwas.append(bw)
            bg = np.zeros((128, 256), f32)
            bg[:, 0:128] = inp["rwkv_g_lora_b"][0][0:128, pc]
            bg[0:32, 128:256] = inp["rwkv_g_lora_b"][0][128:160, pc]
            bgs.append(bg)
        shared["rw_pair"] = np.stack(pairs)
        shared["rw_wo"] = np.stack(wos)
        shared["rw_bwa"] = np.ascontiguousarray(np.stack(bwas))
        shared["rw_bg"] = np.stack(bgs)
    if 1 in cfg["mixers"]:
        W = inp["m2_in_proj"][0]
        ins = []
        for g in range(8):
            cols = np.concatenate([2048 + np.arange(g * 256, (g + 1) * 256), 4096 + np.arange(g * 128, (g + 1) * 128),
                                   5120 + np.arange(g * 128, (g + 1) * 128), np.arange(g * 256, (g + 1) * 256),
                                   6144 + np.arange(g * 4, (g + 1) * 4)])
            ins.append(pack_rows(W[:, cols]))
        shared["ssd_in"] = np.stack(ins)
        shared["ssd_out"] = np.stack([pack_rows(inp["m2_out_proj"][0][g * 256:(g + 1) * 256, :]) for g in range(8)])
        ii = np.arange(128)
        nm = np.where(ii[:, None] <= ii[None, :], 0.0, -30000.0).astype(f32)
        shared["ssd_cst"] = np.ascontiguousarray(np.tile(nm, (1, 4)))
    if 2 in cfg["mixers"]:
        W = inp["gla_in_proj"][0]
        ins = []
        for hd in range(4):
            cols = np.concatenate([np.arange(hd * 128, (hd + 1) * 128), 512 + np.arange(hd * 128, (hd + 1) * 128),
                                   3072 + np.arange(16), 1024 + np.arange(hd * 256, (hd + 1) * 256),
                                   2048 + np.arange(hd * 256, (hd + 1) * 256)])
            ins.append(pack_rows(W[:, cols]))
        shared["gla_in"] = np.stack(ins)
        shared["gla_out"] = np.stack([pack_rows(inp["gla_out_proj"][0][hd * 256:(hd + 1) * 256, :]) for hd in range(4)])
        shared["gla_gup"] = np.ascontiguousarray(inp["gla_gate_up"]).astype(f32)
    if 3 in cfg["mixers"]:
        W = inp["ret_in_proj"][0]
        ins = []
        for hd in range(4):
            cols = np.concatenate([np.arange(hd * 256, (hd + 1) * 256), 1024 + np.arange(hd * 256, (hd + 1) * 256),
                                   2048 + np.arange(hd * 512, (hd + 1) * 512), 4096 + np.arange(hd * 512, (hd + 1) * 512)])
            ins.append(pack_rows(W[:, cols]))
        shared["ret_in"] = np.stack(ins)
        shared["ret_out"] = np.stack([pack_rows(inp["ret_out_proj"][0][hd * 512:(hd + 1) * 512, :]) for hd in range(4)])
        ii = np.arange(128)
        dec, rowp, kd128, kd16 = [], [], [], []
        for hd in range(4):
            lg = np.log1p(-np.exp2(-5.0 - hd))
            diff = (ii[None, :] - ii[:, None]).astype(np.float64)
            dec.append(np.where(diff >= 0, np.exp(lg * diff), 0.0))
            rowp.append(np.broadcast_to(np.exp(lg * (ii + 1.0))[None, :], (128, 128)))
            kd128.append(np.exp(lg * (127.0 - ii)))
            kd16.append(np.exp(lg * (15.0 - ii)))
        inv_freq = (1.0 / (10000.0 ** np.linspace(0.0, 1.0, 128, dtype=f32))).astype(f32)
        ang = (np.arange(L, dtype=f32)[None, :] * inv_freq[:, None]).astype(f32).astype(np.float64)
        shared["ret_cst"] = np.ascontiguousarray(np.concatenate(
            dec + rowp + [np.stack(kd128, 1), np.stack(kd16, 1), np.cos(ang), np.sin(ang)], axis=1).astype(f32))
        cfg["nretc"] = shared["ret_cst"].shape[1]
    return shared


def run(inputs, cfg):
    inp = {k: np.asarray(v) for k, v in inputs.items()}
    shared = host_pack(inp, cfg)
    b = Builder(cfg)
    nc = b.build()
    in_maps = []
    for c in range(8):
        m = dict(shared)
        m["x"] = np.ascontiguousarray(inp["x"][c])
        m["meta"] = np.ascontiguousarray(inp["meta_tokens"])
        in_maps.append(m)
    ncores = cfg.get("ncores", 8)
    in_maps = in_maps[:ncores]
    res = run_bass_kernel_spmd(nc, in_maps, core_ids=list(range(ncores)))
    out = np.stack([np.asarray(r["out"]) for r in res.results]).astype(np.float32)
    if cfg.get("debug"):
        return out, np.stack([np.asarray(r["dbg"]) for r in res.results])
    return out


def kernel(**inputs):
    cfg = {"mixers": [0, 1, 2, 3], "mlps": [0, 1, 2, 3]}
    return run(inputs, cfg)
```
